# Optimizing a Trainium2 kernel written in Bass

```python
import jax, jax.numpy as jnp
from jax import lax
import numpy as np

D_MODEL = 1024
BATCH = 8
SEQ = 4096
DEPTH = 1
DEC_BATCH = 2
DEC_SEQ = 8192
PAST_LEN = 128

N_META = 16
GRID_W = 64
EPS = 1e-6
NEG = -1e30
NA_HEADS = 8
NA_HEAD_DIM = 64
NA_WIDTH = NA_HEADS * NA_HEAD_DIM
NA_WIN_ROWS = 8
NA_WIN_COLS = 16
NA_QBLOCK = 16
NA_KCOLS = 32
ML_HEADS = 4
ML_HEAD_DIM = 128
ML_WIDTH = ML_HEADS * ML_HEAD_DIM
ML_CHUNK = 64
ML_CONV = 3
SPLITS = (NA_WIDTH, NA_WIDTH, NA_WIDTH, NA_WIDTH, 2 * ML_WIDTH, ML_WIDTH, ML_WIDTH, ML_WIDTH, 4 * ML_HEADS, 2 * D_MODEL)
N_IN = sum(SPLITS)
SPLIT_IDX = tuple(int(s) for s in np.cumsum(SPLITS)[:-1])
GATE_OFF = 4 * NA_WIDTH + 5 * ML_WIDTH

kernel_name = 'hybrid_na_mlstm_encoder'


def _rmsnorm(x, g):
    xf = x.astype(jnp.float32)
    y = xf * lax.rsqrt(jnp.mean(xf * xf, axis=-1, keepdims=True) + EPS)
    return (y * g.astype(jnp.float32)).astype(x.dtype)


def _na_column_tables():
    n_cb = GRID_W // NA_QBLOCK
    c = np.arange(GRID_W).reshape(n_cb, NA_QBLOCK)
    c0 = np.clip(c - NA_WIN_COLS // 2, 0, GRID_W - NA_WIN_COLS)
    kc0 = np.clip(np.arange(n_cb) * NA_QBLOCK - NA_WIN_COLS // 2, 0, GRID_W - NA_KCOLS)
    kc = kc0[:, None] + np.arange(NA_KCOLS)[None]
    valid = (kc[:, None, :] >= c0[..., None]) & (kc[:, None, :] < c0[..., None] + NA_WIN_COLS)
    dc = np.clip(kc[:, None, :] - c[..., None] + NA_WIN_COLS - 1, 0, 2 * NA_WIN_COLS - 2)
    return kc, valid, dc


def _neighbourhood_attention(q, k, v, rpb):
    B, L, H, dh = q.shape
    T = L - N_META
    rows = T // GRID_W
    wr = min(NA_WIN_ROWS, rows)
    n_cb = GRID_W // NA_QBLOCK
    scale = dh ** -0.5
    kc, valid, dc = _na_column_tables()
    qm, km, vm = q[:, :N_META], k[:, :N_META], v[:, :N_META]
    s_mm = jnp.einsum('bqhd,bkhd->bhqk', qm, km).astype(jnp.float32) * scale
    p_mm = jax.nn.softmax(s_mm, axis=-1).astype(v.dtype)
    out_meta = jnp.einsum('bhqk,bkhd->bqhd', p_mm, vm)
    qg = q[:, N_META:].reshape(B, rows, n_cb, NA_QBLOCK, H, dh)
    kg = k[:, N_META:].reshape(B, rows, GRID_W, H, dh)
    vg = v[:, N_META:].reshape(B, rows, GRID_W, H, dh)
    bias_tab = rpb.astype(jnp.float32)[:, :, dc]

    def row_block(r):
        r0 = jnp.clip(r - wr // 2, 0, rows - wr)
        k_blk = lax.dynamic_slice_in_dim(kg, r0, wr, axis=1)[:, :, kc]
        v_blk = lax.dynamic_slice_in_dim(vg, r0, wr, axis=1)[:, :, kc]
        q_blk = lax.dynamic_index_in_dim(qg, r, axis=1, keepdims=False)
        s_win = jnp.einsum('bcqhd,bwckhd->bhcqwk', q_blk, k_blk).astype(jnp.float32) * scale
        dr = r0 + jnp.arange(wr) - r + (NA_WIN_ROWS - 1)
        bias = jnp.take(bias_tab, dr, axis=1).transpose(0, 2, 3, 1, 4)
        s_win = jnp.where(valid[:, :, None, :], s_win + bias, NEG)
        s_win = s_win.reshape(B, H, n_cb, NA_QBLOCK, wr * NA_KCOLS)
        s_met = jnp.einsum('bcqhd,bmhd->bhcqm', q_blk, km).astype(jnp.float32) * scale
        p = jax.nn.softmax(jnp.concatenate([s_met, s_win], axis=-1), axis=-1).astype(v.dtype)
        p_met = p[..., :N_META]
        p_win = p[..., N_META:].reshape(B, H, n_cb, NA_QBLOCK, wr, NA_KCOLS)
        return (jnp.einsum('bhcqm,bmhd->bcqhd', p_met, vm)
                + jnp.einsum('bhcqwk,bwckhd->bcqhd', p_win, v_blk))

    out = lax.map(row_block, jnp.arange(rows))
    out_grid = jnp.moveaxis(out, 0, 1).reshape(B, T, H, dh)
    return jnp.concatenate([out_meta, out_grid], axis=1)


def _mlstm_chunkwise(q, k, v, ig, lf):
    B, H, L, d = q.shape
    nc = L // ML_CHUNK
    q = q.reshape(B, H, nc, ML_CHUNK, d)
    k = k.reshape(B, H, nc, ML_CHUNK, d)
    v = v.reshape(B, H, nc, ML_CHUNK, d)
    ig = ig.reshape(B, H, nc, ML_CHUNK)
    lf = lf.reshape(B, H, nc, ML_CHUNK)
    b = jnp.cumsum(lf, axis=-1)
    g = b[..., -1]
    a = g[..., None] - b + ig
    m_loc = jnp.max(a, axis=-1)
    w = jnp.exp(a - m_loc[..., None])
    dC = jnp.einsum('bhnsv,bhnsk->bhnvk', w[..., None] * v, k)
    dn = jnp.einsum('bhns,bhnsk->bhnk', w, k)

    def step(carry, xs):
        C, n, m = carry
        g_c, ml_c, dC_c, dn_c = xs
        m_new = jnp.maximum(g_c + m, ml_c)
        fs = jnp.exp(g_c + m - m_new)
        isc = jnp.exp(ml_c - m_new)
        C_new = fs[..., None, None] * C + isc[..., None, None] * dC_c
        n_new = fs[..., None] * n + isc[..., None] * dn_c
        return (C_new, n_new, m_new), (C, n, m)

    init = (jnp.zeros((B, H, d, d), q.dtype), jnp.zeros((B, H, d), q.dtype), jnp.zeros((B, H), q.dtype))
    xs = (jnp.moveaxis(g, 2, 0), jnp.moveaxis(m_loc, 2, 0), jnp.moveaxis(dC, 2, 0), jnp.moveaxis(dn, 2, 0))
    _, (Cs, ns, ms) = lax.scan(step, init, xs)
    Cs = jnp.moveaxis(Cs, 0, 2)
    ns = jnp.moveaxis(ns, 0, 2)
    ms = jnp.moveaxis(ms, 0, 2)
    lower = np.tril(np.ones((ML_CHUNK, ML_CHUNK), dtype=bool))
    logD = jnp.where(lower, b[..., :, None] - b[..., None, :] + ig[..., None, :], NEG)
    inter = b + ms[..., None]
    m_t = jnp.maximum(jnp.max(logD, axis=-1), inter)
    S = jnp.einsum('bhnjd,bhnsd->bhnjs', q, k) * jnp.exp(logD - m_t[..., None])
    e_inter = jnp.exp(inter - m_t)
    num = jnp.einsum('bhnjs,bhnsv->bhnjv', S, v) + e_inter[..., None] * jnp.einsum('bhnvk,bhnjk->bhnjv', Cs, q)
    den = jnp.sum(S, axis=-1) + e_inter * jnp.einsum('bhnk,bhnjk->bhnj', ns, q)
    h = num / jnp.maximum(jnp.abs(den), jnp.exp(-m_t))[..., None]
    return h.reshape(B, H, L, d)


def _mlstm_branch(qk_pre, v, gate_pre, o_pre, conv_w, head_g):
    B, L, _ = v.shape
    f32 = jnp.float32
    cw = conv_w.astype(qk_pre.dtype)[:, None, :]
    qk = jax.nn.silu(lax.conv_general_dilated(qk_pre, cw, (1,), 'SAME',
                                              dimension_numbers=('NWC', 'WIO', 'NWC'),
                                              feature_group_count=qk_pre.shape[-1]))
    q, k = jnp.split(qk, 2, axis=-1)

    def heads(t):
        return t.astype(f32).reshape(B, L, ML_HEADS, ML_HEAD_DIM).transpose(0, 2, 1, 3)

    q, k, vh = heads(q), heads(k) * (ML_HEAD_DIM ** -0.5), heads(v)
    gates = gate_pre.astype(f32).reshape(B, L, 4, ML_HEADS).transpose(2, 0, 3, 1)
    ig_f, lf_f = gates[0], jax.nn.log_sigmoid(gates[1])
    ig_b, lf_b = gates[2], jax.nn.log_sigmoid(gates[3])
    P = ML_CHUNK - N_META

    def pad_t(t, val):
        return jnp.pad(t, [(0, 0), (0, 0), (P, 0)] + [(0, 0)] * (t.ndim - 3), constant_values=val)

    def flip(t):
        return jnp.flip(t, axis=2)

    qp, kp, vp = pad_t(q, 0.0), pad_t(k, 0.0), pad_t(vh, 0.0)
    h_f = _mlstm_chunkwise(qp, kp, vp, pad_t(ig_f, NEG), pad_t(lf_f, 0.0))
    h_b = flip(_mlstm_chunkwise(flip(qp), flip(kp), flip(vp), flip(pad_t(ig_b, NEG)), flip(pad_t(lf_b, 0.0))))
    h = (h_f + h_b)[:, :, P:].transpose(0, 2, 1, 3)
    h = jax.nn.sigmoid(o_pre.astype(f32)).reshape(B, L, ML_HEADS, ML_HEAD_DIM) * h
    mu = jnp.mean(h, axis=-1, keepdims=True)
    var = jnp.mean(jnp.square(h - mu), axis=-1, keepdims=True)
    h = (h - mu) * lax.rsqrt(var + EPS)
    return (h.reshape(B, L, ML_WIDTH) * head_g.astype(f32)).astype(v.dtype)


def _encoder_layer(h, g_pre, w_in, b_in, na_rpb, ml_conv_w, ml_head_g, w_a, w_b, w_out, g_post):
    B, L, _ = h.shape
    xn = _rmsnorm(h, g_pre)
    proj = xn @ w_in + b_in.astype(xn.dtype)
    na_q, na_k, na_v, na_z, ml_qk, ml_v, ml_z, ml_o, ml_g, mg = jnp.split(proj, SPLIT_IDX, axis=-1)
    hd = (B, L, NA_HEADS, NA_HEAD_DIM)
    ya = _neighbourhood_attention(na_q.reshape(hd), na_k.reshape(hd), na_v.reshape(hd), na_rpb)
    ya = (ya.reshape(B, L, NA_WIDTH) * jax.nn.silu(na_z)) @ w_a
    yb = (_mlstm_branch(ml_qk, ml_v, ml_g, ml_o, ml_conv_w, ml_head_g) * jax.nn.silu(ml_z)) @ w_b
    ga, gb = jnp.split(jax.nn.sigmoid(mg), 2, axis=-1)
    y = (ga * ya + gb * yb) @ w_out
    return h + _rmsnorm(y, g_post)


def _encoder_trunk(x, meta_tokens, g_pre, w_in, b_in, na_rpb, ml_conv_w, ml_head_g, w_a, w_b, w_out, g_post):
    B = x.shape[0]
    meta = jnp.broadcast_to(meta_tokens.astype(x.dtype)[None], (B, N_META, x.shape[-1]))
    h = jnp.concatenate([meta, x], axis=1)
    for l in range(DEPTH):
        h = _encoder_layer(h, g_pre[l], w_in[l], b_in[l], na_rpb[l], ml_conv_w[l], ml_head_g[l],
                           w_a[l], w_b[l], w_out[l], g_post[l])
    return h[:, N_META:]


def setup_inputs(seed: int = 0) -> dict:
    key = jax.random.key(seed)
    ks = jax.random.split(key, 16)
    f32 = jnp.float32
    nrm = jax.random.normal
    x_prompt = nrm(ks[0], (BATCH, SEQ, D_MODEL), f32)
    x_sample = nrm(ks[1], (DEC_BATCH, DEC_SEQ, D_MODEL), f32)
    meta_tokens = nrm(ks[2], (N_META, D_MODEL), f32)
    g_pre = 1.0 + 0.02 * nrm(ks[3], (DEPTH, D_MODEL), f32)
    w_in = nrm(ks[4], (DEPTH, D_MODEL, N_IN), f32) * (D_MODEL ** -0.5)
    b_in = 0.02 * nrm(ks[5], (DEPTH, N_IN), f32)
    fb = jnp.linspace(3.0, 6.0, ML_HEADS, dtype=f32)
    b_in = b_in.at[:, GATE_OFF + ML_HEADS:GATE_OFF + 2 * ML_HEADS].add(fb)
    b_in = b_in.at[:, GATE_OFF + 3 * ML_HEADS:GATE_OFF + 4 * ML_HEADS].add(fb)
    na_rpb = 0.1 * nrm(ks[6], (DEPTH, NA_HEADS, 2 * NA_WIN_ROWS - 1, 2 * NA_WIN_COLS - 1), f32)
    ml_conv_w = nrm(ks[7], (DEPTH, ML_CONV, 2 * ML_WIDTH), f32) * (ML_CONV ** -0.5)
    ml_head_g = 1.0 + 0.02 * nrm(ks[8], (DEPTH, ML_WIDTH), f32)
    w_a = nrm(ks[9], (DEPTH, NA_WIDTH, D_MODEL), f32) * (NA_WIDTH ** -0.5)
    w_b = nrm(ks[10], (DEPTH, ML_WIDTH, D_MODEL), f32) * (ML_WIDTH ** -0.5)
    w_out = nrm(ks[11], (DEPTH, D_MODEL, D_MODEL), f32) * (D_MODEL ** -0.5)
    g_post = 1.0 + 0.02 * nrm(ks[12], (DEPTH, D_MODEL), f32)
    return {'x_prompt': x_prompt, 'x_sample': x_sample, 'meta_tokens': meta_tokens, 'g_pre': g_pre,
            'w_in': w_in, 'b_in': b_in, 'na_rpb': na_rpb, 'ml_conv_w': ml_conv_w, 'ml_head_g': ml_head_g,
            'w_a': w_a, 'w_b': w_b, 'w_out': w_out, 'g_post': g_post}


def reference(x_prompt, x_sample, meta_tokens, g_pre, w_in, b_in, na_rpb, ml_conv_w, ml_head_g, w_a, w_b, w_out, g_post):
    y_prompt = _encoder_trunk(x_prompt, meta_tokens, g_pre, w_in, b_in, na_rpb, ml_conv_w, ml_head_g,
                              w_a, w_b, w_out, g_post)
    y_sample = _encoder_trunk(x_sample, meta_tokens, g_pre, w_in, b_in, na_rpb, ml_conv_w, ml_head_g,
                              w_a, w_b, w_out, g_post)
    return (y_prompt, y_sample)
```

```python
from contextlib import ExitStack
import numpy as np
import concourse.bass as bass
import concourse.mybir as mybir
from concourse.bass_utils import run_bass_kernel_spmd

F32 = mybir.dt.float32
BF16 = mybir.dt.bfloat16
AF = mybir.ActivationFunctionType
ALU = mybir.AluOpType
AX = mybir.AxisListType

ENGS = ("pe", "act", "dve", "pool", "sp")
NOSYNC = set()


class Op:
    __slots__ = ("eng", "emit", "deps", "dma", "semkey", "tok", "sig")

    def __init__(self, eng, emit, dma, semkey):
        self.eng = eng
        self.emit = emit
        self.deps = []
        self.dma = dma
        self.semkey = semkey
        self.tok = None
        self.sig = False


class Prog:
    def __init__(self, nc, same_engine_sync=True):
        self.nc = nc
        self.q = {e: [] for e in ENGS}
        self.last_w = {}
        self.readers = {}
        self.same_engine_sync = same_engine_sync

    def add(self, eng, emit, reads=(), writes=(), dma=False, semkey=None):
        op = Op(eng, emit, dma, semkey)
        self.count = getattr(self, "count", 0) + 1
        if self.count > getattr(self, "limit", 10 ** 9):
            return op
        deps = {}
        for k in reads:
            w = self.last_w.get(k)
            if w is not None:
                deps[id(w)] = w
            if isinstance(k, str) and k.startswith("ps"):
                for r in self.readers.get(k, ()):
                    if r.eng != eng:
                        deps[id(r)] = r
        for k in writes:
            w = self.last_w.get(k)
            if w is not None:
                deps[id(w)] = w
            for r in self.readers.get(k, ()):
                deps[id(r)] = r
        op.deps = list(deps.values())
        for k in writes:
            self.last_w[k] = op
            self.readers[k] = []
        for k in reads:
            self.readers.setdefault(k, []).append(op)
        self.q[eng].append(op)
        return op

    def pe(self, emit, reads=(), writes=()):
        return self.add("pe", emit, reads, writes)

    def act(self, emit, reads=(), writes=()):
        return self.add("act", emit, reads, writes)

    def dve(self, emit, reads=(), writes=()):
        return self.add("dve", emit, reads, writes)

    def pool(self, emit, reads=(), writes=()):
        return self.add("pool", emit, reads, writes)

    def ew(self, eng, emit, reads=(), writes=()):
        return self.add(eng, emit, reads, writes)

    def dma(self, emit, reads=(), writes=(), semkey=None, queue="sp"):
        return self.add(queue, emit, reads, writes, dma=True, semkey=semkey)

    def _skip(self, d, op):
        if d.dma and op.dma and isinstance(d.semkey, str) and d.semkey.startswith("setup") and d.semkey == op.semkey:
            return True
        return (not d.dma) and (not op.dma) and d.eng == op.eng and (d.eng == "pe" or d.eng in NOSYNC or not self.same_engine_sync)

    def finalize(self, stack):
        nc = self.nc
        for e in ENGS:
            for op in self.q[e]:
                for d in op.deps:
                    if d.dma or self._skip(d, op):
                        continue
                    d.sig = True
        eng_sems = {e: [stack.enter_context(nc.semaphore("s_%s0" % e))] for e in ENGS}
        dma_sems = {}
        dma_cnt = {}
        LIM = 30000
        for e in ENGS:
            cnt = 0
            for op in self.q[e]:
                if op.dma:
                    if op.semkey not in dma_sems:
                        dma_sems[op.semkey] = stack.enter_context(nc.semaphore("d%d" % len(dma_sems)))
                        dma_cnt[op.semkey] = 0
                    dma_cnt[op.semkey] += 16
                    op.tok = (dma_sems[op.semkey], dma_cnt[op.semkey])
                elif op.sig:
                    if cnt >= LIM:
                        eng_sems[e].append(stack.enter_context(nc.semaphore("s_%s%d" % (e, len(eng_sems[e])))))
                        cnt = 0
                    cnt += 1
                    op.tok = (eng_sems[e][-1], cnt)
        assert max([0] + list(dma_cnt.values())) < 60000, "dma sem overflow"
        for e in ENGS:
            for op in self.q[e]:
                if op.dma and isinstance(op.semkey, str) and op.semkey.startswith("setup"):
                    op.tok = (dma_sems[op.semkey], dma_cnt[op.semkey])
        self.dma_sems = dma_sems
        self.dma_cnt = dma_cnt
        block = stack.enter_context(nc.Block())
        prog = self

        def run_queue(e, engine):
            known = {}
            for op in prog.q[e]:
                need = {}
                for d in op.deps:
                    if prog._skip(d, op):
                        continue
                    sem, val = d.tok
                    key = sem.num
                    if known.get(key, 0) >= val:
                        continue
                    if key not in need or need[key][1] < val:
                        need[key] = (sem, val)
                for key, (sem, val) in need.items():
                    engine.wait_ge(sem, val)
                    known[key] = val
                ins = op.emit(engine)
                if op.dma:
                    ins.then_inc(op.tok[0], 16)
                elif op.sig:
                    ins.then_inc(op.tok[0], 1)

        @block.tensor
        def _(eng):
            run_queue("pe", eng)

        @block.scalar
        def _(eng):
            run_queue("act", eng)

        @block.vector
        def _(eng):
            run_queue("dve", eng)

        @block.gpsimd
        def _(eng):
            run_queue("pool", eng)

        @block.sync
        def _(eng):
            run_queue("sp", eng)
            for key, sem in prog.dma_sems.items():
                eng.wait_ge(sem, prog.dma_cnt[key])


def MM(out, lhsT, rhs, start=True, stop=True):
    return lambda e: e.matmul(out, lhsT=lhsT, rhs=rhs, start=start, stop=stop)


def TR(out, in_, ident):
    return lambda e: e.transpose(out=out, in_=in_, identity=ident)


def ACTF(out, in_, func, bias=None, scale=1.0, accum_out=None):
    def f(e):
        kw = {}
        if bias is not None:
            kw["bias"] = bias
        if accum_out is not None:
            kw["accum_out"] = accum_out
        return e.activation(out=out, in_=in_, func=func, scale=scale, **kw)
    return f


def TT(out, in0, in1, op):
    return lambda e: e.tensor_tensor(out=out, in0=in0, in1=in1, op=op)


def TS(out, in0, s1, s2=None, op0=ALU.mult, op1=None):
    if op1 is None:
        return lambda e: e.tensor_scalar(out=out, in0=in0, scalar1=s1, scalar2=None, op0=op0)
    return lambda e: e.tensor_scalar(out=out, in0=in0, scalar1=s1, scalar2=s2, op0=op0, op1=op1)


def STT(out, in0, scalar, in1, op0, op1):
    return lambda e: e.scalar_tensor_tensor(out=out, in0=in0, scalar=scalar, in1=in1, op0=op0, op1=op1)


def CP(out, in_):
    return lambda e: e.tensor_copy(out=out, in_=in_)


def MS(ap, val):
    return lambda e: e.memset(ap, val)


def RECIP(out, in_):
    return lambda e: e.reciprocal(out=out, in_=in_)


def DMA(out, in_):
    return lambda e: e.dma_start(out=out, in_=in_)


D = 1024
NIN = 6672
GATE_OFF = 4608
EPS = 1e-6
KAPPA = 0.25 * (128.0 ** -0.5)
CH_OFF = [0, 512, 1024, 1536, 2048, 2560, 3072, 3584, 4096, 4624, 5136, 5648, 6160]
C_NAQ, C_NAK, C_NAV, C_NAZ, C_MLQ, C_MLK, C_MLV, C_MLZ, C_MLO, C_GA0, C_GA1, C_GB0, C_GB1 = range(13)
C_WA, C_WB, C_WO0, C_WO1 = 13, 14, 15, 16
FM_B = {C_NAQ: 0, C_NAK: 4, C_MLQ: 8, C_MLK: 12, C_GA0: 16, C_GA1: 20, C_GB0: 24, C_GB1: 28}
FM_CW = 32
FM_G = 56
FM_N = 64
TB = {C_NAV: 0, C_NAZ: 512, C_MLV: 1024, C_MLZ: 1536, C_MLO: 2048}


LIMIT = [10 ** 9]


def build(R, dbg=False, stop=99):
    U = 2
    NR = U * R
    NTOK = NR * 64
    NT = NR // 8
    NS = NR // 2
    TPU = R // 8
    RING_T = 3
    RROWS = RING_T * 8
    nc = bass.Bass("TRN2", target_bir_lowering=False)

    def din(name, shape):
        return nc.dram_tensor(name, shape, F32, kind="ExternalInput").ap()

    xu = din("xu", [NTOK, D])
    meta = din("meta", [16, D])
    g_pre = din("g_pre", [1, D])
    w_in = din("w_in", [D, NIN])
    b_in = din("b_in", [1, NIN])
    tz = din("tz", [64, 8, 15, 64])
    cmask = din("cmask", [128, 64])
    conv_w = din("conv_w", [3, D])
    head_g = din("head_g", [1, 512])
    w_a = din("w_a", [512, D])
    w_b = din("w_b", [512, D])
    w_out = din("w_out", [D, D])
    g_post = din("g_post", [1, D])
    flags = din("flags", [128, 4])
    y = nc.dram_tensor("y", [NTOK, D], F32, kind="ExternalOutput").ap()
    wq = nc.dram_tensor("wq", [17, 128, 8, 512], BF16, kind="Internal").ap()
    cbs = nc.dram_tensor("cbs", [NS, 128, 4 * 129], BF16, kind="Internal").ap()
    dbg_out = {}
    if dbg:
        dbg_out["d_h"] = nc.dram_tensor("d_h", [NTOK, 512], F32, kind="ExternalOutput").ap()
        dbg_out["d_na"] = nc.dram_tensor("d_na", [NTOK, 512], F32, kind="ExternalOutput").ap()

    st = ExitStack()
    with st:
        def sb(name, shape, dt=F32):
            return st.enter_context(nc.sbuf_tensor(name, shape, dt))

        P = Prog(nc)
        P.limit = LIMIT[0]
        psb = [st.enter_context(nc.psum_tensor("ps%d" % i, [128, 512], F32)) for i in range(4)]
        psS = [st.enter_context(nc.psum_tensor("psS%d" % i, [128, 1024], F32)) for i in range(2)]
        psb += [psS[0][:, 0:512], psS[0][:, 512:1024], psS[1][:, 0:512], psS[1][:, 512:1024]]
        PK = ["ps%d" % i for i in range(8)]

        ident = sb("ident", [128, 128], BF16)
        identf = sb("identf", [128, 128])
        maskf = sb("maskf", [128, 128])
        maskb = sb("maskb", [128, 128])
        fl = sb("fl", [128, 4])
        fm = sb("fm", [128, FM_N])
        fmh = sb("fmh", [128, FM_N])
        cst = sb("cst", [128, 2])
        gbias = sb("gbias", [8, 2])
        wg = sb("wg", [128, 8, 16], BF16)
        etab = sb("etab", [128, 8, 16, 64], BF16)
        bias_bc = sb("bias_bc", [128, 2560])
        gpost_bc = sb("gpost_bc", [128, D])
        hg_bc = sb("hg_bc", [128, 512])
        SC = sb("SC", [128, NS + U, 24])
        EG = sb("EG", [128, NS + U, 8])
        kmT = sb("kmT", [128, 4, 16], BF16)
        vmp = sb("vmp", [128, 8, 65], BF16)
        prem = sb("prem", [128, 8])
        premk = sb("premk", [128, 4, 16])
        vmeta = sb("vmeta", [128, 4, 129], BF16)
        kmetaT = sb("kmetaT", [128, 4, 128], BF16)
        firstpre = sb("firstpre", [128, U, 8, 1])
        Cmeta = sb("Cmeta", [128, U, 4, 129])
        Cb = sb("Cb", [128, 4, 129])
        Cf = sb("Cf", [128, 4, 129])
        Cfb = sb("Cfb", [128, 4, 129], BF16)
        ws = [sb("ws%d" % i, [128, 8, 512], BF16) for i in range(3)]
        xs = [sb("xs%d" % i, [128, D]) for i in range(2)]
        xnb = [sb("xnb%d" % i, [128, D], BF16) for i in range(2)]
        xnT = [sb("xnT%d" % i, [128, 8, 512], BF16) for i in range(2)]
        st1 = sb("st1", [128, 16])
        st2 = sb("st2", [128, 32])
        KT = sb("KT", [128, 4, RROWS * 64], BF16)
        VP = sb("VP", [128, RROWS // 2, 8, 65], BF16)
        zs = sb("zs", [128, 4, 512], BF16)
        oG = sb("oG", [128, 4, 512], BF16)
        pre = [sb("pre0", [128, 514])] * 2
        cvt = [sb("cvt0", [128, 512])] * 2
        qkT = sb("qkT", [128, 8, 512], BF16)
        carry = sb("carry", [128, 8, 2])
        halo = sb("halo", [128, 8, 2])
        hadj = sb("hadj", [128, 8, 2])
        vml = sb("vml", [128, 4, 4, 129], BF16)
        kk = sb("kk", [128, 4, 128], BF16)
        Sf = [sb("Sf%d" % i, [128, 128], BF16) for i in range(4)]
        Sb_ = [sb("Sb%d" % i, [128, 128], BF16) for i in range(4)]
        uv = [sb("uv%d" % i, [128, 129], BF16) for i in range(2)]
        cbl = [sb("cbl%d" % i, [128, 4, 129], BF16) for i in range(2)]
        cbst = [sb("cbst%d" % i, [128, 4, 129], BF16) for i in range(2)]
        hbuf = sb("hbuf", [128, 512])
        h2 = sb("h2", [128, 512])
        gs2 = sb("gs2", [8, 8])
        onesg = sb("onesg", [8, 128])
        ypT = sb("ypT", [128, 8, 512], BF16)
        gsc = ypT[0:8, :, :].rearrange("p a b -> p (a b)").bitcast(F32).rearrange("p (a b) -> p a b", a=4)
        nao = [hbuf, h2]
        HBK = [("hbuf", h) for h in range(4)]
        NAOK = [HBK, ["h2"]]

        cont = fl[:, 0:1]
        ncont = fl[:, 1:2]

        for (dst, so) in ((0, 0), (4, 8), (8, 4), (12, 12)):
            src = w_in[:, GATE_OFF + so:GATE_OFF + so + 4].rearrange("(kc p) j -> p kc j", p=128)
            P.dma(DMA(wg[:, :, dst:dst + 4], src), writes=["wg"], semkey="setup_w", queue="pool")
        for c in (C_NAK, C_NAV, C_MLQ, C_MLK, C_MLV, C_NAQ, C_NAZ, C_MLZ, C_MLO, C_GA0, C_GA1, C_GB0, C_GB1):
            src = w_in[:, CH_OFF[c]:CH_OFF[c] + 512].rearrange("(kc p) j -> p kc j", p=128)
            P.dma(DMA(wq[c], src), writes=[("wq", c)], semkey=("wqc", c), queue="pool")
        P.dma(DMA(wq[C_WA].rearrange("p a b -> p (a b)").rearrange("p (fc n) -> p fc n", fc=4),
                  w_a.rearrange("(fc p) n -> p fc n", p=128)), writes=[("wq", C_WA)], semkey=("wqc", C_WA), queue="pool")
        P.dma(DMA(wq[C_WB].rearrange("p a b -> p (a b)").rearrange("p (fc n) -> p fc n", fc=4),
                  w_b.rearrange("(fc p) n -> p fc n", p=128)), writes=[("wq", C_WB)], semkey=("wqc", C_WB), queue="pool")
        for hf in range(2):
            P.dma(DMA(wq[C_WO0 + hf], w_out[:, hf * 512:(hf + 1) * 512].rearrange("(fc p) n -> p fc n", p=128)),
                  writes=[("wq", C_WO0 + hf)], semkey=("wqc", C_WO0 + hf), queue="pool")

        P.dma(DMA(fl[:], flags), writes=["fl"], semkey="setup_c")
        P.dma(DMA(bias_bc[:, 0:1024], b_in[:, 1024:2048].partition_broadcast(128)), writes=["bias_bc"], semkey="setup_c")
        P.dma(DMA(bias_bc[:, 1024:2560], b_in[:, 3072:4608].partition_broadcast(128)), writes=["bias_bc"], semkey="setup_c")
        P.dma(DMA(gpost_bc[:], g_post.partition_broadcast(128)), writes=["gpost_bc"], semkey="setup_c")
        P.dma(DMA(hg_bc[:], head_g.partition_broadcast(128)), writes=["hg_bc"], semkey="setup_c")
        P.pool(MS(identf[:], 0.0), writes=["identf"])
        P.pool(lambda e: e.affine_select(out=identf[:], in_=identf[:], pattern=[[-1, 128]], compare_op=ALU.not_equal,
                                         fill=1.0, base=0, channel_multiplier=1), reads=["identf"], writes=["identf"])
        P.dve(CP(ident[:], identf[:]), reads=["identf"], writes=["ident"])
        P.pool(MS(maskf[:], 1.0), writes=["maskf"])
        P.pool(lambda e: e.affine_select(out=maskf[:], in_=maskf[:], pattern=[[1, 128]], compare_op=ALU.is_ge,
                                         fill=0.0, base=0, channel_multiplier=-1), reads=["maskf"], writes=["maskf"])
        P.pool(MS(maskb[:], 1.0), writes=["maskb"])
        P.pool(lambda e: e.affine_select(out=maskb[:], in_=maskb[:], pattern=[[-1, 128]], compare_op=ALU.is_ge,
                                         fill=0.0, base=0, channel_multiplier=1), reads=["maskb"], writes=["maskb"])
        P.pool(MS(cst[:, 0:1], -0.5), writes=["cst"])
        P.pool(MS(onesg[:], 1.0), writes=["onesg"])
        P.pool(MS(Cb[:], 0.0), writes=["Cb"])
        P.pool(MS(VP[:, :, :, 64:65], 1.0), writes=[("VP", i) for i in range(RING_T)])
        P.pool(MS(vmp[:], 0.0), writes=["vmp"])
        P.pool(MS(vmp[0:16, :, 64:65], 1.0), reads=["vmp"], writes=["vmp"])
        P.pool(MS(vml[:, :, :, 128:129], 1.0), writes=[("vml", j) for j in range(4)])
        P.pool(MS(carry[:], 0.0), writes=["carry"])
        P.pool(MS(vmeta[:], 0.0), writes=["vmeta"])
        P.pool(MS(kmetaT[:], 0.0), writes=["kmetaT"])

        stg = ExitStack()
        rowst = stg.enter_context(nc.sbuf_tensor("rowst", [64, 128], F32))
        tzs = stg.enter_context(nc.sbuf_tensor("tzs", [128, 4, 16, 64], F32))
        cmk = stg.enter_context(nc.sbuf_tensor("cmk", [128, 64], F32))
        xnTm = stg.enter_context(nc.sbuf_tensor("xnTm", [128, 8, 128], BF16))
        P.pool(MS(xnTm[:], 0.0), writes=["xnTm"])
        if True:
            P.pool(MS(rowst[:], 0.0), writes=["rowst"])
            for c, col in FM_B.items():
                P.dma(DMA(rowst[col:col + 4, :], b_in[0, CH_OFF[c]:CH_OFF[c] + 512].rearrange("(c p) -> c p", p=128)),
                      writes=["rowst"], semkey="setup_c")
            for j in range(3):
                P.dma(DMA(rowst[FM_CW + 8 * j:FM_CW + 8 * j + 8, :], conv_w[j, :].rearrange("(c p) -> c p", p=128)),
                      writes=["rowst"], semkey="setup_c")
            P.dma(DMA(rowst[FM_G:FM_G + 8, :], g_pre[0, :].rearrange("(c p) -> c p", p=128)), writes=["rowst"], semkey="setup_c")
            P.pe(MM(psb[0][:, 0:FM_N], rowst[0:FM_N, :], identf[0:FM_N, 0:FM_N]), reads=["rowst", "identf"], writes=[PK[0]])
            P.dve(CP(fm[:], psb[0][:, 0:FM_N]), reads=[PK[0]], writes=["fm"])
            P.dve(TS(fmh[:], fm[:], 0.5), reads=["fm"], writes=["fmh"])
            P.dve(TS(fmh[:, 0:4], fm[:, 0:4], 0.125), reads=["fm", "fmh"], writes=["fmh"])
            P.dve(TT(fmh[:, 56:64], fm[:, FM_CW:FM_CW + 8], fm[:, FM_CW + 8:FM_CW + 16], ALU.add), reads=["fm", "fmh"], writes=["fmh"])
            P.dve(TT(fmh[:, 56:64], fmh[:, 56:64], fm[:, FM_CW + 16:FM_CW + 24], ALU.add), reads=["fm", "fmh"], writes=["fmh"])
            P.dve(TT(fmh[:, 56:64], fmh[:, 56:64], fm[:, 8:16], ALU.mult), reads=["fm", "fmh"], writes=["fmh"])
            for (dst, so, col) in ((0, 0, 0), (4, 8, 0), (0, 4, 1), (4, 12, 1)):
                P.dma(DMA(gbias[dst:dst + 4, col:col + 1], b_in[0, GATE_OFF + so:GATE_OFF + so + 4].rearrange("(p o) -> p o", o=1)),
                      writes=["gbias"], semkey="setup_c")
            P.dve(TS(gbias[:, 1:2], gbias[:, 1:2], -1.0), reads=["gbias"], writes=["gbias"])
            P.dve(TS(hg_bc[:], hg_bc[:], 0.5), reads=["hg_bc"], writes=["hg_bc"])
            P.dma(DMA(cmk[:], cmask), writes=["cmk"], semkey="setup_c")
            for hq in range(2):
                sk = "setup_c" if hq == 0 else "setup_c2"
                hs = slice(4 * hq, 4 * hq + 4)
                P.pool(MS(tzs[:, :, 14:16, :], 0.0), writes=["tzs"])
                P.dma(DMA(tzs[0:64, :, 0:14, :], tz[:, hs, 0:14, :]), writes=["tzs"], semkey=sk)
                P.dma(DMA(tzs[64:128, :, 0:14, :], tz[:, hs, 1:15, :]), writes=["tzs"], semkey=sk)
                P.dma(DMA(tzs[64:128, :, 14, :], tz[:, hs, 3, :]), writes=["tzs"], semkey=sk)
                P.dma(DMA(tzs[0:64, :, 15, :], tz[:, hs, 10, :]), writes=["tzs"], semkey=sk)
                for h4 in range(4):
                    h = 4 * hq + h4
                    P.act(ACTF(tzs[:, h4], tzs[:, h4], AF.Exp), reads=["tzs"], writes=["tzs"])
                    P.dve(TT(etab[:, h], tzs[:, h4], cmk[:, :].unsqueeze(1).to_broadcast([128, 16, 64]), ALU.mult),
                          reads=["tzs", "cmk"], writes=["etab"])
            P.dve(MS(etab[0:64, :, 14, :], 0.0), reads=["etab"], writes=["etab"])
            P.dve(MS(etab[64:128, :, 15, :], 0.0), reads=["etab"], writes=["etab"])
            P.dve(CP(st1[:, 0:1], etab[:, 7, 13, 0:1]), reads=["etab", "fm", "fmh"], writes=["st1a"])

        wstate = {"i": 0}

        def wload(c):
            i = wstate["i"]
            wstate["i"] += 1
            slot = i % 3
            key = ("ws", slot)
            P.dma(DMA(ws[slot][:], wq[c]), reads=[("wq", c)], writes=[key], semkey=("ws", slot))
            return ws[slot], key

        ln_i = {"i": 0}

        def pTv():
            return psb[3][:].bitcast(BF16).rearrange("p (a b) -> p a b", a=8)

        def ln_A(src_ap, meta_rows=None):
            i = ln_i["i"]
            ln_i["i"] += 1
            b = i % 2
            xk, nk = ("xs", b), ("xnb", b)
            if meta_rows is None:
                npart = 128
                P.dma(DMA(xs[b][:], src_ap), writes=[xk], semkey=("xs", b))
            else:
                npart = meta_rows
                P.dma(DMA(xs[b][0:npart, :], src_ap), writes=[xk], semkey=("xs", b))
            c0 = 4 * b
            ka, kb, kc_ = ("ln_a", b), ("ln_b", b), ("ln_c", b)
            P.act(ACTF(xnb[b][0:npart, :], xs[b][0:npart, :], AF.Square, accum_out=st2[0:npart, c0:c0 + 1]), reads=[xk], writes=[nk, ka])
            P.dve(TS(st2[0:npart, c0 + 1:c0 + 2], st2[0:npart, c0:c0 + 1], 1.0 / D, EPS, ALU.mult, ALU.add), reads=[ka], writes=[kb])
            P.pool(TT(st2[0:npart, c0 + 2:c0 + 3], st2[0:npart, c0 + 1:c0 + 2], cst[0:npart, 0:1], ALU.pow), reads=[kb, "cst"], writes=[kc_])
            P.dve(TS(xnb[b][0:npart, :], xs[b][0:npart, :], st2[0:npart, c0 + 2:c0 + 3]), reads=[xk, kc_, nk], writes=[nk])
            return (b, npart)

        def ln_B(tok, dst_ap, dkey):
            b, npart = tok
            nk = ("xnb", b)
            pT = pTv()
            for kc in range(8):
                P.pe(TR(pT[:, kc, 0:npart], xnb[b][0:npart, kc * 128:(kc + 1) * 128], ident[0:npart, 0:npart]),
                     reads=[nk, "ident"], writes=[PK[3]])
            P.dve(TT(dst_ap[:, :, 0:npart], pT[:, :, 0:npart],
                     fm[:, FM_G:FM_G + 8].unsqueeze(2).to_broadcast([128, 8, npart]), ALU.mult),
                  reads=[PK[3], "fm"], writes=[dkey])

        def load_norm_T(src_ap, nsub, dst, dkey, meta_rows=None):
            for j in range(nsub):
                if meta_rows is None:
                    tok = ln_A(src_ap[j * 128:(j + 1) * 128, :])
                else:
                    tok = ln_A(src_ap, meta_rows)
                ln_B(tok, dst[:, :, j * 128:(j + 1) * 128], dkey)

        pbank = {"i": 0}

        def next_bank():
            b = pbank["i"] % 3
            pbank["i"] += 1
            return b

        def proj_F(xT, xkey, ntok, c, handler):
            wsl, wkey = wload(c)
            for fc in range(4):
                b = next_bank()
                for kc in range(8):
                    P.pe(MM(psb[b][:, 0:ntok], wsl[:, kc, fc * 128:(fc + 1) * 128], xT[:, kc, 0:ntok], kc == 0, kc == 7),
                         reads=[wkey, xkey], writes=[PK[b]])
                handler(fc, psb[b][:, 0:ntok], PK[b])

        def proj_T(xT, xkey, nsub, c, handler):
            wsl, wkey = wload(c)
            for j in range(nsub):
                b = next_bank()
                for kc in range(8):
                    P.pe(MM(psb[b][:, :], xT[:, kc, j * 128:(j + 1) * 128], wsl[:, kc, :], kc == 0, kc == 7),
                         reads=[wkey, xkey], writes=[PK[b]])
                handler(j, psb[b][:, :], PK[b])

        def gates_A(xT, xkey, ntok, is_meta=False):
            nch = ntok // 128
            bI = next_bank()
            for kc in range(8):
                P.pe(MM(psb[bI][0:8, 0:ntok], wg[:, kc, 0:8], xT[:, kc, 0:ntok], kc == 0, kc == 7), reads=["wg", xkey], writes=[PK[bI]])
            R0 = gsc[:, 0, 0:ntok]
            R1 = gsc[:, 1, 0:ntok]
            R2 = gsc[:, 2, 0:ntok]
            R3 = gsc[:, 3, 0:ntok]
            K0, K1, K2, K3 = "g_r0", "g_r1", "g_r2", "g_r3"
            P.act(ACTF(R0, psb[bI][0:8, 0:ntok], AF.Exp, bias=gbias[:, 0:1]), reads=[PK[bI], "gbias"], writes=[K0])
            bF = next_bank()
            for kc in range(8):
                P.pe(MM(psb[bF][0:8, 0:ntok], wg[:, kc, 8:16], xT[:, kc, 0:ntok], kc == 0, kc == 7), reads=["wg", xkey], writes=[PK[bF]])
            P.act(ACTF(R1, psb[bF][0:8, 0:ntok], AF.Exp, bias=gbias[:, 1:2], scale=-1.0), reads=[PK[bF], "gbias"], writes=[K1])
            if is_meta:
                P.dve(MS(gsc[:, 1, 16:ntok], 0.0), reads=[K1], writes=[K1])
                P.dve(MS(gsc[:, 0, 16:ntok], 0.0), reads=[K0], writes=[K0])
            P.dve(TS(R1, R1, 1.0, None, ALU.add), reads=[K1], writes=[K1])
            for ci in range(nch):
                sl = slice(ci * 128, (ci + 1) * 128)
                P.dve(lambda e, sl=sl: e.tensor_tensor_scan(out=gsc[:, 2, sl], data0=gsc[:, 1, sl], data1=onesg[:, :], initial=1.0,
                                                            op0=ALU.mult, op1=ALU.mult), reads=[K1, "onesg"], writes=[K2])
            P.dve(RECIP(R3, R2), reads=[K2], writes=[K3])
            for ci in range(nch):
                last = ci * 128 + 127
                P.dve(CP(gs2[:, ci:ci + 1], gsc[:, 3, last:last + 1]), reads=[K3], writes=["g_eg"])
            for ci in range(nch):
                sl = slice(ci * 128, (ci + 1) * 128)
                last = ci * 128 + 127
                P.dve(STT(gsc[:, 1, sl], gsc[:, 3, sl], gsc[:, 2, last:last + 1], gsc[:, 1, sl], ALU.mult, ALU.mult),
                      reads=[K3, K2, K1], writes=[K1])
            P.dve(TT(R3, R2, R1, ALU.subtract), reads=[K2, K1, K3, "g_eg"], writes=[K3])
            P.dve(STT(R2, R3, fl[0:8, 2:3], R1, ALU.mult, ALU.add), reads=[K3, K1, "fl", K2], writes=[K2])
            P.dve(STT(R0, R0, KAPPA, R2, ALU.mult, ALU.mult), reads=[K0, K2], writes=[K0])
            for ci in range(nch):
                sl = slice(ci * 128, (ci + 1) * 128)
                P.dve(TS(gsc[:, 3, sl], gsc[:, 0, sl], gs2[:, ci:ci + 1]), reads=[K0, "g_eg", K3], writes=[K3])
            return nch, (K0, K3, K2)

        def gates_B(tokg, chunks):
            nch, (K0, K3, K2) = tokg
            bS = next_bank()
            for ci in range(nch):
                sl = slice(ci * 128, (ci + 1) * 128)
                for k, (row, key) in enumerate(((0, K0), (3, K3), (2, K2))):
                    P.pe(MM(psb[bS][:, ci * 32 + k * 8:ci * 32 + k * 8 + 8], gsc[:, row, sl], identf[0:8, 0:8]),
                         reads=[key, "identf"], writes=[PK[bS]])
                P.pe(MM(psb[bS][:, ci * 32 + 24:ci * 32 + 32], gs2[:, ci:ci + 1].to_broadcast([8, 128]), identf[0:8, 0:8]),
                     reads=["g_eg", "identf"], writes=[PK[bS]])
            for ci in range(nch):
                slot = chunks[ci]
                P.dve(CP(SC[:, slot, :], psb[bS][:, ci * 32:ci * 32 + 24]), reads=[PK[bS]], writes=[("SC", slot)])
                P.dve(CP(EG[:, slot, :], psb[bS][:, ci * 32 + 24:ci * 32 + 32]), reads=[PK[bS]], writes=[("EG", slot)])

        cv_i = {"i": 0}

        def conv_taps(pr, pk, cv, ck, n, fcg):
            w0 = fm[:, FM_CW + fcg:FM_CW + fcg + 1]
            w1 = fm[:, FM_CW + 8 + fcg:FM_CW + 8 + fcg + 1]
            w2 = fm[:, FM_CW + 16 + fcg:FM_CW + 16 + fcg + 1]
            P.dve(TS(cv[:, 0:n], pr[:, 1:1 + n], w1), reads=[pk, "fm"], writes=[ck])
            P.dve(STT(cv[:, 0:n], pr[:, 0:n], w0, cv[:, 0:n], ALU.mult, ALU.add), reads=[pk, ck, "fm"], writes=[ck])
            P.dve(STT(cv[:, 0:n], pr[:, 2:2 + n], w2, cv[:, 0:n], ALU.mult, ALU.add), reads=[pk, ck, "fm"], writes=[ck])
            P.act(ACTF(pr[:, 1:1 + n], cv[:, 0:n], AF.Tanh, scale=0.5), reads=[ck], writes=[pk])

        def conv_silu(fcg, ps_ap, pkey, ntok, bias_col, lh_ap, rh_ap, dst_ap, dst_key, save_first=None, save_last=None):
            assert ntok == 512
            i = cv_i["i"] % 2
            cv_i["i"] += 1
            if i == 0:
                cv, ck, th, tk = cvt[0][:, 0:512], ("cvt", 0), gA, "gA"
            else:
                cv, ck, th, tk = pre[0][:, 0:512], ("pre", 0), gB, "gB"
            w0 = fm[:, FM_CW + fcg:FM_CW + fcg + 1]
            w1 = fm[:, FM_CW + 8 + fcg:FM_CW + 8 + fcg + 1]
            w2 = fm[:, FM_CW + 16 + fcg:FM_CW + 16 + fcg + 1]
            beta = fmh[:, 56 + fcg:57 + fcg]
            P.act(ACTF(cv, ps_ap, AF.Identity, bias=beta, scale=w1), reads=[pkey, "fm", "fmh"], writes=[ck])
            P.dve(STT(cv[:, 1:512], ps_ap[:, 0:511], w0, cv[:, 1:512], ALU.mult, ALU.add), reads=[pkey, ck, "fm"], writes=[ck])
            P.dve(STT(cv[:, 0:511], ps_ap[:, 1:512], w2, cv[:, 0:511], ALU.mult, ALU.add), reads=[pkey, ck, "fm"], writes=[ck])
            P.dve(STT(cv[:, 0:1], hadj[:, fcg, 0:1], w0, cv[:, 0:1], ALU.mult, ALU.add), reads=["hadj", ck, "fm"], writes=[ck])
            P.dve(STT(cv[:, 511:512], hadj[:, fcg, 1:2], w2, cv[:, 511:512], ALU.mult, ALU.add), reads=["hadj", ck, "fm"], writes=[ck])
            if save_first is not None:
                P.dve(TS(save_first, ps_ap[:, 0:1], fm[:, bias_col:bias_col + 1], None, ALU.add), reads=[pkey, "fm"], writes=["carry"])
            if save_last is not None:
                P.dve(TS(save_last, ps_ap[:, 511:512], fm[:, bias_col:bias_col + 1], None, ALU.add), reads=[pkey, "fm"], writes=["carry"])
            P.act(ACTF(th[:, :], cv, AF.Tanh, scale=0.5), reads=[ck], writes=[tk])
            P.dve(STT(dst_ap, th[:, :], 1.0, cv, ALU.add, ALU.mult), reads=[tk, ck], writes=[dst_key])

        def state_prep(kT_ap, kkeys, v_ap, vkey, slot, dirn, banks=None):
            pT = pTv()
            for h in range(4):
                P.pe(TR(pT[:, h, :], kT_ap[:, h, :], ident[:, :]), reads=list(kkeys) + ["ident"], writes=[PK[3]])
            P.act(ACTF(kk[:, :, :], pT[:, 0:4, :], AF.Copy), reads=[PK[3]], writes=["kk"])
            if banks is None:
                banks = [next_bank(), next_bank()]
            outs = []
            for h in range(4):
                u = uv[h % 2]
                uk = ("uv", h % 2)
                col = 8 + dirn * 4 + h
                P.act(ACTF(u[:, :], v_ap[:, h, :], AF.Copy, scale=SC[:, slot, col:col + 1]), reads=[vkey, ("SC", slot)], writes=[uk])
                bank = banks[h // 2]
                cols = slice((h % 2) * 129, (h % 2) * 129 + 129)
                P.pe(MM(psb[bank][:, cols], kk[:, h, :], u[:, :]), reads=["kk", uk], writes=[PK[bank]])
                outs.append((bank, cols))
            return outs

        def state_apply(Cst, ckey, outs, slot, dirn):
            for h in range(4):
                bank, cols = outs[h]
                P.dve(STT(Cst[:, h, :], Cst[:, h, :], EG[:, slot, dirn * 4 + h:dirn * 4 + h + 1], psb[bank][:, cols], ALU.mult, ALU.add),
                      reads=[ckey, ("EG", slot), PK[bank]], writes=[ckey])

        def state_update(Cst, ckey, kT_ap, kkeys, v_ap, vkey, slot, dirn, bank):
            outs = state_prep(kT_ap, kkeys, v_ap, vkey, slot, dirn)
            state_apply(Cst, ckey, outs, slot, dirn)

        if stop <= 0:
            P.finalize(st)
            return nc
        load_norm_T(meta, 1, xnTm, "xnTm", meta_rows=16)

        def h_kmeta(fc, ps_ap, pkey):
            P.act(ACTF(kmT[:, fc, :], ps_ap[:, 0:16], AF.Identity, bias=fm[:, FM_B[C_NAK] + fc:FM_B[C_NAK] + fc + 1]),
                  reads=[pkey, "fm"], writes=["kmT"])
        proj_F(xnTm, "xnTm", 128, C_NAK, h_kmeta)

        def h_vmeta(j, ps_ap, pkey):
            P.dve(TT(vmp[0:16, :, 0:64], ps_ap[0:16, :].rearrange("p (h d) -> p h d", h=8),
                     bias_bc[0:16, TB[C_NAV]:TB[C_NAV] + 512].rearrange("p (h d) -> p h d", h=8), ALU.add),
                  reads=[pkey, "bias_bc"], writes=["vmp"])
        proj_T(xnTm, "xnTm", 1, C_NAV, h_vmeta)

        def h_qmeta(fc, ps_ap, pkey):
            P.act(ACTF(prem[:, fc:fc + 1], ps_ap[:, 15:16], AF.Identity, bias=fm[:, FM_B[C_MLQ] + fc:FM_B[C_MLQ] + fc + 1]),
                  reads=[pkey, "fm"], writes=["prem"])
        proj_F(xnTm, "xnTm", 128, C_MLQ, h_qmeta)

        def h_kmeta2(fc, ps_ap, pkey):
            P.act(ACTF(premk[:, fc, :], ps_ap[:, 0:16], AF.Identity, bias=fm[:, FM_B[C_MLK] + fc:FM_B[C_MLK] + fc + 1]),
                  reads=[pkey, "fm"], writes=["premk"])
            P.pool(CP(prem[:, 4 + fc:5 + fc], premk[:, fc, 15:16]), reads=["premk"], writes=["prem"])
        proj_F(xnTm, "xnTm", 128, C_MLK, h_kmeta2)

        def h_vmeta2(j, ps_ap, pkey):
            P.dve(TT(vmeta[0:16, :, 0:128], ps_ap[0:16, :].rearrange("p (h d) -> p h d", h=4),
                     bias_bc[0:16, TB[C_MLV]:TB[C_MLV] + 512].rearrange("p (h d) -> p h d", h=4), ALU.add),
                  reads=[pkey, "bias_bc"], writes=["vmeta"])
            P.dve(MS(vmeta[0:16, :, 128:129], 1.0), reads=["vmeta"], writes=["vmeta"])
        proj_T(xnTm, "xnTm", 1, C_MLV, h_vmeta2)
        gates_B(gates_A(xnTm, "xnTm", 128, is_meta=True), [NS])
        P.dve(CP(SC[:, NS + 1, :], SC[:, NS, :]), reads=[("SC", NS)], writes=[("SC", NS + 1)])
        P.dve(CP(EG[:, NS + 1, :], EG[:, NS, :]), reads=[("EG", NS)], writes=[("EG", NS + 1)])

        if stop <= 1:
            P.finalize(st)
            return nc
        stg.close()
        ftmp = sb("ftmp", [128, 512])
        Praw = [sb("Praw%d" % i, [128, 640], BF16) for i in range(2)]
        Pn = [sb("Pn%d" % i, [128, 640], BF16) for i in range(2)]
        Pm = sb("Pm", [128, 2, 512], BF16)
        gA = sb("gA", [128, 512], BF16)
        gB = sb("gB", [128, 512], BF16)
        gAT = sb("gAT", [128, 4, 512], BF16)
        gBT = sb("gBT", [128, 4, 512], BF16)
        tgA = sb("tgA", [128, 512])
        tgB = sb("tgB", [128, 512])
        osb = [sb("osb%d" % i, [128, D]) for i in range(2)]
        X1K = ["ftmp", ("Praw", 0), ("Praw", 1), ("Pn", 0), ("Pn", 1), ("Pm", 0), ("Pm", 1), "gA", "gB", "gAT", "gBT", "tgA", "tgB",
               ("osb", 0), ("osb", 1)]
        for eng in ("dve", "act", "pool"):
            P.ew(eng, MS(st1[:, 5:6] if eng == "dve" else st1[:, 6:7], 0.0) if eng != "act" else ACTF(st1[:, 7:8], fl[:, 0:1], AF.Copy),
                 reads=["fl", "etab", "fm", "fmh", "gbias", "kmT", "vmp", "prem", "premk", "vmeta", ("SC", NS), ("EG", NS)],
                 writes=X1K + ["xnTm", "tzs", "rowst", "cmk"])

        P.pool(MS(Pm[:], 0.0), reads=[("Pm", 0), ("Pm", 1)], writes=[("Pm", 0), ("Pm", 1)])

        def meta_state(u):
            P.pool(MS(Cmeta[:, u], 0.0), writes=[("Cmeta", u)])
            for fc in range(4):
                i = cv_i["i"] % 2
                cv_i["i"] += 1
                pk, ck = ("pre", 0), ("cvt", 0)
                pr, cv = pre[i], cvt[i]
                P.pool(MS(pr[:, 0:1], 0.0), writes=[pk])
                P.pool(CP(pr[:, 1:17], premk[:, fc, :]), reads=["premk", pk], writes=[pk])
                P.pool(CP(pr[:, 17:18], firstpre[:, u, 4 + fc, :]), reads=[("firstpre", u), pk], writes=[pk])
                conv_taps(pr, pk, cv, ck, 16, 4 + fc)
                P.dve(STT(kmetaT[:, fc, 0:16], pr[:, 1:17], 1.0, cv[:, 0:16], ALU.add, ALU.mult), reads=[pk, ck], writes=["kmetaT"])
            state_update(Cmeta[:, u], ("Cmeta", u), kmetaT, ["kmetaT"], vmeta, "vmeta", NS + u, 0, 4)

        def tile_src(T):
            return xu[T * 512:(T + 1) * 512, :]

        def small_cols(wsl, wkey, xT, xkey, col):
            hb = next_bank()
            for fc in range(4):
                for kc in range(8):
                    P.pe(MM(psb[hb][:, fc:fc + 1], wsl[:, kc, fc * 128:(fc + 1) * 128], xT[:, kc, col:col + 1], kc == 0, kc == 7),
                         reads=[wkey, xkey], writes=[PK[hb]])
            return hb

        vml2 = KT[:, 0:2, :].rearrange("p a b -> p (a b)")[:, 0:2064].rearrange("p (j h n) -> p j h n", j=4, h=4)
        P.pool(MS(vml2[:, :, :, 128:129], 1.0), writes=[("vml2", j) for j in range(4)])

        def p1_bufs(T):
            if T % 2 == 0:
                return 4, vml, "vml"
            return 0, vml2, "vml2"

        def p1_P(T):
            xt, xkey = xnT[T % 2], ("xnT", T % 2)
            u_of = T // TPU
            first_tile_of_unit = (T % TPU == 0)
            last_tile_of_unit = (T % TPU == TPU - 1)
            fcb, vb, vname = p1_bufs(T)
            wsl, wkey = wload(C_MLK)
            bcol = FM_B[C_MLK]
            if T > 0:
                hb = small_cols(wsl, wkey, xnT[(T - 1) % 2], ("xnT", (T - 1) % 2), 511)
                P.dve(TT(halo[:, 4:8, 0], psb[hb][:, 0:4], fm[:, bcol:bcol + 4], ALU.add), reads=[PK[hb], "fm"], writes=["halo"])
            if first_tile_of_unit:
                if T == 0:
                    P.dve(CP(halo[:, 4:8, 0], prem[:, 4:8]), reads=["prem"], writes=["halo"])
                else:
                    P.dve(TS(halo[:, 4:8, 0], halo[:, 4:8, 0], cont), reads=["halo", "fl"], writes=["halo"])
                    P.dve(STT(halo[:, 4:8, 0], prem[:, 4:8], ncont, halo[:, 4:8, 0], ALU.mult, ALU.add), reads=["halo", "prem", "fl"], writes=["halo"])
            if T == NT - 1:
                P.dve(MS(halo[:, 4:8, 1], 0.0), writes=["halo"])
            elif last_tile_of_unit:
                P.dve(TS(halo[:, 4:8, 1], carry[:, 4:8, 1], cont), reads=["carry", "fl"], writes=["halo"])
            else:
                P.dve(CP(halo[:, 4:8, 1], carry[:, 4:8, 1]), reads=["carry"], writes=["halo"])
            P.dve(TT(hadj[:, 4:8, :], halo[:, 4:8, :], fm[:, bcol:bcol + 4].unsqueeze(2).to_broadcast([128, 4, 2]), ALU.subtract),
                  reads=["halo", "fm"], writes=["hadj"])
            for fc in range(4):
                b = next_bank()
                for kc in range(8):
                    P.pe(MM(psb[b][:, :], wsl[:, kc, fc * 128:(fc + 1) * 128], xt[:, kc, :], kc == 0, kc == 7), reads=[wkey, xkey], writes=[PK[b]])
                conv_silu(4 + fc, psb[b][:, :], PK[b], 512, bcol + fc, halo[:, 4 + fc, 0:1], halo[:, 4 + fc, 1:2],
                          qkT[:, fcb + fc, :], ("qkT", fcb + fc), save_first=carry[:, 4 + fc, 1:2])
                yield
            if first_tile_of_unit:
                P.pool(CP(firstpre[:, u_of, 4:8, 0], carry[:, 4:8, 1]), reads=["carry"], writes=[("firstpre", u_of)])
            wsv, wkv = wload(C_MLV)
            for j in range(4):
                b = next_bank()
                for kc in range(8):
                    P.pe(MM(psb[b][:, :], xt[:, kc, j * 128:(j + 1) * 128], wsv[:, kc, :], kc == 0, kc == 7), reads=[wkv, xkey], writes=[PK[b]])
                P.dve(TT(vb[:, j, :, 0:128], psb[b][:, :].rearrange("p (h d) -> p h d", h=4),
                         bias_bc[:, TB[C_MLV]:TB[C_MLV] + 512].rearrange("p (h d) -> p h d", h=4), ALU.add),
                      reads=[PK[b], "bias_bc"], writes=[(vname, j)])
                yield
            tokg = gates_A(xt, xkey, 512)
            yield
            gates_B(tokg, [T * 4 + c for c in range(4)])
            yield

        def p1_S(T):
            last_tile_of_unit = (T % TPU == TPU - 1)
            fcb, vb, vname = p1_bufs(T)
            KK4 = [("qkT", fcb + h) for h in range(4)]

            def pb(slot):
                return [4 + 2 * (slot % 2), 5 + 2 * (slot % 2)]
            outs_next = state_prep(qkT[:, fcb:fcb + 4, 3 * 128:4 * 128], KK4, vb[:, 3], (vname, 3), T * 4 + 3, 1, banks=pb(T * 4 + 3))
            for c in range(3, -1, -1):
                slot = T * 4 + c
                outs = outs_next
                ltok = None
                if T - 2 >= 0:
                    T2 = T - 2
                    jj = c
                    ltok = ln_A(xu[T2 * 512 + jj * 128:T2 * 512 + (jj + 1) * 128, :])
                if last_tile_of_unit and c == 3 and T != NT - 1:
                    P.dve(TS(Cb[:], Cb[:], cont), reads=["Cb", "fl"], writes=["Cb"])
                i = slot % 2
                P.act(ACTF(cbst[i][:], Cb[:], AF.Copy), reads=["Cb"], writes=[("cbst", i)])
                P.dma(DMA(cbs[slot].rearrange("p (h n) -> p h n", h=4), cbst[i][:]), reads=[("cbst", i)], writes=[("cbs", slot)], semkey=("cbst", i),
                      queue="pool")
                if c > 0:
                    outs_next = state_prep(qkT[:, fcb:fcb + 4, (c - 1) * 128:c * 128], KK4, vb[:, c - 1], (vname, c - 1), slot - 1, 1, banks=pb(slot - 1))
                state_apply(Cb, "Cb", outs, slot, 1)
                if ltok is not None:
                    ln_B(ltok, xnT[T2 % 2][:, :, jj * 128:(jj + 1) * 128], ("xnT", T2 % 2))
                yield

        load_norm_T(tile_src(NT - 1), 4, xnT[(NT - 1) % 2], ("xnT", (NT - 1) % 2))
        if NT > 1:
            load_norm_T(tile_src(NT - 2), 4, xnT[(NT - 2) % 2], ("xnT", (NT - 2) % 2))
        for _ in p1_P(NT - 1):
            pass
        for T in range(NT - 1, -1, -1):
            gS = p1_S(T)
            gP = p1_P(T - 1) if T > 0 else iter(())
            for k in range(4):
                next(gS, None)
                for _ in range(3 if k == 0 else 2):
                    next(gP, None)
            for _ in gS:
                pass
            for _ in gP:
                pass
            if T % TPU == 0:
                meta_state(T // TPU)
        for eng in ("act", "dve", "pool"):
            P.ew(eng, MS(st1[:, 5:6] if eng == "dve" else st1[:, 6:7], 0.0) if eng != "act" else ACTF(st1[:, 7:8], fl[:, 0:1], AF.Copy),
                 reads=["fl"], writes=[("vml2", j) for j in range(4)] + [("KT", i) for i in range(RING_T)])
        if stop <= 2:
            P.finalize(st)
            return nc
        def kv_proj(T):
            xt, xkey = xnT[T % 2], ("xnT", T % 2)
            rt = T % RING_T

            def h_k(fc, ps_ap, pkey):
                P.act(ACTF(KT[:, fc, rt * 512:(rt + 1) * 512], ps_ap, AF.Identity, bias=fm[:, FM_B[C_NAK] + fc:FM_B[C_NAK] + fc + 1]),
                      reads=[pkey, "fm"], writes=[("KT", rt)])
            proj_F(xt, xkey, 512, C_NAK, h_k)

            def h_v(j, ps_ap, pkey):
                P.dve(TT(VP[:, rt * 4 + j, :, 0:64], ps_ap.rearrange("p (h d) -> p h d", h=8),
                         bias_bc[:, TB[C_NAV]:TB[C_NAV] + 512].rearrange("p (h d) -> p h d", h=8), ALU.add),
                      reads=[pkey, "bias_bc"], writes=[("VP", rt)])
            proj_T(xt, xkey, 4, C_NAV, h_v)

        def ring_row(r):
            return r % RROWS

        def na_q_z(T):
            xt, xkey = xnT[T % 2], ("xnT", T % 2)
            P.pool(MS(qkT[64:128, 0:4, :], 0.0), writes=[("qkT", f) for f in range(4)])
            P.pool(MS(qkT[0:64, 4:8, :], 0.0), writes=[("qkT", 4 + f) for f in range(4)])

            def h_q(fc, ps_ap, pkey):
                bq = FM_B[C_NAQ] + fc
                P.act(ACTF(qkT[0:64, fc, :], ps_ap[0:64, :], AF.Identity, bias=fmh[0:64, bq:bq + 1], scale=0.125),
                      reads=[pkey, "fmh", ("qkT", fc)], writes=[("qkT", fc)])
                P.act(ACTF(qkT[64:128, 4 + fc, :], ps_ap[64:128, :], AF.Identity, bias=fmh[64:128, bq:bq + 1], scale=0.125),
                      reads=[pkey, "fmh", ("qkT", 4 + fc)], writes=[("qkT", 4 + fc)])
            proj_F(xt, xkey, 512, C_NAQ, h_q)

            def h_z(j, ps_ap, pkey):
                (za, zak), (zb_, zbk) = ((ftmp, ["ftmp"]), (tgA, ["tgA"])) if j % 2 == 0 else ((tgB, ["tgB"]), (h2, ["h2"]))
                P.dve(TT(za[:, :], ps_ap, bias_bc[:, TB[C_NAZ]:TB[C_NAZ] + 512], ALU.add), reads=[pkey, "bias_bc"], writes=zak)
                P.act(ACTF(zb_[:, :], za[:, :], AF.Tanh, scale=0.5), reads=zak, writes=zbk)
                P.pool(TS(zb_[:, :], zb_[:, :], 1.0, 0.5, ALU.add, ALU.mult), reads=zbk, writes=zbk)
                P.pool(TT(zs[:, j, :], za[:, :], zb_[:, :], ALU.mult), reads=zak + zbk, writes=[("zs", j)])
            proj_T(xt, xkey, 4, C_NAZ, h_z)

        def q_ap(h, qcol):
            fc, hh = h // 2, h % 2
            return qkT[:, 4 * hh + fc, qcol:qcol + 64], ("qkT", 4 * hh + fc)

        def na_meta_scores(lrt, slot):
            qcol = lrt * 64
            ms = psb[3][0:16, :]
            for h in range(8):
                qap, qk = q_ap(h, qcol)
                P.pe(MM(ms[:, h * 64:(h + 1) * 64], kmT[:, h // 2, :], qap), reads=["kmT", qk], writes=[PK[3]])
            P.act(ACTF(Pm[0:16, slot, :], ms, AF.Exp), reads=[PK[3]], writes=[("Pm", slot)])

        na_i = {"i": 0, "cur": 0}
        PB = [Praw[0], Praw[1], Pn[0], Pn[1]]
        PBK = [("Praw", 0), ("Praw", 1), ("Pn", 0), ("Pn", 1)]

        def na_scores(lr_tile, r0, delta, hp):
            qcol = lr_tile * 64
            i = na_i["i"] % 4
            na_i["i"] += 1
            odd = (r0 % 2 == 1)
            ng = 5 if odd else 4
            W = 2 * ng * 64
            cur = na_i["cur"]
            if odd:
                if cur % 2 == 1:
                    cur = (cur + 1) % 4
                reg = psS[cur // 2][:, 0:W]
                skeys = [PK[4 + cur], PK[5 + cur]]
                na_i["cur"] = (cur + 2) % 4
            else:
                reg = psb[4 + cur][:, 0:W]
                skeys = [PK[4 + cur]]
                na_i["cur"] = (cur + 1) % 4
            sps = reg.rearrange("p (h g q) -> p h g q", h=2, g=ng)
            base = r0 - 1 if odd else r0
            rk = list(dict.fromkeys([("KT", (rr // 8) % RING_T) for rr in range(base, base + 2 * ng)]))
            for hh in range(2):
                qap, qk = q_ap(2 * hp + hh, qcol)
                for g in range(ng):
                    c0 = ring_row(base + 2 * g) * 64
                    P.pe(MM(sps[:, hh, g, :], KT[:, hp, c0:c0 + 128], qap), reads=rk + [qk], writes=skeys)
            P.act(ACTF(PB[i][:, 0:W], reg, AF.Exp), reads=skeys, writes=[PBK[i]])
            pn4 = PB[i][:, 0:W].rearrange("p (h g q) -> p h g q", h=2, g=ng)
            e2 = etab[:, 2 * hp:2 * hp + 2]
            if not odd:
                ei0 = 7 - delta
                P.dve(TT(pn4, pn4, e2[:, :, ei0:ei0 + 7:2, :], ALU.mult), reads=[PBK[i], "etab"], writes=[PBK[i]])
            else:
                P.dve(TT(pn4[:, :, 0, :], pn4[:, :, 0, :], e2[:, :, 14, :], ALU.mult), reads=[PBK[i], "etab"], writes=[PBK[i]])
                P.dve(TT(pn4[:, :, 1:4, :], pn4[:, :, 1:4, :], e2[:, :, 4:9:2, :], ALU.mult), reads=[PBK[i], "etab"], writes=[PBK[i]])
                P.dve(TT(pn4[:, :, 4, :], pn4[:, :, 4, :], e2[:, :, 15, :], ALU.mult), reads=[PBK[i], "etab"], writes=[PBK[i]])
            return (PBK[i], pn4, ng, base)

        def na_pv(desc, hp, pv_bank, row_half):
            pkey, pn4, ng, base = desc
            vk = list(dict.fromkeys([("VP", (rr // 8) % RING_T) for rr in range(base, base + 2 * ng)]))
            for hh in range(2):
                h = 2 * hp + hh
                o = psb[pv_bank][row_half * 64:(row_half + 1) * 64, (h % 4) * 65:(h % 4) * 65 + 65]
                rd = [pkey] + vk
                for g in range(ng):
                    vt = ring_row(base + 2 * g) // 2
                    P.pe(MM(o, pn4[:, hh, g, :], VP[:, vt, h, :], g == 0, False), reads=rd, writes=[PK[pv_bank]])
                P.pe(MM(o, Pm[:, row_half, h * 64:(h + 1) * 64], vmp[:, h, :], False, True), reads=[("Pm", row_half), "vmp"], writes=[PK[pv_bank]])

        def na_tile(T):
            items = []
            for sp in range(4):
                rows = [8 * T + 2 * sp, 8 * T + 2 * sp + 1]
                variants = []
                for r in rows:
                    u = r // R
                    lr = r % R
                    r0 = min(max(lr - 4, 0), R - 8)
                    v = [(u * R + r0, lr - r0)]
                    if R - 4 <= r < R + 4:
                        v.append((r - 4, 4))
                    variants.append(v)
                nvar = max(len(v) for v in variants)
                items.append(("meta", sp, rows))
                for vi in range(nvar):
                    for half in range(2):
                        grp = (sp, vi, half)
                        for rh, r in enumerate(rows):
                            r0, delta = variants[rh][min(vi, len(variants[rh]) - 1)]
                            for hp in (2 * half, 2 * half + 1):
                                items.append(("unit", r - 8 * T, r0, delta, hp, grp, rh))
                        items.append(("norm", grp, vi, half))
                items.append(("fin", sp, nvar))
            banks = {}
            q = []
            DEPTH = 3

            def do_pv(ent):
                _, desc, hp, grp, rh = ent
                if grp not in banks:
                    banks[grp] = next_bank()
                na_pv(desc, hp, banks[grp], rh)

            def drain(maxpv):
                while sum(1 for e in q if e[0] == "pv") > maxpv:
                    while q and q[0][0] != "pv":
                        run(q.pop(0)[1])
                    do_pv(q.pop(0))
                    while q and q[0][0] != "pv":
                        run(q.pop(0)[1])

            def run(it):
                if it[0] == "meta":
                    for rh, r in enumerate(it[2]):
                        na_meta_scores(r - 8 * T, rh)
                elif it[0] == "norm":
                    _, grp, vi, half = it
                    pvb = banks[grp]
                    pv = psb[pvb][:, 0:260].rearrange("p (h n) -> p h n", h=4)
                    P.dve(RECIP(st1[:, 4:8], pv[:, :, 64]), reads=[PK[pvb]], writes=["st1r"])
                    dst = nao[vi][:, half * 256:(half + 1) * 256].rearrange("p (h d) -> p h d", h=4)
                    P.dve(TT(dst, pv[:, :, 0:64], st1[:, 4:8].unsqueeze(2).to_broadcast([128, 4, 64]), ALU.mult),
                          reads=[PK[pvb], "st1r"] + NAOK[vi], writes=NAOK[vi])
                else:
                    _, sp, nvar = it
                    if nvar == 2:
                        P.dve(TS(nao[0][:, :], nao[0][:, :], ncont), reads=NAOK[0] + ["fl"], writes=NAOK[0])
                        P.dve(STT(nao[0][:, :], nao[1][:, :], cont, nao[0][:, :], ALU.mult, ALU.add), reads=NAOK[0] + NAOK[1] + ["fl"], writes=NAOK[0])
                    if dbg:
                        t0 = (8 * T + 2 * sp) * 64
                        P.dma(DMA(dbg_out["d_na"][t0:t0 + 128, :], nao[0][:, :]), reads=NAOK[0], semkey="dbg1")
                    P.dve(TT(gA[:, :], nao[0][:, :], zs[:, sp, :], ALU.mult), reads=NAOK[0] + [("zs", sp)], writes=["gA"])
                    pT = pTv()
                    for fc in range(4):
                        P.pe(TR(pT[:, fc, :], gA[:, fc * 128:(fc + 1) * 128], ident[:, :]), reads=["gA", "ident"], writes=[PK[3]])
                    P.act(ACTF(gAT[:, :, sp * 128:(sp + 1) * 128], pT[:, 0:4, :], AF.Copy), reads=[PK[3]], writes=["gAT"])

            for it in items:
                if it[0] == "unit":
                    _, lrt, r0, delta, hp, grp, rh = it
                    desc = na_scores(lrt, r0, delta, hp)
                    q.append(("pv", desc, hp, grp, rh))
                    drain(DEPTH)
                elif not q:
                    run(it)
                else:
                    q.append(("act", it))
            drain(0)
            while q:
                run(q.pop(0)[1])

        def mlstm_tile(T, filler=None):
            xt, xkey = xnT[T % 2], ("xnT", T % 2)
            u_of = T // TPU
            first_tile_of_unit = (T % TPU == 0)
            last_tile_of_unit = (T % TPU == TPU - 1)
            for (cch, fco) in ((C_MLQ, 0), (C_MLK, 4)):
                wsl, wkey = wload(cch)
                bcol = FM_B[cch]
                if T < NT - 1:
                    hb = small_cols(wsl, wkey, xnT[(T + 1) % 2], ("xnT", (T + 1) % 2), 0)
                    P.dve(TT(halo[:, fco:fco + 4, 1], psb[hb][:, 0:4], fm[:, bcol:bcol + 4], ALU.add), reads=[PK[hb], "fm"], writes=["halo"])
                    if last_tile_of_unit:
                        P.dve(TS(halo[:, fco:fco + 4, 1], halo[:, fco:fco + 4, 1], cont), reads=["halo", "fl"], writes=["halo"])
                else:
                    P.dve(MS(halo[:, fco:fco + 4, 1], 0.0), writes=["halo"])
                if T == 0:
                    P.dve(CP(halo[:, fco:fco + 4, 0], prem[:, fco:fco + 4]), reads=["prem"], writes=["halo"])
                elif first_tile_of_unit:
                    P.dve(TS(halo[:, fco:fco + 4, 0], carry[:, fco:fco + 4, 0], cont), reads=["carry", "fl"], writes=["halo"])
                    P.dve(STT(halo[:, fco:fco + 4, 0], prem[:, fco:fco + 4], ncont, halo[:, fco:fco + 4, 0], ALU.mult, ALU.add),
                          reads=["halo", "prem", "fl"], writes=["halo"])
                else:
                    P.dve(CP(halo[:, fco:fco + 4, 0], carry[:, fco:fco + 4, 0]), reads=["carry"], writes=["halo"])
                P.dve(TT(hadj[:, fco:fco + 4, :], halo[:, fco:fco + 4, :], fm[:, bcol:bcol + 4].unsqueeze(2).to_broadcast([128, 4, 2]), ALU.subtract),
                      reads=["halo", "fm"], writes=["hadj"])
                for fc in range(4):
                    b = next_bank()
                    for kc in range(8):
                        P.pe(MM(psb[b][:, :], wsl[:, kc, fc * 128:(fc + 1) * 128], xt[:, kc, :], kc == 0, kc == 7), reads=[wkey, xkey], writes=[PK[b]])
                    conv_silu(fco + fc, psb[b][:, :], PK[b], 512, bcol + fc, halo[:, fco + fc, 0:1], halo[:, fco + fc, 1:2],
                              qkT[:, fco + fc, :], ("qkT", fco + fc), save_last=carry[:, fco + fc, 0:1])

            def h_v(j, ps_ap, pkey):
                P.dve(TT(vml[:, j, :, 0:128], ps_ap.rearrange("p (h d) -> p h d", h=4),
                         bias_bc[:, TB[C_MLV]:TB[C_MLV] + 512].rearrange("p (h d) -> p h d", h=4), ALU.add),
                      reads=[pkey, "bias_bc"], writes=[("vml", j)])
            proj_T(xt, xkey, 4, C_MLV, h_v)

            def h_z(j, ps_ap, pkey):
                (za, zak), (zb_, zbk) = ((ftmp, ["ftmp"]), (tgA, ["tgA"])) if j % 2 == 0 else ((tgB, ["tgB"]), (h2, ["h2"]))
                P.dve(TT(za[:, :], ps_ap, bias_bc[:, TB[C_MLZ]:TB[C_MLZ] + 512], ALU.add), reads=[pkey, "bias_bc"], writes=zak)
                P.act(ACTF(zb_[:, :], za[:, :], AF.Tanh, scale=0.5), reads=zak, writes=zbk)
                P.dve(STT(zb_[:, :], zb_[:, :], 1.0, za[:, :], ALU.add, ALU.mult), reads=zak + zbk, writes=zbk)
                P.pool(TT(zs[:, j, :], zb_[:, :], hg_bc[:, :], ALU.mult), reads=zbk + ["hg_bc"], writes=[("zs", j)])
            proj_T(xt, xkey, 4, C_MLZ, h_z)

            def h_o(j, ps_ap, pkey):
                za, zak = (tgB, ["tgB"]) if j % 2 == 0 else (ftmp, ["ftmp"])
                P.dve(TT(za[:, :], ps_ap, bias_bc[:, TB[C_MLO]:TB[C_MLO] + 512], ALU.add), reads=[pkey, "bias_bc"], writes=zak)
                P.act(ACTF(oG[:, j, :], za[:, :], AF.Tanh, scale=0.5), reads=zak, writes=[("oG", j)])
            proj_T(xt, xkey, 4, C_MLO, h_o)
            def ml_X(c):
                slot = T * 4 + c
                csl = slice(c * 128, (c + 1) * 128)
                if first_tile_of_unit and c == 0:
                    if T == 0:
                        P.dve(CP(Cf[:], Cmeta[:, 0]), reads=[("Cmeta", 0)], writes=["Cf"])
                    else:
                        P.dve(TS(Cf[:], Cf[:], cont), reads=["Cf", "fl"], writes=["Cf"])
                        P.dve(STT(Cf[:], Cmeta[:, u_of], ncont, Cf[:], ALU.mult, ALU.add), reads=["Cf", ("Cmeta", u_of), "fl"], writes=["Cf"])
                    P.act(ACTF(Cfb[:], Cf[:], AF.Copy), reads=["Cf"], writes=["Cfb"])
                li = slot % 2
                P.dma(DMA(cbl[li][:], cbs[slot].rearrange("p (h n) -> p h n", h=4)), reads=[("cbs", slot)], writes=[("cbl", li)], semkey=("cbl", li))
                souts = state_prep(qkT[:, 4:8, csl], [("qkT", 4 + h) for h in range(4)], vml[:, c], ("vml", c), slot, 0)
                sps = psb[4][:, :].rearrange("p (h t) -> p h t", h=4)
                for h in range(4):
                    P.pe(MM(sps[:, h, :], qkT[:, 4 + h, csl], qkT[:, h, csl]), reads=[("qkT", 4 + h), ("qkT", h)], writes=[PK[4]])
                for h in range(4):
                    P.dve(STT(Sf[h][:, :], sps[:, h, :], SC[:, slot, h:h + 1], maskf[:, :], ALU.mult, ALU.mult),
                          reads=[PK[4], ("SC", slot), "maskf"], writes=[("Sf", h)])
                    P.dve(STT(Sb_[h][:, :], sps[:, h, :], SC[:, slot, 4 + h:5 + h], maskb[:, :], ALU.mult, ALU.mult),
                          reads=[PK[4], ("SC", slot), "maskb"], writes=[("Sb", h)])
                nbs = [5, 6, 7, next_bank()]
                for h in range(4):
                    nb = nbs[h]
                    nf = psb[nb][:, 0:129]
                    nbk = psb[nb][:, 129:258]
                    P.pe(MM(nf, Sf[h][:, :], vml[:, c, h, :], True, False), reads=[("Sf", h), ("vml", c)], writes=[PK[nb]])
                    P.pe(MM(nf, qkT[:, h, csl], Cfb[:, h, :], False, True), reads=[("qkT", h), "Cfb"], writes=[PK[nb]])
                    P.pe(MM(nbk, Sb_[h][:, :], vml[:, c, h, :], True, False), reads=[("Sb", h), ("vml", c)], writes=[PK[nb]])
                    P.pe(MM(nbk, qkT[:, h, csl], cbl[li][:, h, :], False, True), reads=[("qkT", h), ("cbl", li)], writes=[PK[nb]])
                state_apply(Cf, "Cf", souts, slot, 0)
                P.act(ACTF(Cfb[:], Cf[:], AF.Copy), reads=["Cf"], writes=["Cfb"])
                return nbs

            def ml_Xb(c, nbs):
                slot = T * 4 + c
                for h in range(4):
                    nb = nbs[h]
                    d0 = 16 + 4 * h
                    P.act(ACTF(st2[:, d0:d0 + 2], psb[nb][:, 128:258:129], AF.Abs), reads=[PK[nb]], writes=[("st1d", h)])
                for h in range(4):
                    d0 = 16 + 4 * h
                    P.dve(TT(st2[:, d0:d0 + 2], st2[:, d0:d0 + 2], SC[:, slot, 16 + h:21 + h:4], ALU.max), reads=[("st1d", h), ("SC", slot)], writes=[("st1d", h)])
                    P.dve(RECIP(st2[:, d0 + 2:d0 + 4], st2[:, d0:d0 + 2]), reads=[("st1d", h)], writes=[("st1e", h)])
                for h in range(4):
                    nb = nbs[h]
                    d0 = 16 + 4 * h
                    hs = slice(h * 128, (h + 1) * 128)
                    P.act(ACTF(hbuf[:, hs], psb[nb][:, 0:128], AF.Copy, scale=st2[:, d0 + 2:d0 + 3]), reads=[PK[nb], ("st1e", h)], writes=[("hbuf", h)])
                for h in range(4):
                    nb = nbs[h]
                    d0 = 16 + 4 * h
                    hs = slice(h * 128, (h + 1) * 128)
                    P.dve(STT(hbuf[:, hs], psb[nb][:, 129:257], st2[:, d0 + 3:d0 + 4], hbuf[:, hs], ALU.mult, ALU.add),
                          reads=[PK[nb], ("st1e", h), ("hbuf", h)], writes=[("hbuf", h)])
                if dbg:
                    t0 = slot * 128
                    P.dma(DMA(dbg_out["d_h"][t0:t0 + 128, :], hbuf[:, :]), reads=HBK, semkey="dbg2")

            def ml_Y1(c):
                P.dve(STT(h2[:, :], oG[:, c, :], 1.0, hbuf[:, :], ALU.add, ALU.mult), reads=[("oG", c)] + HBK, writes=["h2"])

            def ml_Y2(c):
                csl = slice(c * 128, (c + 1) * 128)
                h3 = h2[:, :].rearrange("p (h d) -> p h d", h=4)
                P.dve(lambda e, h3=h3: e.tensor_reduce(out=st1[:, 12:16], in_=h3, axis=AX.X, op=ALU.add), reads=["h2"], writes=["st1m"])
                for h in range(4):
                    P.act(ACTF(ftmp[:, h * 128:(h + 1) * 128], h2[:, h * 128:(h + 1) * 128], AF.Square, accum_out=st1[:, 8 + h:9 + h]),
                          reads=["h2"], writes=["ftmp", ("st1q", h)])
                SQK = [("st1q", h) for h in range(4)]
                P.dve(TS(st1[:, 12:16], st1[:, 12:16], 1.0 / 128), reads=["st1m"], writes=["st1m"])
                P.dve(TT(st1[:, 4:8], st1[:, 12:16], st1[:, 12:16], ALU.mult), reads=["st1m"], writes=["st1r"])
                P.dve(STT(st1[:, 8:12], st1[:, 8:12], 1.0 / 128, st1[:, 4:8], ALU.mult, ALU.subtract), reads=SQK + ["st1r"], writes=["st1v"])
                P.dve(TS(st1[:, 8:12], st1[:, 8:12], 4.0 * EPS, None, ALU.add), reads=["st1v"], writes=["st1v"])
                P.pool(TT(st1[:, 8:12], st1[:, 8:12], cst[:, 0:1].to_broadcast([128, 4]), ALU.pow), reads=["st1v", "cst"], writes=["st1v"])
                for h in range(4):
                    hs = slice(h * 128, (h + 1) * 128)
                    P.dve(TS(h2[:, hs], h2[:, hs], st1[:, 12 + h:13 + h], st1[:, 8 + h:9 + h], ALU.subtract, ALU.mult),
                          reads=["h2", "st1m", "st1v"], writes=["h2"])
                P.dve(TT(gB[:, :], h2[:, :], zs[:, c, :], ALU.mult), reads=["h2", ("zs", c)], writes=["gB"])
                pT = pTv()
                for fc in range(4):
                    P.pe(TR(pT[:, fc, :], gB[:, fc * 128:(fc + 1) * 128], ident[:, :]), reads=["gB", "ident"], writes=[PK[3]])
                P.act(ACTF(gBT[:, :, csl], pT[:, 0:4, :], AF.Copy), reads=[PK[3]], writes=["gBT"])

            def fill(k):
                if filler is not None:
                    for _ in range(k):
                        next(filler, None)

            for c in range(4):
                nbs_c = ml_X(c)
                if c > 0:
                    ml_Y2(c - 1)
                ml_Xb(c, nbs_c)
                fill(2)
                ml_Y1(c)
            ml_Y2(3)
            fill(8)

        def out_branch(T, which):
            xt, xkey = xnT[T % 2], ("xnT", T % 2)
            (cbr, cg0, cg1, gsrc, gkey, tg, tgk, first) = ((C_WA, C_GA0, C_GA1, gAT, "gAT", tgA, "tgA", True) if which == 0 else
                                                          (C_WB, C_GB0, C_GB1, gBT, "gBT", tgB, "tgB", False))
            wbr_, kbr = wload(cbr)
            wbr = wbr_[:].rearrange("p a b -> p (a b)").rearrange("p (fc n) -> p fc n", fc=4)
            for half, cg in enumerate((cg0, cg1)):
                wG, kG = wload(cg)
                for fc in range(4):
                    n = half * 4 + fc
                    bG = next_bank()
                    for kc in range(8):
                        P.pe(MM(psb[bG][:, :], wG[:, kc, fc * 128:(fc + 1) * 128], xt[:, kc, :], kc == 0, kc == 7), reads=[kG, xkey], writes=[PK[bG]])
                    P.act(ACTF(tg[:, :], psb[bG][:, :], AF.Tanh, bias=fmh[:, FM_B[cg] + fc:FM_B[cg] + fc + 1], scale=0.5),
                          reads=[PK[bG], "fmh"], writes=[tgk])
                    by = next_bank()
                    for k4 in range(4):
                        P.pe(MM(psb[by][:, :], wbr[:, k4, n * 128:(n + 1) * 128], gsrc[:, k4, :], k4 == 0, k4 == 3), reads=[kbr, gkey], writes=[PK[by]])
                    if first:
                        P.dve(STT(ypT[:, n, :], tg[:, :], 1.0, psb[by][:, :], ALU.add, ALU.mult), reads=[tgk, PK[by]], writes=[("ypT", n)])
                    else:
                        P.dve(STT(tg[:, :], tg[:, :], 1.0, psb[by][:, :], ALU.add, ALU.mult), reads=[tgk, PK[by]], writes=[tgk])
                        P.pool(TT(ypT[:, n, :], ypT[:, n, :], tg[:, :], ALU.add), reads=[tgk, ("ypT", n)], writes=[("ypT", n)])
                    yield n

        def out_tile(T, hook=None, skip_a=False):
            xt, xkey = xnT[T % 2], ("xnT", T % 2)
            for which in ((1,) if skip_a else (0, 1)):
                for _ in out_branch(T, which):
                    pass
            wO0, kO0 = wload(C_WO0)
            wO1, kO1 = wload(C_WO1)
            YK = [("ypT", n) for n in range(8)]
            T2 = T + 2 if hook is not None else None
            ltok = None
            if T2 is not None:
                ltok = ln_A(xu[T2 * 512:T2 * 512 + 128, :])
            for j in range(4):
                i = j % 2
                t0 = T * 512 + j * 128
                P.dma(DMA(osb[i][:], xu[t0:t0 + 128, :]), writes=[("osb", i)], reads=[("y", "st", i)], semkey=("xres", i))
                b0 = next_bank()
                b1 = next_bank()
                for (b, wO, kO) in ((b0, wO0, kO0), (b1, wO1, kO1)):
                    for k8 in range(8):
                        P.pe(MM(psb[b][:, :], ypT[:, k8, j * 128:(j + 1) * 128], wO[:, k8, :], k8 == 0, k8 == 7), reads=YK + [kO], writes=[PK[b]])
                if T2 is not None:
                    ln_B(ltok, xnT[T2 % 2][:, :, j * 128:(j + 1) * 128], ("xnT", T2 % 2))
                    if j < 3:
                        ltok = ln_A(xu[T2 * 512 + (j + 1) * 128:T2 * 512 + (j + 2) * 128, :])
                c0 = 8 + 4 * i
                oa, oa2, ob, oc = ("o_a", i), ("o_a2", i), ("o_b", i), ("o_c", i)
                P.act(ACTF(gA[:, :], psb[b0][:, :], AF.Square, accum_out=st2[:, c0:c0 + 1]), reads=[PK[b0]], writes=["gA", oa])
                P.act(ACTF(gB[:, :], psb[b1][:, :], AF.Square, accum_out=st2[:, c0 + 3:c0 + 4]), reads=[PK[b1]], writes=["gB", oa2])
                P.dve(TT(st2[:, c0 + 1:c0 + 2], st2[:, c0:c0 + 1], st2[:, c0 + 3:c0 + 4], ALU.add), reads=[oa, oa2], writes=[ob])
                P.dve(TS(st2[:, c0 + 1:c0 + 2], st2[:, c0 + 1:c0 + 2], 1.0 / D, 4.0 * EPS, ALU.mult, ALU.add), reads=[ob], writes=[ob])
                P.pool(TT(st2[:, c0 + 2:c0 + 3], st2[:, c0 + 1:c0 + 2], cst[:, 0:1], ALU.pow), reads=[ob, "cst"], writes=[oc])
                P.dve(STT(tgA[:, :], psb[b0][:, :], st2[:, c0 + 2:c0 + 3], gpost_bc[:, 0:512], ALU.mult, ALU.mult),
                      reads=[PK[b0], oc, "gpost_bc"], writes=["tgA"])
                P.dve(STT(tgB[:, :], psb[b1][:, :], st2[:, c0 + 2:c0 + 3], gpost_bc[:, 512:1024], ALU.mult, ALU.mult),
                      reads=[PK[b1], oc, "gpost_bc"], writes=["tgB"])
                P.pool(TT(osb[i][:, 0:512], osb[i][:, 0:512], tgA[:, :], ALU.add), reads=[("osb", i), "tgA"], writes=[("osb", i)])
                P.pool(TT(osb[i][:, 512:1024], osb[i][:, 512:1024], tgB[:, :], ALU.add), reads=[("osb", i), "tgB"], writes=[("osb", i)])
                P.dma(DMA(y[t0:t0 + 128, :], osb[i][:, :]), reads=[("osb", i)], writes=[("y", "st", i)], semkey=("osb", i), queue="pool")

        def ln_sub(T2, j):
            load_norm_T(xu[T2 * 512 + j * 128:T2 * 512 + (j + 1) * 128, :], 1, xnT[T2 % 2][:, :, j * 128:(j + 1) * 128], ("xnT", T2 % 2))

        kv_proj(0)
        for T in range(NT):
            if T + 1 < NT:
                kv_proj(T + 1)
            na_q_z(T)
            na_tile(T)
            mlstm_tile(T, filler=out_branch(T, 0))
            out_tile(T, hook=(lambda j, T=T: ln_sub(T + 2, j)) if T + 2 < NT else None, skip_a=True)

        P.finalize(st)
    return nc


def _host_tables(na_rpb):
    rpb = np.asarray(na_rpb, np.float32).reshape(8, 15, 31)
    kc = np.arange(64)[:, None]
    qc = np.arange(64)[None, :]
    dc = np.clip(kc - qc + 15, 0, 30)
    tzv = np.ascontiguousarray(np.transpose(rpb[:, :, dc], (2, 0, 1, 3)))
    c0 = np.clip(qc - 8, 0, 48)
    valid = ((kc >= c0) & (kc < c0 + 16)).astype(np.float32)
    cm = np.concatenate([valid, valid], axis=0)
    return tzv.astype(np.float32), cm.astype(np.float32)


def _make_in_maps(inputs, units_per_core, conts):
    tzv, cm = _host_tables(inputs["na_rpb"])
    common = {
        "meta": np.ascontiguousarray(inputs["meta_tokens"], np.float32),
        "g_pre": np.ascontiguousarray(inputs["g_pre"], np.float32).reshape(1, D),
        "w_in": np.ascontiguousarray(inputs["w_in"], np.float32).reshape(D, NIN),
        "b_in": np.ascontiguousarray(inputs["b_in"], np.float32).reshape(1, NIN),
        "tz": tzv, "cmask": cm,
        "conv_w": np.ascontiguousarray(inputs["ml_conv_w"], np.float32).reshape(3, D),
        "head_g": np.ascontiguousarray(inputs["ml_head_g"], np.float32).reshape(1, 512),
        "w_a": np.ascontiguousarray(inputs["w_a"], np.float32).reshape(512, D),
        "w_b": np.ascontiguousarray(inputs["w_b"], np.float32).reshape(512, D),
        "w_out": np.ascontiguousarray(inputs["w_out"], np.float32).reshape(D, D),
        "g_post": np.ascontiguousarray(inputs["g_post"], np.float32).reshape(1, D),
    }
    maps = []
    for xu, cont in zip(units_per_core, conts):
        fl = np.zeros((128, 4), np.float32)
        fl[:, 0] = cont
        fl[:, 1] = 1.0 - cont
        fl[0:4, 2] = 1.0
        m = dict(common)
        m["xu"] = np.ascontiguousarray(xu, np.float32)
        m["flags"] = fl
        maps.append(m)
    return maps


_NC_CACHE = {}


def kernel(x_prompt, x_sample, meta_tokens, g_pre, w_in, b_in, na_rpb, ml_conv_w, ml_head_g, w_a, w_b, w_out, g_post):
    x_prompt = np.asarray(x_prompt, np.float32)
    x_sample = np.asarray(x_sample, np.float32)
    inputs = dict(meta_tokens=meta_tokens, g_pre=g_pre, w_in=w_in, b_in=b_in, na_rpb=na_rpb, ml_conv_w=ml_conv_w,
                  ml_head_g=ml_head_g, w_a=w_a, w_b=w_b, w_out=w_out, g_post=g_post)
    R = 64
    units, conts = [], []
    for s in range(2):
        units.append(x_sample[s])
        conts.append(1.0)
    for c in range(4):
        units.append(np.concatenate([x_prompt[2 * c], x_prompt[2 * c + 1]], axis=0))
        conts.append(0.0)
    for c in range(2):
        units.append(np.concatenate([x_prompt[2 * c], x_prompt[2 * c + 1]], axis=0))
        conts.append(0.0)
    if R not in _NC_CACHE:
        _NC_CACHE[R] = build(R)
    nc = _NC_CACHE[R]
    in_maps = _make_in_maps(inputs, units, conts)
    res = run_bass_kernel_spmd(nc, in_maps, core_ids=list(range(8)))
    outs = [np.asarray(r["y"], np.float32) for r in res.results]
    y_sample = np.stack([outs[0], outs[1]], axis=0)
    yp = []
    for c in range(4):
        yp.append(outs[2 + c][:4096])
        yp.append(outs[2 + c][4096:])
    y_prompt = np.stack(yp, axis=0)
    return (y_prompt, y_sample)
```

```python
from contextlib import ExitStack
import numpy as np
import concourse.bass as bass
import concourse.mybir as mybir
from concourse.bass_utils import run_bass_kernel_spmd

F32 = mybir.dt.float32
BF16 = mybir.dt.bfloat16
AF = mybir.ActivationFunctionType
ALU = mybir.AluOpType
AX = mybir.AxisListType

ENGS = ("pe", "act", "dve", "pool", "sp")
NOSYNC = set()


class Op:
    __slots__ = ("eng", "emit", "deps", "dma", "semkey", "tok", "sig")

    def __init__(self, eng, emit, dma, semkey):
        self.eng = eng
        self.emit = emit
        self.deps = []
        self.dma = dma
        self.semkey = semkey
        self.tok = None
        self.sig = False


class Prog:
    def __init__(self, nc, same_engine_sync=True):
        self.nc = nc
        self.q = {e: [] for e in ENGS}
        self.last_w = {}
        self.readers = {}
        self.same_engine_sync = same_engine_sync

    def add(self, eng, emit, reads=(), writes=(), dma=False, semkey=None):
        op = Op(eng, emit, dma, semkey)
        self.count = getattr(self, "count", 0) + 1
        if self.count > getattr(self, "limit", 10 ** 9):
            return op
        deps = {}
        for k in reads:
            w = self.last_w.get(k)
            if w is not None:
                deps[id(w)] = w
            if isinstance(k, str) and k.startswith("ps"):
                for r in self.readers.get(k, ()):
                    if r.eng != eng:
                        deps[id(r)] = r
        for k in writes:
            w = self.last_w.get(k)
            if w is not None:
                deps[id(w)] = w
            for r in self.readers.get(k, ()):
                deps[id(r)] = r
        op.deps = list(deps.values())
        for k in writes:
            self.last_w[k] = op
            self.readers[k] = []
        for k in reads:
            self.readers.setdefault(k, []).append(op)
        self.q[eng].append(op)
        return op

    def pe(self, emit, reads=(), writes=()):
        return self.add("pe", emit, reads, writes)

    def act(self, emit, reads=(), writes=()):
        return self.add("act", emit, reads, writes)

    def dve(self, emit, reads=(), writes=()):
        return self.add("dve", emit, reads, writes)

    def pool(self, emit, reads=(), writes=()):
        return self.add("pool", emit, reads, writes)

    def ew(self, eng, emit, reads=(), writes=()):
        return self.add(eng, emit, reads, writes)

    def dma(self, emit, reads=(), writes=(), semkey=None, queue="sp"):
        return self.add(queue, emit, reads, writes, dma=True, semkey=semkey)

    def _skip(self, d, op):
        if d.dma and op.dma and isinstance(d.semkey, str) and d.semkey.startswith("setup") and d.semkey == op.semkey:
            return True
        return (not d.dma) and (not op.dma) and d.eng == op.eng and (d.eng == "pe" or d.eng in NOSYNC or not self.same_engine_sync)

    def finalize(self, stack):
        nc = self.nc
        for e in ENGS:
            for op in self.q[e]:
                for d in op.deps:
                    if d.dma or self._skip(d, op):
                        continue
                    d.sig = True
        eng_sems = {e: [stack.enter_context(nc.semaphore("s_%s0" % e))] for e in ENGS}
        dma_sems = {}
        dma_cnt = {}
        LIM = 30000
        for e in ENGS:
            cnt = 0
            for op in self.q[e]:
                if op.dma:
                    if op.semkey not in dma_sems:
                        dma_sems[op.semkey] = stack.enter_context(nc.semaphore("d%d" % len(dma_sems)))
                        dma_cnt[op.semkey] = 0
                    dma_cnt[op.semkey] += 16
                    op.tok = (dma_sems[op.semkey], dma_cnt[op.semkey])
                elif op.sig:
                    if cnt >= LIM:
                        eng_sems[e].append(stack.enter_context(nc.semaphore("s_%s%d" % (e, len(eng_sems[e])))))
                        cnt = 0
                    cnt += 1
                    op.tok = (eng_sems[e][-1], cnt)
        assert max([0] + list(dma_cnt.values())) < 60000, "dma sem overflow"
        for e in ENGS:
            for op in self.q[e]:
                if op.dma and isinstance(op.semkey, str) and op.semkey.startswith("setup"):
                    op.tok = (dma_sems[op.semkey], dma_cnt[op.semkey])
        self.dma_sems = dma_sems
        self.dma_cnt = dma_cnt
        block = stack.enter_context(nc.Block())
        prog = self

        def run_queue(e, engine):
            known = {}
            for op in prog.q[e]:
                need = {}
                for d in op.deps:
                    if prog._skip(d, op):
                        continue
                    sem, val = d.tok
                    key = sem.num
                    if known.get(key, 0) >= val:
                        continue
                    if key not in need or need[key][1] < val:
                        need[key] = (sem, val)
                for key, (sem, val) in need.items():
                    engine.wait_ge(sem, val)
                    known[key] = val
                ins = op.emit(engine)
                if op.dma:
                    ins.then_inc(op.tok[0], 16)
                elif op.sig:
                    ins.then_inc(op.tok[0], 1)

        @block.tensor
        def _(eng):
            run_queue("pe", eng)

        @block.scalar
        def _(eng):
            run_queue("act", eng)

        @block.vector
        def _(eng):
            run_queue("dve", eng)

        @block.gpsimd
        def _(eng):
            run_queue("pool", eng)

        @block.sync
        def _(eng):
            run_queue("sp", eng)
            for key, sem in prog.dma_sems.items():
                eng.wait_ge(sem, prog.dma_cnt[key])


def MM(out, lhsT, rhs, start=True, stop=True):
    return lambda e: e.matmul(out, lhsT=lhsT, rhs=rhs, start=start, stop=stop)


def TR(out, in_, ident):
    return lambda e: e.transpose(out=out, in_=in_, identity=ident)


def ACTF(out, in_, func, bias=None, scale=1.0, accum_out=None):
    def f(e):
        kw = {}
        if bias is not None:
            kw["bias"] = bias
        if accum_out is not None:
            kw["accum_out"] = accum_out
        return e.activation(out=out, in_=in_, func=func, scale=scale, **kw)
    return f


def TT(out, in0, in1, op):
    return lambda e: e.tensor_tensor(out=out, in0=in0, in1=in1, op=op)


def TS(out, in0, s1, s2=None, op0=ALU.mult, op1=None):
    if op1 is None:
        return lambda e: e.tensor_scalar(out=out, in0=in0, scalar1=s1, scalar2=None, op0=op0)
    return lambda e: e.tensor_scalar(out=out, in0=in0, scalar1=s1, scalar2=s2, op0=op0, op1=op1)


def STT(out, in0, scalar, in1, op0, op1):
    return lambda e: e.scalar_tensor_tensor(out=out, in0=in0, scalar=scalar, in1=in1, op0=op0, op1=op1)


def CP(out, in_):
    return lambda e: e.tensor_copy(out=out, in_=in_)


def MS(ap, val):
    return lambda e: e.memset(ap, val)


def RECIP(out, in_):
    return lambda e: e.reciprocal(out=out, in_=in_)


def DMA(out, in_):
    return lambda e: e.dma_start(out=out, in_=in_)


D = 1024
NIN = 6672
GATE_OFF = 4608
EPS = 1e-6
KAPPA = 0.25 * (128.0 ** -0.5)
CH_OFF = [0, 512, 1024, 1536, 2048, 2560, 3072, 3584, 4096, 4624, 5136, 5648, 6160]
C_NAQ, C_NAK, C_NAV, C_NAZ, C_MLQ, C_MLK, C_MLV, C_MLZ, C_MLO, C_GA0, C_GA1, C_GB0, C_GB1 = range(13)
C_WA, C_WB, C_WO0, C_WO1 = 13, 14, 15, 16
FM_B = {C_NAQ: 0, C_NAK: 4, C_MLQ: 8, C_MLK: 12, C_GA0: 16, C_GA1: 20, C_GB0: 24, C_GB1: 28}
FM_CW = 32
FM_G = 56
FM_N = 64
TB = {C_NAV: 0, C_NAZ: 512, C_MLV: 1024, C_MLZ: 1536, C_MLO: 2048}


LIMIT = [10 ** 9]


def build(R, dbg=False, stop=99):
    U = 2
    NR = U * R
    NTOK = NR * 64
    NT = NR // 8
    NS = NR // 2
    TPU = R // 8
    RING_T = 3
    RROWS = RING_T * 8
    nc = bass.Bass("TRN2", target_bir_lowering=False)

    def din(name, shape):
        return nc.dram_tensor(name, shape, F32, kind="ExternalInput").ap()

    xu = din("xu", [NTOK, D])
    meta = din("meta", [16, D])
    g_pre = din("g_pre", [1, D])
    w_in = din("w_in", [D, NIN])
    b_in = din("b_in", [1, NIN])
    tz = din("tz", [64, 8, 15, 64])
    cmask = din("cmask", [128, 64])
    conv_w = din("conv_w", [3, D])
    head_g = din("head_g", [1, 512])
    w_a = din("w_a", [512, D])
    w_b = din("w_b", [512, D])
    w_out = din("w_out", [D, D])
    g_post = din("g_post", [1, D])
    flags = din("flags", [128, 4])
    y = nc.dram_tensor("y", [NTOK, D], F32, kind="ExternalOutput").ap()
    wq = nc.dram_tensor("wq", [17, 128, 8, 512], BF16, kind="Internal").ap()
    cbs = nc.dram_tensor("cbs", [NS, 128, 4 * 129], BF16, kind="Internal").ap()
    dbg_out = {}
    if dbg:
        dbg_out["d_h"] = nc.dram_tensor("d_h", [NTOK, 512], F32, kind="ExternalOutput").ap()
        dbg_out["d_na"] = nc.dram_tensor("d_na", [NTOK, 512], F32, kind="ExternalOutput").ap()

    st = ExitStack()
    with st:
        def sb(name, shape, dt=F32):
            return st.enter_context(nc.sbuf_tensor(name, shape, dt))

        P = Prog(nc)
        P.limit = LIMIT[0]
        psb = [st.enter_context(nc.psum_tensor("ps%d" % i, [128, 512], F32)) for i in range(4)]
        psS = [st.enter_context(nc.psum_tensor("psS%d" % i, [128, 1024], F32)) for i in range(2)]
        psb += [psS[0][:, 0:512], psS[0][:, 512:1024], psS[1][:, 0:512], psS[1][:, 512:1024]]
        PK = ["ps%d" % i for i in range(8)]

        ident = sb("ident", [128, 128], BF16)
        identf = sb("identf", [128, 128])
        maskf = sb("maskf", [128, 128])
        maskb = sb("maskb", [128, 128])
        fl = sb("fl", [128, 4])
        fm = sb("fm", [128, FM_N])
        fmh = sb("fmh", [128, FM_N])
        cst = sb("cst", [128, 2])
        gbias = sb("gbias", [8, 2])
        wg = sb("wg", [128, 8, 16], BF16)
        etab = sb("etab", [128, 8, 16, 64], BF16)
        bias_bc = sb("bias_bc", [128, 2560])
        gpost_bc = sb("gpost_bc", [128, D])
        hg_bc = sb("hg_bc", [128, 512])
        SC = sb("SC", [128, NS + U, 24])
        EG = sb("EG", [128, NS + U, 8])
        kmT = sb("kmT", [128, 4, 16], BF16)
        vmp = sb("vmp", [128, 8, 65], BF16)
        prem = sb("prem", [128, 8])
        premk = sb("premk", [128, 4, 16])
        vmeta = sb("vmeta", [128, 4, 129], BF16)
        kmetaT = sb("kmetaT", [128, 4, 128], BF16)
        firstpre = sb("firstpre", [128, U, 8, 1])
        Cmeta = sb("Cmeta", [128, U, 4, 129])
        Cb = sb("Cb", [128, 4, 129])
        Cf = sb("Cf", [128, 4, 129])
        Cfb = sb("Cfb", [128, 4, 129], BF16)
        ws = [sb("ws%d" % i, [128, 8, 512], BF16) for i in range(3)]
        xs = [sb("xs%d" % i, [128, D]) for i in range(2)]
        xnb = [sb("xnb%d" % i, [128, D], BF16) for i in range(2)]
        xnT = [sb("xnT%d" % i, [128, 8, 512], BF16) for i in range(2)]
        st1 = sb("st1", [128, 16])
        st2 = sb("st2", [128, 32])
        KT = sb("KT", [128, 4, RROWS * 64], BF16)
        VP = sb("VP", [128, RROWS // 2, 8, 65], BF16)
        zs = sb("zs", [128, 4, 512], BF16)
        oG = sb("oG", [128, 4, 512], BF16)
        pre = [sb("pre0", [128, 514])] * 2
        cvt = [sb("cvt0", [128, 512])] * 2
        qkT = sb("qkT", [128, 8, 512], BF16)
        carry = sb("carry", [128, 8, 2])
        halo = sb("halo", [128, 8, 2])
        hadj = sb("hadj", [128, 8, 2])
        vml = sb("vml", [128, 4, 4, 129], BF16)
        kk = sb("kk", [128, 4, 128], BF16)
        Sf = [sb("Sf%d" % i, [128, 128], BF16) for i in range(4)]
        Sb_ = [sb("Sb%d" % i, [128, 128], BF16) for i in range(4)]
        uv = [sb("uv%d" % i, [128, 129], BF16) for i in range(2)]
        cbl = [sb("cbl%d" % i, [128, 4, 129], BF16) for i in range(2)]
        cbst = [sb("cbst%d" % i, [128, 4, 129], BF16) for i in range(2)]
        hbuf = sb("hbuf", [128, 512])
        h2 = sb("h2", [128, 512])
        gs2 = sb("gs2", [8, 8])
        onesg = sb("onesg", [8, 128])
        ypT = sb("ypT", [128, 8, 512], BF16)
        gsc = ypT[0:8, :, :].rearrange("p a b -> p (a b)").bitcast(F32).rearrange("p (a b) -> p a b", a=4)
        nao = [hbuf, h2]
        HBK = [("hbuf", h) for h in range(4)]
        NAOK = [HBK, ["h2"]]

        cont = fl[:, 0:1]
        ncont = fl[:, 1:2]

        for (dst, so) in ((0, 0), (4, 8), (8, 4), (12, 12)):
            src = w_in[:, GATE_OFF + so:GATE_OFF + so + 4].rearrange("(kc p) j -> p kc j", p=128)
            P.dma(DMA(wg[:, :, dst:dst + 4], src), writes=["wg"], semkey="setup_w", queue="pool")
        for c in (C_NAK, C_NAV, C_MLQ, C_MLK, C_MLV, C_NAQ, C_NAZ, C_MLZ, C_MLO, C_GA0, C_GA1, C_GB0, C_GB1):
            src = w_in[:, CH_OFF[c]:CH_OFF[c] + 512].rearrange("(kc p) j -> p kc j", p=128)
            P.dma(DMA(wq[c], src), writes=[("wq", c)], semkey=("wqc", c), queue="pool")
        P.dma(DMA(wq[C_WA].rearrange("p a b -> p (a b)").rearrange("p (fc n) -> p fc n", fc=4),
                  w_a.rearrange("(fc p) n -> p fc n", p=128)), writes=[("wq", C_WA)], semkey=("wqc", C_WA), queue="pool")
        P.dma(DMA(wq[C_WB].rearrange("p a b -> p (a b)").rearrange("p (fc n) -> p fc n", fc=4),
                  w_b.rearrange("(fc p) n -> p fc n", p=128)), writes=[("wq", C_WB)], semkey=("wqc", C_WB), queue="pool")
        for hf in range(2):
            P.dma(DMA(wq[C_WO0 + hf], w_out[:, hf * 512:(hf + 1) * 512].rearrange("(fc p) n -> p fc n", p=128)),
                  writes=[("wq", C_WO0 + hf)], semkey=("wqc", C_WO0 + hf), queue="pool")

        P.dma(DMA(fl[:], flags), writes=["fl"], semkey="setup_c")
        P.dma(DMA(bias_bc[:, 0:1024], b_in[:, 1024:2048].partition_broadcast(128)), writes=["bias_bc"], semkey="setup_c")
        P.dma(DMA(bias_bc[:, 1024:2560], b_in[:, 3072:4608].partition_broadcast(128)), writes=["bias_bc"], semkey="setup_c")
        P.dma(DMA(gpost_bc[:], g_post.partition_broadcast(128)), writes=["gpost_bc"], semkey="setup_c")
        P.dma(DMA(hg_bc[:], head_g.partition_broadcast(128)), writes=["hg_bc"], semkey="setup_c")
        P.pool(MS(identf[:], 0.0), writes=["identf"])
        P.pool(lambda e: e.affine_select(out=identf[:], in_=identf[:], pattern=[[-1, 128]], compare_op=ALU.not_equal,
                                         fill=1.0, base=0, channel_multiplier=1), reads=["identf"], writes=["identf"])
        P.dve(CP(ident[:], identf[:]), reads=["identf"], writes=["ident"])
        P.pool(MS(maskf[:], 1.0), writes=["maskf"])
        P.pool(lambda e: e.affine_select(out=maskf[:], in_=maskf[:], pattern=[[1, 128]], compare_op=ALU.is_ge,
                                         fill=0.0, base=0, channel_multiplier=-1), reads=["maskf"], writes=["maskf"])
        P.pool(MS(maskb[:], 1.0), writes=["maskb"])
        P.pool(lambda e: e.affine_select(out=maskb[:], in_=maskb[:], pattern=[[-1, 128]], compare_op=ALU.is_ge,
                                         fill=0.0, base=0, channel_multiplier=1), reads=["maskb"], writes=["maskb"])
        P.pool(MS(cst[:, 0:1], -0.5), writes=["cst"])
        P.pool(MS(onesg[:], 1.0), writes=["onesg"])
        P.pool(MS(Cb[:], 0.0), writes=["Cb"])
        P.pool(MS(VP[:, :, :, 64:65], 1.0), writes=[("VP", i) for i in range(RING_T)])
        P.pool(MS(vmp[:], 0.0), writes=["vmp"])
        P.pool(MS(vmp[0:16, :, 64:65], 1.0), reads=["vmp"], writes=["vmp"])
        P.pool(MS(vml[:, :, :, 128:129], 1.0), writes=[("vml", j) for j in range(4)])
        P.pool(MS(carry[:], 0.0), writes=["carry"])
        P.pool(MS(vmeta[:], 0.0), writes=["vmeta"])
        P.pool(MS(kmetaT[:], 0.0), writes=["kmetaT"])

        stg = ExitStack()
        rowst = stg.enter_context(nc.sbuf_tensor("rowst", [64, 128], F32))
        tzs = stg.enter_context(nc.sbuf_tensor("tzs", [128, 4, 16, 64], F32))
        cmk = stg.enter_context(nc.sbuf_tensor("cmk", [128, 64], F32))
        xnTm = stg.enter_context(nc.sbuf_tensor("xnTm", [128, 8, 128], BF16))
        P.pool(MS(xnTm[:], 0.0), writes=["xnTm"])
        if True:
            P.pool(MS(rowst[:], 0.0), writes=["rowst"])
            for c, col in FM_B.items():
                P.dma(DMA(rowst[col:col + 4, :], b_in[0, CH_OFF[c]:CH_OFF[c] + 512].rearrange("(c p) -> c p", p=128)),
                      writes=["rowst"], semkey="setup_c")
            for j in range(3):
                P.dma(DMA(rowst[FM_CW + 8 * j:FM_CW + 8 * j + 8, :], conv_w[j, :].rearrange("(c p) -> c p", p=128)),
                      writes=["rowst"], semkey="setup_c")
            P.dma(DMA(rowst[FM_G:FM_G + 8, :], g_pre[0, :].rearrange("(c p) -> c p", p=128)), writes=["rowst"], semkey="setup_c")
            P.pe(MM(psb[0][:, 0:FM_N], rowst[0:FM_N, :], identf[0:FM_N, 0:FM_N]), reads=["rowst", "identf"], writes=[PK[0]])
            P.dve(CP(fm[:], psb[0][:, 0:FM_N]), reads=[PK[0]], writes=["fm"])
            P.dve(TS(fmh[:], fm[:], 0.5), reads=["fm"], writes=["fmh"])
            P.dve(TS(fmh[:, 0:4], fm[:, 0:4], 0.125), reads=["fm", "fmh"], writes=["fmh"])
            P.dve(TT(fmh[:, 56:64], fm[:, FM_CW:FM_CW + 8], fm[:, FM_CW + 8:FM_CW + 16], ALU.add), reads=["fm", "fmh"], writes=["fmh"])
            P.dve(TT(fmh[:, 56:64], fmh[:, 56:64], fm[:, FM_CW + 16:FM_CW + 24], ALU.add), reads=["fm", "fmh"], writes=["fmh"])
            P.dve(TT(fmh[:, 56:64], fmh[:, 56:64], fm[:, 8:16], ALU.mult), reads=["fm", "fmh"], writes=["fmh"])
            for (dst, so, col) in ((0, 0, 0), (4, 8, 0), (0, 4, 1), (4, 12, 1)):
                P.dma(DMA(gbias[dst:dst + 4, col:col + 1], b_in[0, GATE_OFF + so:GATE_OFF + so + 4].rearrange("(p o) -> p o", o=1)),
                      writes=["gbias"], semkey="setup_c")
            P.dve(TS(gbias[:, 1:2], gbias[:, 1:2], -1.0), reads=["gbias"], writes=["gbias"])
            P.dve(TS(hg_bc[:], hg_bc[:], 0.5), reads=["hg_bc"], writes=["hg_bc"])
            P.dma(DMA(cmk[:], cmask), writes=["cmk"], semkey="setup_c")
            for hq in range(2):
                sk = "setup_c" if hq == 0 else "setup_c2"
                hs = slice(4 * hq, 4 * hq + 4)
                P.pool(MS(tzs[:, :, 14:16, :], 0.0), writes=["tzs"])
                P.dma(DMA(tzs[0:64, :, 0:14, :], tz[:, hs, 0:14, :]), writes=["tzs"], semkey=sk)
                P.dma(DMA(tzs[64:128, :, 0:14, :], tz[:, hs, 1:15, :]), writes=["tzs"], semkey=sk)
                P.dma(DMA(tzs[64:128, :, 14, :], tz[:, hs, 3, :]), writes=["tzs"], semkey=sk)
                P.dma(DMA(tzs[0:64, :, 15, :], tz[:, hs, 10, :]), writes=["tzs"], semkey=sk)
                for h4 in range(4):
                    h = 4 * hq + h4
                    P.act(ACTF(tzs[:, h4], tzs[:, h4], AF.Exp), reads=["tzs"], writes=["tzs"])
                    P.dve(TT(etab[:, h], tzs[:, h4], cmk[:, :].unsqueeze(1).to_broadcast([128, 16, 64]), ALU.mult),
                          reads=["tzs", "cmk"], writes=["etab"])
            P.dve(MS(etab[0:64, :, 14, :], 0.0), reads=["etab"], writes=["etab"])
            P.dve(MS(etab[64:128, :, 15, :], 0.0), reads=["etab"], writes=["etab"])
            P.dve(CP(st1[:, 0:1], etab[:, 7, 13, 0:1]), reads=["etab", "fm", "fmh"], writes=["st1a"])

        wstate = {"i": 0}

        def wload(c):
            i = wstate["i"]
            wstate["i"] += 1
            slot = i % 3
            key = ("ws", slot)
            P.dma(DMA(ws[slot][:], wq[c]), reads=[("wq", c)], writes=[key], semkey=("ws", slot))
            return ws[slot], key

        ln_i = {"i": 0}

        def pTv():
            return psb[3][:].bitcast(BF16).rearrange("p (a b) -> p a b", a=8)

        def ln_A(src_ap, meta_rows=None):
            i = ln_i["i"]
            ln_i["i"] += 1
            b = i % 2
            xk, nk = ("xs", b), ("xnb", b)
            if meta_rows is None:
                npart = 128
                P.dma(DMA(xs[b][:], src_ap), writes=[xk], semkey=("xs", b))
            else:
                npart = meta_rows
                P.dma(DMA(xs[b][0:npart, :], src_ap), writes=[xk], semkey=("xs", b))
            c0 = 4 * b
            ka, kb, kc_ = ("ln_a", b), ("ln_b", b), ("ln_c", b)
            P.act(ACTF(xnb[b][0:npart, :], xs[b][0:npart, :], AF.Square, accum_out=st2[0:npart, c0:c0 + 1]), reads=[xk], writes=[nk, ka])
            P.dve(TS(st2[0:npart, c0 + 1:c0 + 2], st2[0:npart, c0:c0 + 1], 1.0 / D, EPS, ALU.mult, ALU.add), reads=[ka], writes=[kb])
            P.pool(TT(st2[0:npart, c0 + 2:c0 + 3], st2[0:npart, c0 + 1:c0 + 2], cst[0:npart, 0:1], ALU.pow), reads=[kb, "cst"], writes=[kc_])
            P.dve(TS(xnb[b][0:npart, :], xs[b][0:npart, :], st2[0:npart, c0 + 2:c0 + 3]), reads=[xk, kc_, nk], writes=[nk])
            return (b, npart)

        def ln_B(tok, dst_ap, dkey):
            b, npart = tok
            nk = ("xnb", b)
            pT = pTv()
            for kc in range(8):
                P.pe(TR(pT[:, kc, 0:npart], xnb[b][0:npart, kc * 128:(kc + 1) * 128], ident[0:npart, 0:npart]),
                     reads=[nk, "ident"], writes=[PK[3]])
            P.dve(TT(dst_ap[:, :, 0:npart], pT[:, :, 0:npart],
                     fm[:, FM_G:FM_G + 8].unsqueeze(2).to_broadcast([128, 8, npart]), ALU.mult),
                  reads=[PK[3], "fm"], writes=[dkey])

        def load_norm_T(src_ap, nsub, dst, dkey, meta_rows=None):
            for j in range(nsub):
                if meta_rows is None:
                    tok = ln_A(src_ap[j * 128:(j + 1) * 128, :])
                else:
                    tok = ln_A(src_ap, meta_rows)
                ln_B(tok, dst[:, :, j * 128:(j + 1) * 128], dkey)

        pbank = {"i": 0}

        def next_bank():
            b = pbank["i"] % 3
            pbank["i"] += 1
            return b

        def proj_F(xT, xkey, ntok, c, handler):
            wsl, wkey = wload(c)
            for fc in range(4):
                b = next_bank()
                for kc in range(8):
                    P.pe(MM(psb[b][:, 0:ntok], wsl[:, kc, fc * 128:(fc + 1) * 128], xT[:, kc, 0:ntok], kc == 0, kc == 7),
                         reads=[wkey, xkey], writes=[PK[b]])
                handler(fc, psb[b][:, 0:ntok], PK[b])

        def proj_T(xT, xkey, nsub, c, handler):
            wsl, wkey = wload(c)
            for j in range(nsub):
                b = next_bank()
                for kc in range(8):
                    P.pe(MM(psb[b][:, :], xT[:, kc, j * 128:(j + 1) * 128], wsl[:, kc, :], kc == 0, kc == 7),
                         reads=[wkey, xkey], writes=[PK[b]])
                handler(j, psb[b][:, :], PK[b])

        def gates_A(xT, xkey, ntok, is_meta=False):
            nch = ntok // 128
            bI = next_bank()
            for kc in range(8):
                P.pe(MM(psb[bI][0:8, 0:ntok], wg[:, kc, 0:8], xT[:, kc, 0:ntok], kc == 0, kc == 7), reads=["wg", xkey], writes=[PK[bI]])
            R0 = gsc[:, 0, 0:ntok]
            R1 = gsc[:, 1, 0:ntok]
            R2 = gsc[:, 2, 0:ntok]
            R3 = gsc[:, 3, 0:ntok]
            K0, K1, K2, K3 = "g_r0", "g_r1", "g_r2", "g_r3"
            P.act(ACTF(R0, psb[bI][0:8, 0:ntok], AF.Exp, bias=gbias[:, 0:1]), reads=[PK[bI], "gbias"], writes=[K0])
            bF = next_bank()
            for kc in range(8):
                P.pe(MM(psb[bF][0:8, 0:ntok], wg[:, kc, 8:16], xT[:, kc, 0:ntok], kc == 0, kc == 7), reads=["wg", xkey], writes=[PK[bF]])
            P.act(ACTF(R1, psb[bF][0:8, 0:ntok], AF.Exp, bias=gbias[:, 1:2], scale=-1.0), reads=[PK[bF], "gbias"], writes=[K1])
            if is_meta:
                P.dve(MS(gsc[:, 1, 16:ntok], 0.0), reads=[K1], writes=[K1])
                P.dve(MS(gsc[:, 0, 16:ntok], 0.0), reads=[K0], writes=[K0])
            P.dve(TS(R1, R1, 1.0, None, ALU.add), reads=[K1], writes=[K1])
            for ci in range(nch):
                sl = slice(ci * 128, (ci + 1) * 128)
                P.dve(lambda e, sl=sl: e.tensor_tensor_scan(out=gsc[:, 2, sl], data0=gsc[:, 1, sl], data1=onesg[:, :], initial=1.0,
                                                            op0=ALU.mult, op1=ALU.mult), reads=[K1, "onesg"], writes=[K2])
            P.dve(RECIP(R3, R2), reads=[K2], writes=[K3])
            for ci in range(nch):
                last = ci * 128 + 127
                P.dve(CP(gs2[:, ci:ci + 1], gsc[:, 3, last:last + 1]), reads=[K3], writes=["g_eg"])
            for ci in range(nch):
                sl = slice(ci * 128, (ci + 1) * 128)
                last = ci * 128 + 127
                P.dve(STT(gsc[:, 1, sl], gsc[:, 3, sl], gsc[:, 2, last:last + 1], gsc[:, 1, sl], ALU.mult, ALU.mult),
                      reads=[K3, K2, K1], writes=[K1])
            P.dve(TT(R3, R2, R1, ALU.subtract), reads=[K2, K1, K3, "g_eg"], writes=[K3])
            P.dve(STT(R2, R3, fl[0:8, 2:3], R1, ALU.mult, ALU.add), reads=[K3, K1, "fl", K2], writes=[K2])
            P.dve(STT(R0, R0, KAPPA, R2, ALU.mult, ALU.mult), reads=[K0, K2], writes=[K0])
            for ci in range(nch):
                sl = slice(ci * 128, (ci + 1) * 128)
                P.dve(TS(gsc[:, 3, sl], gsc[:, 0, sl], gs2[:, ci:ci + 1]), reads=[K0, "g_eg", K3], writes=[K3])
            return nch, (K0, K3, K2)

        def gates_B(tokg, chunks):
            nch, (K0, K3, K2) = tokg
            bS = next_bank()
            for ci in range(nch):
                sl = slice(ci * 128, (ci + 1) * 128)
                for k, (row, key) in enumerate(((0, K0), (3, K3), (2, K2))):
                    P.pe(MM(psb[bS][:, ci * 32 + k * 8:ci * 32 + k * 8 + 8], gsc[:, row, sl], identf[0:8, 0:8]),
                         reads=[key, "identf"], writes=[PK[bS]])
                P.pe(MM(psb[bS][:, ci * 32 + 24:ci * 32 + 32], gs2[:, ci:ci + 1].to_broadcast([8, 128]), identf[0:8, 0:8]),
                     reads=["g_eg", "identf"], writes=[PK[bS]])
            for ci in range(nch):
                slot = chunks[ci]
                P.dve(CP(SC[:, slot, :], psb[bS][:, ci * 32:ci * 32 + 24]), reads=[PK[bS]], writes=[("SC", slot)])
                P.dve(CP(EG[:, slot, :], psb[bS][:, ci * 32 + 24:ci * 32 + 32]), reads=[PK[bS]], writes=[("EG", slot)])

        cv_i = {"i": 0}

        def conv_taps(pr, pk, cv, ck, n, fcg):
            w0 = fm[:, FM_CW + fcg:FM_CW + fcg + 1]
            w1 = fm[:, FM_CW + 8 + fcg:FM_CW + 8 + fcg + 1]
            w2 = fm[:, FM_CW + 16 + fcg:FM_CW + 16 + fcg + 1]
            P.dve(TS(cv[:, 0:n], pr[:, 1:1 + n], w1), reads=[pk, "fm"], writes=[ck])
            P.dve(STT(cv[:, 0:n], pr[:, 0:n], w0, cv[:, 0:n], ALU.mult, ALU.add), reads=[pk, ck, "fm"], writes=[ck])
            P.dve(STT(cv[:, 0:n], pr[:, 2:2 + n], w2, cv[:, 0:n], ALU.mult, ALU.add), reads=[pk, ck, "fm"], writes=[ck])
            P.act(ACTF(pr[:, 1:1 + n], cv[:, 0:n], AF.Tanh, scale=0.5), reads=[ck], writes=[pk])

        def conv_silu(fcg, ps_ap, pkey, ntok, bias_col, lh_ap, rh_ap, dst_ap, dst_key, save_first=None, save_last=None):
            assert ntok == 512
            i = cv_i["i"] % 2
            cv_i["i"] += 1
            if i == 0:
                cv, ck, th, tk = cvt[0][:, 0:512], ("cvt", 0), gA, "gA"
            else:
                cv, ck, th, tk = pre[0][:, 0:512], ("pre", 0), gB, "gB"
            w0 = fm[:, FM_CW + fcg:FM_CW + fcg + 1]
            w1 = fm[:, FM_CW + 8 + fcg:FM_CW + 8 + fcg + 1]
            w2 = fm[:, FM_CW + 16 + fcg:FM_CW + 16 + fcg + 1]
            beta = fmh[:, 56 + fcg:57 + fcg]
            P.act(ACTF(cv, ps_ap, AF.Identity, bias=beta, scale=w1), reads=[pkey, "fm", "fmh"], writes=[ck])
            P.dve(STT(cv[:, 1:512], ps_ap[:, 0:511], w0, cv[:, 1:512], ALU.mult, ALU.add), reads=[pkey, ck, "fm"], writes=[ck])
            P.dve(STT(cv[:, 0:511], ps_ap[:, 1:512], w2, cv[:, 0:511], ALU.mult, ALU.add), reads=[pkey, ck, "fm"], writes=[ck])
            P.dve(STT(cv[:, 0:1], hadj[:, fcg, 0:1], w0, cv[:, 0:1], ALU.mult, ALU.add), reads=["hadj", ck, "fm"], writes=[ck])
            P.dve(STT(cv[:, 511:512], hadj[:, fcg, 1:2], w2, cv[:, 511:512], ALU.mult, ALU.add), reads=["hadj", ck, "fm"], writes=[ck])
            if save_first is not None:
                P.dve(TS(save_first, ps_ap[:, 0:1], fm[:, bias_col:bias_col + 1], None, ALU.add), reads=[pkey, "fm"], writes=["carry"])
            if save_last is not None:
                P.dve(TS(save_last, ps_ap[:, 511:512], fm[:, bias_col:bias_col + 1], None, ALU.add), reads=[pkey, "fm"], writes=["carry"])
            P.act(ACTF(th[:, :], cv, AF.Tanh, scale=0.5), reads=[ck], writes=[tk])
            P.dve(STT(dst_ap, th[:, :], 1.0, cv, ALU.add, ALU.mult), reads=[tk, ck], writes=[dst_key])

        def state_prep(kT_ap, kkeys, v_ap, vkey, slot, dirn, banks=None):
            pT = pTv()
            for h in range(4):
                P.pe(TR(pT[:, h, :], kT_ap[:, h, :], ident[:, :]), reads=list(kkeys) + ["ident"], writes=[PK[3]])
            P.act(ACTF(kk[:, :, :], pT[:, 0:4, :], AF.Copy), reads=[PK[3]], writes=["kk"])
            if banks is None:
                banks = [next_bank(), next_bank()]
            outs = []
            for h in range(4):
                u = uv[h % 2]
                uk = ("uv", h % 2)
                col = 8 + dirn * 4 + h
                P.act(ACTF(u[:, :], v_ap[:, h, :], AF.Copy, scale=SC[:, slot, col:col + 1]), reads=[vkey, ("SC", slot)], writes=[uk])
                bank = banks[h // 2]
                cols = slice((h % 2) * 129, (h % 2) * 129 + 129)
                P.pe(MM(psb[bank][:, cols], kk[:, h, :], u[:, :]), reads=["kk", uk], writes=[PK[bank]])
                outs.append((bank, cols))
            return outs

        def state_apply(Cst, ckey, outs, slot, dirn):
            for h in range(4):
                bank, cols = outs[h]
                P.dve(STT(Cst[:, h, :], Cst[:, h, :], EG[:, slot, dirn * 4 + h:dirn * 4 + h + 1], psb[bank][:, cols], ALU.mult, ALU.add),
                      reads=[ckey, ("EG", slot), PK[bank]], writes=[ckey])

        def state_update(Cst, ckey, kT_ap, kkeys, v_ap, vkey, slot, dirn, bank):
            outs = state_prep(kT_ap, kkeys, v_ap, vkey, slot, dirn)
            state_apply(Cst, ckey, outs, slot, dirn)

        if stop <= 0:
            P.finalize(st)
            return nc
        load_norm_T(meta, 1, xnTm, "xnTm", meta_rows=16)

        def h_kmeta(fc, ps_ap, pkey):
            P.act(ACTF(kmT[:, fc, :], ps_ap[:, 0:16], AF.Identity, bias=fm[:, FM_B[C_NAK] + fc:FM_B[C_NAK] + fc + 1]),
                  reads=[pkey, "fm"], writes=["kmT"])
        proj_F(xnTm, "xnTm", 128, C_NAK, h_kmeta)

        def h_vmeta(j, ps_ap, pkey):
            P.dve(TT(vmp[0:16, :, 0:64], ps_ap[0:16, :].rearrange("p (h d) -> p h d", h=8),
                     bias_bc[0:16, TB[C_NAV]:TB[C_NAV] + 512].rearrange("p (h d) -> p h d", h=8), ALU.add),
                  reads=[pkey, "bias_bc"], writes=["vmp"])
        proj_T(xnTm, "xnTm", 1, C_NAV, h_vmeta)

        def h_qmeta(fc, ps_ap, pkey):
            P.act(ACTF(prem[:, fc:fc + 1], ps_ap[:, 15:16], AF.Identity, bias=fm[:, FM_B[C_MLQ] + fc:FM_B[C_MLQ] + fc + 1]),
                  reads=[pkey, "fm"], writes=["prem"])
        proj_F(xnTm, "xnTm", 128, C_MLQ, h_qmeta)

        def h_kmeta2(fc, ps_ap, pkey):
            P.act(ACTF(premk[:, fc, :], ps_ap[:, 0:16], AF.Identity, bias=fm[:, FM_B[C_MLK] + fc:FM_B[C_MLK] + fc + 1]),
                  reads=[pkey, "fm"], writes=["premk"])
            P.pool(CP(prem[:, 4 + fc:5 + fc], premk[:, fc, 15:16]), reads=["premk"], writes=["prem"])
        proj_F(xnTm, "xnTm", 128, C_MLK, h_kmeta2)

        def h_vmeta2(j, ps_ap, pkey):
            P.dve(TT(vmeta[0:16, :, 0:128], ps_ap[0:16, :].rearrange("p (h d) -> p h d", h=4),
                     bias_bc[0:16, TB[C_MLV]:TB[C_MLV] + 512].rearrange("p (h d) -> p h d", h=4), ALU.add),
                  reads=[pkey, "bias_bc"], writes=["vmeta"])
            P.dve(MS(vmeta[0:16, :, 128:129], 1.0), reads=["vmeta"], writes=["vmeta"])
        proj_T(xnTm, "xnTm", 1, C_MLV, h_vmeta2)
        gates_B(gates_A(xnTm, "xnTm", 128, is_meta=True), [NS])
        P.dve(CP(SC[:, NS + 1, :], SC[:, NS, :]), reads=[("SC", NS)], writes=[("SC", NS + 1)])
        P.dve(CP(EG[:, NS + 1, :], EG[:, NS, :]), reads=[("EG", NS)], writes=[("EG", NS + 1)])

        if stop <= 1:
            P.finalize(st)
            return nc
        stg.close()
        ftmp = sb("ftmp", [128, 512])
        Praw = [sb("Praw%d" % i, [128, 640], BF16) for i in range(2)]
        Pn = [sb("Pn%d" % i, [128, 640], BF16) for i in range(2)]
        Pm = sb("Pm", [128, 2, 512], BF16)
        gA = sb("gA", [128, 512], BF16)
        gB = sb("gB", [128, 512], BF16)
        gAT = sb("gAT", [128, 4, 512], BF16)
        gBT = sb("gBT", [128, 4, 512], BF16)
        tgA = sb("tgA", [128, 512])
        tgB = sb("tgB", [128, 512])
        osb = [sb("osb%d" % i, [128, D]) for i in range(2)]
        X1K = ["ftmp", ("Praw", 0), ("Praw", 1), ("Pn", 0), ("Pn", 1), ("Pm", 0), ("Pm", 1), "gA", "gB", "gAT", "gBT", "tgA", "tgB",
               ("osb", 0), ("osb", 1)]
        for eng in ("dve", "act", "pool"):
            P.ew(eng, MS(st1[:, 5:6] if eng == "dve" else st1[:, 6:7], 0.0) if eng != "act" else ACTF(st1[:, 7:8], fl[:, 0:1], AF.Copy),
                 reads=["fl", "etab", "fm", "fmh", "gbias", "kmT", "vmp", "prem", "premk", "vmeta", ("SC", NS), ("EG", NS)],
                 writes=X1K + ["xnTm", "tzs", "rowst", "cmk"])

        P.pool(MS(Pm[:], 0.0), reads=[("Pm", 0), ("Pm", 1)], writes=[("Pm", 0), ("Pm", 1)])

        def meta_state(u):
            P.pool(MS(Cmeta[:, u], 0.0), writes=[("Cmeta", u)])
            for fc in range(4):
                i = cv_i["i"] % 2
                cv_i["i"] += 1
                pk, ck = ("pre", 0), ("cvt", 0)
                pr, cv = pre[i], cvt[i]
                P.pool(MS(pr[:, 0:1], 0.0), writes=[pk])
                P.pool(CP(pr[:, 1:17], premk[:, fc, :]), reads=["premk", pk], writes=[pk])
                P.pool(CP(pr[:, 17:18], firstpre[:, u, 4 + fc, :]), reads=[("firstpre", u), pk], writes=[pk])
                conv_taps(pr, pk, cv, ck, 16, 4 + fc)
                P.dve(STT(kmetaT[:, fc, 0:16], pr[:, 1:17], 1.0, cv[:, 0:16], ALU.add, ALU.mult), reads=[pk, ck], writes=["kmetaT"])
            state_update(Cmeta[:, u], ("Cmeta", u), kmetaT, ["kmetaT"], vmeta, "vmeta", NS + u, 0, 4)

        def tile_src(T):
            return xu[T * 512:(T + 1) * 512, :]

        def small_cols(wsl, wkey, xT, xkey, col):
            hb = next_bank()
            for fc in range(4):
                for kc in range(8):
                    P.pe(MM(psb[hb][:, fc:fc + 1], wsl[:, kc, fc * 128:(fc + 1) * 128], xT[:, kc, col:col + 1], kc == 0, kc == 7),
                         reads=[wkey, xkey], writes=[PK[hb]])
            return hb

        vml2 = KT[:, 0:2, :].rearrange("p a b -> p (a b)")[:, 0:2064].rearrange("p (j h n) -> p j h n", j=4, h=4)
        P.pool(MS(vml2[:, :, :, 128:129], 1.0), writes=[("vml2", j) for j in range(4)])

        def p1_bufs(T):
            if T % 2 == 0:
                return 4, vml, "vml"
            return 0, vml2, "vml2"

        def p1_P(T):
            xt, xkey = xnT[T % 2], ("xnT", T % 2)
            u_of = T // TPU
            first_tile_of_unit = (T % TPU == 0)
            last_tile_of_unit = (T % TPU == TPU - 1)
            fcb, vb, vname = p1_bufs(T)
            wsl, wkey = wload(C_MLK)
            bcol = FM_B[C_MLK]
            if T > 0:
                hb = small_cols(wsl, wkey, xnT[(T - 1) % 2], ("xnT", (T - 1) % 2), 511)
                P.dve(TT(halo[:, 4:8, 0], psb[hb][:, 0:4], fm[:, bcol:bcol + 4], ALU.add), reads=[PK[hb], "fm"], writes=["halo"])
            if first_tile_of_unit:
                if T == 0:
                    P.dve(CP(halo[:, 4:8, 0], prem[:, 4:8]), reads=["prem"], writes=["halo"])
                else:
                    P.dve(TS(halo[:, 4:8, 0], halo[:, 4:8, 0], cont), reads=["halo", "fl"], writes=["halo"])
                    P.dve(STT(halo[:, 4:8, 0], prem[:, 4:8], ncont, halo[:, 4:8, 0], ALU.mult, ALU.add), reads=["halo", "prem", "fl"], writes=["halo"])
            if T == NT - 1:
                P.dve(MS(halo[:, 4:8, 1], 0.0), writes=["halo"])
            elif last_tile_of_unit:
                P.dve(TS(halo[:, 4:8, 1], carry[:, 4:8, 1], cont), reads=["carry", "fl"], writes=["halo"])
            else:
                P.dve(CP(halo[:, 4:8, 1], carry[:, 4:8, 1]), reads=["carry"], writes=["halo"])
            P.dve(TT(hadj[:, 4:8, :], halo[:, 4:8, :], fm[:, bcol:bcol + 4].unsqueeze(2).to_broadcast([128, 4, 2]), ALU.subtract),
                  reads=["halo", "fm"], writes=["hadj"])
            for fc in range(4):
                b = next_bank()
                for kc in range(8):
                    P.pe(MM(psb[b][:, :], wsl[:, kc, fc * 128:(fc + 1) * 128], xt[:, kc, :], kc == 0, kc == 7), reads=[wkey, xkey], writes=[PK[b]])
                conv_silu(4 + fc, psb[b][:, :], PK[b], 512, bcol + fc, halo[:, 4 + fc, 0:1], halo[:, 4 + fc, 1:2],
                          qkT[:, fcb + fc, :], ("qkT", fcb + fc), save_first=carry[:, 4 + fc, 1:2])
                yield
            if first_tile_of_unit:
                P.pool(CP(firstpre[:, u_of, 4:8, 0], carry[:, 4:8, 1]), reads=["carry"], writes=[("firstpre", u_of)])
            wsv, wkv = wload(C_MLV)
            for j in range(4):
                b = next_bank()
                for kc in range(8):
                    P.pe(MM(psb[b][:, :], xt[:, kc, j * 128:(j + 1) * 128], wsv[:, kc, :], kc == 0, kc == 7), reads=[wkv, xkey], writes=[PK[b]])
                P.dve(TT(vb[:, j, :, 0:128], psb[b][:, :].rearrange("p (h d) -> p h d", h=4),
                         bias_bc[:, TB[C_MLV]:TB[C_MLV] + 512].rearrange("p (h d) -> p h d", h=4), ALU.add),
                      reads=[PK[b], "bias_bc"], writes=[(vname, j)])
                yield
            tokg = gates_A(xt, xkey, 512)
            yield
            gates_B(tokg, [T * 4 + c for c in range(4)])
            yield

        def p1_S(T):
            last_tile_of_unit = (T % TPU == TPU - 1)
            fcb, vb, vname = p1_bufs(T)
            KK4 = [("qkT", fcb + h) for h in range(4)]

            def pb(slot):
                return [4 + 2 * (slot % 2), 5 + 2 * (slot % 2)]
            outs_next = state_prep(qkT[:, fcb:fcb + 4, 3 * 128:4 * 128], KK4, vb[:, 3], (vname, 3), T * 4 + 3, 1, banks=pb(T * 4 + 3))
            for c in range(3, -1, -1):
                slot = T * 4 + c
                outs = outs_next
                ltok = None
                if T - 2 >= 0:
                    T2 = T - 2
                    jj = c
                    ltok = ln_A(xu[T2 * 512 + jj * 128:T2 * 512 + (jj + 1) * 128, :])
                if last_tile_of_unit and c == 3 and T != NT - 1:
                    P.dve(TS(Cb[:], Cb[:], cont), reads=["Cb", "fl"], writes=["Cb"])
                i = slot % 2
                P.act(ACTF(cbst[i][:], Cb[:], AF.Copy), reads=["Cb"], writes=[("cbst", i)])
                P.dma(DMA(cbs[slot].rearrange("p (h n) -> p h n", h=4), cbst[i][:]), reads=[("cbst", i)], writes=[("cbs", slot)], semkey=("cbst", i),
                      queue="pool")
                if c > 0:
                    outs_next = state_prep(qkT[:, fcb:fcb + 4, (c - 1) * 128:c * 128], KK4, vb[:, c - 1], (vname, c - 1), slot - 1, 1, banks=pb(slot - 1))
                state_apply(Cb, "Cb", outs, slot, 1)
                if ltok is not None:
                    ln_B(ltok, xnT[T2 % 2][:, :, jj * 128:(jj + 1) * 128], ("xnT", T2 % 2))
                yield

        load_norm_T(tile_src(NT - 1), 4, xnT[(NT - 1) % 2], ("xnT", (NT - 1) % 2))
        if NT > 1:
            load_norm_T(tile_src(NT - 2), 4, xnT[(NT - 2) % 2], ("xnT", (NT - 2) % 2))
        for _ in p1_P(NT - 1):
            pass
        for T in range(NT - 1, -1, -1):
            gS = p1_S(T)
            gP = p1_P(T - 1) if T > 0 else iter(())
            for k in range(4):
                next(gS, None)
                for _ in range(3 if k == 0 else 2):
                    next(gP, None)
            for _ in gS:
                pass
            for _ in gP:
                pass
            if T % TPU == 0:
                meta_state(T // TPU)
        for eng in ("act", "dve", "pool"):
            P.ew(eng, MS(st1[:, 5:6] if eng == "dve" else st1[:, 6:7], 0.0) if eng != "act" else ACTF(st1[:, 7:8], fl[:, 0:1], AF.Copy),
                 reads=["fl"], writes=[("vml2", j) for j in range(4)] + [("KT", i) for i in range(RING_T)])
        if stop <= 2:
            P.finalize(st)
            return nc
        def kv_proj(T):
            xt, xkey = xnT[T % 2], ("xnT", T % 2)
            rt = T % RING_T

            def h_k(fc, ps_ap, pkey):
                P.act(ACTF(KT[:, fc, rt * 512:(rt + 1) * 512], ps_ap, AF.Identity, bias=fm[:, FM_B[C_NAK] + fc:FM_B[C_NAK] + fc + 1]),
                      reads=[pkey, "fm"], writes=[("KT", rt)])
            proj_F(xt, xkey, 512, C_NAK, h_k)

            def h_v(j, ps_ap, pkey):
                P.dve(TT(VP[:, rt * 4 + j, :, 0:64], ps_ap.rearrange("p (h d) -> p h d", h=8),
                         bias_bc[:, TB[C_NAV]:TB[C_NAV] + 512].rearrange("p (h d) -> p h d", h=8), ALU.add),
                      reads=[pkey, "bias_bc"], writes=[("VP", rt)])
            proj_T(xt, xkey, 4, C_NAV, h_v)

        def ring_row(r):
            return r % RROWS

        def na_q_z(T):
            xt, xkey = xnT[T % 2], ("xnT", T % 2)
            P.pool(MS(qkT[64:128, 0:4, :], 0.0), writes=[("qkT", f) for f in range(4)])
            P.pool(MS(qkT[0:64, 4:8, :], 0.0), writes=[("qkT", 4 + f) for f in range(4)])

            def h_q(fc, ps_ap, pkey):
                bq = FM_B[C_NAQ] + fc
                P.act(ACTF(qkT[0:64, fc, :], ps_ap[0:64, :], AF.Identity, bias=fmh[0:64, bq:bq + 1], scale=0.125),
                      reads=[pkey, "fmh", ("qkT", fc)], writes=[("qkT", fc)])
                P.act(ACTF(qkT[64:128, 4 + fc, :], ps_ap[64:128, :], AF.Identity, bias=fmh[64:128, bq:bq + 1], scale=0.125),
                      reads=[pkey, "fmh", ("qkT", 4 + fc)], writes=[("qkT", 4 + fc)])
            proj_F(xt, xkey, 512, C_NAQ, h_q)

            def h_z(j, ps_ap, pkey):
                (za, zak), (zb_, zbk) = ((ftmp, ["ftmp"]), (tgA, ["tgA"])) if j % 2 == 0 else ((tgB, ["tgB"]), (h2, ["h2"]))
                P.dve(TT(za[:, :], ps_ap, bias_bc[:, TB[C_NAZ]:TB[C_NAZ] + 512], ALU.add), reads=[pkey, "bias_bc"], writes=zak)
                P.act(ACTF(zb_[:, :], za[:, :], AF.Tanh, scale=0.5), reads=zak, writes=zbk)
                P.pool(TS(zb_[:, :], zb_[:, :], 1.0, 0.5, ALU.add, ALU.mult), reads=zbk, writes=zbk)
                P.pool(TT(zs[:, j, :], za[:, :], zb_[:, :], ALU.mult), reads=zak + zbk, writes=[("zs", j)])
            proj_T(xt, xkey, 4, C_NAZ, h_z)

        def q_ap(h, qcol):
            fc, hh = h // 2, h % 2
            return qkT[:, 4 * hh + fc, qcol:qcol + 64], ("qkT", 4 * hh + fc)

        def na_meta_scores(lrt, slot):
            qcol = lrt * 64
            ms = psb[3][0:16, :]
            for h in range(8):
                qap, qk = q_ap(h, qcol)
                P.pe(MM(ms[:, h * 64:(h + 1) * 64], kmT[:, h // 2, :], qap), reads=["kmT", qk], writes=[PK[3]])
            P.act(ACTF(Pm[0:16, slot, :], ms, AF.Exp), reads=[PK[3]], writes=[("Pm", slot)])

        na_i = {"i": 0, "cur": 0}
        PB = [Praw[0], Praw[1], Pn[0], Pn[1]]
        PBK = [("Praw", 0), ("Praw", 1), ("Pn", 0), ("Pn", 1)]

        def na_scores(lr_tile, r0, delta, hp):
            qcol = lr_tile * 64
            i = na_i["i"] % 4
            na_i["i"] += 1
            odd = (r0 % 2 == 1)
            ng = 5 if odd else 4
            W = 2 * ng * 64
            cur = na_i["cur"]
            if odd:
                if cur % 2 == 1:
                    cur = (cur + 1) % 4
                reg = psS[cur // 2][:, 0:W]
                skeys = [PK[4 + cur], PK[5 + cur]]
                na_i["cur"] = (cur + 2) % 4
            else:
                reg = psb[4 + cur][:, 0:W]
                skeys = [PK[4 + cur]]
                na_i["cur"] = (cur + 1) % 4
            sps = reg.rearrange("p (h g q) -> p h g q", h=2, g=ng)
            base = r0 - 1 if odd else r0
            rk = list(dict.fromkeys([("KT", (rr // 8) % RING_T) for rr in range(base, base + 2 * ng)]))
            for hh in range(2):
                qap, qk = q_ap(2 * hp + hh, qcol)
                for g in range(ng):
                    c0 = ring_row(base + 2 * g) * 64
                    P.pe(MM(sps[:, hh, g, :], KT[:, hp, c0:c0 + 128], qap), reads=rk + [qk], writes=skeys)
            P.act(ACTF(PB[i][:, 0:W], reg, AF.Exp), reads=skeys, writes=[PBK[i]])
            pn4 = PB[i][:, 0:W].rearrange("p (h g q) -> p h g q", h=2, g=ng)
            e2 = etab[:, 2 * hp:2 * hp + 2]
            if not odd:
                ei0 = 7 - delta
                P.dve(TT(pn4, pn4, e2[:, :, ei0:ei0 + 7:2, :], ALU.mult), reads=[PBK[i], "etab"], writes=[PBK[i]])
            else:
                P.dve(TT(pn4[:, :, 0, :], pn4[:, :, 0, :], e2[:, :, 14, :], ALU.mult), reads=[PBK[i], "etab"], writes=[PBK[i]])
                P.dve(TT(pn4[:, :, 1:4, :], pn4[:, :, 1:4, :], e2[:, :, 4:9:2, :], ALU.mult), reads=[PBK[i], "etab"], writes=[PBK[i]])
                P.dve(TT(pn4[:, :, 4, :], pn4[:, :, 4, :], e2[:, :, 15, :], ALU.mult), reads=[PBK[i], "etab"], writes=[PBK[i]])
            return (PBK[i], pn4, ng, base)

        def na_pv(desc, hp, pv_bank, row_half):
            pkey, pn4, ng, base = desc
            vk = list(dict.fromkeys([("VP", (rr // 8) % RING_T) for rr in range(base, base + 2 * ng)]))
            for hh in range(2):
                h = 2 * hp + hh
                o = psb[pv_bank][row_half * 64:(row_half + 1) * 64, (h % 4) * 65:(h % 4) * 65 + 65]
                rd = [pkey] + vk
                for g in range(ng):
                    vt = ring_row(base + 2 * g) // 2
                    P.pe(MM(o, pn4[:, hh, g, :], VP[:, vt, h, :], g == 0, False), reads=rd, writes=[PK[pv_bank]])
                P.pe(MM(o, Pm[:, row_half, h * 64:(h + 1) * 64], vmp[:, h, :], False, True), reads=[("Pm", row_half), "vmp"], writes=[PK[pv_bank]])

        def na_tile(T):
            items = []
            for sp in range(4):
                rows = [8 * T + 2 * sp, 8 * T + 2 * sp + 1]
                variants = []
                for r in rows:
                    u = r // R
                    lr = r % R
                    r0 = min(max(lr - 4, 0), R - 8)
                    v = [(u * R + r0, lr - r0)]
                    if R - 4 <= r < R + 4:
                        v.append((r - 4, 4))
                    variants.append(v)
                nvar = max(len(v) for v in variants)
                items.append(("meta", sp, rows))
                for vi in range(nvar):
                    for half in range(2):
                        grp = (sp, vi, half)
                        for rh, r in enumerate(rows):
                            r0, delta = variants[rh][min(vi, len(variants[rh]) - 1)]
                            for hp in (2 * half, 2 * half + 1):
                                items.append(("unit", r - 8 * T, r0, delta, hp, grp, rh))
                        items.append(("norm", grp, vi, half))
                items.append(("fin", sp, nvar))
            banks = {}
            q = []
            DEPTH = 3

            def do_pv(ent):
                _, desc, hp, grp, rh = ent
                if grp not in banks:
                    banks[grp] = next_bank()
                na_pv(desc, hp, banks[grp], rh)

            def drain(maxpv):
                while sum(1 for e in q if e[0] == "pv") > maxpv:
                    while q and q[0][0] != "pv":
                        run(q.pop(0)[1])
                    do_pv(q.pop(0))
                    while q and q[0][0] != "pv":
                        run(q.pop(0)[1])

            def run(it):
                if it[0] == "meta":
                    for rh, r in enumerate(it[2]):
                        na_meta_scores(r - 8 * T, rh)
                elif it[0] == "norm":
                    _, grp, vi, half = it
                    pvb = banks[grp]
                    pv = psb[pvb][:, 0:260].rearrange("p (h n) -> p h n", h=4)
                    P.dve(RECIP(st1[:, 4:8], pv[:, :, 64]), reads=[PK[pvb]], writes=["st1r"])
                    dst = nao[vi][:, half * 256:(half + 1) * 256].rearrange("p (h d) -> p h d", h=4)
                    P.dve(TT(dst, pv[:, :, 0:64], st1[:, 4:8].unsqueeze(2).to_broadcast([128, 4, 64]), ALU.mult),
                          reads=[PK[pvb], "st1r"] + NAOK[vi], writes=NAOK[vi])
                else:
                    _, sp, nvar = it
                    if nvar == 2:
                        P.dve(TS(nao[0][:, :], nao[0][:, :], ncont), reads=NAOK[0] + ["fl"], writes=NAOK[0])
                        P.dve(STT(nao[0][:, :], nao[1][:, :], cont, nao[0][:, :], ALU.mult, ALU.add), reads=NAOK[0] + NAOK[1] + ["fl"], writes=NAOK[0])
                    if dbg:
                        t0 = (8 * T + 2 * sp) * 64
                        P.dma(DMA(dbg_out["d_na"][t0:t0 + 128, :], nao[0][:, :]), reads=NAOK[0], semkey="dbg1")
                    P.dve(TT(gA[:, :], nao[0][:, :], zs[:, sp, :], ALU.mult), reads=NAOK[0] + [("zs", sp)], writes=["gA"])
                    pT = pTv()
                    for fc in range(4):
                        P.pe(TR(pT[:, fc, :], gA[:, fc * 128:(fc + 1) * 128], ident[:, :]), reads=["gA", "ident"], writes=[PK[3]])
                    P.act(ACTF(gAT[:, :, sp * 128:(sp + 1) * 128], pT[:, 0:4, :], AF.Copy), reads=[PK[3]], writes=["gAT"])

            for it in items:
                if it[0] == "unit":
                    _, lrt, r0, delta, hp, grp, rh = it
                    desc = na_scores(lrt, r0, delta, hp)
                    q.append(("pv", desc, hp, grp, rh))
                    drain(DEPTH)
                elif not q:
                    run(it)
                else:
                    q.append(("act", it))
            drain(0)
            while q:
                run(q.pop(0)[1])

        def mlstm_tile(T, filler=None):
            xt, xkey = xnT[T % 2], ("xnT", T % 2)
            u_of = T // TPU
            first_tile_of_unit = (T % TPU == 0)
            last_tile_of_unit = (T % TPU == TPU - 1)
            for (cch, fco) in ((C_MLQ, 0), (C_MLK, 4)):
                wsl, wkey = wload(cch)
                bcol = FM_B[cch]
                if T < NT - 1:
                    hb = small_cols(wsl, wkey, xnT[(T + 1) % 2], ("xnT", (T + 1) % 2), 0)
                    P.dve(TT(halo[:, fco:fco + 4, 1], psb[hb][:, 0:4], fm[:, bcol:bcol + 4], ALU.add), reads=[PK[hb], "fm"], writes=["halo"])
                    if last_tile_of_unit:
                        P.dve(TS(halo[:, fco:fco + 4, 1], halo[:, fco:fco + 4, 1], cont), reads=["halo", "fl"], writes=["halo"])
                else:
                    P.dve(MS(halo[:, fco:fco + 4, 1], 0.0), writes=["halo"])
                if T == 0:
                    P.dve(CP(halo[:, fco:fco + 4, 0], prem[:, fco:fco + 4]), reads=["prem"], writes=["halo"])
                elif first_tile_of_unit:
                    P.dve(TS(halo[:, fco:fco + 4, 0], carry[:, fco:fco + 4, 0], cont), reads=["carry", "fl"], writes=["halo"])
                    P.dve(STT(halo[:, fco:fco + 4, 0], prem[:, fco:fco + 4], ncont, halo[:, fco:fco + 4, 0], ALU.mult, ALU.add),
                          reads=["halo", "prem", "fl"], writes=["halo"])
                else:
                    P.dve(CP(halo[:, fco:fco + 4, 0], carry[:, fco:fco + 4, 0]), reads=["carry"], writes=["halo"])
                P.dve(TT(hadj[:, fco:fco + 4, :], halo[:, fco:fco + 4, :], fm[:, bcol:bcol + 4].unsqueeze(2).to_broadcast([128, 4, 2]), ALU.subtract),
                      reads=["halo", "fm"], writes=["hadj"])
                for fc in range(4):
                    b = next_bank()
                    for kc in range(8):
                        P.pe(MM(psb[b][:, :], wsl[:, kc, fc * 128:(fc + 1) * 128], xt[:, kc, :], kc == 0, kc == 7), reads=[wkey, xkey], writes=[PK[b]])
                    conv_silu(fco + fc, psb[b][:, :], PK[b], 512, bcol + fc, halo[:, fco + fc, 0:1], halo[:, fco + fc, 1:2],
                              qkT[:, fco + fc, :], ("qkT", fco + fc), save_last=carry[:, fco + fc, 0:1])

            def h_v(j, ps_ap, pkey):
                P.dve(TT(vml[:, j, :, 0:128], ps_ap.rearrange("p (h d) -> p h d", h=4),
                         bias_bc[:, TB[C_MLV]:TB[C_MLV] + 512].rearrange("p (h d) -> p h d", h=4), ALU.add),
                      reads=[pkey, "bias_bc"], writes=[("vml", j)])
            proj_T(xt, xkey, 4, C_MLV, h_v)

            def h_z(j, ps_ap, pkey):
                (za, zak), (zb_, zbk) = ((ftmp, ["ftmp"]), (tgA, ["tgA"])) if j % 2 == 0 else ((tgB, ["tgB"]), (h2, ["h2"]))
                P.dve(TT(za[:, :], ps_ap, bias_bc[:, TB[C_MLZ]:TB[C_MLZ] + 512], ALU.add), reads=[pkey, "bias_bc"], writes=zak)
                P.act(ACTF(zb_[:, :], za[:, :], AF.Tanh, scale=0.5), reads=zak, writes=zbk)
                P.dve(STT(zb_[:, :], zb_[:, :], 1.0, za[:, :], ALU.add, ALU.mult), reads=zak + zbk, writes=zbk)
                P.pool(TT(zs[:, j, :], zb_[:, :], hg_bc[:, :], ALU.mult), reads=zbk + ["hg_bc"], writes=[("zs", j)])
            proj_T(xt, xkey, 4, C_MLZ, h_z)

            def h_o(j, ps_ap, pkey):
                za, zak = (tgB, ["tgB"]) if j % 2 == 0 else (ftmp, ["ftmp"])
                P.dve(TT(za[:, :], ps_ap, bias_bc[:, TB[C_MLO]:TB[C_MLO] + 512], ALU.add), reads=[pkey, "bias_bc"], writes=zak)
                P.act(ACTF(oG[:, j, :], za[:, :], AF.Tanh, scale=0.5), reads=zak, writes=[("oG", j)])
            proj_T(xt, xkey, 4, C_MLO, h_o)
            def ml_X(c):
                slot = T * 4 + c
                csl = slice(c * 128, (c + 1) * 128)
                if first_tile_of_unit and c == 0:
                    if T == 0:
                        P.dve(CP(Cf[:], Cmeta[:, 0]), reads=[("Cmeta", 0)], writes=["Cf"])
                    else:
                        P.dve(TS(Cf[:], Cf[:], cont), reads=["Cf", "fl"], writes=["Cf"])
                        P.dve(STT(Cf[:], Cmeta[:, u_of], ncont, Cf[:], ALU.mult, ALU.add), reads=["Cf", ("Cmeta", u_of), "fl"], writes=["Cf"])
                    P.act(ACTF(Cfb[:], Cf[:], AF.Copy), reads=["Cf"], writes=["Cfb"])
                li = slot % 2
                P.dma(DMA(cbl[li][:], cbs[slot].rearrange("p (h n) -> p h n", h=4)), reads=[("cbs", slot)], writes=[("cbl", li)], semkey=("cbl", li))
                sps = psb[4][:, :].rearrange("p (h t) -> p h t", h=4)
                for h in range(4):
                    P.pe(MM(sps[:, h, :], qkT[:, 4 + h, csl], qkT[:, h, csl]), reads=[("qkT", 4 + h), ("qkT", h)], writes=[PK[4]])
                for h in range(4):
                    P.dve(STT(Sf[h][:, :], sps[:, h, :], SC[:, slot, h:h + 1], maskf[:, :], ALU.mult, ALU.mult),
                          reads=[PK[4], ("SC", slot), "maskf"], writes=[("Sf", h)])
                    P.dve(STT(Sb_[h][:, :], sps[:, h, :], SC[:, slot, 4 + h:5 + h], maskb[:, :], ALU.mult, ALU.mult),
                          reads=[PK[4], ("SC", slot), "maskb"], writes=[("Sb", h)])
                souts = state_prep(qkT[:, 4:8, csl], [("qkT", 4 + h) for h in range(4)], vml[:, c], ("vml", c), slot, 0)
                nbs = [5, 6, 7, next_bank()]
                for h in range(4):
                    nb = nbs[h]
                    nf = psb[nb][:, 0:129]
                    nbk = psb[nb][:, 129:258]
                    P.pe(MM(nf, Sf[h][:, :], vml[:, c, h, :], True, False), reads=[("Sf", h), ("vml", c)], writes=[PK[nb]])
                    P.pe(MM(nf, qkT[:, h, csl], Cfb[:, h, :], False, True), reads=[("qkT", h), "Cfb"], writes=[PK[nb]])
                    P.pe(MM(nbk, Sb_[h][:, :], vml[:, c, h, :], True, False), reads=[("Sb", h), ("vml", c)], writes=[PK[nb]])
                    P.pe(MM(nbk, qkT[:, h, csl], cbl[li][:, h, :], False, True), reads=[("qkT", h), ("cbl", li)], writes=[PK[nb]])
                state_apply(Cf, "Cf", souts, slot, 0)
                P.act(ACTF(Cfb[:], Cf[:], AF.Copy), reads=["Cf"], writes=["Cfb"])
                for h in range(4):
                    nb = nbs[h]
                    d0 = 16 + 4 * h
                    P.act(ACTF(st2[:, d0:d0 + 2], psb[nb][:, 128:258:129], AF.Abs), reads=[PK[nb]], writes=[("st1d", h)])
                for h in range(4):
                    d0 = 16 + 4 * h
                    P.dve(TT(st2[:, d0:d0 + 2], st2[:, d0:d0 + 2], SC[:, slot, 16 + h:21 + h:4], ALU.max), reads=[("st1d", h), ("SC", slot)], writes=[("st1d", h)])
                    P.dve(RECIP(st2[:, d0 + 2:d0 + 4], st2[:, d0:d0 + 2]), reads=[("st1d", h)], writes=[("st1e", h)])
                for h in range(4):
                    nb = nbs[h]
                    d0 = 16 + 4 * h
                    hs = slice(h * 128, (h + 1) * 128)
                    P.act(ACTF(hbuf[:, hs], psb[nb][:, 0:128], AF.Copy, scale=st2[:, d0 + 2:d0 + 3]), reads=[PK[nb], ("st1e", h)], writes=[("hbuf", h)])
                for h in range(4):
                    nb = nbs[h]
                    d0 = 16 + 4 * h
                    hs = slice(h * 128, (h + 1) * 128)
                    P.dve(STT(hbuf[:, hs], psb[nb][:, 129:257], st2[:, d0 + 3:d0 + 4], hbuf[:, hs], ALU.mult, ALU.add),
                          reads=[PK[nb], ("st1e", h), ("hbuf", h)], writes=[("hbuf", h)])
                if dbg:
                    t0 = slot * 128
                    P.dma(DMA(dbg_out["d_h"][t0:t0 + 128, :], hbuf[:, :]), reads=HBK, semkey="dbg2")

            def ml_Y1(c):
                P.dve(STT(h2[:, :], oG[:, c, :], 1.0, hbuf[:, :], ALU.add, ALU.mult), reads=[("oG", c)] + HBK, writes=["h2"])

            def ml_Y2(c):
                csl = slice(c * 128, (c + 1) * 128)
                h3 = h2[:, :].rearrange("p (h d) -> p h d", h=4)
                P.dve(lambda e, h3=h3: e.tensor_reduce(out=st1[:, 12:16], in_=h3, axis=AX.X, op=ALU.add), reads=["h2"], writes=["st1m"])
                for h in range(4):
                    P.act(ACTF(ftmp[:, h * 128:(h + 1) * 128], h2[:, h * 128:(h + 1) * 128], AF.Square, accum_out=st1[:, 8 + h:9 + h]),
                          reads=["h2"], writes=["ftmp", ("st1q", h)])
                SQK = [("st1q", h) for h in range(4)]
                P.dve(TS(st1[:, 12:16], st1[:, 12:16], 1.0 / 128), reads=["st1m"], writes=["st1m"])
                P.dve(TT(st1[:, 4:8], st1[:, 12:16], st1[:, 12:16], ALU.mult), reads=["st1m"], writes=["st1r"])
                P.dve(STT(st1[:, 8:12], st1[:, 8:12], 1.0 / 128, st1[:, 4:8], ALU.mult, ALU.subtract), reads=SQK + ["st1r"], writes=["st1v"])
                P.dve(TS(st1[:, 8:12], st1[:, 8:12], 4.0 * EPS, None, ALU.add), reads=["st1v"], writes=["st1v"])
                P.pool(TT(st1[:, 8:12], st1[:, 8:12], cst[:, 0:1].to_broadcast([128, 4]), ALU.pow), reads=["st1v", "cst"], writes=["st1v"])
                for h in range(4):
                    hs = slice(h * 128, (h + 1) * 128)
                    P.dve(TS(h2[:, hs], h2[:, hs], st1[:, 12 + h:13 + h], st1[:, 8 + h:9 + h], ALU.subtract, ALU.mult),
                          reads=["h2", "st1m", "st1v"], writes=["h2"])
                P.dve(TT(gB[:, :], h2[:, :], zs[:, c, :], ALU.mult), reads=["h2", ("zs", c)], writes=["gB"])
                pT = pTv()
                for fc in range(4):
                    P.pe(TR(pT[:, fc, :], gB[:, fc * 128:(fc + 1) * 128], ident[:, :]), reads=["gB", "ident"], writes=[PK[3]])
                P.act(ACTF(gBT[:, :, csl], pT[:, 0:4, :], AF.Copy), reads=[PK[3]], writes=["gBT"])

            def fill(k):
                if filler is not None:
                    for _ in range(k):
                        next(filler, None)

            ml_X(0)
            fill(2)
            ml_Y1(0)
            for c in range(1, 4):
                ml_X(c)
                fill(2)
                ml_Y2(c - 1)
                ml_Y1(c)
            ml_Y2(3)
            fill(8)

        def out_branch(T, which):
            xt, xkey = xnT[T % 2], ("xnT", T % 2)
            (cbr, cg0, cg1, gsrc, gkey, tg, tgk, first) = ((C_WA, C_GA0, C_GA1, gAT, "gAT", tgA, "tgA", True) if which == 0 else
                                                          (C_WB, C_GB0, C_GB1, gBT, "gBT", tgB, "tgB", False))
            wbr_, kbr = wload(cbr)
            wbr = wbr_[:].rearrange("p a b -> p (a b)").rearrange("p (fc n) -> p fc n", fc=4)
            for half, cg in enumerate((cg0, cg1)):
                wG, kG = wload(cg)
                for fc in range(4):
                    n = half * 4 + fc
                    bG = next_bank()
                    for kc in range(8):
                        P.pe(MM(psb[bG][:, :], wG[:, kc, fc * 128:(fc + 1) * 128], xt[:, kc, :], kc == 0, kc == 7), reads=[kG, xkey], writes=[PK[bG]])
                    P.act(ACTF(tg[:, :], psb[bG][:, :], AF.Tanh, bias=fmh[:, FM_B[cg] + fc:FM_B[cg] + fc + 1], scale=0.5),
                          reads=[PK[bG], "fmh"], writes=[tgk])
                    by = next_bank()
                    for k4 in range(4):
                        P.pe(MM(psb[by][:, :], wbr[:, k4, n * 128:(n + 1) * 128], gsrc[:, k4, :], k4 == 0, k4 == 3), reads=[kbr, gkey], writes=[PK[by]])
                    if first:
                        P.dve(STT(ypT[:, n, :], tg[:, :], 1.0, psb[by][:, :], ALU.add, ALU.mult), reads=[tgk, PK[by]], writes=[("ypT", n)])
                    else:
                        P.dve(STT(tg[:, :], tg[:, :], 1.0, psb[by][:, :], ALU.add, ALU.mult), reads=[tgk, PK[by]], writes=[tgk])
                        P.pool(TT(ypT[:, n, :], ypT[:, n, :], tg[:, :], ALU.add), reads=[tgk, ("ypT", n)], writes=[("ypT", n)])
                    yield n

        def out_tile(T, hook=None, skip_a=False):
            xt, xkey = xnT[T % 2], ("xnT", T % 2)
            for which in ((1,) if skip_a else (0, 1)):
                for _ in out_branch(T, which):
                    pass
            wO0, kO0 = wload(C_WO0)
            wO1, kO1 = wload(C_WO1)
            YK = [("ypT", n) for n in range(8)]
            T2 = T + 2 if hook is not None else None
            ltok = None
            if T2 is not None:
                ltok = ln_A(xu[T2 * 512:T2 * 512 + 128, :])
            for j in range(4):
                i = j % 2
                t0 = T * 512 + j * 128
                P.dma(DMA(osb[i][:], xu[t0:t0 + 128, :]), writes=[("osb", i)], reads=[("y", "st", i)], semkey=("xres", i))
                b0 = next_bank()
                b1 = next_bank()
                for (b, wO, kO) in ((b0, wO0, kO0), (b1, wO1, kO1)):
                    for k8 in range(8):
                        P.pe(MM(psb[b][:, :], ypT[:, k8, j * 128:(j + 1) * 128], wO[:, k8, :], k8 == 0, k8 == 7), reads=YK + [kO], writes=[PK[b]])
                if T2 is not None:
                    ln_B(ltok, xnT[T2 % 2][:, :, j * 128:(j + 1) * 128], ("xnT", T2 % 2))
                    if j < 3:
                        ltok = ln_A(xu[T2 * 512 + (j + 1) * 128:T2 * 512 + (j + 2) * 128, :])
                c0 = 8 + 4 * i
                oa, oa2, ob, oc = ("o_a", i), ("o_a2", i), ("o_b", i), ("o_c", i)
                P.act(ACTF(gA[:, :], psb[b0][:, :], AF.Square, accum_out=st2[:, c0:c0 + 1]), reads=[PK[b0]], writes=["gA", oa])
                P.act(ACTF(gB[:, :], psb[b1][:, :], AF.Square, accum_out=st2[:, c0 + 3:c0 + 4]), reads=[PK[b1]], writes=["gB", oa2])
                P.dve(TT(st2[:, c0 + 1:c0 + 2], st2[:, c0:c0 + 1], st2[:, c0 + 3:c0 + 4], ALU.add), reads=[oa, oa2], writes=[ob])
                P.dve(TS(st2[:, c0 + 1:c0 + 2], st2[:, c0 + 1:c0 + 2], 1.0 / D, 4.0 * EPS, ALU.mult, ALU.add), reads=[ob], writes=[ob])
                P.pool(TT(st2[:, c0 + 2:c0 + 3], st2[:, c0 + 1:c0 + 2], cst[:, 0:1], ALU.pow), reads=[ob, "cst"], writes=[oc])
                P.dve(STT(tgA[:, :], psb[b0][:, :], st2[:, c0 + 2:c0 + 3], gpost_bc[:, 0:512], ALU.mult, ALU.mult),
                      reads=[PK[b0], oc, "gpost_bc"], writes=["tgA"])
                P.dve(STT(tgB[:, :], psb[b1][:, :], st2[:, c0 + 2:c0 + 3], gpost_bc[:, 512:1024], ALU.mult, ALU.mult),
                      reads=[PK[b1], oc, "gpost_bc"], writes=["tgB"])
                P.pool(TT(osb[i][:, 0:512], osb[i][:, 0:512], tgA[:, :], ALU.add), reads=[("osb", i), "tgA"], writes=[("osb", i)])
                P.pool(TT(osb[i][:, 512:1024], osb[i][:, 512:1024], tgB[:, :], ALU.add), reads=[("osb", i), "tgB"], writes=[("osb", i)])
                P.dma(DMA(y[t0:t0 + 128, :], osb[i][:, :]), reads=[("osb", i)], writes=[("y", "st", i)], semkey=("osb", i), queue="pool")

        def ln_sub(T2, j):
            load_norm_T(xu[T2 * 512 + j * 128:T2 * 512 + (j + 1) * 128, :], 1, xnT[T2 % 2][:, :, j * 128:(j + 1) * 128], ("xnT", T2 % 2))

        kv_proj(0)
        for T in range(NT):
            if T + 1 < NT:
                kv_proj(T + 1)
            na_q_z(T)
            na_tile(T)
            mlstm_tile(T, filler=out_branch(T, 0))
            out_tile(T, hook=(lambda j, T=T: ln_sub(T + 2, j)) if T + 2 < NT else None, skip_a=True)

        P.finalize(st)
    return nc


def _host_tables(na_rpb):
    rpb = np.asarray(na_rpb, np.float32).reshape(8, 15, 31)
    kc = np.arange(64)[:, None]
    qc = np.arange(64)[None, :]
    dc = np.clip(kc - qc + 15, 0, 30)
    tzv = np.ascontiguousarray(np.transpose(rpb[:, :, dc], (2, 0, 1, 3)))
    c0 = np.clip(qc - 8, 0, 48)
    valid = ((kc >= c0) & (kc < c0 + 16)).astype(np.float32)
    cm = np.concatenate([valid, valid], axis=0)
    return tzv.astype(np.float32), cm.astype(np.float32)


def _make_in_maps(inputs, units_per_core, conts):
    tzv, cm = _host_tables(inputs["na_rpb"])
    common = {
        "meta": np.ascontiguousarray(inputs["meta_tokens"], np.float32),
        "g_pre": np.ascontiguousarray(inputs["g_pre"], np.float32).reshape(1, D),
        "w_in": np.ascontiguousarray(inputs["w_in"], np.float32).reshape(D, NIN),
        "b_in": np.ascontiguousarray(inputs["b_in"], np.float32).reshape(1, NIN),
        "tz": tzv, "cmask": cm,
        "conv_w": np.ascontiguousarray(inputs["ml_conv_w"], np.float32).reshape(3, D),
        "head_g": np.ascontiguousarray(inputs["ml_head_g"], np.float32).reshape(1, 512),
        "w_a": np.ascontiguousarray(inputs["w_a"], np.float32).reshape(512, D),
        "w_b": np.ascontiguousarray(inputs["w_b"], np.float32).reshape(512, D),
        "w_out": np.ascontiguousarray(inputs["w_out"], np.float32).reshape(D, D),
        "g_post": np.ascontiguousarray(inputs["g_post"], np.float32).reshape(1, D),
    }
    maps = []
    for xu, cont in zip(units_per_core, conts):
        fl = np.zeros((128, 4), np.float32)
        fl[:, 0] = cont
        fl[:, 1] = 1.0 - cont
        fl[0:4, 2] = 1.0
        m = dict(common)
        m["xu"] = np.ascontiguousarray(xu, np.float32)
        m["flags"] = fl
        maps.append(m)
    return maps


_NC_CACHE = {}


def kernel(x_prompt, x_sample, meta_tokens, g_pre, w_in, b_in, na_rpb, ml_conv_w, ml_head_g, w_a, w_b, w_out, g_post):
    x_prompt = np.asarray(x_prompt, np.float32)
    x_sample = np.asarray(x_sample, np.float32)
    inputs = dict(meta_tokens=meta_tokens, g_pre=g_pre, w_in=w_in, b_in=b_in, na_rpb=na_rpb, ml_conv_w=ml_conv_w,
                  ml_head_g=ml_head_g, w_a=w_a, w_b=w_b, w_out=w_out, g_post=g_post)
    R = 64
    units, conts = [], []
    for s in range(2):
        units.append(x_sample[s])
        conts.append(1.0)
    for c in range(4):
        units.append(np.concatenate([x_prompt[2 * c], x_prompt[2 * c + 1]], axis=0))
        conts.append(0.0)
    for c in range(2):
        units.append(np.concatenate([x_prompt[2 * c], x_prompt[2 * c + 1]], axis=0))
        conts.append(0.0)
    if R not in _NC_CACHE:
        _NC_CACHE[R] = build(R)
    nc = _NC_CACHE[R]
    in_maps = _make_in_maps(inputs, units, conts)
    res = run_bass_kernel_spmd(nc, in_maps, core_ids=list(range(8)))
    outs = [np.asarray(r["y"], np.float32) for r in res.results]
    y_sample = np.stack([outs[0], outs[1]], axis=0)
    yp = []
    for c in range(4):
        yp.append(outs[2 + c][:4096])
        yp.append(outs[2 + c][4096:])
    y_prompt = np.stack(yp, axis=0)
    return (y_prompt, y_sample)
```

```python
from contextlib import ExitStack
import numpy as np
import concourse.bass as bass
import concourse.mybir as mybir
from concourse.bass_utils import run_bass_kernel_spmd

F32 = mybir.dt.float32
BF16 = mybir.dt.bfloat16
AF = mybir.ActivationFunctionType
ALU = mybir.AluOpType
AX = mybir.AxisListType

ENGS = ("pe", "act", "dve", "pool", "sp")
NOSYNC = set()


class Op:
    __slots__ = ("eng", "emit", "deps", "dma", "semkey", "tok", "sig")

    def __init__(self, eng, emit, dma, semkey):
        self.eng = eng
        self.emit = emit
        self.deps = []
        self.dma = dma
        self.semkey = semkey
        self.tok = None
        self.sig = False


class Prog:
    def __init__(self, nc, same_engine_sync=True):
        self.nc = nc
        self.q = {e: [] for e in ENGS}
        self.last_w = {}
        self.readers = {}
        self.same_engine_sync = same_engine_sync

    def add(self, eng, emit, reads=(), writes=(), dma=False, semkey=None):
        op = Op(eng, emit, dma, semkey)
        self.count = getattr(self, "count", 0) + 1
        if self.count > getattr(self, "limit", 10 ** 9):
            return op
        deps = {}
        for k in reads:
            w = self.last_w.get(k)
            if w is not None:
                deps[id(w)] = w
            if isinstance(k, str) and k.startswith("ps"):
                for r in self.readers.get(k, ()):
                    if r.eng != eng:
                        deps[id(r)] = r
        for k in writes:
            w = self.last_w.get(k)
            if w is not None:
                deps[id(w)] = w
            for r in self.readers.get(k, ()):
                deps[id(r)] = r
        op.deps = list(deps.values())
        for k in writes:
            self.last_w[k] = op
            self.readers[k] = []
        for k in reads:
            self.readers.setdefault(k, []).append(op)
        self.q[eng].append(op)
        return op

    def pe(self, emit, reads=(), writes=()):
        return self.add("pe", emit, reads, writes)

    def act(self, emit, reads=(), writes=()):
        return self.add("act", emit, reads, writes)

    def dve(self, emit, reads=(), writes=()):
        return self.add("dve", emit, reads, writes)

    def pool(self, emit, reads=(), writes=()):
        return self.add("pool", emit, reads, writes)

    def ew(self, eng, emit, reads=(), writes=()):
        return self.add(eng, emit, reads, writes)

    def dma(self, emit, reads=(), writes=(), semkey=None, queue="sp"):
        return self.add(queue, emit, reads, writes, dma=True, semkey=semkey)

    def _skip(self, d, op):
        if d.dma and op.dma and isinstance(d.semkey, str) and d.semkey.startswith("setup") and d.semkey == op.semkey:
            return True
        return (not d.dma) and (not op.dma) and d.eng == op.eng and (d.eng == "pe" or d.eng in NOSYNC or not self.same_engine_sync)

    def finalize(self, stack):
        nc = self.nc
        for e in ENGS:
            for op in self.q[e]:
                for d in op.deps:
                    if d.dma or self._skip(d, op):
                        continue
                    d.sig = True
        eng_sems = {e: [stack.enter_context(nc.semaphore("s_%s0" % e))] for e in ENGS}
        dma_sems = {}
        dma_cnt = {}
        LIM = 30000
        for e in ENGS:
            cnt = 0
            for op in self.q[e]:
                if op.dma:
                    if op.semkey not in dma_sems:
                        dma_sems[op.semkey] = stack.enter_context(nc.semaphore("d%d" % len(dma_sems)))
                        dma_cnt[op.semkey] = 0
                    dma_cnt[op.semkey] += 16
                    op.tok = (dma_sems[op.semkey], dma_cnt[op.semkey])
                elif op.sig:
                    if cnt >= LIM:
                        eng_sems[e].append(stack.enter_context(nc.semaphore("s_%s%d" % (e, len(eng_sems[e])))))
                        cnt = 0
                    cnt += 1
                    op.tok = (eng_sems[e][-1], cnt)
        assert max([0] + list(dma_cnt.values())) < 60000, "dma sem overflow"
        for e in ENGS:
            for op in self.q[e]:
                if op.dma and isinstance(op.semkey, str) and op.semkey.startswith("setup"):
                    op.tok = (dma_sems[op.semkey], dma_cnt[op.semkey])
        self.dma_sems = dma_sems
        self.dma_cnt = dma_cnt
        block = stack.enter_context(nc.Block())
        prog = self

        def run_queue(e, engine):
            known = {}
            for op in prog.q[e]:
                need = {}
                for d in op.deps:
                    if prog._skip(d, op):
                        continue
                    sem, val = d.tok
                    key = sem.num
                    if known.get(key, 0) >= val:
                        continue
                    if key not in need or need[key][1] < val:
                        need[key] = (sem, val)
                for key, (sem, val) in need.items():
                    engine.wait_ge(sem, val)
                    known[key] = val
                ins = op.emit(engine)
                if op.dma:
                    ins.then_inc(op.tok[0], 16)
                elif op.sig:
                    ins.then_inc(op.tok[0], 1)

        @block.tensor
        def _(eng):
            run_queue("pe", eng)

        @block.scalar
        def _(eng):
            run_queue("act", eng)

        @block.vector
        def _(eng):
            run_queue("dve", eng)

        @block.gpsimd
        def _(eng):
            run_queue("pool", eng)

        @block.sync
        def _(eng):
            run_queue("sp", eng)
            for key, sem in prog.dma_sems.items():
                eng.wait_ge(sem, prog.dma_cnt[key])


def MM(out, lhsT, rhs, start=True, stop=True):
    return lambda e: e.matmul(out, lhsT=lhsT, rhs=rhs, start=start, stop=stop)


def TR(out, in_, ident):
    return lambda e: e.transpose(out=out, in_=in_, identity=ident)


def ACTF(out, in_, func, bias=None, scale=1.0, accum_out=None):
    def f(e):
        kw = {}
        if bias is not None:
            kw["bias"] = bias
        if accum_out is not None:
            kw["accum_out"] = accum_out
        return e.activation(out=out, in_=in_, func=func, scale=scale, **kw)
    return f


def TT(out, in0, in1, op):
    return lambda e: e.tensor_tensor(out=out, in0=in0, in1=in1, op=op)


def TS(out, in0, s1, s2=None, op0=ALU.mult, op1=None):
    if op1 is None:
        return lambda e: e.tensor_scalar(out=out, in0=in0, scalar1=s1, scalar2=None, op0=op0)
    return lambda e: e.tensor_scalar(out=out, in0=in0, scalar1=s1, scalar2=s2, op0=op0, op1=op1)


def STT(out, in0, scalar, in1, op0, op1):
    return lambda e: e.scalar_tensor_tensor(out=out, in0=in0, scalar=scalar, in1=in1, op0=op0, op1=op1)


def CP(out, in_):
    return lambda e: e.tensor_copy(out=out, in_=in_)


def MS(ap, val):
    return lambda e: e.memset(ap, val)


def RECIP(out, in_):
    return lambda e: e.reciprocal(out=out, in_=in_)


def DMA(out, in_):
    return lambda e: e.dma_start(out=out, in_=in_)


D = 1024
NIN = 6672
GATE_OFF = 4608
EPS = 1e-6
KAPPA = 0.25 * (128.0 ** -0.5)
CH_OFF = [0, 512, 1024, 1536, 2048, 2560, 3072, 3584, 4096, 4624, 5136, 5648, 6160]
C_NAQ, C_NAK, C_NAV, C_NAZ, C_MLQ, C_MLK, C_MLV, C_MLZ, C_MLO, C_GA0, C_GA1, C_GB0, C_GB1 = range(13)
C_WA, C_WB, C_WO0, C_WO1 = 13, 14, 15, 16
FM_B = {C_NAQ: 0, C_NAK: 4, C_MLQ: 8, C_MLK: 12, C_GA0: 16, C_GA1: 20, C_GB0: 24, C_GB1: 28}
FM_CW = 32
FM_G = 56
FM_N = 64
TB = {C_NAV: 0, C_NAZ: 512, C_MLV: 1024, C_MLZ: 1536, C_MLO: 2048}


LIMIT = [10 ** 9]


def build(R, dbg=False, stop=99):
    U = 2
    NR = U * R
    NTOK = NR * 64
    NT = NR // 8
    NS = NR // 2
    TPU = R // 8
    RING_T = 3
    RROWS = RING_T * 8
    nc = bass.Bass("TRN2", target_bir_lowering=False)

    def din(name, shape):
        return nc.dram_tensor(name, shape, F32, kind="ExternalInput").ap()

    xu = din("xu", [NTOK, D])
    meta = din("meta", [16, D])
    g_pre = din("g_pre", [1, D])
    w_in = din("w_in", [D, NIN])
    b_in = din("b_in", [1, NIN])
    tz = din("tz", [64, 8, 15, 64])
    cmask = din("cmask", [128, 64])
    conv_w = din("conv_w", [3, D])
    head_g = din("head_g", [1, 512])
    w_a = din("w_a", [512, D])
    w_b = din("w_b", [512, D])
    w_out = din("w_out", [D, D])
    g_post = din("g_post", [1, D])
    flags = din("flags", [128, 4])
    y = nc.dram_tensor("y", [NTOK, D], F32, kind="ExternalOutput").ap()
    wq = nc.dram_tensor("wq", [17, 128, 8, 512], BF16, kind="Internal").ap()
    cbs = nc.dram_tensor("cbs", [NS, 128, 4 * 129], BF16, kind="Internal").ap()
    dbg_out = {}
    if dbg:
        dbg_out["d_h"] = nc.dram_tensor("d_h", [NTOK, 512], F32, kind="ExternalOutput").ap()
        dbg_out["d_na"] = nc.dram_tensor("d_na", [NTOK, 512], F32, kind="ExternalOutput").ap()

    st = ExitStack()
    with st:
        def sb(name, shape, dt=F32):
            return st.enter_context(nc.sbuf_tensor(name, shape, dt))

        P = Prog(nc)
        P.limit = LIMIT[0]
        psb = [st.enter_context(nc.psum_tensor("ps%d" % i, [128, 512], F32)) for i in range(4)]
        psS = [st.enter_context(nc.psum_tensor("psS%d" % i, [128, 1024], F32)) for i in range(2)]
        psb += [psS[0][:, 0:512], psS[0][:, 512:1024], psS[1][:, 0:512], psS[1][:, 512:1024]]
        PK = ["ps%d" % i for i in range(8)]

        ident = sb("ident", [128, 128], BF16)
        identf = sb("identf", [128, 128])
        maskf = sb("maskf", [128, 128])
        maskb = sb("maskb", [128, 128])
        fl = sb("fl", [128, 4])
        fm = sb("fm", [128, FM_N])
        fmh = sb("fmh", [128, FM_N])
        cst = sb("cst", [128, 2])
        gbias = sb("gbias", [8, 2])
        wg = sb("wg", [128, 8, 16], BF16)
        etab = sb("etab", [128, 8, 16, 64], BF16)
        bias_bc = sb("bias_bc", [128, 2560])
        gpost_bc = sb("gpost_bc", [128, D])
        hg_bc = sb("hg_bc", [128, 512])
        SC = sb("SC", [128, NS + U, 24])
        EG = sb("EG", [128, NS + U, 8])
        kmT = sb("kmT", [128, 4, 16], BF16)
        vmp = sb("vmp", [128, 8, 65], BF16)
        prem = sb("prem", [128, 8])
        premk = sb("premk", [128, 4, 16])
        vmeta = sb("vmeta", [128, 4, 129], BF16)
        kmetaT = sb("kmetaT", [128, 4, 128], BF16)
        firstpre = sb("firstpre", [128, U, 8, 1])
        Cmeta = sb("Cmeta", [128, U, 4, 129])
        Cb = sb("Cb", [128, 4, 129])
        Cf = sb("Cf", [128, 4, 129])
        Cfb = sb("Cfb", [128, 4, 129], BF16)
        ws = [sb("ws%d" % i, [128, 8, 512], BF16) for i in range(3)]
        xs = [sb("xs%d" % i, [128, D]) for i in range(2)]
        xnb = [sb("xnb%d" % i, [128, D], BF16) for i in range(2)]
        xnT = [sb("xnT%d" % i, [128, 8, 512], BF16) for i in range(2)]
        st1 = sb("st1", [128, 16])
        st2 = sb("st2", [128, 32])
        KT = sb("KT", [128, 4, RROWS * 64], BF16)
        VP = sb("VP", [128, RROWS // 2, 8, 65], BF16)
        zs = sb("zs", [128, 4, 512], BF16)
        oG = sb("oG", [128, 4, 512], BF16)
        pre = [sb("pre0", [128, 514])] * 2
        cvt = [sb("cvt0", [128, 512])] * 2
        qkT = sb("qkT", [128, 8, 512], BF16)
        carry = sb("carry", [128, 8, 2])
        halo = sb("halo", [128, 8, 2])
        hadj = sb("hadj", [128, 8, 2])
        vml = sb("vml", [128, 4, 4, 129], BF16)
        kk = sb("kk", [128, 4, 128], BF16)
        Sf = [sb("Sf%d" % i, [128, 128], BF16) for i in range(4)]
        Sb_ = [sb("Sb%d" % i, [128, 128], BF16) for i in range(4)]
        uv = [sb("uv%d" % i, [128, 129], BF16) for i in range(2)]
        cbl = [sb("cbl%d" % i, [128, 4, 129], BF16) for i in range(2)]
        cbst = [sb("cbst%d" % i, [128, 4, 129], BF16) for i in range(2)]
        hbuf = sb("hbuf", [128, 512])
        h2 = sb("h2", [128, 512])
        gs2 = sb("gs2", [8, 8])
        onesg = sb("onesg", [8, 128])
        ypT = sb("ypT", [128, 8, 512], BF16)
        gsc = ypT[0:8, :, :].rearrange("p a b -> p (a b)").bitcast(F32).rearrange("p (a b) -> p a b", a=4)
        nao = [hbuf, h2]
        HBK = [("hbuf", h) for h in range(4)]
        NAOK = [HBK, ["h2"]]

        cont = fl[:, 0:1]
        ncont = fl[:, 1:2]

        for (dst, so) in ((0, 0), (4, 8), (8, 4), (12, 12)):
            src = w_in[:, GATE_OFF + so:GATE_OFF + so + 4].rearrange("(kc p) j -> p kc j", p=128)
            P.dma(DMA(wg[:, :, dst:dst + 4], src), writes=["wg"], semkey="setup_w", queue="pool")
        for c in (C_NAK, C_NAV, C_MLQ, C_MLK, C_MLV, C_NAQ, C_NAZ, C_MLZ, C_MLO, C_GA0, C_GA1, C_GB0, C_GB1):
            src = w_in[:, CH_OFF[c]:CH_OFF[c] + 512].rearrange("(kc p) j -> p kc j", p=128)
            P.dma(DMA(wq[c], src), writes=[("wq", c)], semkey=("wqc", c), queue="pool")
        P.dma(DMA(wq[C_WA].rearrange("p a b -> p (a b)").rearrange("p (fc n) -> p fc n", fc=4),
                  w_a.rearrange("(fc p) n -> p fc n", p=128)), writes=[("wq", C_WA)], semkey=("wqc", C_WA), queue="pool")
        P.dma(DMA(wq[C_WB].rearrange("p a b -> p (a b)").rearrange("p (fc n) -> p fc n", fc=4),
                  w_b.rearrange("(fc p) n -> p fc n", p=128)), writes=[("wq", C_WB)], semkey=("wqc", C_WB), queue="pool")
        for hf in range(2):
            P.dma(DMA(wq[C_WO0 + hf], w_out[:, hf * 512:(hf + 1) * 512].rearrange("(fc p) n -> p fc n", p=128)),
                  writes=[("wq", C_WO0 + hf)], semkey=("wqc", C_WO0 + hf), queue="pool")

        P.dma(DMA(fl[:], flags), writes=["fl"], semkey="setup_c")
        P.dma(DMA(bias_bc[:, 0:1024], b_in[:, 1024:2048].partition_broadcast(128)), writes=["bias_bc"], semkey="setup_c")
        P.dma(DMA(bias_bc[:, 1024:2560], b_in[:, 3072:4608].partition_broadcast(128)), writes=["bias_bc"], semkey="setup_c")
        P.dma(DMA(gpost_bc[:], g_post.partition_broadcast(128)), writes=["gpost_bc"], semkey="setup_c")
        P.dma(DMA(hg_bc[:], head_g.partition_broadcast(128)), writes=["hg_bc"], semkey="setup_c")
        P.pool(MS(identf[:], 0.0), writes=["identf"])
        P.pool(lambda e: e.affine_select(out=identf[:], in_=identf[:], pattern=[[-1, 128]], compare_op=ALU.not_equal,
                                         fill=1.0, base=0, channel_multiplier=1), reads=["identf"], writes=["identf"])
        P.dve(CP(ident[:], identf[:]), reads=["identf"], writes=["ident"])
        P.pool(MS(maskf[:], 1.0), writes=["maskf"])
        P.pool(lambda e: e.affine_select(out=maskf[:], in_=maskf[:], pattern=[[1, 128]], compare_op=ALU.is_ge,
                                         fill=0.0, base=0, channel_multiplier=-1), reads=["maskf"], writes=["maskf"])
        P.pool(MS(maskb[:], 1.0), writes=["maskb"])
        P.pool(lambda e: e.affine_select(out=maskb[:], in_=maskb[:], pattern=[[-1, 128]], compare_op=ALU.is_ge,
                                         fill=0.0, base=0, channel_multiplier=1), reads=["maskb"], writes=["maskb"])
        P.pool(MS(cst[:, 0:1], -0.5), writes=["cst"])
        P.pool(MS(onesg[:], 1.0), writes=["onesg"])
        P.pool(MS(Cb[:], 0.0), writes=["Cb"])
        P.pool(MS(VP[:, :, :, 64:65], 1.0), writes=[("VP", i) for i in range(RING_T)])
        P.pool(MS(vmp[:], 0.0), writes=["vmp"])
        P.pool(MS(vmp[0:16, :, 64:65], 1.0), reads=["vmp"], writes=["vmp"])
        P.pool(MS(vml[:, :, :, 128:129], 1.0), writes=[("vml", j) for j in range(4)])
        P.pool(MS(carry[:], 0.0), writes=["carry"])
        P.pool(MS(vmeta[:], 0.0), writes=["vmeta"])
        P.pool(MS(kmetaT[:], 0.0), writes=["kmetaT"])

        stg = ExitStack()
        rowst = stg.enter_context(nc.sbuf_tensor("rowst", [64, 128], F32))
        tzs = stg.enter_context(nc.sbuf_tensor("tzs", [128, 4, 16, 64], F32))
        cmk = stg.enter_context(nc.sbuf_tensor("cmk", [128, 64], F32))
        xnTm = stg.enter_context(nc.sbuf_tensor("xnTm", [128, 8, 128], BF16))
        P.pool(MS(xnTm[:], 0.0), writes=["xnTm"])
        if True:
            P.pool(MS(rowst[:], 0.0), writes=["rowst"])
            for c, col in FM_B.items():
                P.dma(DMA(rowst[col:col + 4, :], b_in[0, CH_OFF[c]:CH_OFF[c] + 512].rearrange("(c p) -> c p", p=128)),
                      writes=["rowst"], semkey="setup_c")
            for j in range(3):
                P.dma(DMA(rowst[FM_CW + 8 * j:FM_CW + 8 * j + 8, :], conv_w[j, :].rearrange("(c p) -> c p", p=128)),
                      writes=["rowst"], semkey="setup_c")
            P.dma(DMA(rowst[FM_G:FM_G + 8, :], g_pre[0, :].rearrange("(c p) -> c p", p=128)), writes=["rowst"], semkey="setup_c")
            P.pe(MM(psb[0][:, 0:FM_N], rowst[0:FM_N, :], identf[0:FM_N, 0:FM_N]), reads=["rowst", "identf"], writes=[PK[0]])
            P.dve(CP(fm[:], psb[0][:, 0:FM_N]), reads=[PK[0]], writes=["fm"])
            P.dve(TS(fmh[:], fm[:], 0.5), reads=["fm"], writes=["fmh"])
            P.dve(TS(fmh[:, 0:4], fm[:, 0:4], 0.125), reads=["fm", "fmh"], writes=["fmh"])
            P.dve(TT(fmh[:, 56:64], fm[:, FM_CW:FM_CW + 8], fm[:, FM_CW + 8:FM_CW + 16], ALU.add), reads=["fm", "fmh"], writes=["fmh"])
            P.dve(TT(fmh[:, 56:64], fmh[:, 56:64], fm[:, FM_CW + 16:FM_CW + 24], ALU.add), reads=["fm", "fmh"], writes=["fmh"])
            P.dve(TT(fmh[:, 56:64], fmh[:, 56:64], fm[:, 8:16], ALU.mult), reads=["fm", "fmh"], writes=["fmh"])
            for (dst, so, col) in ((0, 0, 0), (4, 8, 0), (0, 4, 1), (4, 12, 1)):
                P.dma(DMA(gbias[dst:dst + 4, col:col + 1], b_in[0, GATE_OFF + so:GATE_OFF + so + 4].rearrange("(p o) -> p o", o=1)),
                      writes=["gbias"], semkey="setup_c")
            P.dve(TS(gbias[:, 1:2], gbias[:, 1:2], -1.0), reads=["gbias"], writes=["gbias"])
            P.dve(TS(hg_bc[:], hg_bc[:], 0.5), reads=["hg_bc"], writes=["hg_bc"])
            P.dma(DMA(cmk[:], cmask), writes=["cmk"], semkey="setup_c")
            for hq in range(2):
                sk = "setup_c" if hq == 0 else "setup_c2"
                hs = slice(4 * hq, 4 * hq + 4)
                P.pool(MS(tzs[:, :, 14:16, :], 0.0), writes=["tzs"])
                P.dma(DMA(tzs[0:64, :, 0:14, :], tz[:, hs, 0:14, :]), writes=["tzs"], semkey=sk)
                P.dma(DMA(tzs[64:128, :, 0:14, :], tz[:, hs, 1:15, :]), writes=["tzs"], semkey=sk)
                P.dma(DMA(tzs[64:128, :, 14, :], tz[:, hs, 3, :]), writes=["tzs"], semkey=sk)
                P.dma(DMA(tzs[0:64, :, 15, :], tz[:, hs, 10, :]), writes=["tzs"], semkey=sk)
                for h4 in range(4):
                    h = 4 * hq + h4
                    P.act(ACTF(tzs[:, h4], tzs[:, h4], AF.Exp), reads=["tzs"], writes=["tzs"])
                    P.dve(TT(etab[:, h], tzs[:, h4], cmk[:, :].unsqueeze(1).to_broadcast([128, 16, 64]), ALU.mult),
                          reads=["tzs", "cmk"], writes=["etab"])
            P.dve(MS(etab[0:64, :, 14, :], 0.0), reads=["etab"], writes=["etab"])
            P.dve(MS(etab[64:128, :, 15, :], 0.0), reads=["etab"], writes=["etab"])
            P.dve(CP(st1[:, 0:1], etab[:, 7, 13, 0:1]), reads=["etab", "fm", "fmh"], writes=["st1a"])

        wstate = {"i": 0}

        def wload(c):
            i = wstate["i"]
            wstate["i"] += 1
            slot = i % 3
            key = ("ws", slot)
            P.dma(DMA(ws[slot][:], wq[c]), reads=[("wq", c)], writes=[key], semkey=("ws", slot))
            return ws[slot], key

        ln_i = {"i": 0}

        def pTv():
            return psb[3][:].bitcast(BF16).rearrange("p (a b) -> p a b", a=8)

        def ln_A(src_ap, meta_rows=None):
            i = ln_i["i"]
            ln_i["i"] += 1
            b = i % 2
            xk, nk = ("xs", b), ("xnb", b)
            if meta_rows is None:
                npart = 128
                P.dma(DMA(xs[b][:], src_ap), writes=[xk], semkey=("xs", b))
            else:
                npart = meta_rows
                P.dma(DMA(xs[b][0:npart, :], src_ap), writes=[xk], semkey=("xs", b))
            c0 = 4 * b
            ka, kb, kc_ = ("ln_a", b), ("ln_b", b), ("ln_c", b)
            P.act(ACTF(xnb[b][0:npart, :], xs[b][0:npart, :], AF.Square, accum_out=st2[0:npart, c0:c0 + 1]), reads=[xk], writes=[nk, ka])
            P.dve(TS(st2[0:npart, c0 + 1:c0 + 2], st2[0:npart, c0:c0 + 1], 1.0 / D, EPS, ALU.mult, ALU.add), reads=[ka], writes=[kb])
            P.pool(TT(st2[0:npart, c0 + 2:c0 + 3], st2[0:npart, c0 + 1:c0 + 2], cst[0:npart, 0:1], ALU.pow), reads=[kb, "cst"], writes=[kc_])
            P.dve(TS(xnb[b][0:npart, :], xs[b][0:npart, :], st2[0:npart, c0 + 2:c0 + 3]), reads=[xk, kc_, nk], writes=[nk])
            return (b, npart)

        def ln_B(tok, dst_ap, dkey):
            b, npart = tok
            nk = ("xnb", b)
            pT = pTv()
            for kc in range(8):
                P.pe(TR(pT[:, kc, 0:npart], xnb[b][0:npart, kc * 128:(kc + 1) * 128], ident[0:npart, 0:npart]),
                     reads=[nk, "ident"], writes=[PK[3]])
            P.dve(TT(dst_ap[:, :, 0:npart], pT[:, :, 0:npart],
                     fm[:, FM_G:FM_G + 8].unsqueeze(2).to_broadcast([128, 8, npart]), ALU.mult),
                  reads=[PK[3], "fm"], writes=[dkey])

        def load_norm_T(src_ap, nsub, dst, dkey, meta_rows=None):
            for j in range(nsub):
                if meta_rows is None:
                    tok = ln_A(src_ap[j * 128:(j + 1) * 128, :])
                else:
                    tok = ln_A(src_ap, meta_rows)
                ln_B(tok, dst[:, :, j * 128:(j + 1) * 128], dkey)

        pbank = {"i": 0}

        def next_bank():
            b = pbank["i"] % 3
            pbank["i"] += 1
            return b

        def proj_F(xT, xkey, ntok, c, handler):
            wsl, wkey = wload(c)
            for fc in range(4):
                b = next_bank()
                for kc in range(8):
                    P.pe(MM(psb[b][:, 0:ntok], wsl[:, kc, fc * 128:(fc + 1) * 128], xT[:, kc, 0:ntok], kc == 0, kc == 7),
                         reads=[wkey, xkey], writes=[PK[b]])
                handler(fc, psb[b][:, 0:ntok], PK[b])

        def proj_T(xT, xkey, nsub, c, handler):
            wsl, wkey = wload(c)
            for j in range(nsub):
                b = next_bank()
                for kc in range(8):
                    P.pe(MM(psb[b][:, :], xT[:, kc, j * 128:(j + 1) * 128], wsl[:, kc, :], kc == 0, kc == 7),
                         reads=[wkey, xkey], writes=[PK[b]])
                handler(j, psb[b][:, :], PK[b])

        def gates_A(xT, xkey, ntok, is_meta=False):
            nch = ntok // 128
            bI = next_bank()
            for kc in range(8):
                P.pe(MM(psb[bI][0:8, 0:ntok], wg[:, kc, 0:8], xT[:, kc, 0:ntok], kc == 0, kc == 7), reads=["wg", xkey], writes=[PK[bI]])
            R0 = gsc[:, 0, 0:ntok]
            R1 = gsc[:, 1, 0:ntok]
            R2 = gsc[:, 2, 0:ntok]
            R3 = gsc[:, 3, 0:ntok]
            K0, K1, K2, K3 = "g_r0", "g_r1", "g_r2", "g_r3"
            P.act(ACTF(R0, psb[bI][0:8, 0:ntok], AF.Exp, bias=gbias[:, 0:1]), reads=[PK[bI], "gbias"], writes=[K0])
            bF = next_bank()
            for kc in range(8):
                P.pe(MM(psb[bF][0:8, 0:ntok], wg[:, kc, 8:16], xT[:, kc, 0:ntok], kc == 0, kc == 7), reads=["wg", xkey], writes=[PK[bF]])
            P.act(ACTF(R1, psb[bF][0:8, 0:ntok], AF.Exp, bias=gbias[:, 1:2], scale=-1.0), reads=[PK[bF], "gbias"], writes=[K1])
            if is_meta:
                P.dve(MS(gsc[:, 1, 16:ntok], 0.0), reads=[K1], writes=[K1])
                P.dve(MS(gsc[:, 0, 16:ntok], 0.0), reads=[K0], writes=[K0])
            P.dve(TS(R1, R1, 1.0, None, ALU.add), reads=[K1], writes=[K1])
            for ci in range(nch):
                sl = slice(ci * 128, (ci + 1) * 128)
                P.dve(lambda e, sl=sl: e.tensor_tensor_scan(out=gsc[:, 2, sl], data0=gsc[:, 1, sl], data1=onesg[:, :], initial=1.0,
                                                            op0=ALU.mult, op1=ALU.mult), reads=[K1, "onesg"], writes=[K2])
            P.dve(RECIP(R3, R2), reads=[K2], writes=[K3])
            for ci in range(nch):
                last = ci * 128 + 127
                P.dve(CP(gs2[:, ci:ci + 1], gsc[:, 3, last:last + 1]), reads=[K3], writes=["g_eg"])
            for ci in range(nch):
                sl = slice(ci * 128, (ci + 1) * 128)
                last = ci * 128 + 127
                P.dve(STT(gsc[:, 1, sl], gsc[:, 3, sl], gsc[:, 2, last:last + 1], gsc[:, 1, sl], ALU.mult, ALU.mult),
                      reads=[K3, K2, K1], writes=[K1])
            P.dve(TT(R3, R2, R1, ALU.subtract), reads=[K2, K1, K3, "g_eg"], writes=[K3])
            P.dve(STT(R2, R3, fl[0:8, 2:3], R1, ALU.mult, ALU.add), reads=[K3, K1, "fl", K2], writes=[K2])
            P.dve(STT(R0, R0, KAPPA, R2, ALU.mult, ALU.mult), reads=[K0, K2], writes=[K0])
            for ci in range(nch):
                sl = slice(ci * 128, (ci + 1) * 128)
                P.dve(TS(gsc[:, 3, sl], gsc[:, 0, sl], gs2[:, ci:ci + 1]), reads=[K0, "g_eg", K3], writes=[K3])
            return nch, (K0, K3, K2)

        def gates_B(tokg, chunks):
            nch, (K0, K3, K2) = tokg
            bS = next_bank()
            for ci in range(nch):
                sl = slice(ci * 128, (ci + 1) * 128)
                for k, (row, key) in enumerate(((0, K0), (3, K3), (2, K2))):
                    P.pe(MM(psb[bS][:, ci * 32 + k * 8:ci * 32 + k * 8 + 8], gsc[:, row, sl], identf[0:8, 0:8]),
                         reads=[key, "identf"], writes=[PK[bS]])
                P.pe(MM(psb[bS][:, ci * 32 + 24:ci * 32 + 32], gs2[:, ci:ci + 1].to_broadcast([8, 128]), identf[0:8, 0:8]),
                     reads=["g_eg", "identf"], writes=[PK[bS]])
            for ci in range(nch):
                slot = chunks[ci]
                P.dve(CP(SC[:, slot, :], psb[bS][:, ci * 32:ci * 32 + 24]), reads=[PK[bS]], writes=[("SC", slot)])
                P.dve(CP(EG[:, slot, :], psb[bS][:, ci * 32 + 24:ci * 32 + 32]), reads=[PK[bS]], writes=[("EG", slot)])

        cv_i = {"i": 0}

        def conv_taps(pr, pk, cv, ck, n, fcg):
            w0 = fm[:, FM_CW + fcg:FM_CW + fcg + 1]
            w1 = fm[:, FM_CW + 8 + fcg:FM_CW + 8 + fcg + 1]
            w2 = fm[:, FM_CW + 16 + fcg:FM_CW + 16 + fcg + 1]
            P.dve(TS(cv[:, 0:n], pr[:, 1:1 + n], w1), reads=[pk, "fm"], writes=[ck])
            P.dve(STT(cv[:, 0:n], pr[:, 0:n], w0, cv[:, 0:n], ALU.mult, ALU.add), reads=[pk, ck, "fm"], writes=[ck])
            P.dve(STT(cv[:, 0:n], pr[:, 2:2 + n], w2, cv[:, 0:n], ALU.mult, ALU.add), reads=[pk, ck, "fm"], writes=[ck])
            P.act(ACTF(pr[:, 1:1 + n], cv[:, 0:n], AF.Tanh, scale=0.5), reads=[ck], writes=[pk])

        def conv_silu(fcg, ps_ap, pkey, ntok, bias_col, lh_ap, rh_ap, dst_ap, dst_key, save_first=None, save_last=None):
            assert ntok == 512
            i = cv_i["i"] % 2
            cv_i["i"] += 1
            if i == 0:
                cv, ck, th, tk = cvt[0][:, 0:512], ("cvt", 0), gA, "gA"
            else:
                cv, ck, th, tk = pre[0][:, 0:512], ("pre", 0), gB, "gB"
            w0 = fm[:, FM_CW + fcg:FM_CW + fcg + 1]
            w1 = fm[:, FM_CW + 8 + fcg:FM_CW + 8 + fcg + 1]
            w2 = fm[:, FM_CW + 16 + fcg:FM_CW + 16 + fcg + 1]
            beta = fmh[:, 56 + fcg:57 + fcg]
            P.act(ACTF(cv, ps_ap, AF.Identity, bias=beta, scale=w1), reads=[pkey, "fm", "fmh"], writes=[ck])
            P.dve(STT(cv[:, 1:512], ps_ap[:, 0:511], w0, cv[:, 1:512], ALU.mult, ALU.add), reads=[pkey, ck, "fm"], writes=[ck])
            P.dve(STT(cv[:, 0:511], ps_ap[:, 1:512], w2, cv[:, 0:511], ALU.mult, ALU.add), reads=[pkey, ck, "fm"], writes=[ck])
            P.dve(STT(cv[:, 0:1], hadj[:, fcg, 0:1], w0, cv[:, 0:1], ALU.mult, ALU.add), reads=["hadj", ck, "fm"], writes=[ck])
            P.dve(STT(cv[:, 511:512], hadj[:, fcg, 1:2], w2, cv[:, 511:512], ALU.mult, ALU.add), reads=["hadj", ck, "fm"], writes=[ck])
            if save_first is not None:
                P.dve(TS(save_first, ps_ap[:, 0:1], fm[:, bias_col:bias_col + 1], None, ALU.add), reads=[pkey, "fm"], writes=["carry"])
            if save_last is not None:
                P.dve(TS(save_last, ps_ap[:, 511:512], fm[:, bias_col:bias_col + 1], None, ALU.add), reads=[pkey, "fm"], writes=["carry"])
            P.act(ACTF(th[:, :], cv, AF.Tanh, scale=0.5), reads=[ck], writes=[tk])
            P.dve(STT(dst_ap, th[:, :], 1.0, cv, ALU.add, ALU.mult), reads=[tk, ck], writes=[dst_key])

        def state_prep(kT_ap, kkeys, v_ap, vkey, slot, dirn, banks=None):
            pT = pTv()
            for h in range(4):
                P.pe(TR(pT[:, h, :], kT_ap[:, h, :], ident[:, :]), reads=list(kkeys) + ["ident"], writes=[PK[3]])
            P.act(ACTF(kk[:, :, :], pT[:, 0:4, :], AF.Copy), reads=[PK[3]], writes=["kk"])
            if banks is None:
                banks = [next_bank(), next_bank()]
            outs = []
            for h in range(4):
                u = uv[h % 2]
                uk = ("uv", h % 2)
                col = 8 + dirn * 4 + h
                P.act(ACTF(u[:, :], v_ap[:, h, :], AF.Copy, scale=SC[:, slot, col:col + 1]), reads=[vkey, ("SC", slot)], writes=[uk])
                bank = banks[h // 2]
                cols = slice((h % 2) * 129, (h % 2) * 129 + 129)
                P.pe(MM(psb[bank][:, cols], kk[:, h, :], u[:, :]), reads=["kk", uk], writes=[PK[bank]])
                outs.append((bank, cols))
            return outs

        def state_apply(Cst, ckey, outs, slot, dirn):
            for h in range(4):
                bank, cols = outs[h]
                P.dve(STT(Cst[:, h, :], Cst[:, h, :], EG[:, slot, dirn * 4 + h:dirn * 4 + h + 1], psb[bank][:, cols], ALU.mult, ALU.add),
                      reads=[ckey, ("EG", slot), PK[bank]], writes=[ckey])

        def state_update(Cst, ckey, kT_ap, kkeys, v_ap, vkey, slot, dirn, bank):
            outs = state_prep(kT_ap, kkeys, v_ap, vkey, slot, dirn)
            state_apply(Cst, ckey, outs, slot, dirn)

        if stop <= 0:
            P.finalize(st)
            return nc
        load_norm_T(meta, 1, xnTm, "xnTm", meta_rows=16)

        def h_kmeta(fc, ps_ap, pkey):
            P.act(ACTF(kmT[:, fc, :], ps_ap[:, 0:16], AF.Identity, bias=fm[:, FM_B[C_NAK] + fc:FM_B[C_NAK] + fc + 1]),
                  reads=[pkey, "fm"], writes=["kmT"])
        proj_F(xnTm, "xnTm", 128, C_NAK, h_kmeta)

        def h_vmeta(j, ps_ap, pkey):
            P.dve(TT(vmp[0:16, :, 0:64], ps_ap[0:16, :].rearrange("p (h d) -> p h d", h=8),
                     bias_bc[0:16, TB[C_NAV]:TB[C_NAV] + 512].rearrange("p (h d) -> p h d", h=8), ALU.add),
                  reads=[pkey, "bias_bc"], writes=["vmp"])
        proj_T(xnTm, "xnTm", 1, C_NAV, h_vmeta)

        def h_qmeta(fc, ps_ap, pkey):
            P.act(ACTF(prem[:, fc:fc + 1], ps_ap[:, 15:16], AF.Identity, bias=fm[:, FM_B[C_MLQ] + fc:FM_B[C_MLQ] + fc + 1]),
                  reads=[pkey, "fm"], writes=["prem"])
        proj_F(xnTm, "xnTm", 128, C_MLQ, h_qmeta)

        def h_kmeta2(fc, ps_ap, pkey):
            P.act(ACTF(premk[:, fc, :], ps_ap[:, 0:16], AF.Identity, bias=fm[:, FM_B[C_MLK] + fc:FM_B[C_MLK] + fc + 1]),
                  reads=[pkey, "fm"], writes=["premk"])
            P.pool(CP(prem[:, 4 + fc:5 + fc], premk[:, fc, 15:16]), reads=["premk"], writes=["prem"])
        proj_F(xnTm, "xnTm", 128, C_MLK, h_kmeta2)

        def h_vmeta2(j, ps_ap, pkey):
            P.dve(TT(vmeta[0:16, :, 0:128], ps_ap[0:16, :].rearrange("p (h d) -> p h d", h=4),
                     bias_bc[0:16, TB[C_MLV]:TB[C_MLV] + 512].rearrange("p (h d) -> p h d", h=4), ALU.add),
                  reads=[pkey, "bias_bc"], writes=["vmeta"])
            P.dve(MS(vmeta[0:16, :, 128:129], 1.0), reads=["vmeta"], writes=["vmeta"])
        proj_T(xnTm, "xnTm", 1, C_MLV, h_vmeta2)
        gates_B(gates_A(xnTm, "xnTm", 128, is_meta=True), [NS])
        P.dve(CP(SC[:, NS + 1, :], SC[:, NS, :]), reads=[("SC", NS)], writes=[("SC", NS + 1)])
        P.dve(CP(EG[:, NS + 1, :], EG[:, NS, :]), reads=[("EG", NS)], writes=[("EG", NS + 1)])

        if stop <= 1:
            P.finalize(st)
            return nc
        stg.close()
        ftmp = sb("ftmp", [128, 512])
        Praw = [sb("Praw%d" % i, [128, 640], BF16) for i in range(2)]
        Pn = [sb("Pn%d" % i, [128, 640], BF16) for i in range(2)]
        Pm = sb("Pm", [128, 2, 512], BF16)
        gA = sb("gA", [128, 512], BF16)
        gB = sb("gB", [128, 512], BF16)
        gAT = sb("gAT", [128, 4, 512], BF16)
        gBT = sb("gBT", [128, 4, 512], BF16)
        tgA = sb("tgA", [128, 512])
        tgB = sb("tgB", [128, 512])
        osb = [sb("osb%d" % i, [128, D]) for i in range(2)]
        X1K = ["ftmp", ("Praw", 0), ("Praw", 1), ("Pn", 0), ("Pn", 1), ("Pm", 0), ("Pm", 1), "gA", "gB", "gAT", "gBT", "tgA", "tgB",
               ("osb", 0), ("osb", 1)]
        for eng in ("dve", "act", "pool"):
            P.ew(eng, MS(st1[:, 5:6] if eng == "dve" else st1[:, 6:7], 0.0) if eng != "act" else ACTF(st1[:, 7:8], fl[:, 0:1], AF.Copy),
                 reads=["fl", "etab", "fm", "fmh", "gbias", "kmT", "vmp", "prem", "premk", "vmeta", ("SC", NS), ("EG", NS)],
                 writes=X1K + ["xnTm", "tzs", "rowst", "cmk"])

        P.pool(MS(Pm[:], 0.0), reads=[("Pm", 0), ("Pm", 1)], writes=[("Pm", 0), ("Pm", 1)])

        def meta_state(u):
            P.pool(MS(Cmeta[:, u], 0.0), writes=[("Cmeta", u)])
            for fc in range(4):
                i = cv_i["i"] % 2
                cv_i["i"] += 1
                pk, ck = ("pre", 0), ("cvt", 0)
                pr, cv = pre[i], cvt[i]
                P.pool(MS(pr[:, 0:1], 0.0), writes=[pk])
                P.pool(CP(pr[:, 1:17], premk[:, fc, :]), reads=["premk", pk], writes=[pk])
                P.pool(CP(pr[:, 17:18], firstpre[:, u, 4 + fc, :]), reads=[("firstpre", u), pk], writes=[pk])
                conv_taps(pr, pk, cv, ck, 16, 4 + fc)
                P.dve(STT(kmetaT[:, fc, 0:16], pr[:, 1:17], 1.0, cv[:, 0:16], ALU.add, ALU.mult), reads=[pk, ck], writes=["kmetaT"])
            state_update(Cmeta[:, u], ("Cmeta", u), kmetaT, ["kmetaT"], vmeta, "vmeta", NS + u, 0, 4)

        def tile_src(T):
            return xu[T * 512:(T + 1) * 512, :]

        def small_cols(wsl, wkey, xT, xkey, col):
            hb = next_bank()
            for fc in range(4):
                for kc in range(8):
                    P.pe(MM(psb[hb][:, fc:fc + 1], wsl[:, kc, fc * 128:(fc + 1) * 128], xT[:, kc, col:col + 1], kc == 0, kc == 7),
                         reads=[wkey, xkey], writes=[PK[hb]])
            return hb

        vml2 = KT[:, 0:2, :].rearrange("p a b -> p (a b)")[:, 0:2064].rearrange("p (j h n) -> p j h n", j=4, h=4)
        P.pool(MS(vml2[:, :, :, 128:129], 1.0), writes=[("vml2", j) for j in range(4)])

        def p1_bufs(T):
            if T % 2 == 0:
                return 4, vml, "vml"
            return 0, vml2, "vml2"

        def p1_P(T):
            xt, xkey = xnT[T % 2], ("xnT", T % 2)
            u_of = T // TPU
            first_tile_of_unit = (T % TPU == 0)
            last_tile_of_unit = (T % TPU == TPU - 1)
            fcb, vb, vname = p1_bufs(T)
            wsl, wkey = wload(C_MLK)
            bcol = FM_B[C_MLK]
            if T > 0:
                hb = small_cols(wsl, wkey, xnT[(T - 1) % 2], ("xnT", (T - 1) % 2), 511)
                P.dve(TT(halo[:, 4:8, 0], psb[hb][:, 0:4], fm[:, bcol:bcol + 4], ALU.add), reads=[PK[hb], "fm"], writes=["halo"])
            if first_tile_of_unit:
                if T == 0:
                    P.dve(CP(halo[:, 4:8, 0], prem[:, 4:8]), reads=["prem"], writes=["halo"])
                else:
                    P.dve(TS(halo[:, 4:8, 0], halo[:, 4:8, 0], cont), reads=["halo", "fl"], writes=["halo"])
                    P.dve(STT(halo[:, 4:8, 0], prem[:, 4:8], ncont, halo[:, 4:8, 0], ALU.mult, ALU.add), reads=["halo", "prem", "fl"], writes=["halo"])
            if T == NT - 1:
                P.dve(MS(halo[:, 4:8, 1], 0.0), writes=["halo"])
            elif last_tile_of_unit:
                P.dve(TS(halo[:, 4:8, 1], carry[:, 4:8, 1], cont), reads=["carry", "fl"], writes=["halo"])
            else:
                P.dve(CP(halo[:, 4:8, 1], carry[:, 4:8, 1]), reads=["carry"], writes=["halo"])
            P.dve(TT(hadj[:, 4:8, :], halo[:, 4:8, :], fm[:, bcol:bcol + 4].unsqueeze(2).to_broadcast([128, 4, 2]), ALU.subtract),
                  reads=["halo", "fm"], writes=["hadj"])
            for fc in range(4):
                b = next_bank()
                for kc in range(8):
                    P.pe(MM(psb[b][:, :], wsl[:, kc, fc * 128:(fc + 1) * 128], xt[:, kc, :], kc == 0, kc == 7), reads=[wkey, xkey], writes=[PK[b]])
                conv_silu(4 + fc, psb[b][:, :], PK[b], 512, bcol + fc, halo[:, 4 + fc, 0:1], halo[:, 4 + fc, 1:2],
                          qkT[:, fcb + fc, :], ("qkT", fcb + fc), save_first=carry[:, 4 + fc, 1:2])
                yield
            if first_tile_of_unit:
                P.pool(CP(firstpre[:, u_of, 4:8, 0], carry[:, 4:8, 1]), reads=["carry"], writes=[("firstpre", u_of)])
            wsv, wkv = wload(C_MLV)
            for j in range(4):
                b = next_bank()
                for kc in range(8):
                    P.pe(MM(psb[b][:, :], xt[:, kc, j * 128:(j + 1) * 128], wsv[:, kc, :], kc == 0, kc == 7), reads=[wkv, xkey], writes=[PK[b]])
                P.dve(TT(vb[:, j, :, 0:128], psb[b][:, :].rearrange("p (h d) -> p h d", h=4),
                         bias_bc[:, TB[C_MLV]:TB[C_MLV] + 512].rearrange("p (h d) -> p h d", h=4), ALU.add),
                      reads=[PK[b], "bias_bc"], writes=[(vname, j)])
                yield
            tokg = gates_A(xt, xkey, 512)
            yield
            gates_B(tokg, [T * 4 + c for c in range(4)])
            yield

        def p1_S(T):
            last_tile_of_unit = (T % TPU == TPU - 1)
            fcb, vb, vname = p1_bufs(T)
            KK4 = [("qkT", fcb + h) for h in range(4)]

            def pb(slot):
                return [4 + 2 * (slot % 2), 5 + 2 * (slot % 2)]
            outs_next = state_prep(qkT[:, fcb:fcb + 4, 3 * 128:4 * 128], KK4, vb[:, 3], (vname, 3), T * 4 + 3, 1, banks=pb(T * 4 + 3))
            for c in range(3, -1, -1):
                slot = T * 4 + c
                outs = outs_next
                ltok = None
                if T - 2 >= 0:
                    T2 = T - 2
                    jj = c
                    ltok = ln_A(xu[T2 * 512 + jj * 128:T2 * 512 + (jj + 1) * 128, :])
                if last_tile_of_unit and c == 3 and T != NT - 1:
                    P.dve(TS(Cb[:], Cb[:], cont), reads=["Cb", "fl"], writes=["Cb"])
                i = slot % 2
                P.act(ACTF(cbst[i][:], Cb[:], AF.Copy), reads=["Cb"], writes=[("cbst", i)])
                P.dma(DMA(cbs[slot].rearrange("p (h n) -> p h n", h=4), cbst[i][:]), reads=[("cbst", i)], writes=[("cbs", slot)], semkey=("cbst", i),
                      queue="pool")
                if c > 0:
                    outs_next = state_prep(qkT[:, fcb:fcb + 4, (c - 1) * 128:c * 128], KK4, vb[:, c - 1], (vname, c - 1), slot - 1, 1, banks=pb(slot - 1))
                state_apply(Cb, "Cb", outs, slot, 1)
                if ltok is not None:
                    ln_B(ltok, xnT[T2 % 2][:, :, jj * 128:(jj + 1) * 128], ("xnT", T2 % 2))
                yield

        load_norm_T(tile_src(NT - 1), 4, xnT[(NT - 1) % 2], ("xnT", (NT - 1) % 2))
        if NT > 1:
            load_norm_T(tile_src(NT - 2), 4, xnT[(NT - 2) % 2], ("xnT", (NT - 2) % 2))
        for _ in p1_P(NT - 1):
            pass
        for T in range(NT - 1, -1, -1):
            gS = p1_S(T)
            gP = p1_P(T - 1) if T > 0 else iter(())
            for k in range(4):
                next(gS, None)
                for _ in range(3 if k == 0 else 2):
                    next(gP, None)
            for _ in gS:
                pass
            for _ in gP:
                pass
            if T % TPU == 0:
                meta_state(T // TPU)
        for eng in ("act", "dve", "pool"):
            P.ew(eng, MS(st1[:, 5:6] if eng == "dve" else st1[:, 6:7], 0.0) if eng != "act" else ACTF(st1[:, 7:8], fl[:, 0:1], AF.Copy),
                 reads=["fl"], writes=[("vml2", j) for j in range(4)] + [("KT", i) for i in range(RING_T)])
        if stop <= 2:
            P.finalize(st)
            return nc
        def kv_proj(T):
            xt, xkey = xnT[T % 2], ("xnT", T % 2)
            rt = T % RING_T

            def h_k(fc, ps_ap, pkey):
                P.act(ACTF(KT[:, fc, rt * 512:(rt + 1) * 512], ps_ap, AF.Identity, bias=fm[:, FM_B[C_NAK] + fc:FM_B[C_NAK] + fc + 1]),
                      reads=[pkey, "fm"], writes=[("KT", rt)])
            proj_F(xt, xkey, 512, C_NAK, h_k)

            def h_v(j, ps_ap, pkey):
                P.dve(TT(VP[:, rt * 4 + j, :, 0:64], ps_ap.rearrange("p (h d) -> p h d", h=8),
                         bias_bc[:, TB[C_NAV]:TB[C_NAV] + 512].rearrange("p (h d) -> p h d", h=8), ALU.add),
                      reads=[pkey, "bias_bc"], writes=[("VP", rt)])
            proj_T(xt, xkey, 4, C_NAV, h_v)

        def ring_row(r):
            return r % RROWS

        def na_q_z(T):
            xt, xkey = xnT[T % 2], ("xnT", T % 2)
            P.pool(MS(qkT[64:128, 0:4, :], 0.0), writes=[("qkT", f) for f in range(4)])
            P.pool(MS(qkT[0:64, 4:8, :], 0.0), writes=[("qkT", 4 + f) for f in range(4)])

            def h_q(fc, ps_ap, pkey):
                bq = FM_B[C_NAQ] + fc
                P.act(ACTF(qkT[0:64, fc, :], ps_ap[0:64, :], AF.Identity, bias=fmh[0:64, bq:bq + 1], scale=0.125),
                      reads=[pkey, "fmh", ("qkT", fc)], writes=[("qkT", fc)])
                P.act(ACTF(qkT[64:128, 4 + fc, :], ps_ap[64:128, :], AF.Identity, bias=fmh[64:128, bq:bq + 1], scale=0.125),
                      reads=[pkey, "fmh", ("qkT", 4 + fc)], writes=[("qkT", 4 + fc)])
            proj_F(xt, xkey, 512, C_NAQ, h_q)

            def h_z(j, ps_ap, pkey):
                (za, zak), (zb_, zbk) = ((ftmp, ["ftmp"]), (tgA, ["tgA"])) if j % 2 == 0 else ((tgB, ["tgB"]), (h2, ["h2"]))
                P.dve(TT(za[:, :], ps_ap, bias_bc[:, TB[C_NAZ]:TB[C_NAZ] + 512], ALU.add), reads=[pkey, "bias_bc"], writes=zak)
                P.act(ACTF(zb_[:, :], za[:, :], AF.Tanh, scale=0.5), reads=zak, writes=zbk)
                P.pool(TS(zb_[:, :], zb_[:, :], 1.0, 0.5, ALU.add, ALU.mult), reads=zbk, writes=zbk)
                P.pool(TT(zs[:, j, :], za[:, :], zb_[:, :], ALU.mult), reads=zak + zbk, writes=[("zs", j)])
            proj_T(xt, xkey, 4, C_NAZ, h_z)

        def q_ap(h, qcol):
            fc, hh = h // 2, h % 2
            return qkT[:, 4 * hh + fc, qcol:qcol + 64], ("qkT", 4 * hh + fc)

        def na_meta_scores(lrt, slot):
            qcol = lrt * 64
            ms = psb[3][0:16, :]
            for h in range(8):
                qap, qk = q_ap(h, qcol)
                P.pe(MM(ms[:, h * 64:(h + 1) * 64], kmT[:, h // 2, :], qap), reads=["kmT", qk], writes=[PK[3]])
            P.act(ACTF(Pm[0:16, slot, :], ms, AF.Exp), reads=[PK[3]], writes=[("Pm", slot)])

        na_i = {"i": 0, "cur": 0}
        PB = [Praw[0], Praw[1], Pn[0], Pn[1]]
        PBK = [("Praw", 0), ("Praw", 1), ("Pn", 0), ("Pn", 1)]

        def na_scores(lr_tile, r0, delta, hp):
            qcol = lr_tile * 64
            i = na_i["i"] % 4
            na_i["i"] += 1
            odd = (r0 % 2 == 1)
            ng = 5 if odd else 4
            W = 2 * ng * 64
            cur = na_i["cur"]
            if odd:
                if cur % 2 == 1:
                    cur = (cur + 1) % 4
                reg = psS[cur // 2][:, 0:W]
                skeys = [PK[4 + cur], PK[5 + cur]]
                na_i["cur"] = (cur + 2) % 4
            else:
                reg = psb[4 + cur][:, 0:W]
                skeys = [PK[4 + cur]]
                na_i["cur"] = (cur + 1) % 4
            sps = reg.rearrange("p (h g q) -> p h g q", h=2, g=ng)
            base = r0 - 1 if odd else r0
            rk = list(dict.fromkeys([("KT", (rr // 8) % RING_T) for rr in range(base, base + 2 * ng)]))
            for hh in range(2):
                qap, qk = q_ap(2 * hp + hh, qcol)
                for g in range(ng):
                    c0 = ring_row(base + 2 * g) * 64
                    P.pe(MM(sps[:, hh, g, :], KT[:, hp, c0:c0 + 128], qap), reads=rk + [qk], writes=skeys)
            P.act(ACTF(PB[i][:, 0:W], reg, AF.Exp), reads=skeys, writes=[PBK[i]])
            pn4 = PB[i][:, 0:W].rearrange("p (h g q) -> p h g q", h=2, g=ng)
            e2 = etab[:, 2 * hp:2 * hp + 2]
            if not odd:
                ei0 = 7 - delta
                P.dve(TT(pn4, pn4, e2[:, :, ei0:ei0 + 7:2, :], ALU.mult), reads=[PBK[i], "etab"], writes=[PBK[i]])
            else:
                P.dve(TT(pn4[:, :, 0, :], pn4[:, :, 0, :], e2[:, :, 14, :], ALU.mult), reads=[PBK[i], "etab"], writes=[PBK[i]])
                P.dve(TT(pn4[:, :, 1:4, :], pn4[:, :, 1:4, :], e2[:, :, 4:9:2, :], ALU.mult), reads=[PBK[i], "etab"], writes=[PBK[i]])
                P.dve(TT(pn4[:, :, 4, :], pn4[:, :, 4, :], e2[:, :, 15, :], ALU.mult), reads=[PBK[i], "etab"], writes=[PBK[i]])
            return (PBK[i], pn4, ng, base)

        def na_pv(desc, hp, pv_bank, row_half):
            pkey, pn4, ng, base = desc
            vk = list(dict.fromkeys([("VP", (rr // 8) % RING_T) for rr in range(base, base + 2 * ng)]))
            for hh in range(2):
                h = 2 * hp + hh
                o = psb[pv_bank][row_half * 64:(row_half + 1) * 64, (h % 4) * 65:(h % 4) * 65 + 65]
                rd = [pkey] + vk
                for g in range(ng):
                    vt = ring_row(base + 2 * g) // 2
                    P.pe(MM(o, pn4[:, hh, g, :], VP[:, vt, h, :], g == 0, False), reads=rd, writes=[PK[pv_bank]])
                P.pe(MM(o, Pm[:, row_half, h * 64:(h + 1) * 64], vmp[:, h, :], False, True), reads=[("Pm", row_half), "vmp"], writes=[PK[pv_bank]])

        def na_tile(T):
            items = []
            for sp in range(4):
                rows = [8 * T + 2 * sp, 8 * T + 2 * sp + 1]
                variants = []
                for r in rows:
                    u = r // R
                    lr = r % R
                    r0 = min(max(lr - 4, 0), R - 8)
                    v = [(u * R + r0, lr - r0)]
                    if R - 4 <= r < R + 4:
                        v.append((r - 4, 4))
                    variants.append(v)
                nvar = max(len(v) for v in variants)
                items.append(("meta", sp, rows))
                for vi in range(nvar):
                    for half in range(2):
                        grp = (sp, vi, half)
                        for rh, r in enumerate(rows):
                            r0, delta = variants[rh][min(vi, len(variants[rh]) - 1)]
                            for hp in (2 * half, 2 * half + 1):
                                items.append(("unit", r - 8 * T, r0, delta, hp, grp, rh))
                        items.append(("norm", grp, vi, half))
                items.append(("fin", sp, nvar))
            banks = {}
            q = []
            DEPTH = 3

            def do_pv(ent):
                _, desc, hp, grp, rh = ent
                if grp not in banks:
                    banks[grp] = next_bank()
                na_pv(desc, hp, banks[grp], rh)

            def drain(maxpv):
                while sum(1 for e in q if e[0] == "pv") > maxpv:
                    while q and q[0][0] != "pv":
                        run(q.pop(0)[1])
                    do_pv(q.pop(0))
                    while q and q[0][0] != "pv":
                        run(q.pop(0)[1])

            def run(it):
                if it[0] == "meta":
                    for rh, r in enumerate(it[2]):
                        na_meta_scores(r - 8 * T, rh)
                elif it[0] == "norm":
                    _, grp, vi, half = it
                    pvb = banks[grp]
                    pv = psb[pvb][:, 0:260].rearrange("p (h n) -> p h n", h=4)
                    P.dve(RECIP(st1[:, 4:8], pv[:, :, 64]), reads=[PK[pvb]], writes=["st1r"])
                    dst = nao[vi][:, half * 256:(half + 1) * 256].rearrange("p (h d) -> p h d", h=4)
                    P.dve(TT(dst, pv[:, :, 0:64], st1[:, 4:8].unsqueeze(2).to_broadcast([128, 4, 64]), ALU.mult),
                          reads=[PK[pvb], "st1r"] + NAOK[vi], writes=NAOK[vi])
                else:
                    _, sp, nvar = it
                    if nvar == 2:
                        P.dve(TS(nao[0][:, :], nao[0][:, :], ncont), reads=NAOK[0] + ["fl"], writes=NAOK[0])
                        P.dve(STT(nao[0][:, :], nao[1][:, :], cont, nao[0][:, :], ALU.mult, ALU.add), reads=NAOK[0] + NAOK[1] + ["fl"], writes=NAOK[0])
                    if dbg:
                        t0 = (8 * T + 2 * sp) * 64
                        P.dma(DMA(dbg_out["d_na"][t0:t0 + 128, :], nao[0][:, :]), reads=NAOK[0], semkey="dbg1")
                    P.dve(TT(gA[:, :], nao[0][:, :], zs[:, sp, :], ALU.mult), reads=NAOK[0] + [("zs", sp)], writes=["gA"])
                    pT = pTv()
                    for fc in range(4):
                        P.pe(TR(pT[:, fc, :], gA[:, fc * 128:(fc + 1) * 128], ident[:, :]), reads=["gA", "ident"], writes=[PK[3]])
                    P.act(ACTF(gAT[:, :, sp * 128:(sp + 1) * 128], pT[:, 0:4, :], AF.Copy), reads=[PK[3]], writes=["gAT"])

            for it in items:
                if it[0] == "unit":
                    _, lrt, r0, delta, hp, grp, rh = it
                    desc = na_scores(lrt, r0, delta, hp)
                    q.append(("pv", desc, hp, grp, rh))
                    drain(DEPTH)
                elif not q:
                    run(it)
                else:
                    q.append(("act", it))
            drain(0)
            while q:
                run(q.pop(0)[1])

        def mlstm_tile(T, filler=None):
            xt, xkey = xnT[T % 2], ("xnT", T % 2)
            u_of = T // TPU
            first_tile_of_unit = (T % TPU == 0)
            last_tile_of_unit = (T % TPU == TPU - 1)
            for (cch, fco) in ((C_MLQ, 0), (C_MLK, 4)):
                wsl, wkey = wload(cch)
                bcol = FM_B[cch]
                if T < NT - 1:
                    hb = small_cols(wsl, wkey, xnT[(T + 1) % 2], ("xnT", (T + 1) % 2), 0)
                    P.dve(TT(halo[:, fco:fco + 4, 1], psb[hb][:, 0:4], fm[:, bcol:bcol + 4], ALU.add), reads=[PK[hb], "fm"], writes=["halo"])
                    if last_tile_of_unit:
                        P.dve(TS(halo[:, fco:fco + 4, 1], halo[:, fco:fco + 4, 1], cont), reads=["halo", "fl"], writes=["halo"])
                else:
                    P.dve(MS(halo[:, fco:fco + 4, 1], 0.0), writes=["halo"])
                if T == 0:
                    P.dve(CP(halo[:, fco:fco + 4, 0], prem[:, fco:fco + 4]), reads=["prem"], writes=["halo"])
                elif first_tile_of_unit:
                    P.dve(TS(halo[:, fco:fco + 4, 0], carry[:, fco:fco + 4, 0], cont), reads=["carry", "fl"], writes=["halo"])
                    P.dve(STT(halo[:, fco:fco + 4, 0], prem[:, fco:fco + 4], ncont, halo[:, fco:fco + 4, 0], ALU.mult, ALU.add),
                          reads=["halo", "prem", "fl"], writes=["halo"])
                else:
                    P.dve(CP(halo[:, fco:fco + 4, 0], carry[:, fco:fco + 4, 0]), reads=["carry"], writes=["halo"])
                P.dve(TT(hadj[:, fco:fco + 4, :], halo[:, fco:fco + 4, :], fm[:, bcol:bcol + 4].unsqueeze(2).to_broadcast([128, 4, 2]), ALU.subtract),
                      reads=["halo", "fm"], writes=["hadj"])
                for fc in range(4):
                    b = next_bank()
                    for kc in range(8):
                        P.pe(MM(psb[b][:, :], wsl[:, kc, fc * 128:(fc + 1) * 128], xt[:, kc, :], kc == 0, kc == 7), reads=[wkey, xkey], writes=[PK[b]])
                    conv_silu(fco + fc, psb[b][:, :], PK[b], 512, bcol + fc, halo[:, fco + fc, 0:1], halo[:, fco + fc, 1:2],
                              qkT[:, fco + fc, :], ("qkT", fco + fc), save_last=carry[:, fco + fc, 0:1])

            def h_v(j, ps_ap, pkey):
                P.dve(TT(vml[:, j, :, 0:128], ps_ap.rearrange("p (h d) -> p h d", h=4),
                         bias_bc[:, TB[C_MLV]:TB[C_MLV] + 512].rearrange("p (h d) -> p h d", h=4), ALU.add),
                      reads=[pkey, "bias_bc"], writes=[("vml", j)])
            proj_T(xt, xkey, 4, C_MLV, h_v)

            def h_z(j, ps_ap, pkey):
                (za, zak), (zb_, zbk) = ((ftmp, ["ftmp"]), (tgA, ["tgA"])) if j % 2 == 0 else ((tgB, ["tgB"]), (h2, ["h2"]))
                P.dve(TT(za[:, :], ps_ap, bias_bc[:, TB[C_MLZ]:TB[C_MLZ] + 512], ALU.add), reads=[pkey, "bias_bc"], writes=zak)
                P.act(ACTF(zb_[:, :], za[:, :], AF.Tanh, scale=0.5), reads=zak, writes=zbk)
                P.dve(STT(zb_[:, :], zb_[:, :], 1.0, za[:, :], ALU.add, ALU.mult), reads=zak + zbk, writes=zbk)
                P.pool(TT(zs[:, j, :], zb_[:, :], hg_bc[:, :], ALU.mult), reads=zbk + ["hg_bc"], writes=[("zs", j)])
            proj_T(xt, xkey, 4, C_MLZ, h_z)

            def h_o(j, ps_ap, pkey):
                za, zak = (tgB, ["tgB"]) if j % 2 == 0 else (ftmp, ["ftmp"])
                P.dve(TT(za[:, :], ps_ap, bias_bc[:, TB[C_MLO]:TB[C_MLO] + 512], ALU.add), reads=[pkey, "bias_bc"], writes=zak)
                P.act(ACTF(oG[:, j, :], za[:, :], AF.Tanh, scale=0.5), reads=zak, writes=[("oG", j)])
            proj_T(xt, xkey, 4, C_MLO, h_o)
            def ml_X(c):
                slot = T * 4 + c
                csl = slice(c * 128, (c + 1) * 128)
                if first_tile_of_unit and c == 0:
                    if T == 0:
                        P.dve(CP(Cf[:], Cmeta[:, 0]), reads=[("Cmeta", 0)], writes=["Cf"])
                    else:
                        P.dve(TS(Cf[:], Cf[:], cont), reads=["Cf", "fl"], writes=["Cf"])
                        P.dve(STT(Cf[:], Cmeta[:, u_of], ncont, Cf[:], ALU.mult, ALU.add), reads=["Cf", ("Cmeta", u_of), "fl"], writes=["Cf"])
                    P.act(ACTF(Cfb[:], Cf[:], AF.Copy), reads=["Cf"], writes=["Cfb"])
                li = slot % 2
                P.dma(DMA(cbl[li][:], cbs[slot].rearrange("p (h n) -> p h n", h=4)), reads=[("cbs", slot)], writes=[("cbl", li)], semkey=("cbl", li))
                sps = psb[4][:, :].rearrange("p (h t) -> p h t", h=4)
                for h in range(4):
                    P.pe(MM(sps[:, h, :], qkT[:, 4 + h, csl], qkT[:, h, csl]), reads=[("qkT", 4 + h), ("qkT", h)], writes=[PK[4]])
                for h in range(4):
                    P.dve(STT(Sf[h][:, :], sps[:, h, :], SC[:, slot, h:h + 1], maskf[:, :], ALU.mult, ALU.mult),
                          reads=[PK[4], ("SC", slot), "maskf"], writes=[("Sf", h)])
                    P.dve(STT(Sb_[h][:, :], sps[:, h, :], SC[:, slot, 4 + h:5 + h], maskb[:, :], ALU.mult, ALU.mult),
                          reads=[PK[4], ("SC", slot), "maskb"], writes=[("Sb", h)])
                souts = state_prep(qkT[:, 4:8, csl], [("qkT", 4 + h) for h in range(4)], vml[:, c], ("vml", c), slot, 0)
                nbs = [5, 6, 7, next_bank()]
                for h in range(4):
                    nb = nbs[h]
                    nf = psb[nb][:, 0:129]
                    nbk = psb[nb][:, 129:258]
                    P.pe(MM(nf, Sf[h][:, :], vml[:, c, h, :], True, False), reads=[("Sf", h), ("vml", c)], writes=[PK[nb]])
                    P.pe(MM(nf, qkT[:, h, csl], Cfb[:, h, :], False, True), reads=[("qkT", h), "Cfb"], writes=[PK[nb]])
                    P.pe(MM(nbk, Sb_[h][:, :], vml[:, c, h, :], True, False), reads=[("Sb", h), ("vml", c)], writes=[PK[nb]])
                    P.pe(MM(nbk, qkT[:, h, csl], cbl[li][:, h, :], False, True), reads=[("qkT", h), ("cbl", li)], writes=[PK[nb]])
                state_apply(Cf, "Cf", souts, slot, 0)
                P.act(ACTF(Cfb[:], Cf[:], AF.Copy), reads=["Cf"], writes=["Cfb"])
                for h in range(4):
                    nb = nbs[h]
                    d0 = 16 + 4 * h
                    P.act(ACTF(st2[:, d0:d0 + 2], psb[nb][:, 128:258:129], AF.Abs), reads=[PK[nb]], writes=[("st1d", h)])
                for h in range(4):
                    d0 = 16 + 4 * h
                    P.dve(TT(st2[:, d0:d0 + 2], st2[:, d0:d0 + 2], SC[:, slot, 16 + h:21 + h:4], ALU.max), reads=[("st1d", h), ("SC", slot)], writes=[("st1d", h)])
                    P.dve(RECIP(st2[:, d0 + 2:d0 + 4], st2[:, d0:d0 + 2]), reads=[("st1d", h)], writes=[("st1e", h)])
                for h in range(4):
                    nb = nbs[h]
                    d0 = 16 + 4 * h
                    hs = slice(h * 128, (h + 1) * 128)
                    P.act(ACTF(hbuf[:, hs], psb[nb][:, 0:128], AF.Copy, scale=st2[:, d0 + 2:d0 + 3]), reads=[PK[nb], ("st1e", h)], writes=[("hbuf", h)])
                for h in range(4):
                    nb = nbs[h]
                    d0 = 16 + 4 * h
                    hs = slice(h * 128, (h + 1) * 128)
                    P.dve(STT(hbuf[:, hs], psb[nb][:, 129:257], st2[:, d0 + 3:d0 + 4], hbuf[:, hs], ALU.mult, ALU.add),
                          reads=[PK[nb], ("st1e", h), ("hbuf", h)], writes=[("hbuf", h)])
                if dbg:
                    t0 = slot * 128
                    P.dma(DMA(dbg_out["d_h"][t0:t0 + 128, :], hbuf[:, :]), reads=HBK, semkey="dbg2")

            def ml_Y1(c):
                P.dve(STT(h2[:, :], oG[:, c, :], 1.0, hbuf[:, :], ALU.add, ALU.mult), reads=[("oG", c)] + HBK, writes=["h2"])

            def ml_Y2(c):
                csl = slice(c * 128, (c + 1) * 128)
                h3 = h2[:, :].rearrange("p (h d) -> p h d", h=4)
                P.dve(lambda e, h3=h3: e.tensor_reduce(out=st1[:, 12:16], in_=h3, axis=AX.X, op=ALU.add), reads=["h2"], writes=["st1m"])
                for h in range(4):
                    P.act(ACTF(ftmp[:, h * 128:(h + 1) * 128], h2[:, h * 128:(h + 1) * 128], AF.Square, accum_out=st1[:, 8 + h:9 + h]),
                          reads=["h2"], writes=["ftmp", ("st1q", h)])
                SQK = [("st1q", h) for h in range(4)]
                P.dve(TS(st1[:, 12:16], st1[:, 12:16], 1.0 / 128), reads=["st1m"], writes=["st1m"])
                P.dve(TT(st1[:, 4:8], st1[:, 12:16], st1[:, 12:16], ALU.mult), reads=["st1m"], writes=["st1r"])
                P.dve(STT(st1[:, 8:12], st1[:, 8:12], 1.0 / 128, st1[:, 4:8], ALU.mult, ALU.subtract), reads=SQK + ["st1r"], writes=["st1v"])
                P.dve(TS(st1[:, 8:12], st1[:, 8:12], 4.0 * EPS, None, ALU.add), reads=["st1v"], writes=["st1v"])
                P.pool(TT(st1[:, 8:12], st1[:, 8:12], cst[:, 0:1].to_broadcast([128, 4]), ALU.pow), reads=["st1v", "cst"], writes=["st1v"])
                for h in range(4):
                    hs = slice(h * 128, (h + 1) * 128)
                    P.dve(TS(h2[:, hs], h2[:, hs], st1[:, 12 + h:13 + h], st1[:, 8 + h:9 + h], ALU.subtract, ALU.mult),
                          reads=["h2", "st1m", "st1v"], writes=["h2"])
                P.dve(TT(gB[:, :], h2[:, :], zs[:, c, :], ALU.mult), reads=["h2", ("zs", c)], writes=["gB"])

            def ml_Y2b(c):
                csl = slice(c * 128, (c + 1) * 128)
                pT = pTv()
                for fc in range(4):
                    P.pe(TR(pT[:, fc, :], gB[:, fc * 128:(fc + 1) * 128], ident[:, :]), reads=["gB", "ident"], writes=[PK[3]])
                P.act(ACTF(gBT[:, :, csl], pT[:, 0:4, :], AF.Copy), reads=[PK[3]], writes=["gBT"])

            def fill(k):
                if filler is not None:
                    for _ in range(k):
                        next(filler, None)

            for c in range(4):
                ml_X(c)
                if c >= 2:
                    ml_Y2b(c - 2)
                fill(2)
                if c >= 1:
                    ml_Y2(c - 1)
                ml_Y1(c)
            ml_Y2b(2)
            ml_Y2(3)
            ml_Y2b(3)
            fill(8)

        def out_branch(T, which):
            xt, xkey = xnT[T % 2], ("xnT", T % 2)
            (cbr, cg0, cg1, gsrc, gkey, tg, tgk, first) = ((C_WA, C_GA0, C_GA1, gAT, "gAT", tgA, "tgA", True) if which == 0 else
                                                          (C_WB, C_GB0, C_GB1, gBT, "gBT", tgB, "tgB", False))
            wbr_, kbr = wload(cbr)
            wbr = wbr_[:].rearrange("p a b -> p (a b)").rearrange("p (fc n) -> p fc n", fc=4)
            for half, cg in enumerate((cg0, cg1)):
                wG, kG = wload(cg)
                for fc in range(4):
                    n = half * 4 + fc
                    bG = next_bank()
                    for kc in range(8):
                        P.pe(MM(psb[bG][:, :], wG[:, kc, fc * 128:(fc + 1) * 128], xt[:, kc, :], kc == 0, kc == 7), reads=[kG, xkey], writes=[PK[bG]])
                    P.act(ACTF(tg[:, :], psb[bG][:, :], AF.Tanh, bias=fmh[:, FM_B[cg] + fc:FM_B[cg] + fc + 1], scale=0.5),
                          reads=[PK[bG], "fmh"], writes=[tgk])
                    by = next_bank()
                    for k4 in range(4):
                        P.pe(MM(psb[by][:, :], wbr[:, k4, n * 128:(n + 1) * 128], gsrc[:, k4, :], k4 == 0, k4 == 3), reads=[kbr, gkey], writes=[PK[by]])
                    if first:
                        P.dve(STT(ypT[:, n, :], tg[:, :], 1.0, psb[by][:, :], ALU.add, ALU.mult), reads=[tgk, PK[by]], writes=[("ypT", n)])
                    else:
                        P.dve(STT(tg[:, :], tg[:, :], 1.0, psb[by][:, :], ALU.add, ALU.mult), reads=[tgk, PK[by]], writes=[tgk])
                        P.pool(TT(ypT[:, n, :], ypT[:, n, :], tg[:, :], ALU.add), reads=[tgk, ("ypT", n)], writes=[("ypT", n)])
                    yield n

        def out_tile(T, hook=None, skip_a=False):
            xt, xkey = xnT[T % 2], ("xnT", T % 2)
            for which in ((1,) if skip_a else (0, 1)):
                for _ in out_branch(T, which):
                    pass
            wO0, kO0 = wload(C_WO0)
            wO1, kO1 = wload(C_WO1)
            YK = [("ypT", n) for n in range(8)]
            T2 = T + 2 if hook is not None else None
            ltok = None
            if T2 is not None:
                ltok = ln_A(xu[T2 * 512:T2 * 512 + 128, :])
            for j in range(4):
                i = j % 2
                t0 = T * 512 + j * 128
                P.dma(DMA(osb[i][:], xu[t0:t0 + 128, :]), writes=[("osb", i)], reads=[("y", "st", i)], semkey=("xres", i))
                b0 = next_bank()
                b1 = next_bank()
                for (b, wO, kO) in ((b0, wO0, kO0), (b1, wO1, kO1)):
                    for k8 in range(8):
                        P.pe(MM(psb[b][:, :], ypT[:, k8, j * 128:(j + 1) * 128], wO[:, k8, :], k8 == 0, k8 == 7), reads=YK + [kO], writes=[PK[b]])
                if T2 is not None:
                    ln_B(ltok, xnT[T2 % 2][:, :, j * 128:(j + 1) * 128], ("xnT", T2 % 2))
                    if j < 3:
                        ltok = ln_A(xu[T2 * 512 + (j + 1) * 128:T2 * 512 + (j + 2) * 128, :])
                c0 = 8 + 4 * i
                oa, oa2, ob, oc = ("o_a", i), ("o_a2", i), ("o_b", i), ("o_c", i)
                P.act(ACTF(gA[:, :], psb[b0][:, :], AF.Square, accum_out=st2[:, c0:c0 + 1]), reads=[PK[b0]], writes=["gA", oa])
                P.act(ACTF(gB[:, :], psb[b1][:, :], AF.Square, accum_out=st2[:, c0 + 3:c0 + 4]), reads=[PK[b1]], writes=["gB", oa2])
                P.dve(TT(st2[:, c0 + 1:c0 + 2], st2[:, c0:c0 + 1], st2[:, c0 + 3:c0 + 4], ALU.add), reads=[oa, oa2], writes=[ob])
                P.dve(TS(st2[:, c0 + 1:c0 + 2], st2[:, c0 + 1:c0 + 2], 1.0 / D, 4.0 * EPS, ALU.mult, ALU.add), reads=[ob], writes=[ob])
                P.pool(TT(st2[:, c0 + 2:c0 + 3], st2[:, c0 + 1:c0 + 2], cst[:, 0:1], ALU.pow), reads=[ob, "cst"], writes=[oc])
                P.dve(STT(tgA[:, :], psb[b0][:, :], st2[:, c0 + 2:c0 + 3], gpost_bc[:, 0:512], ALU.mult, ALU.mult),
                      reads=[PK[b0], oc, "gpost_bc"], writes=["tgA"])
                P.dve(STT(tgB[:, :], psb[b1][:, :], st2[:, c0 + 2:c0 + 3], gpost_bc[:, 512:1024], ALU.mult, ALU.mult),
                      reads=[PK[b1], oc, "gpost_bc"], writes=["tgB"])
                P.pool(TT(osb[i][:, 0:512], osb[i][:, 0:512], tgA[:, :], ALU.add), reads=[("osb", i), "tgA"], writes=[("osb", i)])
                P.pool(TT(osb[i][:, 512:1024], osb[i][:, 512:1024], tgB[:, :], ALU.add), reads=[("osb", i), "tgB"], writes=[("osb", i)])
                P.dma(DMA(y[t0:t0 + 128, :], osb[i][:, :]), reads=[("osb", i)], writes=[("y", "st", i)], semkey=("osb", i), queue="pool")

        def ln_sub(T2, j):
            load_norm_T(xu[T2 * 512 + j * 128:T2 * 512 + (j + 1) * 128, :], 1, xnT[T2 % 2][:, :, j * 128:(j + 1) * 128], ("xnT", T2 % 2))

        kv_proj(0)
        for T in range(NT):
            if T + 1 < NT:
                kv_proj(T + 1)
            na_q_z(T)
            na_tile(T)
            mlstm_tile(T, filler=out_branch(T, 0))
            out_tile(T, hook=(lambda j, T=T: ln_sub(T + 2, j)) if T + 2 < NT else None, skip_a=True)

        P.finalize(st)
    return nc


def _host_tables(na_rpb):
    rpb = np.asarray(na_rpb, np.float32).reshape(8, 15, 31)
    kc = np.arange(64)[:, None]
    qc = np.arange(64)[None, :]
    dc = np.clip(kc - qc + 15, 0, 30)
    tzv = np.ascontiguousarray(np.transpose(rpb[:, :, dc], (2, 0, 1, 3)))
    c0 = np.clip(qc - 8, 0, 48)
    valid = ((kc >= c0) & (kc < c0 + 16)).astype(np.float32)
    cm = np.concatenate([valid, valid], axis=0)
    return tzv.astype(np.float32), cm.astype(np.float32)


def _make_in_maps(inputs, units_per_core, conts):
    tzv, cm = _host_tables(inputs["na_rpb"])
    common = {
        "meta": np.ascontiguousarray(inputs["meta_tokens"], np.float32),
        "g_pre": np.ascontiguousarray(inputs["g_pre"], np.float32).reshape(1, D),
        "w_in": np.ascontiguousarray(inputs["w_in"], np.float32).reshape(D, NIN),
        "b_in": np.ascontiguousarray(inputs["b_in"], np.float32).reshape(1, NIN),
        "tz": tzv, "cmask": cm,
        "conv_w": np.ascontiguousarray(inputs["ml_conv_w"], np.float32).reshape(3, D),
        "head_g": np.ascontiguousarray(inputs["ml_head_g"], np.float32).reshape(1, 512),
        "w_a": np.ascontiguousarray(inputs["w_a"], np.float32).reshape(512, D),
        "w_b": np.ascontiguousarray(inputs["w_b"], np.float32).reshape(512, D),
        "w_out": np.ascontiguousarray(inputs["w_out"], np.float32).reshape(D, D),
        "g_post": np.ascontiguousarray(inputs["g_post"], np.float32).reshape(1, D),
    }
    maps = []
    for xu, cont in zip(units_per_core, conts):
        fl = np.zeros((128, 4), np.float32)
        fl[:, 0] = cont
        fl[:, 1] = 1.0 - cont
        fl[0:4, 2] = 1.0
        m = dict(common)
        m["xu"] = np.ascontiguousarray(xu, np.float32)
        m["flags"] = fl
        maps.append(m)
    return maps


_NC_CACHE = {}


def kernel(x_prompt, x_sample, meta_tokens, g_pre, w_in, b_in, na_rpb, ml_conv_w, ml_head_g, w_a, w_b, w_out, g_post):
    x_prompt = np.asarray(x_prompt, np.float32)
    x_sample = np.asarray(x_sample, np.float32)
    inputs = dict(meta_tokens=meta_tokens, g_pre=g_pre, w_in=w_in, b_in=b_in, na_rpb=na_rpb, ml_conv_w=ml_conv_w,
                  ml_head_g=ml_head_g, w_a=w_a, w_b=w_b, w_out=w_out, g_post=g_post)
    R = 64
    units, conts = [], []
    for s in range(2):
        units.append(x_sample[s])
        conts.append(1.0)
    for c in range(4):
        units.append(np.concatenate([x_prompt[2 * c], x_prompt[2 * c + 1]], axis=0))
        conts.append(0.0)
    for c in range(2):
        units.append(np.concatenate([x_prompt[2 * c], x_prompt[2 * c + 1]], axis=0))
        conts.append(0.0)
    if R not in _NC_CACHE:
        _NC_CACHE[R] = build(R)
    nc = _NC_CACHE[R]
    in_maps = _make_in_maps(inputs, units, conts)
    res = run_bass_kernel_spmd(nc, in_maps, core_ids=list(range(8)))
    outs = [np.asarray(r["y"], np.float32) for r in res.results]
    y_sample = np.stack([outs[0], outs[1]], axis=0)
    yp = []
    for c in range(4):
        yp.append(outs[2 + c][:4096])
        yp.append(outs[2 + c][4096:])
    y_prompt = np.stack(yp, axis=0)
    return (y_prompt, y_sample)
```

```python
from contextlib import ExitStack
import numpy as np
import concourse.bass as bass
import concourse.mybir as mybir
from concourse.bass_utils import run_bass_kernel_spmd

F32 = mybir.dt.float32
BF16 = mybir.dt.bfloat16
AF = mybir.ActivationFunctionType
ALU = mybir.AluOpType
AX = mybir.AxisListType

ENGS = ("pe", "act", "dve", "pool", "sp")
NOSYNC = set()


class Op:
    __slots__ = ("eng", "emit", "deps", "dma", "semkey", "tok", "sig")

    def __init__(self, eng, emit, dma, semkey):
        self.eng = eng
        self.emit = emit
        self.deps = []
        self.dma = dma
        self.semkey = semkey
        self.tok = None
        self.sig = False


class Prog:
    def __init__(self, nc, same_engine_sync=True):
        self.nc = nc
        self.q = {e: [] for e in ENGS}
        self.last_w = {}
        self.readers = {}
        self.same_engine_sync = same_engine_sync

    def add(self, eng, emit, reads=(), writes=(), dma=False, semkey=None):
        op = Op(eng, emit, dma, semkey)
        self.count = getattr(self, "count", 0) + 1
        if self.count > getattr(self, "limit", 10 ** 9):
            return op
        deps = {}
        for k in reads:
            w = self.last_w.get(k)
            if w is not None:
                deps[id(w)] = w
            if isinstance(k, str) and k.startswith("ps"):
                for r in self.readers.get(k, ()):
                    if r.eng != eng:
                        deps[id(r)] = r
        for k in writes:
            w = self.last_w.get(k)
            if w is not None:
                deps[id(w)] = w
            for r in self.readers.get(k, ()):
                deps[id(r)] = r
        op.deps = list(deps.values())
        for k in writes:
            self.last_w[k] = op
            self.readers[k] = []
        for k in reads:
            self.readers.setdefault(k, []).append(op)
        self.q[eng].append(op)
        return op

    def pe(self, emit, reads=(), writes=()):
        return self.add("pe", emit, reads, writes)

    def act(self, emit, reads=(), writes=()):
        return self.add("act", emit, reads, writes)

    def dve(self, emit, reads=(), writes=()):
        return self.add("dve", emit, reads, writes)

    def pool(self, emit, reads=(), writes=()):
        return self.add("pool", emit, reads, writes)

    def ew(self, eng, emit, reads=(), writes=()):
        return self.add(eng, emit, reads, writes)

    def dma(self, emit, reads=(), writes=(), semkey=None, queue="sp"):
        return self.add(queue, emit, reads, writes, dma=True, semkey=semkey)

    def _skip(self, d, op):
        if d.dma and op.dma and isinstance(d.semkey, str) and d.semkey.startswith("setup") and d.semkey == op.semkey:
            return True
        return (not d.dma) and (not op.dma) and d.eng == op.eng and (d.eng == "pe" or d.eng in NOSYNC or not self.same_engine_sync)

    def finalize(self, stack):
        nc = self.nc
        for e in ENGS:
            for op in self.q[e]:
                for d in op.deps:
                    if d.dma or self._skip(d, op):
                        continue
                    d.sig = True
        eng_sems = {e: [stack.enter_context(nc.semaphore("s_%s0" % e))] for e in ENGS}
        dma_sems = {}
        dma_cnt = {}
        LIM = 30000
        for e in ENGS:
            cnt = 0
            for op in self.q[e]:
                if op.dma:
                    if op.semkey not in dma_sems:
                        dma_sems[op.semkey] = stack.enter_context(nc.semaphore("d%d" % len(dma_sems)))
                        dma_cnt[op.semkey] = 0
                    dma_cnt[op.semkey] += 16
                    op.tok = (dma_sems[op.semkey], dma_cnt[op.semkey])
                elif op.sig:
                    if cnt >= LIM:
                        eng_sems[e].append(stack.enter_context(nc.semaphore("s_%s%d" % (e, len(eng_sems[e])))))
                        cnt = 0
                    cnt += 1
                    op.tok = (eng_sems[e][-1], cnt)
        assert max([0] + list(dma_cnt.values())) < 60000, "dma sem overflow"
        for e in ENGS:
            for op in self.q[e]:
                if op.dma and isinstance(op.semkey, str) and op.semkey.startswith("setup"):
                    op.tok = (dma_sems[op.semkey], dma_cnt[op.semkey])
        self.dma_sems = dma_sems
        self.dma_cnt = dma_cnt
        block = stack.enter_context(nc.Block())
        prog = self

        def run_queue(e, engine):
            known = {}
            for op in prog.q[e]:
                need = {}
                for d in op.deps:
                    if prog._skip(d, op):
                        continue
                    sem, val = d.tok
                    key = sem.num
                    if known.get(key, 0) >= val:
                        continue
                    if key not in need or need[key][1] < val:
                        need[key] = (sem, val)
                for key, (sem, val) in need.items():
                    engine.wait_ge(sem, val)
                    known[key] = val
                ins = op.emit(engine)
                if op.dma:
                    ins.then_inc(op.tok[0], 16)
                elif op.sig:
                    ins.then_inc(op.tok[0], 1)

        @block.tensor
        def _(eng):
            run_queue("pe", eng)

        @block.scalar
        def _(eng):
            run_queue("act", eng)

        @block.vector
        def _(eng):
            run_queue("dve", eng)

        @block.gpsimd
        def _(eng):
            run_queue("pool", eng)

        @block.sync
        def _(eng):
            run_queue("sp", eng)
            for key, sem in prog.dma_sems.items():
                eng.wait_ge(sem, prog.dma_cnt[key])


def MM(out, lhsT, rhs, start=True, stop=True):
    return lambda e: e.matmul(out, lhsT=lhsT, rhs=rhs, start=start, stop=stop)


def TR(out, in_, ident):
    return lambda e: e.transpose(out=out, in_=in_, identity=ident)


def ACTF(out, in_, func, bias=None, scale=1.0, accum_out=None):
    def f(e):
        kw = {}
        if bias is not None:
            kw["bias"] = bias
        if accum_out is not None:
            kw["accum_out"] = accum_out
        return e.activation(out=out, in_=in_, func=func, scale=scale, **kw)
    return f


def TT(out, in0, in1, op):
    return lambda e: e.tensor_tensor(out=out, in0=in0, in1=in1, op=op)


def TS(out, in0, s1, s2=None, op0=ALU.mult, op1=None):
    if op1 is None:
        return lambda e: e.tensor_scalar(out=out, in0=in0, scalar1=s1, scalar2=None, op0=op0)
    return lambda e: e.tensor_scalar(out=out, in0=in0, scalar1=s1, scalar2=s2, op0=op0, op1=op1)


def STT(out, in0, scalar, in1, op0, op1):
    return lambda e: e.scalar_tensor_tensor(out=out, in0=in0, scalar=scalar, in1=in1, op0=op0, op1=op1)


def CP(out, in_):
    return lambda e: e.tensor_copy(out=out, in_=in_)


def MS(ap, val):
    return lambda e: e.memset(ap, val)


def RECIP(out, in_):
    return lambda e: e.reciprocal(out=out, in_=in_)


def DMA(out, in_):
    return lambda e: e.dma_start(out=out, in_=in_)


D = 1024
NIN = 6672
GATE_OFF = 4608
EPS = 1e-6
KAPPA = 0.25 * (128.0 ** -0.5)
CH_OFF = [0, 512, 1024, 1536, 2048, 2560, 3072, 3584, 4096, 4624, 5136, 5648, 6160]
C_NAQ, C_NAK, C_NAV, C_NAZ, C_MLQ, C_MLK, C_MLV, C_MLZ, C_MLO, C_GA0, C_GA1, C_GB0, C_GB1 = range(13)
C_WA, C_WB, C_WO0, C_WO1 = 13, 14, 15, 16
FM_B = {C_NAQ: 0, C_NAK: 4, C_MLQ: 8, C_MLK: 12, C_GA0: 16, C_GA1: 20, C_GB0: 24, C_GB1: 28}
FM_CW = 32
FM_G = 56
FM_N = 64
TB = {C_NAV: 0, C_NAZ: 512, C_MLV: 1024, C_MLZ: 1536, C_MLO: 2048}


LIMIT = [10 ** 9]


def build(R, dbg=False, stop=99):
    U = 2
    NR = U * R
    NTOK = NR * 64
    NT = NR // 8
    NS = NR // 2
    TPU = R // 8
    RING_T = 3
    RROWS = RING_T * 8
    nc = bass.Bass("TRN2", target_bir_lowering=False)

    def din(name, shape):
        return nc.dram_tensor(name, shape, F32, kind="ExternalInput").ap()

    xu = din("xu", [NTOK, D])
    meta = din("meta", [16, D])
    g_pre = din("g_pre", [1, D])
    w_in = din("w_in", [D, NIN])
    b_in = din("b_in", [1, NIN])
    tz = din("tz", [64, 8, 15, 64])
    cmask = din("cmask", [128, 64])
    conv_w = din("conv_w", [3, D])
    head_g = din("head_g", [1, 512])
    w_a = din("w_a", [512, D])
    w_b = din("w_b", [512, D])
    w_out = din("w_out", [D, D])
    g_post = din("g_post", [1, D])
    flags = din("flags", [128, 4])
    y = nc.dram_tensor("y", [NTOK, D], F32, kind="ExternalOutput").ap()
    wq = nc.dram_tensor("wq", [17, 128, 8, 512], BF16, kind="Internal").ap()
    cbs = nc.dram_tensor("cbs", [NS, 128, 4 * 129], BF16, kind="Internal").ap()
    dbg_out = {}
    if dbg:
        dbg_out["d_h"] = nc.dram_tensor("d_h", [NTOK, 512], F32, kind="ExternalOutput").ap()
        dbg_out["d_na"] = nc.dram_tensor("d_na", [NTOK, 512], F32, kind="ExternalOutput").ap()

    st = ExitStack()
    with st:
        def sb(name, shape, dt=F32):
            return st.enter_context(nc.sbuf_tensor(name, shape, dt))

        P = Prog(nc)
        P.limit = LIMIT[0]
        psb = [st.enter_context(nc.psum_tensor("ps%d" % i, [128, 512], F32)) for i in range(4)]
        psS = [st.enter_context(nc.psum_tensor("psS%d" % i, [128, 1024], F32)) for i in range(2)]
        psb += [psS[0][:, 0:512], psS[0][:, 512:1024], psS[1][:, 0:512], psS[1][:, 512:1024]]
        PK = ["ps%d" % i for i in range(8)]

        ident = sb("ident", [128, 128], BF16)
        identf = sb("identf", [128, 128])
        maskf = sb("maskf", [128, 128])
        maskb = sb("maskb", [128, 128])
        fl = sb("fl", [128, 4])
        fm = sb("fm", [128, FM_N])
        fmh = sb("fmh", [128, FM_N])
        cst = sb("cst", [128, 2])
        gbias = sb("gbias", [8, 2])
        wg = sb("wg", [128, 8, 16], BF16)
        etab = sb("etab", [128, 8, 16, 64], BF16)
        bias_bc = sb("bias_bc", [128, 2560])
        gpost_bc = sb("gpost_bc", [128, D])
        hg_bc = sb("hg_bc", [128, 512])
        SC = sb("SC", [128, NS + U, 24])
        EG = sb("EG", [128, NS + U, 8])
        kmT = sb("kmT", [128, 4, 16], BF16)
        vmp = sb("vmp", [128, 8, 65], BF16)
        prem = sb("prem", [128, 8])
        premk = sb("premk", [128, 4, 16])
        vmeta = sb("vmeta", [128, 4, 129], BF16)
        kmetaT = sb("kmetaT", [128, 4, 128], BF16)
        firstpre = sb("firstpre", [128, U, 8, 1])
        Cmeta = sb("Cmeta", [128, U, 4, 129])
        Cb = sb("Cb", [128, 4, 129])
        Cf = sb("Cf", [128, 4, 129])
        Cfb = sb("Cfb", [128, 4, 129], BF16)
        ws = [sb("ws%d" % i, [128, 8, 512], BF16) for i in range(3)]
        xs = [sb("xs%d" % i, [128, D]) for i in range(2)]
        xnb = [sb("xnb%d" % i, [128, D], BF16) for i in range(2)]
        xnT = [sb("xnT%d" % i, [128, 8, 512], BF16) for i in range(2)]
        st1 = sb("st1", [128, 16])
        st2 = sb("st2", [128, 32])
        KT = sb("KT", [128, 4, RROWS * 64], BF16)
        VP = sb("VP", [128, RROWS // 2, 8, 65], BF16)
        zs = sb("zs", [128, 4, 512], BF16)
        oG = sb("oG", [128, 4, 512], BF16)
        pre = [sb("pre0", [128, 514])] * 2
        cvt = [sb("cvt0", [128, 512])] * 2
        qkT = sb("qkT", [128, 8, 512], BF16)
        carry = sb("carry", [128, 8, 2])
        halo = sb("halo", [128, 8, 2])
        hadj = sb("hadj", [128, 8, 2])
        vml = sb("vml", [128, 4, 4, 129], BF16)
        kk = sb("kk", [128, 4, 128], BF16)
        Sf = [sb("Sf%d" % i, [128, 128], BF16) for i in range(4)]
        Sb_ = [sb("Sb%d" % i, [128, 128], BF16) for i in range(4)]
        uv = [sb("uv%d" % i, [128, 129], BF16) for i in range(2)]
        cbl = [sb("cbl%d" % i, [128, 4, 129], BF16) for i in range(2)]
        cbst = [sb("cbst%d" % i, [128, 4, 129], BF16) for i in range(2)]
        hbuf = sb("hbuf", [128, 512])
        h2 = sb("h2", [128, 512])
        gs2 = sb("gs2", [8, 8])
        onesg = sb("onesg", [8, 128])
        ypT = sb("ypT", [128, 8, 512], BF16)
        gsc = ypT[0:8, :, :].rearrange("p a b -> p (a b)").bitcast(F32).rearrange("p (a b) -> p a b", a=4)
        nao = [hbuf, h2]
        HBK = [("hbuf", h) for h in range(4)]
        NAOK = [HBK, ["h2"]]

        cont = fl[:, 0:1]
        ncont = fl[:, 1:2]

        for (dst, so) in ((0, 0), (4, 8), (8, 4), (12, 12)):
            src = w_in[:, GATE_OFF + so:GATE_OFF + so + 4].rearrange("(kc p) j -> p kc j", p=128)
            P.dma(DMA(wg[:, :, dst:dst + 4], src), writes=["wg"], semkey="setup_w", queue="pool")
        for c in (C_NAK, C_NAV, C_MLQ, C_MLK, C_MLV, C_NAQ, C_NAZ, C_MLZ, C_MLO, C_GA0, C_GA1, C_GB0, C_GB1):
            src = w_in[:, CH_OFF[c]:CH_OFF[c] + 512].rearrange("(kc p) j -> p kc j", p=128)
            P.dma(DMA(wq[c], src), writes=[("wq", c)], semkey=("wqc", c), queue="pool")
        P.dma(DMA(wq[C_WA].rearrange("p a b -> p (a b)").rearrange("p (fc n) -> p fc n", fc=4),
                  w_a.rearrange("(fc p) n -> p fc n", p=128)), writes=[("wq", C_WA)], semkey=("wqc", C_WA), queue="pool")
        P.dma(DMA(wq[C_WB].rearrange("p a b -> p (a b)").rearrange("p (fc n) -> p fc n", fc=4),
                  w_b.rearrange("(fc p) n -> p fc n", p=128)), writes=[("wq", C_WB)], semkey=("wqc", C_WB), queue="pool")
        for hf in range(2):
            P.dma(DMA(wq[C_WO0 + hf], w_out[:, hf * 512:(hf + 1) * 512].rearrange("(fc p) n -> p fc n", p=128)),
                  writes=[("wq", C_WO0 + hf)], semkey=("wqc", C_WO0 + hf), queue="pool")

        P.dma(DMA(fl[:], flags), writes=["fl"], semkey="setup_c")
        P.dma(DMA(bias_bc[:, 0:1024], b_in[:, 1024:2048].partition_broadcast(128)), writes=["bias_bc"], semkey="setup_c")
        P.dma(DMA(bias_bc[:, 1024:2560], b_in[:, 3072:4608].partition_broadcast(128)), writes=["bias_bc"], semkey="setup_c")
        P.dma(DMA(gpost_bc[:], g_post.partition_broadcast(128)), writes=["gpost_bc"], semkey="setup_c")
        P.dma(DMA(hg_bc[:], head_g.partition_broadcast(128)), writes=["hg_bc"], semkey="setup_c")
        P.pool(MS(identf[:], 0.0), writes=["identf"])
        P.pool(lambda e: e.affine_select(out=identf[:], in_=identf[:], pattern=[[-1, 128]], compare_op=ALU.not_equal,
                                         fill=1.0, base=0, channel_multiplier=1), reads=["identf"], writes=["identf"])
        P.dve(CP(ident[:], identf[:]), reads=["identf"], writes=["ident"])
        P.pool(MS(maskf[:], 1.0), writes=["maskf"])
        P.pool(lambda e: e.affine_select(out=maskf[:], in_=maskf[:], pattern=[[1, 128]], compare_op=ALU.is_ge,
                                         fill=0.0, base=0, channel_multiplier=-1), reads=["maskf"], writes=["maskf"])
        P.pool(MS(maskb[:], 1.0), writes=["maskb"])
        P.pool(lambda e: e.affine_select(out=maskb[:], in_=maskb[:], pattern=[[-1, 128]], compare_op=ALU.is_ge,
                                         fill=0.0, base=0, channel_multiplier=1), reads=["maskb"], writes=["maskb"])
        P.pool(MS(cst[:, 0:1], -0.5), writes=["cst"])
        P.pool(MS(onesg[:], 1.0), writes=["onesg"])
        P.pool(MS(Cb[:], 0.0), writes=["Cb"])
        P.pool(MS(VP[:, :, :, 64:65], 1.0), writes=[("VP", i) for i in range(RING_T)])
        P.pool(MS(vmp[:], 0.0), writes=["vmp"])
        P.pool(MS(vmp[0:16, :, 64:65], 1.0), reads=["vmp"], writes=["vmp"])
        P.pool(MS(vml[:, :, :, 128:129], 1.0), writes=[("vml", j) for j in range(4)])
        P.pool(MS(carry[:], 0.0), writes=["carry"])
        P.pool(MS(vmeta[:], 0.0), writes=["vmeta"])
        P.pool(MS(kmetaT[:], 0.0), writes=["kmetaT"])

        stg = ExitStack()
        rowst = stg.enter_context(nc.sbuf_tensor("rowst", [64, 128], F32))
        tzs = stg.enter_context(nc.sbuf_tensor("tzs", [128, 4, 16, 64], F32))
        cmk = stg.enter_context(nc.sbuf_tensor("cmk", [128, 64], F32))
        xnTm = stg.enter_context(nc.sbuf_tensor("xnTm", [128, 8, 128], BF16))
        P.pool(MS(xnTm[:], 0.0), writes=["xnTm"])
        if True:
            P.pool(MS(rowst[:], 0.0), writes=["rowst"])
            for c, col in FM_B.items():
                P.dma(DMA(rowst[col:col + 4, :], b_in[0, CH_OFF[c]:CH_OFF[c] + 512].rearrange("(c p) -> c p", p=128)),
                      writes=["rowst"], semkey="setup_c")
            for j in range(3):
                P.dma(DMA(rowst[FM_CW + 8 * j:FM_CW + 8 * j + 8, :], conv_w[j, :].rearrange("(c p) -> c p", p=128)),
                      writes=["rowst"], semkey="setup_c")
            P.dma(DMA(rowst[FM_G:FM_G + 8, :], g_pre[0, :].rearrange("(c p) -> c p", p=128)), writes=["rowst"], semkey="setup_c")
            P.pe(MM(psb[0][:, 0:FM_N], rowst[0:FM_N, :], identf[0:FM_N, 0:FM_N]), reads=["rowst", "identf"], writes=[PK[0]])
            P.dve(CP(fm[:], psb[0][:, 0:FM_N]), reads=[PK[0]], writes=["fm"])
            P.dve(TS(fmh[:], fm[:], 0.5), reads=["fm"], writes=["fmh"])
            P.dve(TS(fmh[:, 0:4], fm[:, 0:4], 0.125), reads=["fm", "fmh"], writes=["fmh"])
            P.dve(TT(fmh[:, 56:64], fm[:, FM_CW:FM_CW + 8], fm[:, FM_CW + 8:FM_CW + 16], ALU.add), reads=["fm", "fmh"], writes=["fmh"])
            P.dve(TT(fmh[:, 56:64], fmh[:, 56:64], fm[:, FM_CW + 16:FM_CW + 24], ALU.add), reads=["fm", "fmh"], writes=["fmh"])
            P.dve(TT(fmh[:, 56:64], fmh[:, 56:64], fm[:, 8:16], ALU.mult), reads=["fm", "fmh"], writes=["fmh"])
            for (dst, so, col) in ((0, 0, 0), (4, 8, 0), (0, 4, 1), (4, 12, 1)):
                P.dma(DMA(gbias[dst:dst + 4, col:col + 1], b_in[0, GATE_OFF + so:GATE_OFF + so + 4].rearrange("(p o) -> p o", o=1)),
                      writes=["gbias"], semkey="setup_c")
            P.dve(TS(gbias[:, 1:2], gbias[:, 1:2], -1.0), reads=["gbias"], writes=["gbias"])
            P.dve(TS(hg_bc[:], hg_bc[:], 0.5), reads=["hg_bc"], writes=["hg_bc"])
            P.dma(DMA(cmk[:], cmask), writes=["cmk"], semkey="setup_c")
            for hq in range(2):
                sk = "setup_c" if hq == 0 else "setup_c2"
                hs = slice(4 * hq, 4 * hq + 4)
                P.pool(MS(tzs[:, :, 14:16, :], 0.0), writes=["tzs"])
                P.dma(DMA(tzs[0:64, :, 0:14, :], tz[:, hs, 0:14, :]), writes=["tzs"], semkey=sk)
                P.dma(DMA(tzs[64:128, :, 0:14, :], tz[:, hs, 1:15, :]), writes=["tzs"], semkey=sk)
                P.dma(DMA(tzs[64:128, :, 14, :], tz[:, hs, 3, :]), writes=["tzs"], semkey=sk)
                P.dma(DMA(tzs[0:64, :, 15, :], tz[:, hs, 10, :]), writes=["tzs"], semkey=sk)
                for h4 in range(4):
                    h = 4 * hq + h4
                    P.act(ACTF(tzs[:, h4], tzs[:, h4], AF.Exp), reads=["tzs"], writes=["tzs"])
                    P.dve(TT(etab[:, h], tzs[:, h4], cmk[:, :].unsqueeze(1).to_broadcast([128, 16, 64]), ALU.mult),
                          reads=["tzs", "cmk"], writes=["etab"])
            P.dve(MS(etab[0:64, :, 14, :], 0.0), reads=["etab"], writes=["etab"])
            P.dve(MS(etab[64:128, :, 15, :], 0.0), reads=["etab"], writes=["etab"])
            P.dve(CP(st1[:, 0:1], etab[:, 7, 13, 0:1]), reads=["etab", "fm", "fmh"], writes=["st1a"])

        wstate = {"i": 0}

        def wload(c):
            i = wstate["i"]
            wstate["i"] += 1
            slot = i % 3
            key = ("ws", slot)
            P.dma(DMA(ws[slot][:], wq[c]), reads=[("wq", c)], writes=[key], semkey=("ws", slot))
            return ws[slot], key

        ln_i = {"i": 0}

        def pTv():
            return psb[3][:].bitcast(BF16).rearrange("p (a b) -> p a b", a=8)

        def ln_A(src_ap, meta_rows=None):
            i = ln_i["i"]
            ln_i["i"] += 1
            b = i % 2
            xk, nk = ("xs", b), ("xnb", b)
            if meta_rows is None:
                npart = 128
                P.dma(DMA(xs[b][:], src_ap), writes=[xk], semkey=("xs", b))
            else:
                npart = meta_rows
                P.dma(DMA(xs[b][0:npart, :], src_ap), writes=[xk], semkey=("xs", b))
            c0 = 4 * b
            ka, kb, kc_ = ("ln_a", b), ("ln_b", b), ("ln_c", b)
            P.act(ACTF(xnb[b][0:npart, :], xs[b][0:npart, :], AF.Square, accum_out=st2[0:npart, c0:c0 + 1]), reads=[xk], writes=[nk, ka])
            P.dve(TS(st2[0:npart, c0 + 1:c0 + 2], st2[0:npart, c0:c0 + 1], 1.0 / D, EPS, ALU.mult, ALU.add), reads=[ka], writes=[kb])
            P.pool(TT(st2[0:npart, c0 + 2:c0 + 3], st2[0:npart, c0 + 1:c0 + 2], cst[0:npart, 0:1], ALU.pow), reads=[kb, "cst"], writes=[kc_])
            P.dve(TS(xnb[b][0:npart, :], xs[b][0:npart, :], st2[0:npart, c0 + 2:c0 + 3]), reads=[xk, kc_, nk], writes=[nk])
            return (b, npart)

        def ln_B(tok, dst_ap, dkey):
            b, npart = tok
            nk = ("xnb", b)
            pT = pTv()
            for kc in range(8):
                P.pe(TR(pT[:, kc, 0:npart], xnb[b][0:npart, kc * 128:(kc + 1) * 128], ident[0:npart, 0:npart]),
                     reads=[nk, "ident"], writes=[PK[3]])
            P.dve(TT(dst_ap[:, :, 0:npart], pT[:, :, 0:npart],
                     fm[:, FM_G:FM_G + 8].unsqueeze(2).to_broadcast([128, 8, npart]), ALU.mult),
                  reads=[PK[3], "fm"], writes=[dkey])

        def load_norm_T(src_ap, nsub, dst, dkey, meta_rows=None):
            for j in range(nsub):
                if meta_rows is None:
                    tok = ln_A(src_ap[j * 128:(j + 1) * 128, :])
                else:
                    tok = ln_A(src_ap, meta_rows)
                ln_B(tok, dst[:, :, j * 128:(j + 1) * 128], dkey)

        pbank = {"i": 0}

        def next_bank():
            b = pbank["i"] % 3
            pbank["i"] += 1
            return b

        def proj_F(xT, xkey, ntok, c, handler):
            wsl, wkey = wload(c)
            for fc in range(4):
                b = next_bank()
                for kc in range(8):
                    P.pe(MM(psb[b][:, 0:ntok], wsl[:, kc, fc * 128:(fc + 1) * 128], xT[:, kc, 0:ntok], kc == 0, kc == 7),
                         reads=[wkey, xkey], writes=[PK[b]])
                handler(fc, psb[b][:, 0:ntok], PK[b])

        def proj_T(xT, xkey, nsub, c, handler):
            wsl, wkey = wload(c)
            for j in range(nsub):
                b = next_bank()
                for kc in range(8):
                    P.pe(MM(psb[b][:, :], xT[:, kc, j * 128:(j + 1) * 128], wsl[:, kc, :], kc == 0, kc == 7),
                         reads=[wkey, xkey], writes=[PK[b]])
                handler(j, psb[b][:, :], PK[b])

        def gates_A(xT, xkey, ntok, is_meta=False):
            nch = ntok // 128
            bI = next_bank()
            for kc in range(8):
                P.pe(MM(psb[bI][0:8, 0:ntok], wg[:, kc, 0:8], xT[:, kc, 0:ntok], kc == 0, kc == 7), reads=["wg", xkey], writes=[PK[bI]])
            R0 = gsc[:, 0, 0:ntok]
            R1 = gsc[:, 1, 0:ntok]
            R2 = gsc[:, 2, 0:ntok]
            R3 = gsc[:, 3, 0:ntok]
            K0, K1, K2, K3 = "g_r0", "g_r1", "g_r2", "g_r3"
            P.act(ACTF(R0, psb[bI][0:8, 0:ntok], AF.Exp, bias=gbias[:, 0:1]), reads=[PK[bI], "gbias"], writes=[K0])
            bF = next_bank()
            for kc in range(8):
                P.pe(MM(psb[bF][0:8, 0:ntok], wg[:, kc, 8:16], xT[:, kc, 0:ntok], kc == 0, kc == 7), reads=["wg", xkey], writes=[PK[bF]])
            P.act(ACTF(R1, psb[bF][0:8, 0:ntok], AF.Exp, bias=gbias[:, 1:2], scale=-1.0), reads=[PK[bF], "gbias"], writes=[K1])
            if is_meta:
                P.dve(MS(gsc[:, 1, 16:ntok], 0.0), reads=[K1], writes=[K1])
                P.dve(MS(gsc[:, 0, 16:ntok], 0.0), reads=[K0], writes=[K0])
            P.dve(TS(R1, R1, 1.0, None, ALU.add), reads=[K1], writes=[K1])
            for ci in range(nch):
                sl = slice(ci * 128, (ci + 1) * 128)
                P.dve(lambda e, sl=sl: e.tensor_tensor_scan(out=gsc[:, 2, sl], data0=gsc[:, 1, sl], data1=onesg[:, :], initial=1.0,
                                                            op0=ALU.mult, op1=ALU.mult), reads=[K1, "onesg"], writes=[K2])
            P.dve(RECIP(R3, R2), reads=[K2], writes=[K3])
            for ci in range(nch):
                last = ci * 128 + 127
                P.dve(CP(gs2[:, ci:ci + 1], gsc[:, 3, last:last + 1]), reads=[K3], writes=["g_eg"])
            for ci in range(nch):
                sl = slice(ci * 128, (ci + 1) * 128)
                last = ci * 128 + 127
                P.dve(STT(gsc[:, 1, sl], gsc[:, 3, sl], gsc[:, 2, last:last + 1], gsc[:, 1, sl], ALU.mult, ALU.mult),
                      reads=[K3, K2, K1], writes=[K1])
            P.dve(TT(R3, R2, R1, ALU.subtract), reads=[K2, K1, K3, "g_eg"], writes=[K3])
            P.dve(STT(R2, R3, fl[0:8, 2:3], R1, ALU.mult, ALU.add), reads=[K3, K1, "fl", K2], writes=[K2])
            P.dve(STT(R0, R0, KAPPA, R2, ALU.mult, ALU.mult), reads=[K0, K2], writes=[K0])
            for ci in range(nch):
                sl = slice(ci * 128, (ci + 1) * 128)
                P.dve(TS(gsc[:, 3, sl], gsc[:, 0, sl], gs2[:, ci:ci + 1]), reads=[K0, "g_eg", K3], writes=[K3])
            return nch, (K0, K3, K2)

        def gates_B(tokg, chunks):
            nch, (K0, K3, K2) = tokg
            bS = next_bank()
            for ci in range(nch):
                sl = slice(ci * 128, (ci + 1) * 128)
                for k, (row, key) in enumerate(((0, K0), (3, K3), (2, K2))):
                    P.pe(MM(psb[bS][:, ci * 32 + k * 8:ci * 32 + k * 8 + 8], gsc[:, row, sl], identf[0:8, 0:8]),
                         reads=[key, "identf"], writes=[PK[bS]])
                P.pe(MM(psb[bS][:, ci * 32 + 24:ci * 32 + 32], gs2[:, ci:ci + 1].to_broadcast([8, 128]), identf[0:8, 0:8]),
                     reads=["g_eg", "identf"], writes=[PK[bS]])
            for ci in range(nch):
                slot = chunks[ci]
                P.dve(CP(SC[:, slot, :], psb[bS][:, ci * 32:ci * 32 + 24]), reads=[PK[bS]], writes=[("SC", slot)])
                P.dve(CP(EG[:, slot, :], psb[bS][:, ci * 32 + 24:ci * 32 + 32]), reads=[PK[bS]], writes=[("EG", slot)])

        cv_i = {"i": 0}

        def conv_taps(pr, pk, cv, ck, n, fcg):
            w0 = fm[:, FM_CW + fcg:FM_CW + fcg + 1]
            w1 = fm[:, FM_CW + 8 + fcg:FM_CW + 8 + fcg + 1]
            w2 = fm[:, FM_CW + 16 + fcg:FM_CW + 16 + fcg + 1]
            P.dve(TS(cv[:, 0:n], pr[:, 1:1 + n], w1), reads=[pk, "fm"], writes=[ck])
            P.dve(STT(cv[:, 0:n], pr[:, 0:n], w0, cv[:, 0:n], ALU.mult, ALU.add), reads=[pk, ck, "fm"], writes=[ck])
            P.dve(STT(cv[:, 0:n], pr[:, 2:2 + n], w2, cv[:, 0:n], ALU.mult, ALU.add), reads=[pk, ck, "fm"], writes=[ck])
            P.act(ACTF(pr[:, 1:1 + n], cv[:, 0:n], AF.Tanh, scale=0.5), reads=[ck], writes=[pk])

        def conv_silu(fcg, ps_ap, pkey, ntok, bias_col, lh_ap, rh_ap, dst_ap, dst_key, save_first=None, save_last=None):
            assert ntok == 512
            i = cv_i["i"] % 2
            cv_i["i"] += 1
            if i == 0:
                cv, ck, th, tk = cvt[0][:, 0:512], ("cvt", 0), gA, "gA"
            else:
                cv, ck, th, tk = pre[0][:, 0:512], ("pre", 0), gB, "gB"
            w0 = fm[:, FM_CW + fcg:FM_CW + fcg + 1]
            w1 = fm[:, FM_CW + 8 + fcg:FM_CW + 8 + fcg + 1]
            w2 = fm[:, FM_CW + 16 + fcg:FM_CW + 16 + fcg + 1]
            beta = fmh[:, 56 + fcg:57 + fcg]
            P.act(ACTF(cv, ps_ap, AF.Identity, bias=beta, scale=w1), reads=[pkey, "fm", "fmh"], writes=[ck])
            P.dve(STT(cv[:, 1:512], ps_ap[:, 0:511], w0, cv[:, 1:512], ALU.mult, ALU.add), reads=[pkey, ck, "fm"], writes=[ck])
            P.dve(STT(cv[:, 0:511], ps_ap[:, 1:512], w2, cv[:, 0:511], ALU.mult, ALU.add), reads=[pkey, ck, "fm"], writes=[ck])
            P.dve(STT(cv[:, 0:1], hadj[:, fcg, 0:1], w0, cv[:, 0:1], ALU.mult, ALU.add), reads=["hadj", ck, "fm"], writes=[ck])
            P.dve(STT(cv[:, 511:512], hadj[:, fcg, 1:2], w2, cv[:, 511:512], ALU.mult, ALU.add), reads=["hadj", ck, "fm"], writes=[ck])
            if save_first is not None:
                P.dve(TS(save_first, ps_ap[:, 0:1], fm[:, bias_col:bias_col + 1], None, ALU.add), reads=[pkey, "fm"], writes=["carry"])
            if save_last is not None:
                P.dve(TS(save_last, ps_ap[:, 511:512], fm[:, bias_col:bias_col + 1], None, ALU.add), reads=[pkey, "fm"], writes=["carry"])
            P.act(ACTF(th[:, :], cv, AF.Tanh, scale=0.5), reads=[ck], writes=[tk])
            P.dve(STT(dst_ap, th[:, :], 1.0, cv, ALU.add, ALU.mult), reads=[tk, ck], writes=[dst_key])

        def state_prep(kT_ap, kkeys, v_ap, vkey, slot, dirn, banks=None):
            pT = pTv()
            for h in range(4):
                P.pe(TR(pT[:, h, :], kT_ap[:, h, :], ident[:, :]), reads=list(kkeys) + ["ident"], writes=[PK[3]])
            P.act(ACTF(kk[:, :, :], pT[:, 0:4, :], AF.Copy), reads=[PK[3]], writes=["kk"])
            if banks is None:
                banks = [next_bank(), next_bank()]
            outs = []
            for h in range(4):
                u = uv[h % 2]
                uk = ("uv", h % 2)
                col = 8 + dirn * 4 + h
                P.act(ACTF(u[:, :], v_ap[:, h, :], AF.Copy, scale=SC[:, slot, col:col + 1]), reads=[vkey, ("SC", slot)], writes=[uk])
                bank = banks[h // 2]
                cols = slice((h % 2) * 129, (h % 2) * 129 + 129)
                P.pe(MM(psb[bank][:, cols], kk[:, h, :], u[:, :]), reads=["kk", uk], writes=[PK[bank]])
                outs.append((bank, cols))
            return outs

        def state_apply(Cst, ckey, outs, slot, dirn):
            for h in range(4):
                bank, cols = outs[h]
                P.dve(STT(Cst[:, h, :], Cst[:, h, :], EG[:, slot, dirn * 4 + h:dirn * 4 + h + 1], psb[bank][:, cols], ALU.mult, ALU.add),
                      reads=[ckey, ("EG", slot), PK[bank]], writes=[ckey])

        def state_update(Cst, ckey, kT_ap, kkeys, v_ap, vkey, slot, dirn, bank):
            outs = state_prep(kT_ap, kkeys, v_ap, vkey, slot, dirn)
            state_apply(Cst, ckey, outs, slot, dirn)

        if stop <= 0:
            P.finalize(st)
            return nc
        load_norm_T(meta, 1, xnTm, "xnTm", meta_rows=16)

        def h_kmeta(fc, ps_ap, pkey):
            P.act(ACTF(kmT[:, fc, :], ps_ap[:, 0:16], AF.Identity, bias=fm[:, FM_B[C_NAK] + fc:FM_B[C_NAK] + fc + 1]),
                  reads=[pkey, "fm"], writes=["kmT"])
        proj_F(xnTm, "xnTm", 128, C_NAK, h_kmeta)

        def h_vmeta(j, ps_ap, pkey):
            P.dve(TT(vmp[0:16, :, 0:64], ps_ap[0:16, :].rearrange("p (h d) -> p h d", h=8),
                     bias_bc[0:16, TB[C_NAV]:TB[C_NAV] + 512].rearrange("p (h d) -> p h d", h=8), ALU.add),
                  reads=[pkey, "bias_bc"], writes=["vmp"])
        proj_T(xnTm, "xnTm", 1, C_NAV, h_vmeta)

        def h_qmeta(fc, ps_ap, pkey):
            P.act(ACTF(prem[:, fc:fc + 1], ps_ap[:, 15:16], AF.Identity, bias=fm[:, FM_B[C_MLQ] + fc:FM_B[C_MLQ] + fc + 1]),
                  reads=[pkey, "fm"], writes=["prem"])
        proj_F(xnTm, "xnTm", 128, C_MLQ, h_qmeta)

        def h_kmeta2(fc, ps_ap, pkey):
            P.act(ACTF(premk[:, fc, :], ps_ap[:, 0:16], AF.Identity, bias=fm[:, FM_B[C_MLK] + fc:FM_B[C_MLK] + fc + 1]),
                  reads=[pkey, "fm"], writes=["premk"])
            P.pool(CP(prem[:, 4 + fc:5 + fc], premk[:, fc, 15:16]), reads=["premk"], writes=["prem"])
        proj_F(xnTm, "xnTm", 128, C_MLK, h_kmeta2)

        def h_vmeta2(j, ps_ap, pkey):
            P.dve(TT(vmeta[0:16, :, 0:128], ps_ap[0:16, :].rearrange("p (h d) -> p h d", h=4),
                     bias_bc[0:16, TB[C_MLV]:TB[C_MLV] + 512].rearrange("p (h d) -> p h d", h=4), ALU.add),
                  reads=[pkey, "bias_bc"], writes=["vmeta"])
            P.dve(MS(vmeta[0:16, :, 128:129], 1.0), reads=["vmeta"], writes=["vmeta"])
        proj_T(xnTm, "xnTm", 1, C_MLV, h_vmeta2)
        gates_B(gates_A(xnTm, "xnTm", 128, is_meta=True), [NS])
        P.dve(CP(SC[:, NS + 1, :], SC[:, NS, :]), reads=[("SC", NS)], writes=[("SC", NS + 1)])
        P.dve(CP(EG[:, NS + 1, :], EG[:, NS, :]), reads=[("EG", NS)], writes=[("EG", NS + 1)])

        if stop <= 1:
            P.finalize(st)
            return nc
        stg.close()
        ftmp = sb("ftmp", [128, 512])
        Praw = [sb("Praw%d" % i, [128, 640], BF16) for i in range(2)]
        Pn = [sb("Pn%d" % i, [128, 640], BF16) for i in range(2)]
        Pm = sb("Pm", [128, 2, 512], BF16)
        gA = sb("gA", [128, 512], BF16)
        gB = sb("gB", [128, 512], BF16)
        gAT = sb("gAT", [128, 4, 512], BF16)
        gBT = sb("gBT", [128, 4, 512], BF16)
        tgA = sb("tgA", [128, 512])
        tgB = sb("tgB", [128, 512])
        osb = [sb("osb%d" % i, [128, D]) for i in range(2)]
        X1K = ["ftmp", ("Praw", 0), ("Praw", 1), ("Pn", 0), ("Pn", 1), ("Pm", 0), ("Pm", 1), "gA", "gB", "gAT", "gBT", "tgA", "tgB",
               ("osb", 0), ("osb", 1)]
        for eng in ("dve", "act", "pool"):
            P.ew(eng, MS(st1[:, 5:6] if eng == "dve" else st1[:, 6:7], 0.0) if eng != "act" else ACTF(st1[:, 7:8], fl[:, 0:1], AF.Copy),
                 reads=["fl", "etab", "fm", "fmh", "gbias", "kmT", "vmp", "prem", "premk", "vmeta", ("SC", NS), ("EG", NS)],
                 writes=X1K + ["xnTm", "tzs", "rowst", "cmk"])

        P.pool(MS(Pm[:], 0.0), reads=[("Pm", 0), ("Pm", 1)], writes=[("Pm", 0), ("Pm", 1)])

        def meta_state(u):
            P.pool(MS(Cmeta[:, u], 0.0), writes=[("Cmeta", u)])
            for fc in range(4):
                i = cv_i["i"] % 2
                cv_i["i"] += 1
                pk, ck = ("pre", 0), ("cvt", 0)
                pr, cv = pre[i], cvt[i]
                P.pool(MS(pr[:, 0:1], 0.0), writes=[pk])
                P.pool(CP(pr[:, 1:17], premk[:, fc, :]), reads=["premk", pk], writes=[pk])
                P.pool(CP(pr[:, 17:18], firstpre[:, u, 4 + fc, :]), reads=[("firstpre", u), pk], writes=[pk])
                conv_taps(pr, pk, cv, ck, 16, 4 + fc)
                P.dve(STT(kmetaT[:, fc, 0:16], pr[:, 1:17], 1.0, cv[:, 0:16], ALU.add, ALU.mult), reads=[pk, ck], writes=["kmetaT"])
            state_update(Cmeta[:, u], ("Cmeta", u), kmetaT, ["kmetaT"], vmeta, "vmeta", NS + u, 0, 4)

        def tile_src(T):
            return xu[T * 512:(T + 1) * 512, :]

        def small_cols(wsl, wkey, xT, xkey, col):
            hb = next_bank()
            for fc in range(4):
                for kc in range(8):
                    P.pe(MM(psb[hb][:, fc:fc + 1], wsl[:, kc, fc * 128:(fc + 1) * 128], xT[:, kc, col:col + 1], kc == 0, kc == 7),
                         reads=[wkey, xkey], writes=[PK[hb]])
            return hb

        vml2 = KT[:, 0:2, :].rearrange("p a b -> p (a b)")[:, 0:2064].rearrange("p (j h n) -> p j h n", j=4, h=4)
        P.pool(MS(vml2[:, :, :, 128:129], 1.0), writes=[("vml2", j) for j in range(4)])

        def p1_bufs(T):
            if T % 2 == 0:
                return 4, vml, "vml"
            return 0, vml2, "vml2"

        def p1_P(T):
            xt, xkey = xnT[T % 2], ("xnT", T % 2)
            u_of = T // TPU
            first_tile_of_unit = (T % TPU == 0)
            last_tile_of_unit = (T % TPU == TPU - 1)
            fcb, vb, vname = p1_bufs(T)
            wsl, wkey = wload(C_MLK)
            bcol = FM_B[C_MLK]
            if T > 0:
                hb = small_cols(wsl, wkey, xnT[(T - 1) % 2], ("xnT", (T - 1) % 2), 511)
                P.dve(TT(halo[:, 4:8, 0], psb[hb][:, 0:4], fm[:, bcol:bcol + 4], ALU.add), reads=[PK[hb], "fm"], writes=["halo"])
            if first_tile_of_unit:
                if T == 0:
                    P.dve(CP(halo[:, 4:8, 0], prem[:, 4:8]), reads=["prem"], writes=["halo"])
                else:
                    P.dve(TS(halo[:, 4:8, 0], halo[:, 4:8, 0], cont), reads=["halo", "fl"], writes=["halo"])
                    P.dve(STT(halo[:, 4:8, 0], prem[:, 4:8], ncont, halo[:, 4:8, 0], ALU.mult, ALU.add), reads=["halo", "prem", "fl"], writes=["halo"])
            if T == NT - 1:
                P.dve(MS(halo[:, 4:8, 1], 0.0), writes=["halo"])
            elif last_tile_of_unit:
                P.dve(TS(halo[:, 4:8, 1], carry[:, 4:8, 1], cont), reads=["carry", "fl"], writes=["halo"])
            else:
                P.dve(CP(halo[:, 4:8, 1], carry[:, 4:8, 1]), reads=["carry"], writes=["halo"])
            P.dve(TT(hadj[:, 4:8, :], halo[:, 4:8, :], fm[:, bcol:bcol + 4].unsqueeze(2).to_broadcast([128, 4, 2]), ALU.subtract),
                  reads=["halo", "fm"], writes=["hadj"])
            for fc in range(4):
                b = next_bank()
                for kc in range(8):
                    P.pe(MM(psb[b][:, :], wsl[:, kc, fc * 128:(fc + 1) * 128], xt[:, kc, :], kc == 0, kc == 7), reads=[wkey, xkey], writes=[PK[b]])
                conv_silu(4 + fc, psb[b][:, :], PK[b], 512, bcol + fc, halo[:, 4 + fc, 0:1], halo[:, 4 + fc, 1:2],
                          qkT[:, fcb + fc, :], ("qkT", fcb + fc), save_first=carry[:, 4 + fc, 1:2])
                yield
            if first_tile_of_unit:
                P.pool(CP(firstpre[:, u_of, 4:8, 0], carry[:, 4:8, 1]), reads=["carry"], writes=[("firstpre", u_of)])
            wsv, wkv = wload(C_MLV)
            for j in range(4):
                b = next_bank()
                for kc in range(8):
                    P.pe(MM(psb[b][:, :], xt[:, kc, j * 128:(j + 1) * 128], wsv[:, kc, :], kc == 0, kc == 7), reads=[wkv, xkey], writes=[PK[b]])
                P.dve(TT(vb[:, j, :, 0:128], psb[b][:, :].rearrange("p (h d) -> p h d", h=4),
                         bias_bc[:, TB[C_MLV]:TB[C_MLV] + 512].rearrange("p (h d) -> p h d", h=4), ALU.add),
                      reads=[PK[b], "bias_bc"], writes=[(vname, j)])
                yield
            tokg = gates_A(xt, xkey, 512)
            yield
            gates_B(tokg, [T * 4 + c for c in range(4)])
            yield

        def p1_S(T):
            last_tile_of_unit = (T % TPU == TPU - 1)
            fcb, vb, vname = p1_bufs(T)
            KK4 = [("qkT", fcb + h) for h in range(4)]

            def pb(slot):
                return [4 + 2 * (slot % 2), 5 + 2 * (slot % 2)]
            outs_next = state_prep(qkT[:, fcb:fcb + 4, 3 * 128:4 * 128], KK4, vb[:, 3], (vname, 3), T * 4 + 3, 1, banks=pb(T * 4 + 3))
            for c in range(3, -1, -1):
                slot = T * 4 + c
                outs = outs_next
                ltok = None
                if T - 2 >= 0:
                    T2 = T - 2
                    jj = c
                    ltok = ln_A(xu[T2 * 512 + jj * 128:T2 * 512 + (jj + 1) * 128, :])
                if last_tile_of_unit and c == 3 and T != NT - 1:
                    P.dve(TS(Cb[:], Cb[:], cont), reads=["Cb", "fl"], writes=["Cb"])
                i = slot % 2
                P.act(ACTF(cbst[i][:], Cb[:], AF.Copy), reads=["Cb"], writes=[("cbst", i)])
                P.dma(DMA(cbs[slot].rearrange("p (h n) -> p h n", h=4), cbst[i][:]), reads=[("cbst", i)], writes=[("cbs", slot)], semkey=("cbst", i),
                      queue="pool")
                if c > 0:
                    outs_next = state_prep(qkT[:, fcb:fcb + 4, (c - 1) * 128:c * 128], KK4, vb[:, c - 1], (vname, c - 1), slot - 1, 1, banks=pb(slot - 1))
                state_apply(Cb, "Cb", outs, slot, 1)
                if ltok is not None:
                    ln_B(ltok, xnT[T2 % 2][:, :, jj * 128:(jj + 1) * 128], ("xnT", T2 % 2))
                yield

        load_norm_T(tile_src(NT - 1), 4, xnT[(NT - 1) % 2], ("xnT", (NT - 1) % 2))
        if NT > 1:
            load_norm_T(tile_src(NT - 2), 4, xnT[(NT - 2) % 2], ("xnT", (NT - 2) % 2))
        for _ in p1_P(NT - 1):
            pass
        for T in range(NT - 1, -1, -1):
            gS = p1_S(T)
            gP = p1_P(T - 1) if T > 0 else iter(())
            for k in range(4):
                next(gS, None)
                for _ in range(3 if k == 0 else 2):
                    next(gP, None)
            for _ in gS:
                pass
            for _ in gP:
                pass
            if T % TPU == 0:
                meta_state(T // TPU)
        for eng in ("act", "dve", "pool"):
            P.ew(eng, MS(st1[:, 5:6] if eng == "dve" else st1[:, 6:7], 0.0) if eng != "act" else ACTF(st1[:, 7:8], fl[:, 0:1], AF.Copy),
                 reads=["fl"], writes=[("vml2", j) for j in range(4)] + [("KT", i) for i in range(RING_T)])
        if stop <= 2:
            P.finalize(st)
            return nc
        def kv_proj(T):
            xt, xkey = xnT[T % 2], ("xnT", T % 2)
            rt = T % RING_T

            def h_k(fc, ps_ap, pkey):
                P.act(ACTF(KT[:, fc, rt * 512:(rt + 1) * 512], ps_ap, AF.Identity, bias=fm[:, FM_B[C_NAK] + fc:FM_B[C_NAK] + fc + 1]),
                      reads=[pkey, "fm"], writes=[("KT", rt)])
            proj_F(xt, xkey, 512, C_NAK, h_k)

            def h_v(j, ps_ap, pkey):
                P.dve(TT(VP[:, rt * 4 + j, :, 0:64], ps_ap.rearrange("p (h d) -> p h d", h=8),
                         bias_bc[:, TB[C_NAV]:TB[C_NAV] + 512].rearrange("p (h d) -> p h d", h=8), ALU.add),
                      reads=[pkey, "bias_bc"], writes=[("VP", rt)])
            proj_T(xt, xkey, 4, C_NAV, h_v)

        def ring_row(r):
            return r % RROWS

        def na_q_z(T):
            xt, xkey = xnT[T % 2], ("xnT", T % 2)
            P.pool(MS(qkT[64:128, 0:4, :], 0.0), writes=[("qkT", f) for f in range(4)])
            P.pool(MS(qkT[0:64, 4:8, :], 0.0), writes=[("qkT", 4 + f) for f in range(4)])

            def h_q(fc, ps_ap, pkey):
                bq = FM_B[C_NAQ] + fc
                P.act(ACTF(qkT[0:64, fc, :], ps_ap[0:64, :], AF.Identity, bias=fmh[0:64, bq:bq + 1], scale=0.125),
                      reads=[pkey, "fmh", ("qkT", fc)], writes=[("qkT", fc)])
                P.act(ACTF(qkT[64:128, 4 + fc, :], ps_ap[64:128, :], AF.Identity, bias=fmh[64:128, bq:bq + 1], scale=0.125),
                      reads=[pkey, "fmh", ("qkT", 4 + fc)], writes=[("qkT", 4 + fc)])
            proj_F(xt, xkey, 512, C_NAQ, h_q)

            def h_z(j, ps_ap, pkey):
                (za, zak), (zb_, zbk) = ((ftmp, ["ftmp"]), (tgA, ["tgA"])) if j % 2 == 0 else ((tgB, ["tgB"]), (h2, ["h2"]))
                P.dve(TT(za[:, :], ps_ap, bias_bc[:, TB[C_NAZ]:TB[C_NAZ] + 512], ALU.add), reads=[pkey, "bias_bc"], writes=zak)
                P.act(ACTF(zb_[:, :], za[:, :], AF.Tanh, scale=0.5), reads=zak, writes=zbk)
                P.pool(TS(zb_[:, :], zb_[:, :], 1.0, 0.5, ALU.add, ALU.mult), reads=zbk, writes=zbk)
                P.pool(TT(zs[:, j, :], za[:, :], zb_[:, :], ALU.mult), reads=zak + zbk, writes=[("zs", j)])
            proj_T(xt, xkey, 4, C_NAZ, h_z)

        def q_ap(h, qcol):
            fc, hh = h // 2, h % 2
            return qkT[:, 4 * hh + fc, qcol:qcol + 64], ("qkT", 4 * hh + fc)

        def na_meta_scores(lrt, slot):
            qcol = lrt * 64
            ms = psb[3][0:16, :]
            for h in range(8):
                qap, qk = q_ap(h, qcol)
                P.pe(MM(ms[:, h * 64:(h + 1) * 64], kmT[:, h // 2, :], qap), reads=["kmT", qk], writes=[PK[3]])
            P.act(ACTF(Pm[0:16, slot, :], ms, AF.Exp), reads=[PK[3]], writes=[("Pm", slot)])

        na_i = {"i": 0, "cur": 0}
        PB = [Praw[0], Praw[1], Pn[0], Pn[1]]
        PBK = [("Praw", 0), ("Praw", 1), ("Pn", 0), ("Pn", 1)]

        def na_scores(lr_tile, r0, delta, hp):
            qcol = lr_tile * 64
            i = na_i["i"] % 4
            na_i["i"] += 1
            odd = (r0 % 2 == 1)
            ng = 5 if odd else 4
            W = 2 * ng * 64
            cur = na_i["cur"]
            if odd:
                if cur % 2 == 1:
                    cur = (cur + 1) % 4
                reg = psS[cur // 2][:, 0:W]
                skeys = [PK[4 + cur], PK[5 + cur]]
                na_i["cur"] = (cur + 2) % 4
            else:
                reg = psb[4 + cur][:, 0:W]
                skeys = [PK[4 + cur]]
                na_i["cur"] = (cur + 1) % 4
            sps = reg.rearrange("p (h g q) -> p h g q", h=2, g=ng)
            base = r0 - 1 if odd else r0
            rk = list(dict.fromkeys([("KT", (rr // 8) % RING_T) for rr in range(base, base + 2 * ng)]))
            for hh in range(2):
                qap, qk = q_ap(2 * hp + hh, qcol)
                for g in range(ng):
                    c0 = ring_row(base + 2 * g) * 64
                    P.pe(MM(sps[:, hh, g, :], KT[:, hp, c0:c0 + 128], qap), reads=rk + [qk], writes=skeys)
            P.act(ACTF(PB[i][:, 0:W], reg, AF.Exp), reads=skeys, writes=[PBK[i]])
            pn4 = PB[i][:, 0:W].rearrange("p (h g q) -> p h g q", h=2, g=ng)
            e2 = etab[:, 2 * hp:2 * hp + 2]
            if not odd:
                ei0 = 7 - delta
                P.dve(TT(pn4, pn4, e2[:, :, ei0:ei0 + 7:2, :], ALU.mult), reads=[PBK[i], "etab"], writes=[PBK[i]])
            else:
                P.dve(TT(pn4[:, :, 0, :], pn4[:, :, 0, :], e2[:, :, 14, :], ALU.mult), reads=[PBK[i], "etab"], writes=[PBK[i]])
                P.dve(TT(pn4[:, :, 1:4, :], pn4[:, :, 1:4, :], e2[:, :, 4:9:2, :], ALU.mult), reads=[PBK[i], "etab"], writes=[PBK[i]])
                P.dve(TT(pn4[:, :, 4, :], pn4[:, :, 4, :], e2[:, :, 15, :], ALU.mult), reads=[PBK[i], "etab"], writes=[PBK[i]])
            return (PBK[i], pn4, ng, base)

        def na_pv(desc, hp, pv_bank, row_half):
            pkey, pn4, ng, base = desc
            vk = list(dict.fromkeys([("VP", (rr // 8) % RING_T) for rr in range(base, base + 2 * ng)]))
            for hh in range(2):
                h = 2 * hp + hh
                o = psb[pv_bank][row_half * 64:(row_half + 1) * 64, (h % 4) * 65:(h % 4) * 65 + 65]
                rd = [pkey] + vk
                for g in range(ng):
                    vt = ring_row(base + 2 * g) // 2
                    P.pe(MM(o, pn4[:, hh, g, :], VP[:, vt, h, :], g == 0, False), reads=rd, writes=[PK[pv_bank]])
                P.pe(MM(o, Pm[:, row_half, h * 64:(h + 1) * 64], vmp[:, h, :], False, True), reads=[("Pm", row_half), "vmp"], writes=[PK[pv_bank]])

        def na_tile(T):
            items = []
            for sp in range(4):
                rows = [8 * T + 2 * sp, 8 * T + 2 * sp + 1]
                variants = []
                for r in rows:
                    u = r // R
                    lr = r % R
                    r0 = min(max(lr - 4, 0), R - 8)
                    v = [(u * R + r0, lr - r0)]
                    if R - 4 <= r < R + 4:
                        v.append((r - 4, 4))
                    variants.append(v)
                nvar = max(len(v) for v in variants)
                items.append(("meta", sp, rows))
                for vi in range(nvar):
                    for half in range(2):
                        grp = (sp, vi, half)
                        for rh, r in enumerate(rows):
                            r0, delta = variants[rh][min(vi, len(variants[rh]) - 1)]
                            for hp in (2 * half, 2 * half + 1):
                                items.append(("unit", r - 8 * T, r0, delta, hp, grp, rh))
                        items.append(("norm", grp, vi, half))
                items.append(("fin", sp, nvar))
            banks = {}
            q = []
            late = []
            DEPTH = 3

            def do_pv(ent):
                _, desc, hp, grp, rh = ent
                if grp not in banks:
                    banks[grp] = next_bank()
                na_pv(desc, hp, banks[grp], rh)

            def drain(maxpv):
                while sum(1 for e in q if e[0] == "pv") > maxpv:
                    while q and q[0][0] != "pv":
                        run(q.pop(0)[1])
                    do_pv(q.pop(0))
                    while late:
                        late.pop(0)()
                    while q and q[0][0] != "pv":
                        run(q.pop(0)[1])

            def run(it):
                if it[0] == "meta":
                    for rh, r in enumerate(it[2]):
                        na_meta_scores(r - 8 * T, rh)
                elif it[0] == "norm":
                    _, grp, vi, half = it
                    pvb = banks[grp]
                    pv = psb[pvb][:, 0:260].rearrange("p (h n) -> p h n", h=4)
                    P.dve(RECIP(st1[:, 4:8], pv[:, :, 64]), reads=[PK[pvb]], writes=["st1r"])
                    dst = nao[vi][:, half * 256:(half + 1) * 256].rearrange("p (h d) -> p h d", h=4)
                    P.dve(TT(dst, pv[:, :, 0:64], st1[:, 4:8].unsqueeze(2).to_broadcast([128, 4, 64]), ALU.mult),
                          reads=[PK[pvb], "st1r"] + NAOK[vi], writes=NAOK[vi])
                else:
                    _, sp, nvar = it
                    if nvar == 2:
                        P.dve(TS(nao[0][:, :], nao[0][:, :], ncont), reads=NAOK[0] + ["fl"], writes=NAOK[0])
                        P.dve(STT(nao[0][:, :], nao[1][:, :], cont, nao[0][:, :], ALU.mult, ALU.add), reads=NAOK[0] + NAOK[1] + ["fl"], writes=NAOK[0])
                    if dbg:
                        t0 = (8 * T + 2 * sp) * 64
                        P.dma(DMA(dbg_out["d_na"][t0:t0 + 128, :], nao[0][:, :]), reads=NAOK[0], semkey="dbg1")
                    P.dve(TT(gA[:, :], nao[0][:, :], zs[:, sp, :], ALU.mult), reads=NAOK[0] + [("zs", sp)], writes=["gA"])

                    def fin_b(sp=sp):
                        pT = pTv()
                        for fc in range(4):
                            P.pe(TR(pT[:, fc, :], gA[:, fc * 128:(fc + 1) * 128], ident[:, :]), reads=["gA", "ident"], writes=[PK[3]])
                        P.act(ACTF(gAT[:, :, sp * 128:(sp + 1) * 128], pT[:, 0:4, :], AF.Copy), reads=[PK[3]], writes=["gAT"])
                    late.append(fin_b)

            for it in items:
                if it[0] == "unit":
                    _, lrt, r0, delta, hp, grp, rh = it
                    desc = na_scores(lrt, r0, delta, hp)
                    q.append(("pv", desc, hp, grp, rh))
                    drain(DEPTH)
                elif not q:
                    run(it)
                else:
                    q.append(("act", it))
            drain(0)
            while q:
                run(q.pop(0)[1])
            while late:
                late.pop(0)()

        def mlstm_tile(T, filler=None):
            xt, xkey = xnT[T % 2], ("xnT", T % 2)
            u_of = T // TPU
            first_tile_of_unit = (T % TPU == 0)
            last_tile_of_unit = (T % TPU == TPU - 1)
            for (cch, fco) in ((C_MLQ, 0), (C_MLK, 4)):
                wsl, wkey = wload(cch)
                bcol = FM_B[cch]
                if T < NT - 1:
                    hb = small_cols(wsl, wkey, xnT[(T + 1) % 2], ("xnT", (T + 1) % 2), 0)
                    P.dve(TT(halo[:, fco:fco + 4, 1], psb[hb][:, 0:4], fm[:, bcol:bcol + 4], ALU.add), reads=[PK[hb], "fm"], writes=["halo"])
                    if last_tile_of_unit:
                        P.dve(TS(halo[:, fco:fco + 4, 1], halo[:, fco:fco + 4, 1], cont), reads=["halo", "fl"], writes=["halo"])
                else:
                    P.dve(MS(halo[:, fco:fco + 4, 1], 0.0), writes=["halo"])
                if T == 0:
                    P.dve(CP(halo[:, fco:fco + 4, 0], prem[:, fco:fco + 4]), reads=["prem"], writes=["halo"])
                elif first_tile_of_unit:
                    P.dve(TS(halo[:, fco:fco + 4, 0], carry[:, fco:fco + 4, 0], cont), reads=["carry", "fl"], writes=["halo"])
                    P.dve(STT(halo[:, fco:fco + 4, 0], prem[:, fco:fco + 4], ncont, halo[:, fco:fco + 4, 0], ALU.mult, ALU.add),
                          reads=["halo", "prem", "fl"], writes=["halo"])
                else:
                    P.dve(CP(halo[:, fco:fco + 4, 0], carry[:, fco:fco + 4, 0]), reads=["carry"], writes=["halo"])
                P.dve(TT(hadj[:, fco:fco + 4, :], halo[:, fco:fco + 4, :], fm[:, bcol:bcol + 4].unsqueeze(2).to_broadcast([128, 4, 2]), ALU.subtract),
                      reads=["halo", "fm"], writes=["hadj"])
                for fc in range(4):
                    b = next_bank()
                    for kc in range(8):
                        P.pe(MM(psb[b][:, :], wsl[:, kc, fc * 128:(fc + 1) * 128], xt[:, kc, :], kc == 0, kc == 7), reads=[wkey, xkey], writes=[PK[b]])
                    conv_silu(fco + fc, psb[b][:, :], PK[b], 512, bcol + fc, halo[:, fco + fc, 0:1], halo[:, fco + fc, 1:2],
                              qkT[:, fco + fc, :], ("qkT", fco + fc), save_last=carry[:, fco + fc, 0:1])

            def h_v(j, ps_ap, pkey):
                P.dve(TT(vml[:, j, :, 0:128], ps_ap.rearrange("p (h d) -> p h d", h=4),
                         bias_bc[:, TB[C_MLV]:TB[C_MLV] + 512].rearrange("p (h d) -> p h d", h=4), ALU.add),
                      reads=[pkey, "bias_bc"], writes=[("vml", j)])
            proj_T(xt, xkey, 4, C_MLV, h_v)

            def h_z(j, ps_ap, pkey):
                (za, zak), (zb_, zbk) = ((ftmp, ["ftmp"]), (tgA, ["tgA"])) if j % 2 == 0 else ((tgB, ["tgB"]), (h2, ["h2"]))
                P.dve(TT(za[:, :], ps_ap, bias_bc[:, TB[C_MLZ]:TB[C_MLZ] + 512], ALU.add), reads=[pkey, "bias_bc"], writes=zak)
                P.act(ACTF(zb_[:, :], za[:, :], AF.Tanh, scale=0.5), reads=zak, writes=zbk)
                P.dve(STT(zb_[:, :], zb_[:, :], 1.0, za[:, :], ALU.add, ALU.mult), reads=zak + zbk, writes=zbk)
                P.pool(TT(zs[:, j, :], zb_[:, :], hg_bc[:, :], ALU.mult), reads=zbk + ["hg_bc"], writes=[("zs", j)])
            proj_T(xt, xkey, 4, C_MLZ, h_z)

            def h_o(j, ps_ap, pkey):
                za, zak = (tgB, ["tgB"]) if j % 2 == 0 else (ftmp, ["ftmp"])
                P.dve(TT(za[:, :], ps_ap, bias_bc[:, TB[C_MLO]:TB[C_MLO] + 512], ALU.add), reads=[pkey, "bias_bc"], writes=zak)
                P.act(ACTF(oG[:, j, :], za[:, :], AF.Tanh, scale=0.5), reads=zak, writes=[("oG", j)])
            proj_T(xt, xkey, 4, C_MLO, h_o)
            def ml_X(c):
                slot = T * 4 + c
                csl = slice(c * 128, (c + 1) * 128)
                if first_tile_of_unit and c == 0:
                    if T == 0:
                        P.dve(CP(Cf[:], Cmeta[:, 0]), reads=[("Cmeta", 0)], writes=["Cf"])
                    else:
                        P.dve(TS(Cf[:], Cf[:], cont), reads=["Cf", "fl"], writes=["Cf"])
                        P.dve(STT(Cf[:], Cmeta[:, u_of], ncont, Cf[:], ALU.mult, ALU.add), reads=["Cf", ("Cmeta", u_of), "fl"], writes=["Cf"])
                    P.act(ACTF(Cfb[:], Cf[:], AF.Copy), reads=["Cf"], writes=["Cfb"])
                li = slot % 2
                P.dma(DMA(cbl[li][:], cbs[slot].rearrange("p (h n) -> p h n", h=4)), reads=[("cbs", slot)], writes=[("cbl", li)], semkey=("cbl", li))
                sps = psb[4][:, :].rearrange("p (h t) -> p h t", h=4)
                for h in range(4):
                    P.pe(MM(sps[:, h, :], qkT[:, 4 + h, csl], qkT[:, h, csl]), reads=[("qkT", 4 + h), ("qkT", h)], writes=[PK[4]])
                for h in range(4):
                    P.dve(STT(Sf[h][:, :], sps[:, h, :], SC[:, slot, h:h + 1], maskf[:, :], ALU.mult, ALU.mult),
                          reads=[PK[4], ("SC", slot), "maskf"], writes=[("Sf", h)])
                    P.dve(STT(Sb_[h][:, :], sps[:, h, :], SC[:, slot, 4 + h:5 + h], maskb[:, :], ALU.mult, ALU.mult),
                          reads=[PK[4], ("SC", slot), "maskb"], writes=[("Sb", h)])
                souts = state_prep(qkT[:, 4:8, csl], [("qkT", 4 + h) for h in range(4)], vml[:, c], ("vml", c), slot, 0)
                nbs = [5, 6, 7, next_bank()]
                for h in range(4):
                    nb = nbs[h]
                    nf = psb[nb][:, 0:129]
                    nbk = psb[nb][:, 129:258]
                    P.pe(MM(nf, Sf[h][:, :], vml[:, c, h, :], True, False), reads=[("Sf", h), ("vml", c)], writes=[PK[nb]])
                    P.pe(MM(nf, qkT[:, h, csl], Cfb[:, h, :], False, True), reads=[("qkT", h), "Cfb"], writes=[PK[nb]])
                    P.pe(MM(nbk, Sb_[h][:, :], vml[:, c, h, :], True, False), reads=[("Sb", h), ("vml", c)], writes=[PK[nb]])
                    P.pe(MM(nbk, qkT[:, h, csl], cbl[li][:, h, :], False, True), reads=[("qkT", h), ("cbl", li)], writes=[PK[nb]])
                state_apply(Cf, "Cf", souts, slot, 0)
                P.act(ACTF(Cfb[:], Cf[:], AF.Copy), reads=["Cf"], writes=["Cfb"])
                for h in range(4):
                    nb = nbs[h]
                    d0 = 16 + 4 * h
                    P.act(ACTF(st2[:, d0:d0 + 2], psb[nb][:, 128:258:129], AF.Abs), reads=[PK[nb]], writes=[("st1d", h)])
                for h in range(4):
                    d0 = 16 + 4 * h
                    P.dve(TT(st2[:, d0:d0 + 2], st2[:, d0:d0 + 2], SC[:, slot, 16 + h:21 + h:4], ALU.max), reads=[("st1d", h), ("SC", slot)], writes=[("st1d", h)])
                    P.dve(RECIP(st2[:, d0 + 2:d0 + 4], st2[:, d0:d0 + 2]), reads=[("st1d", h)], writes=[("st1e", h)])
                for h in range(4):
                    nb = nbs[h]
                    d0 = 16 + 4 * h
                    hs = slice(h * 128, (h + 1) * 128)
                    P.act(ACTF(hbuf[:, hs], psb[nb][:, 0:128], AF.Copy, scale=st2[:, d0 + 2:d0 + 3]), reads=[PK[nb], ("st1e", h)], writes=[("hbuf", h)])
                for h in range(4):
                    nb = nbs[h]
                    d0 = 16 + 4 * h
                    hs = slice(h * 128, (h + 1) * 128)
                    P.dve(STT(hbuf[:, hs], psb[nb][:, 129:257], st2[:, d0 + 3:d0 + 4], hbuf[:, hs], ALU.mult, ALU.add),
                          reads=[PK[nb], ("st1e", h), ("hbuf", h)], writes=[("hbuf", h)])
                if dbg:
                    t0 = slot * 128
                    P.dma(DMA(dbg_out["d_h"][t0:t0 + 128, :], hbuf[:, :]), reads=HBK, semkey="dbg2")

            def ml_Y1(c):
                P.dve(STT(h2[:, :], oG[:, c, :], 1.0, hbuf[:, :], ALU.add, ALU.mult), reads=[("oG", c)] + HBK, writes=["h2"])

            def ml_Y2(c):
                csl = slice(c * 128, (c + 1) * 128)
                h3 = h2[:, :].rearrange("p (h d) -> p h d", h=4)
                P.dve(lambda e, h3=h3: e.tensor_reduce(out=st1[:, 12:16], in_=h3, axis=AX.X, op=ALU.add), reads=["h2"], writes=["st1m"])
                for h in range(4):
                    P.act(ACTF(ftmp[:, h * 128:(h + 1) * 128], h2[:, h * 128:(h + 1) * 128], AF.Square, accum_out=st1[:, 8 + h:9 + h]),
                          reads=["h2"], writes=["ftmp", ("st1q", h)])
                SQK = [("st1q", h) for h in range(4)]
                P.dve(TS(st1[:, 12:16], st1[:, 12:16], 1.0 / 128), reads=["st1m"], writes=["st1m"])
                P.dve(TT(st1[:, 4:8], st1[:, 12:16], st1[:, 12:16], ALU.mult), reads=["st1m"], writes=["st1r"])
                P.dve(STT(st1[:, 8:12], st1[:, 8:12], 1.0 / 128, st1[:, 4:8], ALU.mult, ALU.subtract), reads=SQK + ["st1r"], writes=["st1v"])
                P.dve(TS(st1[:, 8:12], st1[:, 8:12], 4.0 * EPS, None, ALU.add), reads=["st1v"], writes=["st1v"])
                P.pool(TT(st1[:, 8:12], st1[:, 8:12], cst[:, 0:1].to_broadcast([128, 4]), ALU.pow), reads=["st1v", "cst"], writes=["st1v"])
                for h in range(4):
                    hs = slice(h * 128, (h + 1) * 128)
                    P.dve(TS(h2[:, hs], h2[:, hs], st1[:, 12 + h:13 + h], st1[:, 8 + h:9 + h], ALU.subtract, ALU.mult),
                          reads=["h2", "st1m", "st1v"], writes=["h2"])
                P.dve(TT(gB[:, :], h2[:, :], zs[:, c, :], ALU.mult), reads=["h2", ("zs", c)], writes=["gB"])

            def ml_Y2b(c):
                csl = slice(c * 128, (c + 1) * 128)
                pT = pTv()
                for fc in range(4):
                    P.pe(TR(pT[:, fc, :], gB[:, fc * 128:(fc + 1) * 128], ident[:, :]), reads=["gB", "ident"], writes=[PK[3]])
                P.act(ACTF(gBT[:, :, csl], pT[:, 0:4, :], AF.Copy), reads=[PK[3]], writes=["gBT"])

            def fill(k):
                if filler is not None:
                    for _ in range(k):
                        next(filler, None)

            for c in range(4):
                ml_X(c)
                if c >= 2:
                    ml_Y2b(c - 2)
                fill(2)
                if c >= 1:
                    ml_Y2(c - 1)
                ml_Y1(c)
            ml_Y2b(2)
            ml_Y2(3)
            ml_Y2b(3)
            fill(8)

        def out_branch(T, which):
            xt, xkey = xnT[T % 2], ("xnT", T % 2)
            (cbr, cg0, cg1, gsrc, gkey, tg, tgk, first) = ((C_WA, C_GA0, C_GA1, gAT, "gAT", tgA, "tgA", True) if which == 0 else
                                                          (C_WB, C_GB0, C_GB1, gBT, "gBT", tgB, "tgB", False))
            wbr_, kbr = wload(cbr)
            wbr = wbr_[:].rearrange("p a b -> p (a b)").rearrange("p (fc n) -> p fc n", fc=4)
            for half, cg in enumerate((cg0, cg1)):
                wG, kG = wload(cg)
                for fc in range(4):
                    n = half * 4 + fc
                    bG = next_bank()
                    for kc in range(8):
                        P.pe(MM(psb[bG][:, :], wG[:, kc, fc * 128:(fc + 1) * 128], xt[:, kc, :], kc == 0, kc == 7), reads=[kG, xkey], writes=[PK[bG]])
                    P.act(ACTF(tg[:, :], psb[bG][:, :], AF.Tanh, bias=fmh[:, FM_B[cg] + fc:FM_B[cg] + fc + 1], scale=0.5),
                          reads=[PK[bG], "fmh"], writes=[tgk])
                    by = next_bank()
                    for k4 in range(4):
                        P.pe(MM(psb[by][:, :], wbr[:, k4, n * 128:(n + 1) * 128], gsrc[:, k4, :], k4 == 0, k4 == 3), reads=[kbr, gkey], writes=[PK[by]])
                    if first:
                        P.dve(STT(ypT[:, n, :], tg[:, :], 1.0, psb[by][:, :], ALU.add, ALU.mult), reads=[tgk, PK[by]], writes=[("ypT", n)])
                    else:
                        P.dve(STT(tg[:, :], tg[:, :], 1.0, psb[by][:, :], ALU.add, ALU.mult), reads=[tgk, PK[by]], writes=[tgk])
                        P.pool(TT(ypT[:, n, :], ypT[:, n, :], tg[:, :], ALU.add), reads=[tgk, ("ypT", n)], writes=[("ypT", n)])
                    yield n

        def out_tile(T, hook=None, skip_a=False):
            xt, xkey = xnT[T % 2], ("xnT", T % 2)
            for which in ((1,) if skip_a else (0, 1)):
                for _ in out_branch(T, which):
                    pass
            wO0, kO0 = wload(C_WO0)
            wO1, kO1 = wload(C_WO1)
            YK = [("ypT", n) for n in range(8)]
            T2 = T + 2 if hook is not None else None
            ltok = None
            if T2 is not None:
                ltok = ln_A(xu[T2 * 512:T2 * 512 + 128, :])
            for j in range(4):
                i = j % 2
                t0 = T * 512 + j * 128
                P.dma(DMA(osb[i][:], xu[t0:t0 + 128, :]), writes=[("osb", i)], reads=[("y", "st", i)], semkey=("xres", i))
                b0 = next_bank()
                b1 = next_bank()
                for (b, wO, kO) in ((b0, wO0, kO0), (b1, wO1, kO1)):
                    for k8 in range(8):
                        P.pe(MM(psb[b][:, :], ypT[:, k8, j * 128:(j + 1) * 128], wO[:, k8, :], k8 == 0, k8 == 7), reads=YK + [kO], writes=[PK[b]])
                if T2 is not None:
                    ln_B(ltok, xnT[T2 % 2][:, :, j * 128:(j + 1) * 128], ("xnT", T2 % 2))
                    if j < 3:
                        ltok = ln_A(xu[T2 * 512 + (j + 1) * 128:T2 * 512 + (j + 2) * 128, :])
                c0 = 8 + 4 * i
                oa, oa2, ob, oc = ("o_a", i), ("o_a2", i), ("o_b", i), ("o_c", i)
                P.act(ACTF(gA[:, :], psb[b0][:, :], AF.Square, accum_out=st2[:, c0:c0 + 1]), reads=[PK[b0]], writes=["gA", oa])
                P.act(ACTF(gB[:, :], psb[b1][:, :], AF.Square, accum_out=st2[:, c0 + 3:c0 + 4]), reads=[PK[b1]], writes=["gB", oa2])
                P.dve(TT(st2[:, c0 + 1:c0 + 2], st2[:, c0:c0 + 1], st2[:, c0 + 3:c0 + 4], ALU.add), reads=[oa, oa2], writes=[ob])
                P.dve(TS(st2[:, c0 + 1:c0 + 2], st2[:, c0 + 1:c0 + 2], 1.0 / D, 4.0 * EPS, ALU.mult, ALU.add), reads=[ob], writes=[ob])
                P.pool(TT(st2[:, c0 + 2:c0 + 3], st2[:, c0 + 1:c0 + 2], cst[:, 0:1], ALU.pow), reads=[ob, "cst"], writes=[oc])
                P.dve(STT(tgA[:, :], psb[b0][:, :], st2[:, c0 + 2:c0 + 3], gpost_bc[:, 0:512], ALU.mult, ALU.mult),
                      reads=[PK[b0], oc, "gpost_bc"], writes=["tgA"])
                P.dve(STT(tgB[:, :], psb[b1][:, :], st2[:, c0 + 2:c0 + 3], gpost_bc[:, 512:1024], ALU.mult, ALU.mult),
                      reads=[PK[b1], oc, "gpost_bc"], writes=["tgB"])
                P.pool(TT(osb[i][:, 0:512], osb[i][:, 0:512], tgA[:, :], ALU.add), reads=[("osb", i), "tgA"], writes=[("osb", i)])
                P.pool(TT(osb[i][:, 512:1024], osb[i][:, 512:1024], tgB[:, :], ALU.add), reads=[("osb", i), "tgB"], writes=[("osb", i)])
                P.dma(DMA(y[t0:t0 + 128, :], osb[i][:, :]), reads=[("osb", i)], writes=[("y", "st", i)], semkey=("osb", i), queue="pool")

        def ln_sub(T2, j):
            load_norm_T(xu[T2 * 512 + j * 128:T2 * 512 + (j + 1) * 128, :], 1, xnT[T2 % 2][:, :, j * 128:(j + 1) * 128], ("xnT", T2 % 2))

        kv_proj(0)
        for T in range(NT):
            if T + 1 < NT:
                kv_proj(T + 1)
            na_q_z(T)
            na_tile(T)
            mlstm_tile(T, filler=out_branch(T, 0))
            out_tile(T, hook=(lambda j, T=T: ln_sub(T + 2, j)) if T + 2 < NT else None, skip_a=True)

        P.finalize(st)
    return nc


def _host_tables(na_rpb):
    rpb = np.asarray(na_rpb, np.float32).reshape(8, 15, 31)
    kc = np.arange(64)[:, None]
    qc = np.arange(64)[None, :]
    dc = np.clip(kc - qc + 15, 0, 30)
    tzv = np.ascontiguousarray(np.transpose(rpb[:, :, dc], (2, 0, 1, 3)))
    c0 = np.clip(qc - 8, 0, 48)
    valid = ((kc >= c0) & (kc < c0 + 16)).astype(np.float32)
    cm = np.concatenate([valid, valid], axis=0)
    return tzv.astype(np.float32), cm.astype(np.float32)


def _make_in_maps(inputs, units_per_core, conts):
    tzv, cm = _host_tables(inputs["na_rpb"])
    common = {
        "meta": np.ascontiguousarray(inputs["meta_tokens"], np.float32),
        "g_pre": np.ascontiguousarray(inputs["g_pre"], np.float32).reshape(1, D),
        "w_in": np.ascontiguousarray(inputs["w_in"], np.float32).reshape(D, NIN),
        "b_in": np.ascontiguousarray(inputs["b_in"], np.float32).reshape(1, NIN),
        "tz": tzv, "cmask": cm,
        "conv_w": np.ascontiguousarray(inputs["ml_conv_w"], np.float32).reshape(3, D),
        "head_g": np.ascontiguousarray(inputs["ml_head_g"], np.float32).reshape(1, 512),
        "w_a": np.ascontiguousarray(inputs["w_a"], np.float32).reshape(512, D),
        "w_b": np.ascontiguousarray(inputs["w_b"], np.float32).reshape(512, D),
        "w_out": np.ascontiguousarray(inputs["w_out"], np.float32).reshape(D, D),
        "g_post": np.ascontiguousarray(inputs["g_post"], np.float32).reshape(1, D),
    }
    maps = []
    for xu, cont in zip(units_per_core, conts):
        fl = np.zeros((128, 4), np.float32)
        fl[:, 0] = cont
        fl[:, 1] = 1.0 - cont
        fl[0:4, 2] = 1.0
        m = dict(common)
        m["xu"] = np.ascontiguousarray(xu, np.float32)
        m["flags"] = fl
        maps.append(m)
    return maps


_NC_CACHE = {}


def kernel(x_prompt, x_sample, meta_tokens, g_pre, w_in, b_in, na_rpb, ml_conv_w, ml_head_g, w_a, w_b, w_out, g_post):
    x_prompt = np.asarray(x_prompt, np.float32)
    x_sample = np.asarray(x_sample, np.float32)
    inputs = dict(meta_tokens=meta_tokens, g_pre=g_pre, w_in=w_in, b_in=b_in, na_rpb=na_rpb, ml_conv_w=ml_conv_w,
                  ml_head_g=ml_head_g, w_a=w_a, w_b=w_b, w_out=w_out, g_post=g_post)
    R = 64
    units, conts = [], []
    for s in range(2):
        units.append(x_sample[s])
        conts.append(1.0)
    for c in range(4):
        units.append(np.concatenate([x_prompt[2 * c], x_prompt[2 * c + 1]], axis=0))
        conts.append(0.0)
    for c in range(2):
        units.append(np.concatenate([x_prompt[2 * c], x_prompt[2 * c + 1]], axis=0))
        conts.append(0.0)
    if R not in _NC_CACHE:
        _NC_CACHE[R] = build(R)
    nc = _NC_CACHE[R]
    in_maps = _make_in_maps(inputs, units, conts)
    res = run_bass_kernel_spmd(nc, in_maps, core_ids=list(range(8)))
    outs = [np.asarray(r["y"], np.float32) for r in res.results]
    y_sample = np.stack([outs[0], outs[1]], axis=0)
    yp = []
    for c in range(4):
        yp.append(outs[2 + c][:4096])
        yp.append(outs[2 + c][4096:])
    y_prompt = np.stack(yp, axis=0)
    return (y_prompt, y_sample)
```

```python
from contextlib import ExitStack
import numpy as np
import concourse.bass as bass
import concourse.mybir as mybir
from concourse.bass_utils import run_bass_kernel_spmd

F32 = mybir.dt.float32
BF16 = mybir.dt.bfloat16
AF = mybir.ActivationFunctionType
ALU = mybir.AluOpType
AX = mybir.AxisListType

ENGS = ("pe", "act", "dve", "pool", "sp")
NOSYNC = set()


class Op:
    __slots__ = ("eng", "emit", "deps", "dma", "semkey", "tok", "sig")

    def __init__(self, eng, emit, dma, semkey):
        self.eng = eng
        self.emit = emit
        self.deps = []
        self.dma = dma
        self.semkey = semkey
        self.tok = None
        self.sig = False


class Prog:
    def __init__(self, nc, same_engine_sync=True):
        self.nc = nc
        self.q = {e: [] for e in ENGS}
        self.last_w = {}
        self.readers = {}
        self.same_engine_sync = same_engine_sync

    def add(self, eng, emit, reads=(), writes=(), dma=False, semkey=None):
        op = Op(eng, emit, dma, semkey)
        self.count = getattr(self, "count", 0) + 1
        if self.count > getattr(self, "limit", 10 ** 9):
            return op
        deps = {}
        for k in reads:
            w = self.last_w.get(k)
            if w is not None:
                deps[id(w)] = w
            if isinstance(k, str) and k.startswith("ps"):
                for r in self.readers.get(k, ()):
                    if r.eng != eng:
                        deps[id(r)] = r
        for k in writes:
            w = self.last_w.get(k)
            if w is not None:
                deps[id(w)] = w
            for r in self.readers.get(k, ()):
                deps[id(r)] = r
        op.deps = list(deps.values())
        for k in writes:
            self.last_w[k] = op
            self.readers[k] = []
        for k in reads:
            self.readers.setdefault(k, []).append(op)
        self.q[eng].append(op)
        return op

    def pe(self, emit, reads=(), writes=()):
        return self.add("pe", emit, reads, writes)

    def act(self, emit, reads=(), writes=()):
        return self.add("act", emit, reads, writes)

    def dve(self, emit, reads=(), writes=()):
        return self.add("dve", emit, reads, writes)

    def pool(self, emit, reads=(), writes=()):
        return self.add("pool", emit, reads, writes)

    def ew(self, eng, emit, reads=(), writes=()):
        return self.add(eng, emit, reads, writes)

    def dma(self, emit, reads=(), writes=(), semkey=None, queue="sp"):
        return self.add(queue, emit, reads, writes, dma=True, semkey=semkey)

    def _skip(self, d, op):
        if d.dma and op.dma and isinstance(d.semkey, str) and d.semkey.startswith("setup") and d.semkey == op.semkey:
            return True
        return (not d.dma) and (not op.dma) and d.eng == op.eng and (d.eng == "pe" or d.eng in NOSYNC or not self.same_engine_sync)

    def finalize(self, stack):
        nc = self.nc
        for e in ENGS:
            for op in self.q[e]:
                for d in op.deps:
                    if d.dma or self._skip(d, op):
                        continue
                    d.sig = True
        eng_sems = {e: [stack.enter_context(nc.semaphore("s_%s0" % e))] for e in ENGS}
        dma_sems = {}
        dma_cnt = {}
        LIM = 30000
        for e in ENGS:
            cnt = 0
            for op in self.q[e]:
                if op.dma:
                    if op.semkey not in dma_sems:
                        dma_sems[op.semkey] = stack.enter_context(nc.semaphore("d%d" % len(dma_sems)))
                        dma_cnt[op.semkey] = 0
                    dma_cnt[op.semkey] += 16
                    op.tok = (dma_sems[op.semkey], dma_cnt[op.semkey])
                elif op.sig:
                    if cnt >= LIM:
                        eng_sems[e].append(stack.enter_context(nc.semaphore("s_%s%d" % (e, len(eng_sems[e])))))
                        cnt = 0
                    cnt += 1
                    op.tok = (eng_sems[e][-1], cnt)
        assert max([0] + list(dma_cnt.values())) < 60000, "dma sem overflow"
        for e in ENGS:
            for op in self.q[e]:
                if op.dma and isinstance(op.semkey, str) and op.semkey.startswith("setup"):
                    op.tok = (dma_sems[op.semkey], dma_cnt[op.semkey])
        self.dma_sems = dma_sems
        self.dma_cnt = dma_cnt
        block = stack.enter_context(nc.Block())
        prog = self

        def run_queue(e, engine):
            known = {}
            for op in prog.q[e]:
                need = {}
                for d in op.deps:
                    if prog._skip(d, op):
                        continue
                    sem, val = d.tok
                    key = sem.num
                    if known.get(key, 0) >= val:
                        continue
                    if key not in need or need[key][1] < val:
                        need[key] = (sem, val)
                for key, (sem, val) in need.items():
                    engine.wait_ge(sem, val)
                    known[key] = val
                ins = op.emit(engine)
                if op.dma:
                    ins.then_inc(op.tok[0], 16)
                elif op.sig:
                    ins.then_inc(op.tok[0], 1)

        @block.tensor
        def _(eng):
            run_queue("pe", eng)

        @block.scalar
        def _(eng):
            run_queue("act", eng)

        @block.vector
        def _(eng):
            run_queue("dve", eng)

        @block.gpsimd
        def _(eng):
            run_queue("pool", eng)

        @block.sync
        def _(eng):
            run_queue("sp", eng)
            for key, sem in prog.dma_sems.items():
                eng.wait_ge(sem, prog.dma_cnt[key])


def MM(out, lhsT, rhs, start=True, stop=True):
    return lambda e: e.matmul(out, lhsT=lhsT, rhs=rhs, start=start, stop=stop)


def TR(out, in_, ident):
    return lambda e: e.transpose(out=out, in_=in_, identity=ident)


def ACTF(out, in_, func, bias=None, scale=1.0, accum_out=None):
    def f(e):
        kw = {}
        if bias is not None:
            kw["bias"] = bias
        if accum_out is not None:
            kw["accum_out"] = accum_out
        return e.activation(out=out, in_=in_, func=func, scale=scale, **kw)
    return f


def TT(out, in0, in1, op):
    return lambda e: e.tensor_tensor(out=out, in0=in0, in1=in1, op=op)


def TS(out, in0, s1, s2=None, op0=ALU.mult, op1=None):
    if op1 is None:
        return lambda e: e.tensor_scalar(out=out, in0=in0, scalar1=s1, scalar2=None, op0=op0)
    return lambda e: e.tensor_scalar(out=out, in0=in0, scalar1=s1, scalar2=s2, op0=op0, op1=op1)


def STT(out, in0, scalar, in1, op0, op1):
    return lambda e: e.scalar_tensor_tensor(out=out, in0=in0, scalar=scalar, in1=in1, op0=op0, op1=op1)


def CP(out, in_):
    return lambda e: e.tensor_copy(out=out, in_=in_)


def MS(ap, val):
    return lambda e: e.memset(ap, val)


def RECIP(out, in_):
    return lambda e: e.reciprocal(out=out, in_=in_)


def DMA(out, in_):
    return lambda e: e.dma_start(out=out, in_=in_)


D = 1024
NIN = 6672
GATE_OFF = 4608
EPS = 1e-6
KAPPA = 0.25 * (128.0 ** -0.5)
CH_OFF = [0, 512, 1024, 1536, 2048, 2560, 3072, 3584, 4096, 4624, 5136, 5648, 6160]
C_NAQ, C_NAK, C_NAV, C_NAZ, C_MLQ, C_MLK, C_MLV, C_MLZ, C_MLO, C_GA0, C_GA1, C_GB0, C_GB1 = range(13)
C_WA, C_WB, C_WO0, C_WO1 = 13, 14, 15, 16
FM_B = {C_NAQ: 0, C_NAK: 4, C_MLQ: 8, C_MLK: 12, C_GA0: 16, C_GA1: 20, C_GB0: 24, C_GB1: 28}
FM_CW = 32
FM_G = 56
FM_N = 64
TB = {C_NAV: 0, C_NAZ: 512, C_MLV: 1024, C_MLZ: 1536, C_MLO: 2048}


LIMIT = [10 ** 9]


def build(R, dbg=False, stop=99):
    U = 2
    NR = U * R
    NTOK = NR * 64
    NT = NR // 8
    NS = NR // 2
    TPU = R // 8
    RING_T = 3
    RROWS = RING_T * 8
    nc = bass.Bass("TRN2", target_bir_lowering=False)

    def din(name, shape):
        return nc.dram_tensor(name, shape, F32, kind="ExternalInput").ap()

    xu = din("xu", [NTOK, D])
    meta = din("meta", [16, D])
    g_pre = din("g_pre", [1, D])
    w_in = din("w_in", [D, NIN])
    b_in = din("b_in", [1, NIN])
    tz = din("tz", [64, 8, 15, 64])
    cmask = din("cmask", [128, 64])
    conv_w = din("conv_w", [3, D])
    head_g = din("head_g", [1, 512])
    w_a = din("w_a", [512, D])
    w_b = din("w_b", [512, D])
    w_out = din("w_out", [D, D])
    g_post = din("g_post", [1, D])
    flags = din("flags", [128, 4])
    y = nc.dram_tensor("y", [NTOK, D], F32, kind="ExternalOutput").ap()
    wq = nc.dram_tensor("wq", [17, 128, 8, 512], BF16, kind="Internal").ap()
    cbs = nc.dram_tensor("cbs", [NS, 128, 4 * 129], BF16, kind="Internal").ap()
    dbg_out = {}
    if dbg:
        dbg_out["d_h"] = nc.dram_tensor("d_h", [NTOK, 512], F32, kind="ExternalOutput").ap()
        dbg_out["d_na"] = nc.dram_tensor("d_na", [NTOK, 512], F32, kind="ExternalOutput").ap()

    st = ExitStack()
    with st:
        def sb(name, shape, dt=F32):
            return st.enter_context(nc.sbuf_tensor(name, shape, dt))

        P = Prog(nc)
        P.limit = LIMIT[0]
        psb = [st.enter_context(nc.psum_tensor("ps%d" % i, [128, 512], F32)) for i in range(4)]
        psS = [st.enter_context(nc.psum_tensor("psS%d" % i, [128, 1024], F32)) for i in range(2)]
        psb += [psS[0][:, 0:512], psS[0][:, 512:1024], psS[1][:, 0:512], psS[1][:, 512:1024]]
        PK = ["ps%d" % i for i in range(8)]

        ident = sb("ident", [128, 128], BF16)
        identf = sb("identf", [128, 128])
        maskf = sb("maskf", [128, 128])
        maskb = sb("maskb", [128, 128])
        fl = sb("fl", [128, 4])
        fm = sb("fm", [128, FM_N])
        fmh = sb("fmh", [128, FM_N])
        cst = sb("cst", [128, 2])
        gbias = sb("gbias", [8, 2])
        wg = sb("wg", [128, 8, 16], BF16)
        etab = sb("etab", [128, 8, 16, 64], BF16)
        bias_bc = sb("bias_bc", [128, 2560])
        gpost_bc = sb("gpost_bc", [128, D])
        hg_bc = sb("hg_bc", [128, 512])
        SC = sb("SC", [128, NS + U, 24])
        EG = sb("EG", [128, NS + U, 8])
        kmT = sb("kmT", [128, 4, 16], BF16)
        vmp = sb("vmp", [128, 8, 65], BF16)
        prem = sb("prem", [128, 8])
        premk = sb("premk", [128, 4, 16])
        vmeta = sb("vmeta", [128, 4, 129], BF16)
        kmetaT = sb("kmetaT", [128, 4, 128], BF16)
        firstpre = sb("firstpre", [128, U, 8, 1])
        Cmeta = sb("Cmeta", [128, U, 4, 129])
        Cb = sb("Cb", [128, 4, 129])
        Cf = sb("Cf", [128, 4, 129])
        Cfb = sb("Cfb", [128, 4, 129], BF16)
        ws = [sb("ws%d" % i, [128, 8, 512], BF16) for i in range(3)]
        xs = [sb("xs%d" % i, [128, D]) for i in range(2)]
        xnb = [sb("xnb%d" % i, [128, D], BF16) for i in range(2)]
        xnT = [sb("xnT%d" % i, [128, 8, 512], BF16) for i in range(2)]
        st1 = sb("st1", [128, 16])
        st2 = sb("st2", [128, 32])
        KT = sb("KT", [128, 4, RROWS * 64], BF16)
        VP = sb("VP", [128, RROWS // 2, 8, 65], BF16)
        zs = sb("zs", [128, 4, 512], BF16)
        oG = sb("oG", [128, 4, 512], BF16)
        pre = [sb("pre0", [128, 514])] * 2
        cvt = [sb("cvt0", [128, 512])] * 2
        qkT = sb("qkT", [128, 8, 512], BF16)
        carry = sb("carry", [128, 8, 2])
        halo = sb("halo", [128, 8, 2])
        hadj = sb("hadj", [128, 8, 2])
        vml = sb("vml", [128, 4, 4, 129], BF16)
        kk = sb("kk", [128, 4, 128], BF16)
        Sf = [sb("Sf%d" % i, [128, 128], BF16) for i in range(4)]
        Sb_ = [sb("Sb%d" % i, [128, 128], BF16) for i in range(4)]
        uv = [sb("uv%d" % i, [128, 129], BF16) for i in range(2)]
        cbl = [sb("cbl%d" % i, [128, 4, 129], BF16) for i in range(2)]
        cbst = [sb("cbst%d" % i, [128, 4, 129], BF16) for i in range(2)]
        hbuf = sb("hbuf", [128, 512])
        h2 = sb("h2", [128, 512])
        gs2 = sb("gs2", [8, 8])
        onesg = sb("onesg", [8, 128])
        ypT = sb("ypT", [128, 8, 512], BF16)
        gsc = ypT[0:8, :, :].rearrange("p a b -> p (a b)").bitcast(F32).rearrange("p (a b) -> p a b", a=4)
        nao = [hbuf, h2]
        HBK = [("hbuf", h) for h in range(4)]
        NAOK = [HBK, ["h2"]]

        cont = fl[:, 0:1]
        ncont = fl[:, 1:2]

        for (dst, so) in ((0, 0), (4, 8), (8, 4), (12, 12)):
            src = w_in[:, GATE_OFF + so:GATE_OFF + so + 4].rearrange("(kc p) j -> p kc j", p=128)
            P.dma(DMA(wg[:, :, dst:dst + 4], src), writes=["wg"], semkey="setup_w", queue="pool")
        for c in (C_NAK, C_NAV, C_MLQ, C_MLK, C_MLV, C_NAQ, C_NAZ, C_MLZ, C_MLO, C_GA0, C_GA1, C_GB0, C_GB1):
            src = w_in[:, CH_OFF[c]:CH_OFF[c] + 512].rearrange("(kc p) j -> p kc j", p=128)
            P.dma(DMA(wq[c], src), writes=[("wq", c)], semkey=("wqc", c), queue="pool")
        P.dma(DMA(wq[C_WA].rearrange("p a b -> p (a b)").rearrange("p (fc n) -> p fc n", fc=4),
                  w_a.rearrange("(fc p) n -> p fc n", p=128)), writes=[("wq", C_WA)], semkey=("wqc", C_WA), queue="pool")
        P.dma(DMA(wq[C_WB].rearrange("p a b -> p (a b)").rearrange("p (fc n) -> p fc n", fc=4),
                  w_b.rearrange("(fc p) n -> p fc n", p=128)), writes=[("wq", C_WB)], semkey=("wqc", C_WB), queue="pool")
        for hf in range(2):
            P.dma(DMA(wq[C_WO0 + hf], w_out[:, hf * 512:(hf + 1) * 512].rearrange("(fc p) n -> p fc n", p=128)),
                  writes=[("wq", C_WO0 + hf)], semkey=("wqc", C_WO0 + hf), queue="pool")

        P.dma(DMA(fl[:], flags), writes=["fl"], semkey="setup_c")
        P.dma(DMA(bias_bc[:, 0:1024], b_in[:, 1024:2048].partition_broadcast(128)), writes=["bias_bc"], semkey="setup_c")
        P.dma(DMA(bias_bc[:, 1024:2560], b_in[:, 3072:4608].partition_broadcast(128)), writes=["bias_bc"], semkey="setup_c")
        P.dma(DMA(gpost_bc[:], g_post.partition_broadcast(128)), writes=["gpost_bc"], semkey="setup_c")
        P.dma(DMA(hg_bc[:], head_g.partition_broadcast(128)), writes=["hg_bc"], semkey="setup_c")
        P.pool(MS(identf[:], 0.0), writes=["identf"])
        P.pool(lambda e: e.affine_select(out=identf[:], in_=identf[:], pattern=[[-1, 128]], compare_op=ALU.not_equal,
                                         fill=1.0, base=0, channel_multiplier=1), reads=["identf"], writes=["identf"])
        P.dve(CP(ident[:], identf[:]), reads=["identf"], writes=["ident"])
        P.pool(MS(maskf[:], 1.0), writes=["maskf"])
        P.pool(lambda e: e.affine_select(out=maskf[:], in_=maskf[:], pattern=[[1, 128]], compare_op=ALU.is_ge,
                                         fill=0.0, base=0, channel_multiplier=-1), reads=["maskf"], writes=["maskf"])
        P.pool(MS(maskb[:], 1.0), writes=["maskb"])
        P.pool(lambda e: e.affine_select(out=maskb[:], in_=maskb[:], pattern=[[-1, 128]], compare_op=ALU.is_ge,
                                         fill=0.0, base=0, channel_multiplier=1), reads=["maskb"], writes=["maskb"])
        P.pool(MS(cst[:, 0:1], -0.5), writes=["cst"])
        P.pool(MS(onesg[:], 1.0), writes=["onesg"])
        P.pool(MS(Cb[:], 0.0), writes=["Cb"])
        P.pool(MS(VP[:, :, :, 64:65], 1.0), writes=[("VP", i) for i in range(RING_T)])
        P.pool(MS(vmp[:], 0.0), writes=["vmp"])
        P.pool(MS(vmp[0:16, :, 64:65], 1.0), reads=["vmp"], writes=["vmp"])
        P.pool(MS(vml[:, :, :, 128:129], 1.0), writes=[("vml", j) for j in range(4)])
        P.pool(MS(carry[:], 0.0), writes=["carry"])
        P.pool(MS(vmeta[:], 0.0), writes=["vmeta"])
        P.pool(MS(kmetaT[:], 0.0), writes=["kmetaT"])

        stg = ExitStack()
        rowst = stg.enter_context(nc.sbuf_tensor("rowst", [64, 128], F32))
        tzs = stg.enter_context(nc.sbuf_tensor("tzs", [128, 4, 16, 64], F32))
        cmk = stg.enter_context(nc.sbuf_tensor("cmk", [128, 64], F32))
        xnTm = stg.enter_context(nc.sbuf_tensor("xnTm", [128, 8, 128], BF16))
        P.pool(MS(xnTm[:], 0.0), writes=["xnTm"])
        if True:
            P.pool(MS(rowst[:], 0.0), writes=["rowst"])
            for c, col in FM_B.items():
                P.dma(DMA(rowst[col:col + 4, :], b_in[0, CH_OFF[c]:CH_OFF[c] + 512].rearrange("(c p) -> c p", p=128)),
                      writes=["rowst"], semkey="setup_c")
            for j in range(3):
                P.dma(DMA(rowst[FM_CW + 8 * j:FM_CW + 8 * j + 8, :], conv_w[j, :].rearrange("(c p) -> c p", p=128)),
                      writes=["rowst"], semkey="setup_c")
            P.dma(DMA(rowst[FM_G:FM_G + 8, :], g_pre[0, :].rearrange("(c p) -> c p", p=128)), writes=["rowst"], semkey="setup_c")
            P.pe(MM(psb[0][:, 0:FM_N], rowst[0:FM_N, :], identf[0:FM_N, 0:FM_N]), reads=["rowst", "identf"], writes=[PK[0]])
            P.dve(CP(fm[:], psb[0][:, 0:FM_N]), reads=[PK[0]], writes=["fm"])
            P.dve(TS(fmh[:], fm[:], 0.5), reads=["fm"], writes=["fmh"])
            P.dve(TS(fmh[:, 0:4], fm[:, 0:4], 0.125), reads=["fm", "fmh"], writes=["fmh"])
            P.dve(TT(fmh[:, 56:64], fm[:, FM_CW:FM_CW + 8], fm[:, FM_CW + 8:FM_CW + 16], ALU.add), reads=["fm", "fmh"], writes=["fmh"])
            P.dve(TT(fmh[:, 56:64], fmh[:, 56:64], fm[:, FM_CW + 16:FM_CW + 24], ALU.add), reads=["fm", "fmh"], writes=["fmh"])
            P.dve(TT(fmh[:, 56:64], fmh[:, 56:64], fm[:, 8:16], ALU.mult), reads=["fm", "fmh"], writes=["fmh"])
            for (dst, so, col) in ((0, 0, 0), (4, 8, 0), (0, 4, 1), (4, 12, 1)):
                P.dma(DMA(gbias[dst:dst + 4, col:col + 1], b_in[0, GATE_OFF + so:GATE_OFF + so + 4].rearrange("(p o) -> p o", o=1)),
                      writes=["gbias"], semkey="setup_c")
            P.dve(TS(gbias[:, 1:2], gbias[:, 1:2], -1.0), reads=["gbias"], writes=["gbias"])
            P.dve(TS(hg_bc[:], hg_bc[:], 0.5), reads=["hg_bc"], writes=["hg_bc"])
            P.dma(DMA(cmk[:], cmask), writes=["cmk"], semkey="setup_c")
            for hq in range(2):
                sk = "setup_c" if hq == 0 else "setup_c2"
                hs = slice(4 * hq, 4 * hq + 4)
                P.pool(MS(tzs[:, :, 14:16, :], 0.0), writes=["tzs"])
                P.dma(DMA(tzs[0:64, :, 0:14, :], tz[:, hs, 0:14, :]), writes=["tzs"], semkey=sk)
                P.dma(DMA(tzs[64:128, :, 0:14, :], tz[:, hs, 1:15, :]), writes=["tzs"], semkey=sk)
                P.dma(DMA(tzs[64:128, :, 14, :], tz[:, hs, 3, :]), writes=["tzs"], semkey=sk)
                P.dma(DMA(tzs[0:64, :, 15, :], tz[:, hs, 10, :]), writes=["tzs"], semkey=sk)
                for h4 in range(4):
                    h = 4 * hq + h4
                    P.act(ACTF(tzs[:, h4], tzs[:, h4], AF.Exp), reads=["tzs"], writes=["tzs"])
                    P.dve(TT(etab[:, h], tzs[:, h4], cmk[:, :].unsqueeze(1).to_broadcast([128, 16, 64]), ALU.mult),
                          reads=["tzs", "cmk"], writes=["etab"])
            P.dve(MS(etab[0:64, :, 14, :], 0.0), reads=["etab"], writes=["etab"])
            P.dve(MS(etab[64:128, :, 15, :], 0.0), reads=["etab"], writes=["etab"])
            P.dve(CP(st1[:, 0:1], etab[:, 7, 13, 0:1]), reads=["etab", "fm", "fmh"], writes=["st1a"])

        wstate = {"i": 0}

        def wload(c):
            i = wstate["i"]
            wstate["i"] += 1
            slot = i % 3
            key = ("ws", slot)
            P.dma(DMA(ws[slot][:], wq[c]), reads=[("wq", c)], writes=[key], semkey=("ws", slot))
            return ws[slot], key

        ln_i = {"i": 0}

        def pTv():
            return psb[3][:].bitcast(BF16).rearrange("p (a b) -> p a b", a=8)

        def ln_A(src_ap, meta_rows=None):
            i = ln_i["i"]
            ln_i["i"] += 1
            b = i % 2
            xk, nk = ("xs", b), ("xnb", b)
            if meta_rows is None:
                npart = 128
                P.dma(DMA(xs[b][:], src_ap), writes=[xk], semkey=("xs", b))
            else:
                npart = meta_rows
                P.dma(DMA(xs[b][0:npart, :], src_ap), writes=[xk], semkey=("xs", b))
            c0 = 4 * b
            ka, kb, kc_ = ("ln_a", b), ("ln_b", b), ("ln_c", b)
            P.act(ACTF(xnb[b][0:npart, :], xs[b][0:npart, :], AF.Square, accum_out=st2[0:npart, c0:c0 + 1]), reads=[xk], writes=[nk, ka])
            P.dve(TS(st2[0:npart, c0 + 1:c0 + 2], st2[0:npart, c0:c0 + 1], 1.0 / D, EPS, ALU.mult, ALU.add), reads=[ka], writes=[kb])
            P.pool(TT(st2[0:npart, c0 + 2:c0 + 3], st2[0:npart, c0 + 1:c0 + 2], cst[0:npart, 0:1], ALU.pow), reads=[kb, "cst"], writes=[kc_])
            P.dve(TS(xnb[b][0:npart, :], xs[b][0:npart, :], st2[0:npart, c0 + 2:c0 + 3]), reads=[xk, kc_, nk], writes=[nk])
            return (b, npart)

        def ln_B(tok, dst_ap, dkey):
            b, npart = tok
            nk = ("xnb", b)
            pT = pTv()
            for kc in range(8):
                P.pe(TR(pT[:, kc, 0:npart], xnb[b][0:npart, kc * 128:(kc + 1) * 128], ident[0:npart, 0:npart]),
                     reads=[nk, "ident"], writes=[PK[3]])
            P.dve(TT(dst_ap[:, :, 0:npart], pT[:, :, 0:npart],
                     fm[:, FM_G:FM_G + 8].unsqueeze(2).to_broadcast([128, 8, npart]), ALU.mult),
                  reads=[PK[3], "fm"], writes=[dkey])

        def load_norm_T(src_ap, nsub, dst, dkey, meta_rows=None):
            for j in range(nsub):
                if meta_rows is None:
                    tok = ln_A(src_ap[j * 128:(j + 1) * 128, :])
                else:
                    tok = ln_A(src_ap, meta_rows)
                ln_B(tok, dst[:, :, j * 128:(j + 1) * 128], dkey)

        pbank = {"i": 0}

        def next_bank():
            b = pbank["i"] % 3
            pbank["i"] += 1
            return b

        def proj_F(xT, xkey, ntok, c, handler):
            wsl, wkey = wload(c)
            for fc in range(4):
                b = next_bank()
                for kc in range(8):
                    P.pe(MM(psb[b][:, 0:ntok], wsl[:, kc, fc * 128:(fc + 1) * 128], xT[:, kc, 0:ntok], kc == 0, kc == 7),
                         reads=[wkey, xkey], writes=[PK[b]])
                handler(fc, psb[b][:, 0:ntok], PK[b])

        def proj_T(xT, xkey, nsub, c, handler):
            wsl, wkey = wload(c)
            for j in range(nsub):
                b = next_bank()
                for kc in range(8):
                    P.pe(MM(psb[b][:, :], xT[:, kc, j * 128:(j + 1) * 128], wsl[:, kc, :], kc == 0, kc == 7),
                         reads=[wkey, xkey], writes=[PK[b]])
                handler(j, psb[b][:, :], PK[b])

        def gates_A(xT, xkey, ntok, is_meta=False):
            nch = ntok // 128
            bI = next_bank()
            for kc in range(8):
                P.pe(MM(psb[bI][0:8, 0:ntok], wg[:, kc, 0:8], xT[:, kc, 0:ntok], kc == 0, kc == 7), reads=["wg", xkey], writes=[PK[bI]])
            R0 = gsc[:, 0, 0:ntok]
            R1 = gsc[:, 1, 0:ntok]
            R2 = gsc[:, 2, 0:ntok]
            R3 = gsc[:, 3, 0:ntok]
            K0, K1, K2, K3 = "g_r0", "g_r1", "g_r2", "g_r3"
            P.act(ACTF(R0, psb[bI][0:8, 0:ntok], AF.Exp, bias=gbias[:, 0:1]), reads=[PK[bI], "gbias"], writes=[K0])
            bF = next_bank()
            for kc in range(8):
                P.pe(MM(psb[bF][0:8, 0:ntok], wg[:, kc, 8:16], xT[:, kc, 0:ntok], kc == 0, kc == 7), reads=["wg", xkey], writes=[PK[bF]])
            P.act(ACTF(R1, psb[bF][0:8, 0:ntok], AF.Exp, bias=gbias[:, 1:2], scale=-1.0), reads=[PK[bF], "gbias"], writes=[K1])
            if is_meta:
                P.dve(MS(gsc[:, 1, 16:ntok], 0.0), reads=[K1], writes=[K1])
                P.dve(MS(gsc[:, 0, 16:ntok], 0.0), reads=[K0], writes=[K0])
            P.dve(TS(R1, R1, 1.0, None, ALU.add), reads=[K1], writes=[K1])
            for ci in range(nch):
                sl = slice(ci * 128, (ci + 1) * 128)
                P.dve(lambda e, sl=sl: e.tensor_tensor_scan(out=gsc[:, 2, sl], data0=gsc[:, 1, sl], data1=onesg[:, :], initial=1.0,
                                                            op0=ALU.mult, op1=ALU.mult), reads=[K1, "onesg"], writes=[K2])
            P.dve(RECIP(R3, R2), reads=[K2], writes=[K3])
            for ci in range(nch):
                last = ci * 128 + 127
                P.dve(CP(gs2[:, ci:ci + 1], gsc[:, 3, last:last + 1]), reads=[K3], writes=["g_eg"])
            for ci in range(nch):
                sl = slice(ci * 128, (ci + 1) * 128)
                last = ci * 128 + 127
                P.dve(STT(gsc[:, 1, sl], gsc[:, 3, sl], gsc[:, 2, last:last + 1], gsc[:, 1, sl], ALU.mult, ALU.mult),
                      reads=[K3, K2, K1], writes=[K1])
            P.dve(TT(R3, R2, R1, ALU.subtract), reads=[K2, K1, K3, "g_eg"], writes=[K3])
            P.dve(STT(R2, R3, fl[0:8, 2:3], R1, ALU.mult, ALU.add), reads=[K3, K1, "fl", K2], writes=[K2])
            P.dve(STT(R0, R0, KAPPA, R2, ALU.mult, ALU.mult), reads=[K0, K2], writes=[K0])
            for ci in range(nch):
                sl = slice(ci * 128, (ci + 1) * 128)
                P.dve(TS(gsc[:, 3, sl], gsc[:, 0, sl], gs2[:, ci:ci + 1]), reads=[K0, "g_eg", K3], writes=[K3])
            return nch, (K0, K3, K2)

        def gates_B(tokg, chunks):
            nch, (K0, K3, K2) = tokg
            bS = next_bank()
            for ci in range(nch):
                sl = slice(ci * 128, (ci + 1) * 128)
                for k, (row, key) in enumerate(((0, K0), (3, K3), (2, K2))):
                    P.pe(MM(psb[bS][:, ci * 32 + k * 8:ci * 32 + k * 8 + 8], gsc[:, row, sl], identf[0:8, 0:8]),
                         reads=[key, "identf"], writes=[PK[bS]])
                P.pe(MM(psb[bS][:, ci * 32 + 24:ci * 32 + 32], gs2[:, ci:ci + 1].to_broadcast([8, 128]), identf[0:8, 0:8]),
                     reads=["g_eg", "identf"], writes=[PK[bS]])
            for ci in range(nch):
                slot = chunks[ci]
                P.dve(CP(SC[:, slot, :], psb[bS][:, ci * 32:ci * 32 + 24]), reads=[PK[bS]], writes=[("SC", slot)])
                P.dve(CP(EG[:, slot, :], psb[bS][:, ci * 32 + 24:ci * 32 + 32]), reads=[PK[bS]], writes=[("EG", slot)])

        cv_i = {"i": 0}

        def conv_taps(pr, pk, cv, ck, n, fcg):
            w0 = fm[:, FM_CW + fcg:FM_CW + fcg + 1]
            w1 = fm[:, FM_CW + 8 + fcg:FM_CW + 8 + fcg + 1]
            w2 = fm[:, FM_CW + 16 + fcg:FM_CW + 16 + fcg + 1]
            P.dve(TS(cv[:, 0:n], pr[:, 1:1 + n], w1), reads=[pk, "fm"], writes=[ck])
            P.dve(STT(cv[:, 0:n], pr[:, 0:n], w0, cv[:, 0:n], ALU.mult, ALU.add), reads=[pk, ck, "fm"], writes=[ck])
            P.dve(STT(cv[:, 0:n], pr[:, 2:2 + n], w2, cv[:, 0:n], ALU.mult, ALU.add), reads=[pk, ck, "fm"], writes=[ck])
            P.act(ACTF(pr[:, 1:1 + n], cv[:, 0:n], AF.Tanh, scale=0.5), reads=[ck], writes=[pk])

        def conv_silu(fcg, ps_ap, pkey, ntok, bias_col, lh_ap, rh_ap, dst_ap, dst_key, save_first=None, save_last=None):
            assert ntok == 512
            i = cv_i["i"] % 2
            cv_i["i"] += 1
            if i == 0:
                cv, ck, th, tk = cvt[0][:, 0:512], ("cvt", 0), gA, "gA"
            else:
                cv, ck, th, tk = pre[0][:, 0:512], ("pre", 0), gB, "gB"
            w0 = fm[:, FM_CW + fcg:FM_CW + fcg + 1]
            w1 = fm[:, FM_CW + 8 + fcg:FM_CW + 8 + fcg + 1]
            w2 = fm[:, FM_CW + 16 + fcg:FM_CW + 16 + fcg + 1]
            beta = fmh[:, 56 + fcg:57 + fcg]
            P.act(ACTF(cv, ps_ap, AF.Identity, bias=beta, scale=w1), reads=[pkey, "fm", "fmh"], writes=[ck])
            P.dve(STT(cv[:, 1:512], ps_ap[:, 0:511], w0, cv[:, 1:512], ALU.mult, ALU.add), reads=[pkey, ck, "fm"], writes=[ck])
            P.dve(STT(cv[:, 0:511], ps_ap[:, 1:512], w2, cv[:, 0:511], ALU.mult, ALU.add), reads=[pkey, ck, "fm"], writes=[ck])
            P.dve(STT(cv[:, 0:1], hadj[:, fcg, 0:1], w0, cv[:, 0:1], ALU.mult, ALU.add), reads=["hadj", ck, "fm"], writes=[ck])
            P.dve(STT(cv[:, 511:512], hadj[:, fcg, 1:2], w2, cv[:, 511:512], ALU.mult, ALU.add), reads=["hadj", ck, "fm"], writes=[ck])
            if save_first is not None:
                P.dve(TS(save_first, ps_ap[:, 0:1], fm[:, bias_col:bias_col + 1], None, ALU.add), reads=[pkey, "fm"], writes=["carry"])
            if save_last is not None:
                P.dve(TS(save_last, ps_ap[:, 511:512], fm[:, bias_col:bias_col + 1], None, ALU.add), reads=[pkey, "fm"], writes=["carry"])
            P.act(ACTF(th[:, :], cv, AF.Tanh, scale=0.5), reads=[ck], writes=[tk])
            P.dve(STT(dst_ap, th[:, :], 1.0, cv, ALU.add, ALU.mult), reads=[tk, ck], writes=[dst_key])

        def state_prep(kT_ap, kkeys, v_ap, vkey, slot, dirn, banks=None):
            pT = pTv()
            for h in range(4):
                P.pe(TR(pT[:, h, :], kT_ap[:, h, :], ident[:, :]), reads=list(kkeys) + ["ident"], writes=[PK[3]])
            P.act(ACTF(kk[:, :, :], pT[:, 0:4, :], AF.Copy), reads=[PK[3]], writes=["kk"])
            if banks is None:
                banks = [next_bank(), next_bank()]
            outs = []
            for h in range(4):
                u = uv[h % 2]
                uk = ("uv", h % 2)
                col = 8 + dirn * 4 + h
                P.act(ACTF(u[:, :], v_ap[:, h, :], AF.Copy, scale=SC[:, slot, col:col + 1]), reads=[vkey, ("SC", slot)], writes=[uk])
                bank = banks[h // 2]
                cols = slice((h % 2) * 129, (h % 2) * 129 + 129)
                P.pe(MM(psb[bank][:, cols], kk[:, h, :], u[:, :]), reads=["kk", uk], writes=[PK[bank]])
                outs.append((bank, cols))
            return outs

        def state_apply(Cst, ckey, outs, slot, dirn):
            for h in range(4):
                bank, cols = outs[h]
                P.dve(STT(Cst[:, h, :], Cst[:, h, :], EG[:, slot, dirn * 4 + h:dirn * 4 + h + 1], psb[bank][:, cols], ALU.mult, ALU.add),
                      reads=[ckey, ("EG", slot), PK[bank]], writes=[ckey])

        def state_update(Cst, ckey, kT_ap, kkeys, v_ap, vkey, slot, dirn, bank):
            outs = state_prep(kT_ap, kkeys, v_ap, vkey, slot, dirn)
            state_apply(Cst, ckey, outs, slot, dirn)

        if stop <= 0:
            P.finalize(st)
            return nc
        load_norm_T(meta, 1, xnTm, "xnTm", meta_rows=16)

        def h_kmeta(fc, ps_ap, pkey):
            P.act(ACTF(kmT[:, fc, :], ps_ap[:, 0:16], AF.Identity, bias=fm[:, FM_B[C_NAK] + fc:FM_B[C_NAK] + fc + 1]),
                  reads=[pkey, "fm"], writes=["kmT"])
        proj_F(xnTm, "xnTm", 128, C_NAK, h_kmeta)

        def h_vmeta(j, ps_ap, pkey):
            P.dve(TT(vmp[0:16, :, 0:64], ps_ap[0:16, :].rearrange("p (h d) -> p h d", h=8),
                     bias_bc[0:16, TB[C_NAV]:TB[C_NAV] + 512].rearrange("p (h d) -> p h d", h=8), ALU.add),
                  reads=[pkey, "bias_bc"], writes=["vmp"])
        proj_T(xnTm, "xnTm", 1, C_NAV, h_vmeta)

        def h_qmeta(fc, ps_ap, pkey):
            P.act(ACTF(prem[:, fc:fc + 1], ps_ap[:, 15:16], AF.Identity, bias=fm[:, FM_B[C_MLQ] + fc:FM_B[C_MLQ] + fc + 1]),
                  reads=[pkey, "fm"], writes=["prem"])
        proj_F(xnTm, "xnTm", 128, C_MLQ, h_qmeta)

        def h_kmeta2(fc, ps_ap, pkey):
            P.act(ACTF(premk[:, fc, :], ps_ap[:, 0:16], AF.Identity, bias=fm[:, FM_B[C_MLK] + fc:FM_B[C_MLK] + fc + 1]),
                  reads=[pkey, "fm"], writes=["premk"])
            P.pool(CP(prem[:, 4 + fc:5 + fc], premk[:, fc, 15:16]), reads=["premk"], writes=["prem"])
        proj_F(xnTm, "xnTm", 128, C_MLK, h_kmeta2)

        def h_vmeta2(j, ps_ap, pkey):
            P.dve(TT(vmeta[0:16, :, 0:128], ps_ap[0:16, :].rearrange("p (h d) -> p h d", h=4),
                     bias_bc[0:16, TB[C_MLV]:TB[C_MLV] + 512].rearrange("p (h d) -> p h d", h=4), ALU.add),
                  reads=[pkey, "bias_bc"], writes=["vmeta"])
            P.dve(MS(vmeta[0:16, :, 128:129], 1.0), reads=["vmeta"], writes=["vmeta"])
        proj_T(xnTm, "xnTm", 1, C_MLV, h_vmeta2)
        gates_B(gates_A(xnTm, "xnTm", 128, is_meta=True), [NS])
        P.dve(CP(SC[:, NS + 1, :], SC[:, NS, :]), reads=[("SC", NS)], writes=[("SC", NS + 1)])
        P.dve(CP(EG[:, NS + 1, :], EG[:, NS, :]), reads=[("EG", NS)], writes=[("EG", NS + 1)])

        if stop <= 1:
            P.finalize(st)
            return nc
        stg.close()
        ftmp = sb("ftmp", [128, 512])
        Praw = [sb("Praw%d" % i, [128, 640], BF16) for i in range(2)]
        Pn = [sb("Pn%d" % i, [128, 640], BF16) for i in range(2)]
        Pm = sb("Pm", [128, 2, 512], BF16)
        gA = sb("gA", [128, 512], BF16)
        gB = sb("gB", [128, 512], BF16)
        gAT = sb("gAT", [128, 4, 512], BF16)
        gBT = sb("gBT", [128, 4, 512], BF16)
        tgA = sb("tgA", [128, 512])
        tgB = sb("tgB", [128, 512])
        osb = [sb("osb%d" % i, [128, D]) for i in range(2)]
        X1K = ["ftmp", ("Praw", 0), ("Praw", 1), ("Pn", 0), ("Pn", 1), ("Pm", 0), ("Pm", 1), "gA", "gB", "gAT", "gBT", "tgA", "tgB",
               ("osb", 0), ("osb", 1)]
        for eng in ("dve", "act", "pool"):
            P.ew(eng, MS(st1[:, 5:6] if eng == "dve" else st1[:, 6:7], 0.0) if eng != "act" else ACTF(st1[:, 7:8], fl[:, 0:1], AF.Copy),
                 reads=["fl", "etab", "fm", "fmh", "gbias", "kmT", "vmp", "prem", "premk", "vmeta", ("SC", NS), ("EG", NS)],
                 writes=X1K + ["xnTm", "tzs", "rowst", "cmk"])

        P.pool(MS(Pm[:], 0.0), reads=[("Pm", 0), ("Pm", 1)], writes=[("Pm", 0), ("Pm", 1)])

        def meta_state(u):
            P.pool(MS(Cmeta[:, u], 0.0), writes=[("Cmeta", u)])
            for fc in range(4):
                i = cv_i["i"] % 2
                cv_i["i"] += 1
                pk, ck = ("pre", 0), ("cvt", 0)
                pr, cv = pre[i], cvt[i]
                P.pool(MS(pr[:, 0:1], 0.0), writes=[pk])
                P.pool(CP(pr[:, 1:17], premk[:, fc, :]), reads=["premk", pk], writes=[pk])
                P.pool(CP(pr[:, 17:18], firstpre[:, u, 4 + fc, :]), reads=[("firstpre", u), pk], writes=[pk])
                conv_taps(pr, pk, cv, ck, 16, 4 + fc)
                P.dve(STT(kmetaT[:, fc, 0:16], pr[:, 1:17], 1.0, cv[:, 0:16], ALU.add, ALU.mult), reads=[pk, ck], writes=["kmetaT"])
            state_update(Cmeta[:, u], ("Cmeta", u), kmetaT, ["kmetaT"], vmeta, "vmeta", NS + u, 0, 4)

        def tile_src(T):
            return xu[T * 512:(T + 1) * 512, :]

        def small_cols(wsl, wkey, xT, xkey, col):
            hb = next_bank()
            for fc in range(4):
                for kc in range(8):
                    P.pe(MM(psb[hb][:, fc:fc + 1], wsl[:, kc, fc * 128:(fc + 1) * 128], xT[:, kc, col:col + 1], kc == 0, kc == 7),
                         reads=[wkey, xkey], writes=[PK[hb]])
            return hb

        vml2 = KT[:, 0:2, :].rearrange("p a b -> p (a b)")[:, 0:2064].rearrange("p (j h n) -> p j h n", j=4, h=4)
        P.pool(MS(vml2[:, :, :, 128:129], 1.0), writes=[("vml2", j) for j in range(4)])

        def p1_bufs(T):
            if T % 2 == 0:
                return 4, vml, "vml"
            return 0, vml2, "vml2"

        def p1_P(T):
            xt, xkey = xnT[T % 2], ("xnT", T % 2)
            u_of = T // TPU
            first_tile_of_unit = (T % TPU == 0)
            last_tile_of_unit = (T % TPU == TPU - 1)
            fcb, vb, vname = p1_bufs(T)
            tokg = gates_A(xt, xkey, 512)
            yield
            wsl, wkey = wload(C_MLK)
            bcol = FM_B[C_MLK]
            if T > 0:
                hb = small_cols(wsl, wkey, xnT[(T - 1) % 2], ("xnT", (T - 1) % 2), 511)
                P.dve(TT(halo[:, 4:8, 0], psb[hb][:, 0:4], fm[:, bcol:bcol + 4], ALU.add), reads=[PK[hb], "fm"], writes=["halo"])
            if first_tile_of_unit:
                if T == 0:
                    P.dve(CP(halo[:, 4:8, 0], prem[:, 4:8]), reads=["prem"], writes=["halo"])
                else:
                    P.dve(TS(halo[:, 4:8, 0], halo[:, 4:8, 0], cont), reads=["halo", "fl"], writes=["halo"])
                    P.dve(STT(halo[:, 4:8, 0], prem[:, 4:8], ncont, halo[:, 4:8, 0], ALU.mult, ALU.add), reads=["halo", "prem", "fl"], writes=["halo"])
            if T == NT - 1:
                P.dve(MS(halo[:, 4:8, 1], 0.0), writes=["halo"])
            elif last_tile_of_unit:
                P.dve(TS(halo[:, 4:8, 1], carry[:, 4:8, 1], cont), reads=["carry", "fl"], writes=["halo"])
            else:
                P.dve(CP(halo[:, 4:8, 1], carry[:, 4:8, 1]), reads=["carry"], writes=["halo"])
            P.dve(TT(hadj[:, 4:8, :], halo[:, 4:8, :], fm[:, bcol:bcol + 4].unsqueeze(2).to_broadcast([128, 4, 2]), ALU.subtract),
                  reads=["halo", "fm"], writes=["hadj"])
            for fc in range(4):
                b = next_bank()
                for kc in range(8):
                    P.pe(MM(psb[b][:, :], wsl[:, kc, fc * 128:(fc + 1) * 128], xt[:, kc, :], kc == 0, kc == 7), reads=[wkey, xkey], writes=[PK[b]])
                conv_silu(4 + fc, psb[b][:, :], PK[b], 512, bcol + fc, halo[:, 4 + fc, 0:1], halo[:, 4 + fc, 1:2],
                          qkT[:, fcb + fc, :], ("qkT", fcb + fc), save_first=carry[:, 4 + fc, 1:2])
                yield
            if first_tile_of_unit:
                P.pool(CP(firstpre[:, u_of, 4:8, 0], carry[:, 4:8, 1]), reads=["carry"], writes=[("firstpre", u_of)])
            wsv, wkv = wload(C_MLV)
            for j in range(4):
                b = next_bank()
                for kc in range(8):
                    P.pe(MM(psb[b][:, :], xt[:, kc, j * 128:(j + 1) * 128], wsv[:, kc, :], kc == 0, kc == 7), reads=[wkv, xkey], writes=[PK[b]])
                P.dve(TT(vb[:, j, :, 0:128], psb[b][:, :].rearrange("p (h d) -> p h d", h=4),
                         bias_bc[:, TB[C_MLV]:TB[C_MLV] + 512].rearrange("p (h d) -> p h d", h=4), ALU.add),
                      reads=[PK[b], "bias_bc"], writes=[(vname, j)])
                yield
            gates_B(tokg, [T * 4 + c for c in range(4)])
            yield

        def p1_S(T):
            last_tile_of_unit = (T % TPU == TPU - 1)
            fcb, vb, vname = p1_bufs(T)
            KK4 = [("qkT", fcb + h) for h in range(4)]

            def pb(slot):
                return [4 + 2 * (slot % 2), 5 + 2 * (slot % 2)]
            outs_next = state_prep(qkT[:, fcb:fcb + 4, 3 * 128:4 * 128], KK4, vb[:, 3], (vname, 3), T * 4 + 3, 1, banks=pb(T * 4 + 3))
            for c in range(3, -1, -1):
                slot = T * 4 + c
                outs = outs_next
                ltok = None
                if T - 2 >= 0:
                    T2 = T - 2
                    jj = c
                    ltok = ln_A(xu[T2 * 512 + jj * 128:T2 * 512 + (jj + 1) * 128, :])
                if last_tile_of_unit and c == 3 and T != NT - 1:
                    P.dve(TS(Cb[:], Cb[:], cont), reads=["Cb", "fl"], writes=["Cb"])
                i = slot % 2
                P.act(ACTF(cbst[i][:], Cb[:], AF.Copy), reads=["Cb"], writes=[("cbst", i)])
                P.dma(DMA(cbs[slot].rearrange("p (h n) -> p h n", h=4), cbst[i][:]), reads=[("cbst", i)], writes=[("cbs", slot)], semkey=("cbst", i),
                      queue="pool")
                if c > 0:
                    outs_next = state_prep(qkT[:, fcb:fcb + 4, (c - 1) * 128:c * 128], KK4, vb[:, c - 1], (vname, c - 1), slot - 1, 1, banks=pb(slot - 1))
                state_apply(Cb, "Cb", outs, slot, 1)
                if ltok is not None:
                    ln_B(ltok, xnT[T2 % 2][:, :, jj * 128:(jj + 1) * 128], ("xnT", T2 % 2))
                yield

        load_norm_T(tile_src(NT - 1), 4, xnT[(NT - 1) % 2], ("xnT", (NT - 1) % 2))
        if NT > 1:
            load_norm_T(tile_src(NT - 2), 4, xnT[(NT - 2) % 2], ("xnT", (NT - 2) % 2))
        for _ in p1_P(NT - 1):
            pass
        for T in range(NT - 1, -1, -1):
            gS = p1_S(T)
            gP = p1_P(T - 1) if T > 0 else iter(())
            for k in range(4):
                next(gS, None)
                for _ in range(3 if k == 0 else 2):
                    next(gP, None)
            for _ in gS:
                pass
            for _ in gP:
                pass
            if T % TPU == 0:
                meta_state(T // TPU)
        for eng in ("act", "dve", "pool"):
            P.ew(eng, MS(st1[:, 5:6] if eng == "dve" else st1[:, 6:7], 0.0) if eng != "act" else ACTF(st1[:, 7:8], fl[:, 0:1], AF.Copy),
                 reads=["fl"], writes=[("vml2", j) for j in range(4)] + [("KT", i) for i in range(RING_T)])
        if stop <= 2:
            P.finalize(st)
            return nc
        def kv_proj(T):
            xt, xkey = xnT[T % 2], ("xnT", T % 2)
            rt = T % RING_T

            def h_k(fc, ps_ap, pkey):
                P.act(ACTF(KT[:, fc, rt * 512:(rt + 1) * 512], ps_ap, AF.Identity, bias=fm[:, FM_B[C_NAK] + fc:FM_B[C_NAK] + fc + 1]),
                      reads=[pkey, "fm"], writes=[("KT", rt)])
            proj_F(xt, xkey, 512, C_NAK, h_k)

            def h_v(j, ps_ap, pkey):
                P.dve(TT(VP[:, rt * 4 + j, :, 0:64], ps_ap.rearrange("p (h d) -> p h d", h=8),
                         bias_bc[:, TB[C_NAV]:TB[C_NAV] + 512].rearrange("p (h d) -> p h d", h=8), ALU.add),
                      reads=[pkey, "bias_bc"], writes=[("VP", rt)])
            proj_T(xt, xkey, 4, C_NAV, h_v)

        def ring_row(r):
            return r % RROWS

        def na_q_z(T):
            xt, xkey = xnT[T % 2], ("xnT", T % 2)
            P.pool(MS(qkT[64:128, 0:4, :], 0.0), writes=[("qkT", f) for f in range(4)])
            P.pool(MS(qkT[0:64, 4:8, :], 0.0), writes=[("qkT", 4 + f) for f in range(4)])

            def h_q(fc, ps_ap, pkey):
                bq = FM_B[C_NAQ] + fc
                P.act(ACTF(qkT[0:64, fc, :], ps_ap[0:64, :], AF.Identity, bias=fmh[0:64, bq:bq + 1], scale=0.125),
                      reads=[pkey, "fmh", ("qkT", fc)], writes=[("qkT", fc)])
                P.act(ACTF(qkT[64:128, 4 + fc, :], ps_ap[64:128, :], AF.Identity, bias=fmh[64:128, bq:bq + 1], scale=0.125),
                      reads=[pkey, "fmh", ("qkT", 4 + fc)], writes=[("qkT", 4 + fc)])
            proj_F(xt, xkey, 512, C_NAQ, h_q)

            def h_z(j, ps_ap, pkey):
                (za, zak), (zb_, zbk) = ((ftmp, ["ftmp"]), (tgA, ["tgA"])) if j % 2 == 0 else ((tgB, ["tgB"]), (h2, ["h2"]))
                P.dve(TT(za[:, :], ps_ap, bias_bc[:, TB[C_NAZ]:TB[C_NAZ] + 512], ALU.add), reads=[pkey, "bias_bc"], writes=zak)
                P.act(ACTF(zb_[:, :], za[:, :], AF.Tanh, scale=0.5), reads=zak, writes=zbk)
                P.pool(TS(zb_[:, :], zb_[:, :], 1.0, 0.5, ALU.add, ALU.mult), reads=zbk, writes=zbk)
                P.pool(TT(zs[:, j, :], za[:, :], zb_[:, :], ALU.mult), reads=zak + zbk, writes=[("zs", j)])
            proj_T(xt, xkey, 4, C_NAZ, h_z)

        def q_ap(h, qcol):
            fc, hh = h // 2, h % 2
            return qkT[:, 4 * hh + fc, qcol:qcol + 64], ("qkT", 4 * hh + fc)

        def na_meta_scores(lrt, slot):
            qcol = lrt * 64
            ms = psb[3][0:16, :]
            for h in range(8):
                qap, qk = q_ap(h, qcol)
                P.pe(MM(ms[:, h * 64:(h + 1) * 64], kmT[:, h // 2, :], qap), reads=["kmT", qk], writes=[PK[3]])
            P.act(ACTF(Pm[0:16, slot, :], ms, AF.Exp), reads=[PK[3]], writes=[("Pm", slot)])

        na_i = {"i": 0, "cur": 0}
        PB = [Praw[0], Praw[1], Pn[0], Pn[1]]
        PBK = [("Praw", 0), ("Praw", 1), ("Pn", 0), ("Pn", 1)]

        def na_scores(lr_tile, r0, delta, hp):
            qcol = lr_tile * 64
            i = na_i["i"] % 4
            na_i["i"] += 1
            odd = (r0 % 2 == 1)
            ng = 5 if odd else 4
            W = 2 * ng * 64
            cur = na_i["cur"]
            if odd:
                if cur % 2 == 1:
                    cur = (cur + 1) % 4
                reg = psS[cur // 2][:, 0:W]
                skeys = [PK[4 + cur], PK[5 + cur]]
                na_i["cur"] = (cur + 2) % 4
            else:
                reg = psb[4 + cur][:, 0:W]
                skeys = [PK[4 + cur]]
                na_i["cur"] = (cur + 1) % 4
            sps = reg.rearrange("p (h g q) -> p h g q", h=2, g=ng)
            base = r0 - 1 if odd else r0
            rk = list(dict.fromkeys([("KT", (rr // 8) % RING_T) for rr in range(base, base + 2 * ng)]))
            for hh in range(2):
                qap, qk = q_ap(2 * hp + hh, qcol)
                for g in range(ng):
                    c0 = ring_row(base + 2 * g) * 64
                    P.pe(MM(sps[:, hh, g, :], KT[:, hp, c0:c0 + 128], qap), reads=rk + [qk], writes=skeys)
            P.act(ACTF(PB[i][:, 0:W], reg, AF.Exp), reads=skeys, writes=[PBK[i]])
            pn4 = PB[i][:, 0:W].rearrange("p (h g q) -> p h g q", h=2, g=ng)
            e2 = etab[:, 2 * hp:2 * hp + 2]
            if not odd:
                ei0 = 7 - delta
                P.dve(TT(pn4, pn4, e2[:, :, ei0:ei0 + 7:2, :], ALU.mult), reads=[PBK[i], "etab"], writes=[PBK[i]])
            else:
                P.dve(TT(pn4[:, :, 0, :], pn4[:, :, 0, :], e2[:, :, 14, :], ALU.mult), reads=[PBK[i], "etab"], writes=[PBK[i]])
                P.dve(TT(pn4[:, :, 1:4, :], pn4[:, :, 1:4, :], e2[:, :, 4:9:2, :], ALU.mult), reads=[PBK[i], "etab"], writes=[PBK[i]])
                P.dve(TT(pn4[:, :, 4, :], pn4[:, :, 4, :], e2[:, :, 15, :], ALU.mult), reads=[PBK[i], "etab"], writes=[PBK[i]])
            return (PBK[i], pn4, ng, base)

        def na_pv(desc, hp, pv_bank, row_half):
            pkey, pn4, ng, base = desc
            vk = list(dict.fromkeys([("VP", (rr // 8) % RING_T) for rr in range(base, base + 2 * ng)]))
            for hh in range(2):
                h = 2 * hp + hh
                o = psb[pv_bank][row_half * 64:(row_half + 1) * 64, (h % 4) * 65:(h % 4) * 65 + 65]
                rd = [pkey] + vk
                for g in range(ng):
                    vt = ring_row(base + 2 * g) // 2
                    P.pe(MM(o, pn4[:, hh, g, :], VP[:, vt, h, :], g == 0, False), reads=rd, writes=[PK[pv_bank]])
                P.pe(MM(o, Pm[:, row_half, h * 64:(h + 1) * 64], vmp[:, h, :], False, True), reads=[("Pm", row_half), "vmp"], writes=[PK[pv_bank]])

        def na_tile(T):
            items = []
            for sp in range(4):
                rows = [8 * T + 2 * sp, 8 * T + 2 * sp + 1]
                variants = []
                for r in rows:
                    u = r // R
                    lr = r % R
                    r0 = min(max(lr - 4, 0), R - 8)
                    v = [(u * R + r0, lr - r0)]
                    if R - 4 <= r < R + 4:
                        v.append((r - 4, 4))
                    variants.append(v)
                nvar = max(len(v) for v in variants)
                items.append(("meta", sp, rows))
                for vi in range(nvar):
                    for half in range(2):
                        grp = (sp, vi, half)
                        for rh, r in enumerate(rows):
                            r0, delta = variants[rh][min(vi, len(variants[rh]) - 1)]
                            for hp in (2 * half, 2 * half + 1):
                                items.append(("unit", r - 8 * T, r0, delta, hp, grp, rh))
                        items.append(("norm", grp, vi, half))
                items.append(("fin", sp, nvar))
            banks = {}
            q = []
            late = []
            DEPTH = 3

            def do_pv(ent):
                _, desc, hp, grp, rh = ent
                if grp not in banks:
                    banks[grp] = next_bank()
                na_pv(desc, hp, banks[grp], rh)

            def drain(maxpv):
                while sum(1 for e in q if e[0] == "pv") > maxpv:
                    while q and q[0][0] != "pv":
                        run(q.pop(0)[1])
                    do_pv(q.pop(0))
                    while late:
                        late.pop(0)()
                    while q and q[0][0] != "pv":
                        run(q.pop(0)[1])

            def run(it):
                if it[0] == "meta":
                    for rh, r in enumerate(it[2]):
                        na_meta_scores(r - 8 * T, rh)
                elif it[0] == "norm":
                    _, grp, vi, half = it
                    pvb = banks[grp]
                    pv = psb[pvb][:, 0:260].rearrange("p (h n) -> p h n", h=4)
                    P.dve(RECIP(st1[:, 4:8], pv[:, :, 64]), reads=[PK[pvb]], writes=["st1r"])
                    dst = nao[vi][:, half * 256:(half + 1) * 256].rearrange("p (h d) -> p h d", h=4)
                    P.dve(TT(dst, pv[:, :, 0:64], st1[:, 4:8].unsqueeze(2).to_broadcast([128, 4, 64]), ALU.mult),
                          reads=[PK[pvb], "st1r"] + NAOK[vi], writes=NAOK[vi])
                else:
                    _, sp, nvar = it
                    if nvar == 2:
                        P.dve(TS(nao[0][:, :], nao[0][:, :], ncont), reads=NAOK[0] + ["fl"], writes=NAOK[0])
                        P.dve(STT(nao[0][:, :], nao[1][:, :], cont, nao[0][:, :], ALU.mult, ALU.add), reads=NAOK[0] + NAOK[1] + ["fl"], writes=NAOK[0])
                    if dbg:
                        t0 = (8 * T + 2 * sp) * 64
                        P.dma(DMA(dbg_out["d_na"][t0:t0 + 128, :], nao[0][:, :]), reads=NAOK[0], semkey="dbg1")
                    P.dve(TT(gA[:, :], nao[0][:, :], zs[:, sp, :], ALU.mult), reads=NAOK[0] + [("zs", sp)], writes=["gA"])

                    def fin_b(sp=sp):
                        pT = pTv()
                        for fc in range(4):
                            P.pe(TR(pT[:, fc, :], gA[:, fc * 128:(fc + 1) * 128], ident[:, :]), reads=["gA", "ident"], writes=[PK[3]])
                        P.act(ACTF(gAT[:, :, sp * 128:(sp + 1) * 128], pT[:, 0:4, :], AF.Copy), reads=[PK[3]], writes=["gAT"])
                    late.append(fin_b)

            for it in items:
                if it[0] == "unit":
                    _, lrt, r0, delta, hp, grp, rh = it
                    desc = na_scores(lrt, r0, delta, hp)
                    q.append(("pv", desc, hp, grp, rh))
                    drain(DEPTH)
                elif not q:
                    run(it)
                else:
                    q.append(("act", it))
            drain(0)
            while q:
                run(q.pop(0)[1])
            while late:
                late.pop(0)()

        def mlstm_tile(T, filler=None):
            xt, xkey = xnT[T % 2], ("xnT", T % 2)
            u_of = T // TPU
            first_tile_of_unit = (T % TPU == 0)
            last_tile_of_unit = (T % TPU == TPU - 1)
            for (cch, fco) in ((C_MLQ, 0), (C_MLK, 4)):
                wsl, wkey = wload(cch)
                bcol = FM_B[cch]
                if T < NT - 1:
                    hb = small_cols(wsl, wkey, xnT[(T + 1) % 2], ("xnT", (T + 1) % 2), 0)
                    P.dve(TT(halo[:, fco:fco + 4, 1], psb[hb][:, 0:4], fm[:, bcol:bcol + 4], ALU.add), reads=[PK[hb], "fm"], writes=["halo"])
                    if last_tile_of_unit:
                        P.dve(TS(halo[:, fco:fco + 4, 1], halo[:, fco:fco + 4, 1], cont), reads=["halo", "fl"], writes=["halo"])
                else:
                    P.dve(MS(halo[:, fco:fco + 4, 1], 0.0), writes=["halo"])
                if T == 0:
                    P.dve(CP(halo[:, fco:fco + 4, 0], prem[:, fco:fco + 4]), reads=["prem"], writes=["halo"])
                elif first_tile_of_unit:
                    P.dve(TS(halo[:, fco:fco + 4, 0], carry[:, fco:fco + 4, 0], cont), reads=["carry", "fl"], writes=["halo"])
                    P.dve(STT(halo[:, fco:fco + 4, 0], prem[:, fco:fco + 4], ncont, halo[:, fco:fco + 4, 0], ALU.mult, ALU.add),
                          reads=["halo", "prem", "fl"], writes=["halo"])
                else:
                    P.dve(CP(halo[:, fco:fco + 4, 0], carry[:, fco:fco + 4, 0]), reads=["carry"], writes=["halo"])
                P.dve(TT(hadj[:, fco:fco + 4, :], halo[:, fco:fco + 4, :], fm[:, bcol:bcol + 4].unsqueeze(2).to_broadcast([128, 4, 2]), ALU.subtract),
                      reads=["halo", "fm"], writes=["hadj"])
                for fc in range(4):
                    b = next_bank()
                    for kc in range(8):
                        P.pe(MM(psb[b][:, :], wsl[:, kc, fc * 128:(fc + 1) * 128], xt[:, kc, :], kc == 0, kc == 7), reads=[wkey, xkey], writes=[PK[b]])
                    conv_silu(fco + fc, psb[b][:, :], PK[b], 512, bcol + fc, halo[:, fco + fc, 0:1], halo[:, fco + fc, 1:2],
                              qkT[:, fco + fc, :], ("qkT", fco + fc), save_last=carry[:, fco + fc, 0:1])

            def h_v(j, ps_ap, pkey):
                P.dve(TT(vml[:, j, :, 0:128], ps_ap.rearrange("p (h d) -> p h d", h=4),
                         bias_bc[:, TB[C_MLV]:TB[C_MLV] + 512].rearrange("p (h d) -> p h d", h=4), ALU.add),
                      reads=[pkey, "bias_bc"], writes=[("vml", j)])
            proj_T(xt, xkey, 4, C_MLV, h_v)

            def h_z(j, ps_ap, pkey):
                (za, zak), (zb_, zbk) = ((ftmp, ["ftmp"]), (tgA, ["tgA"])) if j % 2 == 0 else ((tgB, ["tgB"]), (h2, ["h2"]))
                P.dve(TT(za[:, :], ps_ap, bias_bc[:, TB[C_MLZ]:TB[C_MLZ] + 512], ALU.add), reads=[pkey, "bias_bc"], writes=zak)
                P.act(ACTF(zb_[:, :], za[:, :], AF.Tanh, scale=0.5), reads=zak, writes=zbk)
                P.dve(STT(zb_[:, :], zb_[:, :], 1.0, za[:, :], ALU.add, ALU.mult), reads=zak + zbk, writes=zbk)
                P.pool(TT(zs[:, j, :], zb_[:, :], hg_bc[:, :], ALU.mult), reads=zbk + ["hg_bc"], writes=[("zs", j)])
            proj_T(xt, xkey, 4, C_MLZ, h_z)

            def h_o(j, ps_ap, pkey):
                za, zak = (tgB, ["tgB"]) if j % 2 == 0 else (ftmp, ["ftmp"])
                P.dve(TT(za[:, :], ps_ap, bias_bc[:, TB[C_MLO]:TB[C_MLO] + 512], ALU.add), reads=[pkey, "bias_bc"], writes=zak)
                P.act(ACTF(oG[:, j, :], za[:, :], AF.Tanh, scale=0.5), reads=zak, writes=[("oG", j)])
            proj_T(xt, xkey, 4, C_MLO, h_o)
            def ml_X(c):
                slot = T * 4 + c
                csl = slice(c * 128, (c + 1) * 128)
                if first_tile_of_unit and c == 0:
                    if T == 0:
                        P.dve(CP(Cf[:], Cmeta[:, 0]), reads=[("Cmeta", 0)], writes=["Cf"])
                    else:
                        P.dve(TS(Cf[:], Cf[:], cont), reads=["Cf", "fl"], writes=["Cf"])
                        P.dve(STT(Cf[:], Cmeta[:, u_of], ncont, Cf[:], ALU.mult, ALU.add), reads=["Cf", ("Cmeta", u_of), "fl"], writes=["Cf"])
                    P.act(ACTF(Cfb[:], Cf[:], AF.Copy), reads=["Cf"], writes=["Cfb"])
                li = slot % 2
                P.dma(DMA(cbl[li][:], cbs[slot].rearrange("p (h n) -> p h n", h=4)), reads=[("cbs", slot)], writes=[("cbl", li)], semkey=("cbl", li))
                sps = psb[4][:, :].rearrange("p (h t) -> p h t", h=4)
                for h in range(4):
                    P.pe(MM(sps[:, h, :], qkT[:, 4 + h, csl], qkT[:, h, csl]), reads=[("qkT", 4 + h), ("qkT", h)], writes=[PK[4]])
                for h in range(4):
                    P.dve(STT(Sf[h][:, :], sps[:, h, :], SC[:, slot, h:h + 1], maskf[:, :], ALU.mult, ALU.mult),
                          reads=[PK[4], ("SC", slot), "maskf"], writes=[("Sf", h)])
                    P.dve(STT(Sb_[h][:, :], sps[:, h, :], SC[:, slot, 4 + h:5 + h], maskb[:, :], ALU.mult, ALU.mult),
                          reads=[PK[4], ("SC", slot), "maskb"], writes=[("Sb", h)])
                souts = state_prep(qkT[:, 4:8, csl], [("qkT", 4 + h) for h in range(4)], vml[:, c], ("vml", c), slot, 0)
                nbs = [5, 6, 7, next_bank()]
                for h in range(4):
                    nb = nbs[h]
                    nf = psb[nb][:, 0:129]
                    nbk = psb[nb][:, 129:258]
                    P.pe(MM(nf, Sf[h][:, :], vml[:, c, h, :], True, False), reads=[("Sf", h), ("vml", c)], writes=[PK[nb]])
                    P.pe(MM(nf, qkT[:, h, csl], Cfb[:, h, :], False, True), reads=[("qkT", h), "Cfb"], writes=[PK[nb]])
                    P.pe(MM(nbk, Sb_[h][:, :], vml[:, c, h, :], True, False), reads=[("Sb", h), ("vml", c)], writes=[PK[nb]])
                    P.pe(MM(nbk, qkT[:, h, csl], cbl[li][:, h, :], False, True), reads=[("qkT", h), ("cbl", li)], writes=[PK[nb]])
                state_apply(Cf, "Cf", souts, slot, 0)
                P.act(ACTF(Cfb[:], Cf[:], AF.Copy), reads=["Cf"], writes=["Cfb"])
                for h in range(4):
                    nb = nbs[h]
                    d0 = 16 + 4 * h
                    P.act(ACTF(st2[:, d0:d0 + 2], psb[nb][:, 128:258:129], AF.Abs), reads=[PK[nb]], writes=[("st1d", h)])
                for h in range(4):
                    d0 = 16 + 4 * h
                    P.dve(TT(st2[:, d0:d0 + 2], st2[:, d0:d0 + 2], SC[:, slot, 16 + h:21 + h:4], ALU.max), reads=[("st1d", h), ("SC", slot)], writes=[("st1d", h)])
                    P.dve(RECIP(st2[:, d0 + 2:d0 + 4], st2[:, d0:d0 + 2]), reads=[("st1d", h)], writes=[("st1e", h)])
                for h in range(4):
                    nb = nbs[h]
                    d0 = 16 + 4 * h
                    hs = slice(h * 128, (h + 1) * 128)
                    P.act(ACTF(hbuf[:, hs], psb[nb][:, 0:128], AF.Copy, scale=st2[:, d0 + 2:d0 + 3]), reads=[PK[nb], ("st1e", h)], writes=[("hbuf", h)])
                for h in range(4):
                    nb = nbs[h]
                    d0 = 16 + 4 * h
                    hs = slice(h * 128, (h + 1) * 128)
                    P.dve(STT(hbuf[:, hs], psb[nb][:, 129:257], st2[:, d0 + 3:d0 + 4], hbuf[:, hs], ALU.mult, ALU.add),
                          reads=[PK[nb], ("st1e", h), ("hbuf", h)], writes=[("hbuf", h)])
                if dbg:
                    t0 = slot * 128
                    P.dma(DMA(dbg_out["d_h"][t0:t0 + 128, :], hbuf[:, :]), reads=HBK, semkey="dbg2")

            def ml_Y1(c):
                P.dve(STT(h2[:, :], oG[:, c, :], 1.0, hbuf[:, :], ALU.add, ALU.mult), reads=[("oG", c)] + HBK, writes=["h2"])

            def ml_Y2(c):
                csl = slice(c * 128, (c + 1) * 128)
                h3 = h2[:, :].rearrange("p (h d) -> p h d", h=4)
                P.dve(lambda e, h3=h3: e.tensor_reduce(out=st1[:, 12:16], in_=h3, axis=AX.X, op=ALU.add), reads=["h2"], writes=["st1m"])
                for h in range(4):
                    P.act(ACTF(ftmp[:, h * 128:(h + 1) * 128], h2[:, h * 128:(h + 1) * 128], AF.Square, accum_out=st1[:, 8 + h:9 + h]),
                          reads=["h2"], writes=["ftmp", ("st1q", h)])
                SQK = [("st1q", h) for h in range(4)]
                P.dve(TS(st1[:, 12:16], st1[:, 12:16], 1.0 / 128), reads=["st1m"], writes=["st1m"])
                P.dve(TT(st1[:, 4:8], st1[:, 12:16], st1[:, 12:16], ALU.mult), reads=["st1m"], writes=["st1r"])
                P.dve(STT(st1[:, 8:12], st1[:, 8:12], 1.0 / 128, st1[:, 4:8], ALU.mult, ALU.subtract), reads=SQK + ["st1r"], writes=["st1v"])
                P.dve(TS(st1[:, 8:12], st1[:, 8:12], 4.0 * EPS, None, ALU.add), reads=["st1v"], writes=["st1v"])
                P.pool(TT(st1[:, 8:12], st1[:, 8:12], cst[:, 0:1].to_broadcast([128, 4]), ALU.pow), reads=["st1v", "cst"], writes=["st1v"])
                for h in range(4):
                    hs = slice(h * 128, (h + 1) * 128)
                    P.dve(TS(h2[:, hs], h2[:, hs], st1[:, 12 + h:13 + h], st1[:, 8 + h:9 + h], ALU.subtract, ALU.mult),
                          reads=["h2", "st1m", "st1v"], writes=["h2"])
                P.dve(TT(gB[:, :], h2[:, :], zs[:, c, :], ALU.mult), reads=["h2", ("zs", c)], writes=["gB"])

            def ml_Y2b(c):
                csl = slice(c * 128, (c + 1) * 128)
                pT = pTv()
                for fc in range(4):
                    P.pe(TR(pT[:, fc, :], gB[:, fc * 128:(fc + 1) * 128], ident[:, :]), reads=["gB", "ident"], writes=[PK[3]])
                P.act(ACTF(gBT[:, :, csl], pT[:, 0:4, :], AF.Copy), reads=[PK[3]], writes=["gBT"])

            def fill(k):
                if filler is not None:
                    for _ in range(k):
                        next(filler, None)

            for c in range(4):
                ml_X(c)
                if c >= 2:
                    ml_Y2b(c - 2)
                fill(2)
                if c >= 1:
                    ml_Y2(c - 1)
                ml_Y1(c)
            ml_Y2b(2)
            ml_Y2(3)
            ml_Y2b(3)
            fill(8)

        def out_branch(T, which):
            xt, xkey = xnT[T % 2], ("xnT", T % 2)
            (cbr, cg0, cg1, gsrc, gkey, tg, tgk, first) = ((C_WA, C_GA0, C_GA1, gAT, "gAT", tgA, "tgA", True) if which == 0 else
                                                          (C_WB, C_GB0, C_GB1, gBT, "gBT", tgB, "tgB", False))
            wbr_, kbr = wload(cbr)
            wbr = wbr_[:].rearrange("p a b -> p (a b)").rearrange("p (fc n) -> p fc n", fc=4)
            for half, cg in enumerate((cg0, cg1)):
                wG, kG = wload(cg)
                for fc in range(4):
                    n = half * 4 + fc
                    bG = next_bank()
                    for kc in range(8):
                        P.pe(MM(psb[bG][:, :], wG[:, kc, fc * 128:(fc + 1) * 128], xt[:, kc, :], kc == 0, kc == 7), reads=[kG, xkey], writes=[PK[bG]])
                    P.act(ACTF(tg[:, :], psb[bG][:, :], AF.Tanh, bias=fmh[:, FM_B[cg] + fc:FM_B[cg] + fc + 1], scale=0.5),
                          reads=[PK[bG], "fmh"], writes=[tgk])
                    by = next_bank()
                    for k4 in range(4):
                        P.pe(MM(psb[by][:, :], wbr[:, k4, n * 128:(n + 1) * 128], gsrc[:, k4, :], k4 == 0, k4 == 3), reads=[kbr, gkey], writes=[PK[by]])
                    if first:
                        P.dve(STT(ypT[:, n, :], tg[:, :], 1.0, psb[by][:, :], ALU.add, ALU.mult), reads=[tgk, PK[by]], writes=[("ypT", n)])
                    else:
                        P.dve(STT(tg[:, :], tg[:, :], 1.0, psb[by][:, :], ALU.add, ALU.mult), reads=[tgk, PK[by]], writes=[tgk])
                        P.pool(TT(ypT[:, n, :], ypT[:, n, :], tg[:, :], ALU.add), reads=[tgk, ("ypT", n)], writes=[("ypT", n)])
                    yield n

        def out_tile(T, hook=None, skip_a=False):
            xt, xkey = xnT[T % 2], ("xnT", T % 2)
            for which in ((1,) if skip_a else (0, 1)):
                for _ in out_branch(T, which):
                    pass
            wO0, kO0 = wload(C_WO0)
            wO1, kO1 = wload(C_WO1)
            YK = [("ypT", n) for n in range(8)]
            T2 = T + 2 if hook is not None else None
            ltok = None
            if T2 is not None:
                ltok = ln_A(xu[T2 * 512:T2 * 512 + 128, :])
            for j in range(4):
                i = j % 2
                t0 = T * 512 + j * 128
                P.dma(DMA(osb[i][:], xu[t0:t0 + 128, :]), writes=[("osb", i)], reads=[("y", "st", i)], semkey=("xres", i))
                b0 = next_bank()
                b1 = next_bank()
                for (b, wO, kO) in ((b0, wO0, kO0), (b1, wO1, kO1)):
                    for k8 in range(8):
                        P.pe(MM(psb[b][:, :], ypT[:, k8, j * 128:(j + 1) * 128], wO[:, k8, :], k8 == 0, k8 == 7), reads=YK + [kO], writes=[PK[b]])
                if T2 is not None:
                    ln_B(ltok, xnT[T2 % 2][:, :, j * 128:(j + 1) * 128], ("xnT", T2 % 2))
                    if j < 3:
                        ltok = ln_A(xu[T2 * 512 + (j + 1) * 128:T2 * 512 + (j + 2) * 128, :])
                c0 = 8 + 4 * i
                oa, oa2, ob, oc = ("o_a", i), ("o_a2", i), ("o_b", i), ("o_c", i)
                P.act(ACTF(gA[:, :], psb[b0][:, :], AF.Square, accum_out=st2[:, c0:c0 + 1]), reads=[PK[b0]], writes=["gA", oa])
                P.act(ACTF(gB[:, :], psb[b1][:, :], AF.Square, accum_out=st2[:, c0 + 3:c0 + 4]), reads=[PK[b1]], writes=["gB", oa2])
                P.dve(TT(st2[:, c0 + 1:c0 + 2], st2[:, c0:c0 + 1], st2[:, c0 + 3:c0 + 4], ALU.add), reads=[oa, oa2], writes=[ob])
                P.dve(TS(st2[:, c0 + 1:c0 + 2], st2[:, c0 + 1:c0 + 2], 1.0 / D, 4.0 * EPS, ALU.mult, ALU.add), reads=[ob], writes=[ob])
                P.pool(TT(st2[:, c0 + 2:c0 + 3], st2[:, c0 + 1:c0 + 2], cst[:, 0:1], ALU.pow), reads=[ob, "cst"], writes=[oc])
                P.dve(STT(tgA[:, :], psb[b0][:, :], st2[:, c0 + 2:c0 + 3], gpost_bc[:, 0:512], ALU.mult, ALU.mult),
                      reads=[PK[b0], oc, "gpost_bc"], writes=["tgA"])
                P.dve(STT(tgB[:, :], psb[b1][:, :], st2[:, c0 + 2:c0 + 3], gpost_bc[:, 512:1024], ALU.mult, ALU.mult),
                      reads=[PK[b1], oc, "gpost_bc"], writes=["tgB"])
                P.pool(TT(osb[i][:, 0:512], osb[i][:, 0:512], tgA[:, :], ALU.add), reads=[("osb", i), "tgA"], writes=[("osb", i)])
                P.pool(TT(osb[i][:, 512:1024], osb[i][:, 512:1024], tgB[:, :], ALU.add), reads=[("osb", i), "tgB"], writes=[("osb", i)])
                P.dma(DMA(y[t0:t0 + 128, :], osb[i][:, :]), reads=[("osb", i)], writes=[("y", "st", i)], semkey=("osb", i), queue="pool")

        def ln_sub(T2, j):
            load_norm_T(xu[T2 * 512 + j * 128:T2 * 512 + (j + 1) * 128, :], 1, xnT[T2 % 2][:, :, j * 128:(j + 1) * 128], ("xnT", T2 % 2))

        kv_proj(0)
        for T in range(NT):
            if T + 1 < NT:
                kv_proj(T + 1)
            na_q_z(T)
            na_tile(T)
            mlstm_tile(T, filler=out_branch(T, 0))
            out_tile(T, hook=(lambda j, T=T: ln_sub(T + 2, j)) if T + 2 < NT else None, skip_a=True)

        P.finalize(st)
    return nc


def _host_tables(na_rpb):
    rpb = np.asarray(na_rpb, np.float32).reshape(8, 15, 31)
    kc = np.arange(64)[:, None]
    qc = np.arange(64)[None, :]
    dc = np.clip(kc - qc + 15, 0, 30)
    tzv = np.ascontiguousarray(np.transpose(rpb[:, :, dc], (2, 0, 1, 3)))
    c0 = np.clip(qc - 8, 0, 48)
    valid = ((kc >= c0) & (kc < c0 + 16)).astype(np.float32)
    cm = np.concatenate([valid, valid], axis=0)
    return tzv.astype(np.float32), cm.astype(np.float32)


def _make_in_maps(inputs, units_per_core, conts):
    tzv, cm = _host_tables(inputs["na_rpb"])
    common = {
        "meta": np.ascontiguousarray(inputs["meta_tokens"], np.float32),
        "g_pre": np.ascontiguousarray(inputs["g_pre"], np.float32).reshape(1, D),
        "w_in": np.ascontiguousarray(inputs["w_in"], np.float32).reshape(D, NIN),
        "b_in": np.ascontiguousarray(inputs["b_in"], np.float32).reshape(1, NIN),
        "tz": tzv, "cmask": cm,
        "conv_w": np.ascontiguousarray(inputs["ml_conv_w"], np.float32).reshape(3, D),
        "head_g": np.ascontiguousarray(inputs["ml_head_g"], np.float32).reshape(1, 512),
        "w_a": np.ascontiguousarray(inputs["w_a"], np.float32).reshape(512, D),
        "w_b": np.ascontiguousarray(inputs["w_b"], np.float32).reshape(512, D),
        "w_out": np.ascontiguousarray(inputs["w_out"], np.float32).reshape(D, D),
        "g_post": np.ascontiguousarray(inputs["g_post"], np.float32).reshape(1, D),
    }
    maps = []
    for xu, cont in zip(units_per_core, conts):
        fl = np.zeros((128, 4), np.float32)
        fl[:, 0] = cont
        fl[:, 1] = 1.0 - cont
        fl[0:4, 2] = 1.0
        m = dict(common)
        m["xu"] = np.ascontiguousarray(xu, np.float32)
        m["flags"] = fl
        maps.append(m)
    return maps


_NC_CACHE = {}


def kernel(x_prompt, x_sample, meta_tokens, g_pre, w_in, b_in, na_rpb, ml_conv_w, ml_head_g, w_a, w_b, w_out, g_post):
    x_prompt = np.asarray(x_prompt, np.float32)
    x_sample = np.asarray(x_sample, np.float32)
    inputs = dict(meta_tokens=meta_tokens, g_pre=g_pre, w_in=w_in, b_in=b_in, na_rpb=na_rpb, ml_conv_w=ml_conv_w,
                  ml_head_g=ml_head_g, w_a=w_a, w_b=w_b, w_out=w_out, g_post=g_post)
    R = 64
    units, conts = [], []
    for s in range(2):
        units.append(x_sample[s])
        conts.append(1.0)
    for c in range(4):
        units.append(np.concatenate([x_prompt[2 * c], x_prompt[2 * c + 1]], axis=0))
        conts.append(0.0)
    for c in range(2):
        units.append(np.concatenate([x_prompt[2 * c], x_prompt[2 * c + 1]], axis=0))
        conts.append(0.0)
    if R not in _NC_CACHE:
        _NC_CACHE[R] = build(R)
    nc = _NC_CACHE[R]
    in_maps = _make_in_maps(inputs, units, conts)
    res = run_bass_kernel_spmd(nc, in_maps, core_ids=list(range(8)))
    outs = [np.asarray(r["y"], np.float32) for r in res.results]
    y_sample = np.stack([outs[0], outs[1]], axis=0)
    yp = []
    for c in range(4):
        yp.append(outs[2 + c][:4096])
        yp.append(outs[2 + c][4096:])
    y_prompt = np.stack(yp, axis=0)
    return (y_prompt, y_sample)
```
